# Optimizing a Trainium2 kernel written in Bass

```python
import math
import jax, jax.numpy as jnp
from jax import lax
import numpy as np

D_MODEL = 2048
BATCH = 4
SEQ = 2048
DEPTH = 4

PE_DIM = 256
GRID_W = 64
N_MIXERS = 3
HEAD_DIM = 128
MIX_WIDTH = D_MODEL
EPS = 1e-6
NEG_INF = -1e30

A_HEADS = MIX_WIDTH // HEAD_DIM
A_KV_HEADS = max(A_HEADS // 4, 1)
A_WINDOW = 128
A_BLOCK = 128
A_Q = A_HEADS * HEAD_DIM
A_KV = A_KV_HEADS * HEAD_DIM
A_IN = A_Q + 2 * A_KV + MIX_WIDTH

B_HEADS = MIX_WIDTH // HEAD_DIM
NB_WIN_H = 8
NB_WIN_W = 16
B_IN = 3 * B_HEADS * HEAD_DIM + MIX_WIDTH

C_HEADS = MIX_WIDTH // (2 * HEAD_DIM)
C_QK_DIM = HEAD_DIM
C_V_DIM = 2 * HEAD_DIM
C_BLOCK = 128
C_QK = C_HEADS * 2 * C_QK_DIM
C_V = C_HEADS * C_V_DIM
C_IN = 2 * C_QK + C_V + MIX_WIDTH

N_A = (DEPTH + 2) // 3
N_B = (DEPTH + 1) // 3
N_C = DEPTH // 3

kernel_name = "hybrid_interleaved_bidir_encoder"


def rmsnorm(x, g):
    xf = x.astype(jnp.float32)
    y = xf * lax.rsqrt(jnp.mean(xf * xf, axis=-1, keepdims=True) + EPS)
    return (y * g.astype(jnp.float32)).astype(x.dtype)


def alibi_slopes(n_heads):
    return jnp.asarray(np.array([2.0 ** (-8.0 * (h + 1) / n_heads) for h in range(n_heads)], dtype=np.float32))


def window_gqa(q, k, v, sink):
    B, S, Hq, Dh = q.shape
    Hkv = k.shape[2]
    G = Hq // Hkv
    nb = S // A_BLOCK
    pad = ((0, 0), (A_BLOCK, A_BLOCK), (0, 0), (0, 0))
    kb = jnp.pad(k, pad).reshape(B, nb + 2, A_BLOCK, Hkv, Dh)
    vb = jnp.pad(v, pad).reshape(B, nb + 2, A_BLOCK, Hkv, Dh)
    kw = jnp.concatenate([kb[:, :-2], kb[:, 1:-1], kb[:, 2:]], axis=2)
    vw = jnp.concatenate([vb[:, :-2], vb[:, 1:-1], vb[:, 2:]], axis=2)
    qb = q.reshape(B, nb, A_BLOCK, Hkv, G, Dh)
    s = jnp.einsum('bnqhgd,bnkhd->bnhgqk', qb, kw).astype(jnp.float32) * (Dh ** -0.5)
    qi = jnp.arange(A_BLOCK)[:, None]
    kr = jnp.arange(3 * A_BLOCK)[None, :]
    rel = A_BLOCK + qi - kr
    blk = jnp.arange(nb)[:, None, None]
    kpos = (blk - 1) * A_BLOCK + kr[None]
    valid = (jnp.abs(rel) <= A_WINDOW)[None] & (kpos >= 0) & (kpos < S)
    bias = -alibi_slopes(Hq)[:, None, None] * jnp.abs(rel).astype(jnp.float32)[None]
    s = s + bias.reshape(Hkv, G, A_BLOCK, 3 * A_BLOCK)[None, None]
    s = jnp.where(valid[None, :, None, None], s, NEG_INF)
    sink_l = jnp.broadcast_to(sink.astype(jnp.float32).reshape(Hkv, G)[None, None, :, :, None, None],
                              s.shape[:-1] + (1,))
    pr = jax.nn.softmax(jnp.concatenate([s, sink_l], axis=-1), axis=-1)[..., :-1]
    o = jnp.einsum('bnhgqk,bnkhd->bnqhgd', pr.astype(v.dtype), vw)
    return o.reshape(B, S, Hq * Dh)


def neighborhood_attn(q, k, v, rpb):
    B, S, H, Dh = q.shape
    rows = S // GRID_W
    kh = min(NB_WIN_H, rows)
    kw = NB_WIN_W
    qg = q.reshape(B, rows, GRID_W, H, Dh)
    kg = k.reshape(B, rows, GRID_W, H, Dh)
    vg = v.reshape(B, rows, GRID_W, H, Dh)
    col = jnp.arange(GRID_W)
    col_start = jnp.clip(col - kw // 2, 0, GRID_W - kw)
    col_valid = (col[None, :] >= col_start[:, None]) & (col[None, :] < col_start[:, None] + kw)
    dx_idx = jnp.clip(col[None, :] - col[:, None] + NB_WIN_W - 1, 0, 2 * NB_WIN_W - 2)
    scale = Dh ** -0.5
    rpb_f = rpb.astype(jnp.float32)

    def row_step(r):
        rs = jnp.clip(r - kh // 2, 0, rows - kh)
        kr = lax.dynamic_slice_in_dim(kg, rs, kh, axis=1)
        vr = lax.dynamic_slice_in_dim(vg, rs, kh, axis=1)
        qr = lax.dynamic_index_in_dim(qg, r, axis=1, keepdims=False)
        s = jnp.einsum('bqhd,bywhd->bhqyw', qr, kr).astype(jnp.float32) * scale
        dy_idx = rs + jnp.arange(kh) - r + NB_WIN_H - 1
        bias = rpb_f[:, dy_idx[None, :, None], dx_idx[:, None, :]]
        s = jnp.where(col_valid[:, None, :], s + bias[None], NEG_INF)
        pr = jax.nn.softmax(s.reshape(B, H, GRID_W, kh * GRID_W), axis=-1).reshape(s.shape)
        return jnp.einsum('bhqyw,bywhd->bqhd', pr.astype(v.dtype), vr)

    o = lax.map(row_step, jnp.arange(rows))
    return jnp.transpose(o, (1, 0, 2, 3, 4)).reshape(B, S, H * Dh)


def diff_attn(q, k, v, lam, lambda_init, subln):
    B, S, H, _, Dqk = q.shape
    nb = S // C_BLOCK
    qb = jnp.transpose(q.reshape(B, nb, C_BLOCK, H, 2, Dqk), (1, 0, 2, 3, 4, 5))
    slopes = alibi_slopes(H)
    kpos = jnp.arange(S)
    scale = Dqk ** -0.5

    def block_step(args):
        n, qblk = args
        s = jnp.einsum('bqhmd,bkhmd->bhmqk', qblk, k).astype(jnp.float32) * scale
        qpos = n * C_BLOCK + jnp.arange(C_BLOCK)
        dist = jnp.abs(qpos[:, None] - kpos[None, :]).astype(jnp.float32)
        s = s - slopes[:, None, None, None] * dist
        pr = jax.nn.softmax(s, axis=-1)
        pdiff = pr[:, :, 0] - lam * pr[:, :, 1]
        return jnp.einsum('bhqk,bkhd->bqhd', pdiff.astype(v.dtype), v)

    o = lax.map(block_step, (jnp.arange(nb), qb))
    o = jnp.transpose(o, (1, 0, 2, 3, 4)).reshape(B, S, H, v.shape[-1])
    o = rmsnorm(o, subln) * (1.0 - lambda_init)
    return o.reshape(B, S, H * v.shape[-1])


def setup_inputs(seed: int = 0) -> dict:
    key = jax.random.key(seed)
    ks = jax.random.split(key, 16)
    f32 = jnp.float32
    x = jax.random.normal(ks[0], (BATCH, SEQ, D_MODEL), f32)
    p = jax.random.normal(ks[1], (DEPTH, BATCH, SEQ, PE_DIM), f32)
    norm_pre = 1.0 + 0.05 * jax.random.normal(ks[2], (DEPTH, D_MODEL), f32)
    norm_post = 1.0 + 0.05 * jax.random.normal(ks[3], (DEPTH, D_MODEL), f32)
    w_out = jax.random.normal(ks[4], (DEPTH, MIX_WIDTH, D_MODEL), f32) * MIX_WIDTH ** -0.5
    pe_proj = jax.random.normal(ks[5], (DEPTH, PE_DIM, D_MODEL), f32) * PE_DIM ** -0.5
    pe_gate = jax.random.normal(ks[6], (DEPTH, D_MODEL, D_MODEL), f32) * D_MODEL ** -0.5
    a_w_in = jax.random.normal(ks[7], (N_A, D_MODEL, A_IN), f32) * D_MODEL ** -0.5
    a_sink = 0.5 * jax.random.normal(ks[8], (N_A, A_HEADS), f32)
    b_w_in = jax.random.normal(ks[9], (N_B, D_MODEL, B_IN), f32) * D_MODEL ** -0.5
    b_rpb = 0.1 * jax.random.normal(ks[10], (N_B, B_HEADS, 2 * NB_WIN_H - 1, 2 * NB_WIN_W - 1), f32)
    c_w_in = jax.random.normal(ks[11], (N_C, D_MODEL, C_IN), f32) * D_MODEL ** -0.5
    c_lambda = 0.1 * jax.random.normal(ks[12], (N_C, 4, C_QK_DIM), f32)
    c_subln = 1.0 + 0.05 * jax.random.normal(ks[13], (N_C, C_V_DIM), f32)
    return {"x": x, "p": p, "norm_pre": norm_pre, "norm_post": norm_post, "w_out": w_out,
            "pe_proj": pe_proj, "pe_gate": pe_gate, "a_w_in": a_w_in, "a_sink": a_sink,
            "b_w_in": b_w_in, "b_rpb": b_rpb, "c_w_in": c_w_in, "c_lambda": c_lambda,
            "c_subln": c_subln}


def reference(x, p, norm_pre, norm_post, w_out, pe_proj, pe_gate, a_w_in, a_sink,
              b_w_in, b_rpb, c_w_in, c_lambda, c_subln):
    B, S, _ = x.shape
    for i in range(DEPTH):
        kind = i % N_MIXERS
        j = i // N_MIXERS
        h = rmsnorm(x, norm_pre[i])
        if kind == 0:
            proj = h @ a_w_in[j]
            q, k, v, g = jnp.split(proj, [A_Q, A_Q + A_KV, A_Q + 2 * A_KV], axis=-1)
            o = window_gqa(q.reshape(B, S, A_HEADS, HEAD_DIM), k.reshape(B, S, A_KV_HEADS, HEAD_DIM),
                           v.reshape(B, S, A_KV_HEADS, HEAD_DIM), a_sink[j])
        elif kind == 1:
            proj = h @ b_w_in[j]
            w = B_HEADS * HEAD_DIM
            q, k, v, g = jnp.split(proj, [w, 2 * w, 3 * w], axis=-1)
            o = neighborhood_attn(q.reshape(B, S, B_HEADS, HEAD_DIM), k.reshape(B, S, B_HEADS, HEAD_DIM),
                                  v.reshape(B, S, B_HEADS, HEAD_DIM), b_rpb[j])
        else:
            proj = h @ c_w_in[j]
            q, k, v, g = jnp.split(proj, [C_QK, 2 * C_QK, 2 * C_QK + C_V], axis=-1)
            lam_p = c_lambda[j].astype(jnp.float32)
            lambda_init = 0.8 - 0.6 * math.exp(-0.3 * i)
            lam = jnp.exp(jnp.sum(lam_p[0] * lam_p[1])) - jnp.exp(jnp.sum(lam_p[2] * lam_p[3])) + lambda_init
            o = diff_attn(q.reshape(B, S, C_HEADS, 2, C_QK_DIM), k.reshape(B, S, C_HEADS, 2, C_QK_DIM),
                          v.reshape(B, S, C_HEADS, C_V_DIM), lam, lambda_init, c_subln[j])
        y = (o * jax.nn.silu(g)) @ w_out[i]
        x = x + rmsnorm(y, norm_post[i])
        gate = jax.nn.sigmoid((x @ pe_gate[i]).astype(jnp.float32)).astype(x.dtype)
        x = x + gate * (p[i] @ pe_proj[i])
    return x
```

```python
import math
import numpy as np
import ml_dtypes
import concourse.bass as bass
import concourse.mybir as mybir
from concourse.bass_utils import run_bass_kernel_spmd

F32 = mybir.dt.float32
BF16 = mybir.dt.bfloat16
ALU = mybir.AluOpType
AF = mybir.ActivationFunctionType
AX = mybir.AxisListType

D = 2048
T = 1024
KC = 16
DEPTH = 4
SCALE = 128.0 ** -0.5
EPS = 1e-6
BIG = 30000.0
SLABW = 256
N_DMA_SEMS = 12
NCORES = 8

IN_W = {0: (2048, 512, 512, 2048), 1: (2048, 2048, 2048, 2048), 2: (2048, 2048, 2048, 2048)}


def alibi(n):
    return [2.0 ** (-8.0 * (h + 1) / n) for h in range(n)]


class Op:
    __slots__ = ("eng", "fn", "deps", "is_dma", "signal", "ticket", "idx", "inc")

    def __init__(self, eng, fn, is_dma):
        self.eng = eng
        self.fn = fn
        self.deps = []
        self.is_dma = is_dma
        self.signal = False
        self.ticket = None
        self.idx = -1
        self.inc = 16


class Sched:
    def __init__(self, nc, sync_same_engine=True):
        self.nc = nc
        self.ops = []
        self.last_writer = {}
        self.readers = {}
        self.sync_same_engine = sync_same_engine
        self.engs = {"pe": nc.tensor, "act": nc.scalar, "dve": nc.vector,
                     "pool": nc.gpsimd, "sp": nc.sync}

    def add(self, eng, fn, reads=(), writes=(), dma=False, inc=16):
        op = Op(eng, fn, dma)
        op.inc = inc
        op.idx = len(self.ops)
        deps = {}
        for k in reads:
            w = self.last_writer.get(k)
            if w is not None:
                deps[w.idx] = w
            if isinstance(k, str) and k.startswith("ps"):
                for r in self.readers.get(k, ()):
                    if r.eng != eng:
                        deps[r.idx] = r
        for k in writes:
            w = self.last_writer.get(k)
            if w is not None:
                deps[w.idx] = w
            for r in self.readers.get(k, ()):
                deps[r.idx] = r
        op.deps = list(deps.values())
        for k in writes:
            self.last_writer[k] = op
            self.readers[k] = []
        for k in reads:
            lst = self.readers.setdefault(k, [])
            if not dma:
                lst[:] = [r for r in lst if r.is_dma or r.eng != eng]
            lst.append(op)
        self.ops.append(op)
        return op

    def _skip(self, d, op):
        return (d.eng == op.eng and not op.is_dma and not d.is_dma
                and (d.eng == "pe" or not self.sync_same_engine))

    def emit(self):
        nc = self.nc
        for op in self.ops:
            for d in op.deps:
                if d.is_dma or self._skip(d, op):
                    continue
                d.signal = True
        sems = {e: nc.alloc_semaphore(name=f"s_{e}") for e in self.engs}
        dma_sems = {q: [nc.alloc_semaphore(name=f"d_{q}_{i}") for i in range(N_DMA_SEMS)]
                    for q in ("sp", "pool", "act")}
        dma_cnt = {q: [0] * N_DMA_SEMS for q in dma_sems}
        dma_rr = {q: 0 for q in dma_sems}
        cnt = {e: 0 for e in self.engs}
        waited = {e: {} for e in self.engs}

        def wait(eng, sem, val):
            key = id(sem)
            if waited[eng].get(key, 0) >= val:
                return
            waited[eng][key] = val
            self.engs[eng].wait_ge(sem, val)

        for op in self.ops:
            e = op.eng
            for d in op.deps:
                if d.is_dma:
                    wait(e, d.ticket[0], d.ticket[1])
                elif not self._skip(d, op):
                    wait(e, d.ticket[0], d.ticket[1])
            if op.is_dma:
                i = dma_rr[e]
                dma_rr[e] = (i + 1) % N_DMA_SEMS
                sem = dma_sems[e][i]
                if dma_cnt[e][i] > 0:
                    wait(e, sem, dma_cnt[e][i])
                ins = op.fn()
                dma_cnt[e][i] += op.inc
                ins.then_inc(sem, op.inc)
                op.ticket = (sem, dma_cnt[e][i])
            else:
                ins = op.fn()
                if op.signal:
                    cnt[e] += 1
                    ins.then_inc(sems[e], 1)
                    op.ticket = (sems[e], cnt[e])
        for q in dma_sems:
            for i in range(N_DMA_SEMS):
                if dma_cnt[q][i] > 0:
                    wait("sp", dma_sems[q][i], dma_cnt[q][i])
        for e2 in self.engs:
            if cnt[e2] > 0:
                wait("sp", sems[e2], cnt[e2])
        return len(self.ops)


class Prog:
    def __init__(self, phases, fused, dbg=False):
        self.phases = phases
        self.fused = fused
        self.nc = nc = bass.Bass("TRN2", target_bir_lowering=False)
        self.S = Sched(nc)
        self.ext_in = []
        self.ext_out = []
        self.dram = {}
        self._alloc()

    def din(self, name, shape, dt=F32):
        t = self.nc.dram_tensor(name, list(shape), dt, kind="ExternalInput")
        self.dram[name] = t
        self.ext_in.append(name)
        return t

    def dout(self, name, shape, dt=F32):
        t = self.nc.dram_tensor(name, list(shape), dt, kind="ExternalOutput")
        self.dram[name] = t
        self.ext_out.append(name)
        return t

    def dint(self, name, shape, dt=F32):
        t = self.nc.dram_tensor(name, list(shape), dt)
        self.dram[name] = t
        return t

    def sb(self, name, shape, dt):
        return self.nc.alloc_sbuf_tensor(name, list(shape), dt)

    def _alloc(self):
        nc = self.nc
        layers = sorted({l for _, l in self.phases})
        self.layers = layers
        kinds = {l % 3 for l in layers}
        self.x_in = self.din("x_in", [D, T])
        self.x_out = self.dout("x_out", [D, T])
        self.gpre_d = self.din("gpre", [128, DEPTH * KC])
        self.gpost_d = self.din("gpost", [128, DEPTH * KC])
        self.pT_d, self.w_out_d, self.pe_proj_d, self.pe_gate_d, self.w_in_d = {}, {}, {}, {}, {}
        for ph, l in self.phases:
            if ph == "P":
                self.w_in_d[l] = self.din(f"w_in{l}", [D, sum(IN_W[l % 3])])
            else:
                self.pT_d[l] = self.din(f"pT{l}", [256, T])
                self.w_out_d[l] = self.din(f"w_out{l}", [D, D])
                self.pe_proj_d[l] = self.din(f"pe_proj{l}", [256, D])
                self.pe_gate_d[l] = self.din(f"pe_gate{l}", [D, D])
        self.sink_d = self.din("a_sink_bc", [128, 32])
        self.tabA_d = self.din("tabA", [4, 128, 1536])
        self.edge_d = self.din("edgeA", [128, 2])
        self.rpr_d = self.din("rpr", [16, 64, 15 * 127])
        self.colmask_d = self.din("colmaskB", [128, 1920])
        self.rowmask_d = self.din("rowmaskB", [2, 128], BF16)
        self.rowind_d = self.din("rowindB", [2, 128], BF16)
        self.lam_d = self.din("c_lambda_bc", [128, 512])
        self.subln_d = self.din("c_subln_t", [128, 2])
        self.Town_d = self.din("TownC", [128, 1920])
        self.Toth_d = self.din("TothC", [128, 1920])
        self.q_w, self.q_r, self.g_w, self.g_r = {}, {}, {}, {}
        self.k_w, self.k_r, self.v_w, self.v_r = {}, {}, {}, {}
        self.k_oth, self.v_oth, self.k_all, self.v_all = {}, {}, {}, {}
        for l in layers:
            _, dk, dv, _ = IN_W[l % 3]
            hasP = ("P", l) in self.phases
            hasA = ("AE", l) in self.phases
            if hasP and hasA:
                assert self.fused
                self.q_w[l] = self.q_r[l] = self.dint(f"q{l}", [D, T], BF16)
                self.g_w[l] = self.g_r[l] = self.dint(f"g{l}", [D, T], BF16)
                self.k_w[l] = self.k_r[l] = self.dint(f"k{l}", [dk, T], BF16)
                self.v_w[l] = self.v_r[l] = self.dint(f"v{l}", [T, dv], BF16)
                self.k_all[l] = self.dint(f"kall{l}", [2 * dk, T], BF16)
                self.v_all[l] = self.dint(f"vall{l}", [2 * T, dv], BF16)
                self.k_oth[l] = self.dint(f"koth{l}", [dk, T], BF16)
                self.v_oth[l] = self.dint(f"voth{l}", [T, dv], BF16)
            else:
                if hasP:
                    self.q_w[l] = self.dout(f"q{l}_o", [D, T], BF16)
                    self.g_w[l] = self.dout(f"g{l}_o", [D, T], BF16)
                    self.k_w[l] = self.dout(f"k{l}_o", [dk, T], BF16)
                    self.v_w[l] = self.dout(f"v{l}_o", [T, dv], BF16)
                if hasA:
                    self.q_r[l] = self.din(f"q{l}_i", [D, T], BF16)
                    self.g_r[l] = self.din(f"g{l}_i", [D, T], BF16)
                    self.k_r[l] = self.din(f"k{l}_i", [dk, T], BF16)
                    self.v_r[l] = self.din(f"v{l}_i", [T, dv], BF16)
                    self.k_oth[l] = self.din(f"ko{l}_i", [dk, T], BF16)
                    self.v_oth[l] = self.din(f"vo{l}_i", [T, dv], BF16)
        self.yT = self.dint("yT", [D, T], F32)
        self.xT = self.sb("xT", [128, KC, T], F32)
        self.hT = self.sb("hT", [128, KC, T], BF16)
        self.wslab = [self.sb(f"wslab{i}", [128, KC, SLABW], BF16) for i in range(2)]
        self.ones = self.sb("ones", [128, 128], BF16)
        self.zeros = self.sb("zeros", [128, 16], F32)
        self.gpre = self.sb("gpre_s", [128, DEPTH * KC], F32)
        self.gpost = self.sb("gpost_s", [128, DEPTH * KC], F32)
        self.sq = [self.sb(f"sq{i}", [128, 2, 512], BF16) for i in range(2)]
        self.rstd = [self.sb(f"rstd{i}", [128, 512], F32) for i in range(2)]
        self.stage = [self.sb(f"stage{i}", [128, T], BF16) for i in range(2)]
        self.ystage = [self.sb(f"ystage{i}", [128, T], F32) for i in range(2)]
        self.gate_st = [self.sb(f"gate{i}", [128, 512], F32) for i in range(2)]
        self.peslab = [self.sb(f"peslab{i}", [128, 2, SLABW], BF16) for i in range(2)]
        self.pTb = self.sb("pTb", [128, 2, T], BF16)
        self.kt_own = self.sb("kt_own", [128, T], BF16)
        self.kt_oth = self.sb("kt_oth", [128, T], BF16)
        self.qb = [self.sb(f"qb{i}", [128, T], BF16) for i in range(2)]
        self.gb = self.sb("gb", [128, 2, T], BF16)
        self.v_own_s = self.sb("v_own", [128, 8, 256], BF16)
        self.v_oth_s = self.sb("v_oth", [128, 8, 256], BF16)
        self.tab = self.sb("tab", [128, 3840], F32)
        self.tmp = self.sb("tmp", [128, 3, 512], F32)
        self.pt = self.sb("pt", [128, 3, 512], BF16)
        self.on0 = self.sb("on0", [128, 2, 512], F32)
        self.o_s = self.sb("o_s", [128, 2, 512], F32)
        self.rz = self.sb("rz", [128, 512], F32)
        self.zt = self.sb("zt", [128, 512], F32)
        self.es = self.sb("es", [128, 32], F32)
        self.sink_s = self.sb("sink_s", [128, 32], F32)
        self.edge = self.sb("edge", [128, 2], F32)
        self.lam_s = self.sb("lam_s", [128, 512], F32)
        self.lam_t = self.sb("lam_t", [128, 8], F32)
        self.subln_s = self.sb("subln_s", [128, 2], F32)
        self.rowmask = self.sb("rowmask", [2, 128], BF16)
        self.rowind = self.sb("rowind", [2, 128], BF16)
        self.ps = [nc.alloc_psum_tensor(f"ps{i}", [128, 512], F32) for i in range(8)]
        self.wi = 0
        self.par = None

    def dma(self, q, out, in_, reads, writes):
        eng = {"sp": self.nc.sync, "pool": self.nc.gpsimd, "act": self.nc.scalar}[q]
        self.S.add(q, lambda: eng.dma_start(out=out, in_=in_), reads, writes, dma=True)

    def act(self, out, in_, func, reads, writes, bias=0.0, scale=1.0):
        nc = self.nc
        self.S.add("act", lambda: nc.scalar.activation(out=out, in_=in_, func=func, bias=bias, scale=scale),
                   reads, writes)

    def tt(self, eng, out, in0, in1, op, reads, writes):
        e = {"dve": self.nc.vector, "pool": self.nc.gpsimd}[eng]
        self.S.add(eng, lambda: e.tensor_tensor(out=out, in0=in0, in1=in1, op=op), reads, writes)

    def stt(self, out, in0, scalar, in1, op0, op1, reads, writes, eng="dve"):
        e = {"dve": self.nc.vector, "pool": self.nc.gpsimd}[eng]
        self.S.add(eng, lambda: e.scalar_tensor_tensor(out=out, in0=in0, scalar=scalar, in1=in1,
                                                        op0=op0, op1=op1), reads, writes)

    def ts(self, out, in0, s1, s2, op0, op1, reads, writes):
        nc = self.nc
        if s2 is None:
            self.S.add("dve", lambda: nc.vector.tensor_scalar(out=out, in0=in0, scalar1=s1, scalar2=None,
                                                              op0=op0), reads, writes)
        else:
            self.S.add("dve", lambda: nc.vector.tensor_scalar(out=out, in0=in0, scalar1=s1, scalar2=s2,
                                                              op0=op0, op1=op1), reads, writes)

    def recip(self, out, in_, reads, writes):
        nc = self.nc
        self.S.add("dve", lambda: nc.vector.reciprocal(out=out, in_=in_), reads, writes)

    def mm(self, out, lhsT, rhs, start, stop, reads, writes):
        nc = self.nc
        self.S.add("pe", lambda: nc.tensor.matmul(out, lhsT, rhs, start=start, stop=stop), reads, writes)

    def memset(self, ap, val, writes):
        nc = self.nc
        self.S.add("dve", lambda: nc.vector.memset(ap, val), (), writes)

    def plan_slabs(self):
        slabs = []
        for ph, l in self.phases:
            if ph == "P":
                for s in range(sum(IN_W[l % 3]) // SLABW):
                    slabs.append((self.w_in_d[l], s))
            elif ph == "AE":
                for s in range(D // SLABW):
                    slabs.append((self.w_out_d[l], s))
                for s in range(D // SLABW):
                    slabs.append((self.pe_gate_d[l], s))
        self.slabs = slabs
        self.slab_issued = 0
        self.slab_used = 0

    def _issue_slab(self):
        i = self.slab_issued
        if i >= len(self.slabs):
            return
        w, s = self.slabs[i]
        src = w[:, s * SLABW:(s + 1) * SLABW].rearrange("(kc p) n -> p kc n", p=128)
        self.dma("pool", self.wslab[i % 2][:], src, (), [f"wslab{i % 2}"])
        self.slab_issued += 1

    def next_slab(self):
        i = self.slab_used
        while self.slab_issued <= min(i + 1, len(self.slabs) - 1):
            self._issue_slab()
        self.slab_used += 1
        return self.wslab[i % 2], f"wslab{i % 2}"

    def build(self):
        nc = self.nc
        self.plan_slabs()
        if self.fused:
            self.par = nc.sync.partition_id() % 2
        self.memset(self.ones[:], 1.0, ["ones"])
        self.memset(self.zeros[:], 0.0, ["zeros"])
        self.dma("sp", self.gpre[:], self.gpre_d.ap(), (), ["gpre"])
        self.dma("sp", self.gpost[:], self.gpost_d.ap(), (), ["gpost"])
        xv = self.x_in.ap().rearrange("(kc p) n -> p kc n", p=128)
        for kc in range(KC):
            self.dma("sp", self.xT[:, kc, :], xv[:, kc, :], (), [f"x{kc}_0", f"x{kc}_1"])
        for ph, l in self.phases:
            if ph == "P":
                self.phase_P(l)
                if self.fused:
                    self.exchange(l)
            else:
                kind = l % 3
                if kind == 0:
                    self.attn_A(l)
                elif kind == 1:
                    self.attn_B(l)
                else:
                    self.attn_C(l)
                self.phase_E(l)
        ov = self.x_out.ap().rearrange("(kc p) n -> p kc n", p=128)
        for kc in range(KC):
            self.dma("sp", ov[:, kc, :], self.xT[:, kc, :], [f"x{kc}_0", f"x{kc}_1"], [])
        n = self.S.emit()
        return n

    def stats_rstd(self, tt, src_fn, src_keys, nchunks, inv_n, rstd_i):
        ps = self.ps[7]
        for kc in range(nchunks):
            sqb = self.sq[kc % 2]
            self.act(sqb[:, 0, :], src_fn(kc), AF.Square, src_keys(kc), [f"sq{kc % 2}"])
            self.mm(ps[:], self.ones[:], sqb[:, 0, :], kc == 0, kc == nchunks - 1,
                    ["ones", f"sq{kc % 2}"], ["ps7"])
        r = self.rstd[rstd_i]
        self.act(r[:], ps[:], AF.Sqrt, ["ps7"], [f"rstd{rstd_i}"], bias=EPS, scale=inv_n)
        self.recip(r[:], r[:], [f"rstd{rstd_i}"], [f"rstd{rstd_i}"])

    def phase_P(self, l):
        kind = l % 3
        wq, wk, wv, wg = IN_W[kind]
        for tt in range(2):
            sl = slice(tt * 512, (tt + 1) * 512)
            self.stats_rstd(tt, lambda kc: self.xT[:, kc, sl], lambda kc: [f"x{kc}_{tt}"], KC, 1.0 / D, tt)
            for kc in range(KC):
                self.stt(self.hT[:, kc, sl], self.xT[:, kc, sl], self.gpre[:, l * KC + kc:l * KC + kc + 1],
                         self.rstd[tt][:], ALU.mult, ALU.mult,
                         [f"x{kc}_{tt}", "gpre", f"rstd{tt}"], [f"h{kc}_{tt}"])
        nslab = (wq + wk + wv + wg) // SLABW
        ev = 0
        for s in range(nslab):
            c0 = s * SLABW
            slab, skey = self.next_slab()
            if c0 < wq:
                seg, dst, r0 = "q", self.q_w[l], c0
            elif c0 < wq + wk:
                seg, dst, r0 = "k", self.k_w[l], c0 - wq
            elif c0 < wq + wk + wv:
                seg, dst, r0 = "v", self.v_w[l], c0 - wq - wk
            else:
                seg, dst, r0 = "g", self.g_w[l], c0 - wq - wk - wv
            if seg != "v":
                for mi in range(SLABW // 128):
                    st = self.stage[ev % 2]
                    stk = f"stage{ev % 2}"
                    for tt in range(2):
                        sl = slice(tt * 512, (tt + 1) * 512)
                        pi = (ev * 2 + tt) % 4
                        ps = self.ps[pi]
                        for kc in range(KC):
                            self.mm(ps[:], slab[:, kc, mi * 128:(mi + 1) * 128], self.hT[:, kc, sl],
                                    kc == 0, kc == KC - 1, [skey, f"h{kc}_{tt}"], [f"ps{pi}"])
                        if seg == "g":
                            self.act(st[:, sl], ps[:], AF.Silu, [f"ps{pi}"], [stk + f"_{tt}"])
                        elif (ev + tt) % 2 == 0:
                            self.act(st[:, sl], ps[:], AF.Copy, [f"ps{pi}"], [stk + f"_{tt}"])
                        else:
                            nc = self.nc
                            self.S.add("dve", (lambda o=st[:, sl], i=ps[:]: nc.vector.tensor_copy(out=o, in_=i)),
                                       [f"ps{pi}"], [stk + f"_{tt}"])
                    row = r0 + mi * 128
                    self.dma("sp", dst[row:row + 128, :], st[:], [stk + "_0", stk + "_1"], [f"{seg}{l}"])
                    ev += 1
            else:
                for tb in range(8):
                    st = self.stage[ev % 2]
                    stk = f"stage{ev % 2}"
                    pi = ev % 4
                    ps = self.ps[pi]
                    for kc in range(KC):
                        self.mm(ps[:, 0:SLABW], self.hT[:, kc, tb * 128:(tb + 1) * 128], slab[:, kc, :],
                                kc == 0, kc == KC - 1, [skey, f"h{kc}_{tb // 4}"], [f"ps{pi}"])
                    nc = self.nc
                    self.S.add("dve", (lambda o=st[:, 0:SLABW], i=ps[:, 0:SLABW]: nc.vector.tensor_copy(out=o, in_=i)),
                               [f"ps{pi}"], [stk + "_0", stk + "_1"])
                    self.dma("sp", dst[tb * 128:(tb + 1) * 128, r0:r0 + SLABW], st[:, 0:SLABW],
                             [stk + "_0", stk + "_1"], [f"v{l}"])
                    ev += 1

    def exchange(self, l):
        nc = self.nc
        groups = [[2 * i, 2 * i + 1] for i in range(NCORES // 2)]
        for nm, src, allt, otht in (("k", self.k_w[l], self.k_all[l], self.k_oth[l]),
                                    ("v", self.v_w[l], self.v_all[l], self.v_oth[l])):
            rows, cols = src.shape
            rc = min(rows, (2 << 20) // (cols * 2))
            nchunk = rows // rc
            for c in range(nchunk):
                self.S.add("pool", (lambda s=src[c * rc:(c + 1) * rc, :], d=allt[c * 2 * rc:(c + 1) * 2 * rc, :]:
                                    nc.gpsimd.collective_compute(
                    "AllGather", mybir.AluOpType.bypass, replica_groups=groups,
                    ins=[s.opt()], outs=[d.opt()])), [f"{nm}{l}"], [f"{nm}all{l}"], dma=True, inc=1)
            gv = allt.ap().rearrange("(c r f) n -> c r f n", c=nchunk, r=2)
            self.dma("sp", otht.ap().rearrange("(c f) n -> c f n", c=nchunk),
                     gv[:, bass.ds(1 - self.par, 1), :, :].rearrange("c 1 f n -> c f n"),
                     [f"{nm}all{l}"], [f"{nm}oth{l}"])

    def oth_k(self, l, f0, c0, c1):
        return self.k_oth[l][f0:f0 + 128, c0:c1], ([f"koth{l}"] if self.fused else [])

    def oth_v(self, l, kb0, nkb, d0, dn):
        return (self.v_oth[l][kb0 * 128:(kb0 + nkb) * 128, d0:d0 + dn].rearrange("(kb p) d -> p kb d", p=128),
                [f"voth{l}"] if self.fused else [])

    def attn_A(self, l):
        j = l // 3
        self.dma("sp", self.sink_s[:], self.sink_d.ap(), (), ["sink_s"])
        self.dma("sp", self.edge[:], self.edge_d.ap(), (), ["edge"])
        for h in range(16):
            c = j * 16 + h
            self.act(self.es[:, c:c + 1], self.zeros[:, 0:1], AF.Exp, ["zeros", "sink_s"], ["es"],
                     bias=self.sink_s[:, c:c + 1], scale=1.0)
        tabv = self.tab[:, 0:1536].rearrange("p (j n) -> p j n", j=3)
        q4 = self.qb[0][:, 0:512]
        g4 = self.gb[:, 0, 0:512]
        for kvh in range(4):
            f0 = kvh * 128
            self.dma("sp", self.kt_own[:], self.k_r[l][f0:f0 + 128, :], [f"k{l}"], ["kt_own"])
            ap, rk = self.oth_k(l, f0, 896, 1024)
            self.dma("sp", self.kt_oth[:, 0:128], ap, rk, ["kt_oth"])
            ap, rk = self.oth_k(l, f0, 0, 128)
            self.dma("sp", self.kt_oth[:, 128:256], ap, rk, ["kt_oth"])
            self.dma("sp", self.v_own_s[:, :, 0:128],
                     self.v_r[l][:, f0:f0 + 128].rearrange("(kb p) d -> p kb d", p=128), [f"v{l}"], ["v_own"])
            ap, rk = self.oth_v(l, 7, 1, f0, 128)
            self.dma("sp", self.v_oth_s[:, 0:1, 0:128], ap, rk, ["v_oth"])
            ap, rk = self.oth_v(l, 0, 1, f0, 128)
            self.dma("sp", self.v_oth_s[:, 1:2, 0:128], ap, rk, ["v_oth"])
            self.dma("sp", self.tab[:, 0:1536], self.tabA_d[kvh], (), ["tabL"])
            for qi in range(8):
                qs = slice(qi * 128, (qi + 1) * 128)
                qsrc = self.q_r[l][kvh * 512:(kvh + 1) * 512, qs].rearrange("(g p) n -> p g n", p=128)
                self.dma("sp", q4.rearrange("p (g n) -> p g n", g=4), qsrc, [f"q{l}"], ["qb0"])
                gsrc = self.g_r[l][kvh * 512:(kvh + 1) * 512, qs].rearrange("(g p) n -> p g n", p=128)
                self.dma("sp", g4.rearrange("p (g n) -> p g n", g=4), gsrc, [f"g{l}"], ["gb"])
                blocks = []
                for jj in range(3):
                    kb = qi + jj - 1
                    if kb < 0:
                        blocks.append((self.kt_oth[:, 0:128], "kt_oth", self.v_oth_s[:, 0, 0:128], "v_oth", 0))
                    elif kb > 7:
                        blocks.append((self.kt_oth[:, 128:256], "kt_oth", self.v_oth_s[:, 1, 0:128], "v_oth", 1))
                    else:
                        blocks.append((self.kt_own[:, kb * 128:(kb + 1) * 128], "kt_own",
                                       self.v_own_s[:, kb, 0:128], "v_own", None))
                for jj, (kap, kk, vap, vk, edge) in enumerate(blocks):
                    self.mm(self.ps[jj][:], kap, q4, True, True, [kk, "qb0"], [f"ps{jj}"])
                    self.tt("dve", self.tmp[:, jj, :], self.ps[jj][:], tabv[:, jj, :], ALU.add,
                            [f"ps{jj}", "tabL"], [f"tmp{jj}"])
                    if edge is None:
                        self.act(self.pt[:, jj, :], self.tmp[:, jj, :], AF.Exp, [f"tmp{jj}"], [f"pt{jj}"],
                                 scale=SCALE)
                    else:
                        self.act(self.pt[:, jj, :], self.tmp[:, jj, :], AF.Exp, [f"tmp{jj}", "edge"], [f"pt{jj}"],
                                 bias=self.edge[:, edge:edge + 1], scale=SCALE)
                for jj, (kap, kk, vap, vk, edge) in enumerate(blocks):
                    self.mm(self.ps[4][:], vap, self.pt[:, jj, :], jj == 0, jj == 2, [vk, f"pt{jj}"], ["ps4"])
                for jj in range(3):
                    self.mm(self.ps[5][:], self.ones[:], self.pt[:, jj, :], jj == 0, jj == 2,
                            ["ones", f"pt{jj}"], ["ps5"])
                for g in range(4):
                    c = j * 16 + kvh * 4 + g
                    gs = slice(g * 128, (g + 1) * 128)
                    self.ts(self.zt[:, gs], self.ps[5][:, gs], self.es[:, c:c + 1], None, ALU.add, None,
                            ["ps5", "es"], ["zt"])
                self.recip(self.rz[:], self.zt[:], ["zt"], ["rz"])
                self.tt("dve", self.o_s[:, 0, :], self.ps[4][:], self.rz[:], ALU.mult, ["ps4", "rz"], ["o_s"])
                okeys = [f"h{kvh * 4 + g}_{qi // 4}" for g in range(4)]
                self.tt("pool", self.hT[:, kvh * 4:(kvh + 1) * 4, qs],
                        self.o_s[:, 0, :].rearrange("p (g n) -> p g n", g=4),
                        g4.rearrange("p (g n) -> p g n", g=4), ALU.mult, ["o_s", "gb"], okeys)

    def attn_B(self, l):
        nc = self.nc
        self.dma("sp", self.rowmask[:], self.rowmask_d.ap(), (), ["rowmask"])
        self.dma("sp", self.rowind[:], self.rowind_d.ap(), (), ["rowind"])
        TB = self.tab[:, 0:1920].rearrange("p (j n) -> p j n", j=30)
        CM = self.tab[:, 1920:3840].rearrange("p (j n) -> p j n", j=30)
        self.dma("sp", self.tab[:, 1920:3840], self.colmask_d.ap(), (), ["tabH"])
        self.memset(self.tab[:, 0:1920], 0.0, ["tabL"])
        for h in range(16):
            f0 = h * 128
            self.dma("sp", self.kt_own[:], self.k_r[l][f0:f0 + 128, :], [f"k{l}"], ["kt_own"])
            ap, rk = self.oth_k(l, f0, 0, 256)
            self.dma("sp", self.kt_oth[:, 0:256], ap, rk, ["kt_oth"])
            ap, rk = self.oth_k(l, f0, 768, 1024)
            self.dma("sp", self.kt_oth[:, 768:1024], ap, rk, ["kt_oth"])
            self.dma("sp", self.v_own_s[:, :, 0:128],
                     self.v_r[l][:, f0:f0 + 128].rearrange("(kb p) d -> p kb d", p=128), [f"v{l}"], ["v_own"])
            ap, rk = self.oth_v(l, 0, 2, f0, 128)
            self.dma("sp", self.v_oth_s[:, 0:2, 0:128], ap, rk, ["v_oth"])
            ap, rk = self.oth_v(l, 6, 2, f0, 128)
            self.dma("sp", self.v_oth_s[:, 6:8, 0:128], ap, rk, ["v_oth"])
            self.dma("sp", self.qb[0][:], self.q_r[l][f0:f0 + 128, :], [f"q{l}"], ["qb0"])
            self.dma("sp", self.gb[:, 0, :], self.g_r[l][f0:f0 + 128, :], [f"g{l}"], ["gb"])
            for a in range(2):
                src = bass.AP(self.rpr_d, h * 64 * 1905 + 63, [[1904, 64], [127, 15], [1, 64]])
                self.dma("sp", TB[a * 64:(a + 1) * 64, a + 7:a + 22, :], src, (), ["tabL"])
            for a in range(2):
                self.stt(TB[a * 64:(a + 1) * 64, a + 7:a + 22, :], TB[a * 64:(a + 1) * 64, a + 7:a + 22, :],
                         1.0 / SCALE, CM[a * 64:(a + 1) * 64, a + 7:a + 22, :], ALU.mult, ALU.add,
                         ["tabL", "tabH"], ["tabL"])
            for tt in range(2):
                sl = slice(tt * 512, (tt + 1) * 512)
                if tt == 0:
                    blks = [("own", kb, 2 * kb + 7) for kb in range(6)] + [("oth", kb, 2 * kb - 9) for kb in (6, 7)]
                else:
                    blks = [("own", kb, 2 * kb - 1) for kb in range(2, 8)] + [("oth", kb, 15 + 2 * kb) for kb in (0, 1)]
                nb = len(blks)
                for i, (who, kb, d0) in enumerate(blks):
                    pidx = tt * 8 + i
                    if who == "own":
                        kap, kk, vap, vk = self.kt_own[:, kb * 128:(kb + 1) * 128], "kt_own", self.v_own_s[:, kb, 0:128], "v_own"
                    else:
                        kap, kk, vap, vk = self.kt_oth[:, kb * 128:(kb + 1) * 128], "kt_oth", self.v_oth_s[:, kb, 0:128], "v_oth"
                    b = i % 2
                    self.mm(self.ps[b][:], kap, self.qb[0][:, sl], True, False, [kk, "qb0"], [f"ps{b}"])
                    self.mm(self.ps[b][:], self.rowind[:], bass.AP(self.rowmask, pidx * 8, [[128, 2], [1, 8], [0, 64]]), False, True,
                            ["rowind", "rowmask"], [f"ps{b}"])
                    j0 = 21 - d0
                    self.tt("dve", self.tmp[:, b, :], self.ps[b][:],
                            TB[:, j0:j0 + 8, :].rearrange("p j n -> p (j n)"), ALU.add,
                            [f"ps{b}", "tabL"], [f"tmp{b}"])
                    self.act(self.pt[:, b, :], self.tmp[:, b, :], AF.Exp, [f"tmp{b}"], [f"pt{b}"], scale=SCALE)
                    self.mm(self.ps[4][:], vap, self.pt[:, b, :], i == 0, i == nb - 1, [vk, f"pt{b}"], ["ps4"])
                    self.mm(self.ps[5][:], self.ones[:], self.pt[:, b, :], i == 0, i == nb - 1,
                            ["ones", f"pt{b}"], ["ps5"])
                self.recip(self.rz[:], self.ps[5][:], ["ps5"], ["rz"])
                self.tt("dve", self.o_s[:, 0, :], self.ps[4][:], self.rz[:], ALU.mult, ["ps4", "rz"], ["o_s"])
                self.tt("pool", self.hT[:, h, sl], self.o_s[:, 0, :], self.gb[:, 0, sl], ALU.mult,
                        ["o_s", "gb"], [f"h{h}_{tt}"])

    def attn_C(self, l):
        nc = self.nc
        j = l // 3
        lam_init = 0.8 - 0.6 * math.exp(-0.3 * l)
        slopes = alibi(8)
        self.dma("sp", self.lam_s[:], self.lam_d.ap(), (), ["lam_s"])
        self.dma("sp", self.subln_s[:], self.subln_d.ap(), (), ["subln_s"])
        self.dma("sp", self.tab[:, 0:1920], self.Town_d.ap(), (), ["tabL"])
        self.dma("sp", self.tab[:, 1920:3840], self.Toth_d.ap(), (), ["tabH"])
        lt = self.lam_t
        for i in range(2):
            self.tt("dve", self.zt[:, 0:128], self.lam_s[:, (2 * i) * 128:(2 * i + 1) * 128],
                    self.lam_s[:, (2 * i + 1) * 128:(2 * i + 2) * 128], ALU.mult, ["lam_s"], ["zt"])
            self.S.add("dve", (lambda i=i: nc.vector.reduce_sum(out=lt[:, i:i + 1], in_=self.zt[:, 0:128], axis=AX.X)),
                       ["zt"], ["lam_t"])
        self.act(lt[:, 2:4], lt[:, 0:2], AF.Exp, ["lam_t"], ["lam_t"])
        self.tt("dve", lt[:, 4:5], lt[:, 3:4], lt[:, 2:3], ALU.subtract, ["lam_t"], ["lam_t"])
        self.ts(lt[:, 5:6], lt[:, 4:5], -lam_init, None, ALU.add, None, ["lam_t"], ["neglam"])
        self.ts(self.subln_s[:], self.subln_s[:], 1.0 - lam_init, None, ALU.mult, None, ["subln_s"], ["subln_s"])
        neglam = lt[:, 5:6]
        Town = self.tab[:, 0:1920]
        Toth = self.tab[:, 1920:3840]
        for h in range(8):
            self.dma("sp", self.v_own_s[:], self.v_r[l][:, h * 256:(h + 1) * 256].rearrange("(kb p) d -> p kb d", p=128),
                     [f"v{l}"], ["v_own"])
            ap, rk = self.oth_v(l, 0, 8, h * 256, 256)
            self.dma("sp", self.v_oth_s[:], ap, rk, ["v_oth"])
            self.dma("sp", self.gb[:], self.g_r[l][h * 256:(h + 1) * 256, :].rearrange("(c p) n -> p c n", p=128),
                     [f"g{l}"], ["gb"])
            for m in range(2):
                f0 = h * 256 + m * 128
                ktown = self.kt_own if m == 0 else self.qb[1]
                ktk = "kt_own" if m == 0 else "qb1"
                self.dma("sp", ktown[:], self.k_r[l][f0:f0 + 128, :], [f"k{l}"], [ktk])
            for tt in range(2):
                sl = slice(tt * 512, (tt + 1) * 512)
                for m in range(2):
                    f0 = h * 256 + m * 128
                    ktown = self.kt_own if m == 0 else self.qb[1]
                    ktk = "kt_own" if m == 0 else "qb1"
                    ap, rk = self.oth_k(l, f0, 0, T)
                    self.dma("sp", self.kt_oth[:], ap, rk, ["kt_oth"])
                    self.dma("sp", self.qb[0][:, 0:512], self.q_r[l][f0:f0 + 128, sl], [f"q{l}"], ["qb0"])
                    po = 2 + 3 * m
                    for i in range(16):
                        who = "own" if i < 8 else "oth"
                        kb = i % 8
                        if who == "own":
                            kap, kk, vap, vk, Tt = ktown[:, kb * 128:(kb + 1) * 128], ktk, self.v_own_s, "v_own", Town
                        else:
                            kap, kk, vap, vk, Tt = self.kt_oth[:, kb * 128:(kb + 1) * 128], "kt_oth", self.v_oth_s, "v_oth", Toth
                        b = i % 2
                        self.mm(self.ps[b][:], kap, self.qb[0][:, 0:512], True, True, [kk, "qb0"], [f"ps{b}"])
                        off = tt * 512 - kb * 128 + 896
                        self.stt(self.tmp[:, b, :], Tt[:, off:off + 512], slopes[h], self.ps[b][:],
                                 ALU.mult, ALU.add, [f"ps{b}", "tabL", "tabH"], [f"tmp{b}"])
                        self.act(self.pt[:, b, :], self.tmp[:, b, :], AF.Exp, [f"tmp{b}"], [f"pt{b}"], scale=SCALE)
                        for hf in range(2):
                            self.mm(self.ps[po + hf][:], vap[:, kb, hf * 128:(hf + 1) * 128], self.pt[:, b, :],
                                    i == 0, i == 15, [vk, f"pt{b}"], [f"ps{po + hf}"])
                        self.mm(self.ps[po + 2][:], self.ones[:], self.pt[:, b, :], i == 0, i == 15,
                                ["ones", f"pt{b}"], [f"ps{po + 2}"])
                    self.recip(self.rz[:], self.ps[po + 2][:], [f"ps{po + 2}"], ["rz"])
                    if m == 0:
                        for hf in range(2):
                            self.tt("dve", self.on0[:, hf, :], self.ps[po + hf][:], self.rz[:], ALU.mult,
                                    [f"ps{po + hf}", "rz"], ["on0"])
                    else:
                        for hf in range(2):
                            self.tt("dve", self.o_s[:, hf, :], self.ps[po + hf][:], self.rz[:], ALU.mult,
                                    [f"ps{po + hf}", "rz"], ["o_s"])
                        self.stt(self.on0[:], self.o_s[:], neglam, self.on0[:], ALU.mult, ALU.add,
                                 ["o_s", "on0", "neglam"], ["on0"])
                sqb = self.sq[0]
                self.act(sqb[:], self.on0[:], AF.Square, ["on0"], ["sq0"])
                for hf in range(2):
                    self.mm(self.ps[7][:], self.ones[:], sqb[:, hf, :], hf == 0, hf == 1, ["ones", "sq0"], ["ps7"])
                self.act(self.rz[:], self.ps[7][:], AF.Sqrt, ["ps7"], ["rz"], bias=EPS, scale=1.0 / 256)
                self.recip(self.rz[:], self.rz[:], ["rz"], ["rz"])
                for hf in range(2):
                    self.stt(self.on0[:, hf, :], self.on0[:, hf, :], self.subln_s[:, hf:hf + 1], self.rz[:],
                             ALU.mult, ALU.mult, ["on0", "subln_s", "rz"], ["on0"])
                self.tt("pool", self.hT[:, 2 * h:2 * h + 2, sl], self.on0[:], self.gb[:, :, sl], ALU.mult,
                        ["on0", "gb"], [f"h{2 * h}_{tt}", f"h{2 * h + 1}_{tt}"])

    def phase_E(self, l):
        nc = self.nc
        ev = 0
        for s in range(D // SLABW):
            slab, skey = self.next_slab()
            for mi in range(SLABW // 128):
                m = s * (SLABW // 128) + mi
                ys = self.ystage[ev % 2]
                ysk = f"ystage{ev % 2}"
                for tt in range(2):
                    sl = slice(tt * 512, (tt + 1) * 512)
                    pi = (ev * 2 + tt) % 4
                    ps = self.ps[pi]
                    for kc in range(KC):
                        self.mm(ps[:], slab[:, kc, mi * 128:(mi + 1) * 128], self.hT[:, kc, sl],
                                kc == 0, kc == KC - 1, [skey, f"h{kc}_{tt}"], [f"ps{pi}"])
                    self.S.add("dve", (lambda o=ys[:, sl], i=ps[:]: nc.vector.tensor_copy(out=o, in_=i)),
                               [f"ps{pi}"], [ysk + f"_{tt}"])
                    sqb = self.sq[tt]
                    self.act(sqb[:, 0, :], ys[:, sl], AF.Square, [ysk + f"_{tt}"], [f"sq{tt}"])
                    self.mm(self.ps[6 + tt][:], self.ones[:], sqb[:, 0, :], m == 0, m == KC - 1,
                            ["ones", f"sq{tt}"], [f"ps{6 + tt}"])
                self.dma("sp", self.yT[m * 128:(m + 1) * 128, :], ys[:], [ysk + "_0", ysk + "_1"], ["yT"])
                ev += 1
        for tt in range(2):
            r = self.rstd[tt]
            self.act(r[:], self.ps[6 + tt][:], AF.Sqrt, [f"ps{6 + tt}"], [f"rstd{tt}"], bias=EPS, scale=1.0 / D)
            self.recip(r[:], r[:], [f"rstd{tt}"], [f"rstd{tt}"])
        for kc in range(KC):
            ys = self.ystage[kc % 2]
            ysk = f"ystage{kc % 2}"
            self.dma("sp", ys[:], self.yT[kc * 128:(kc + 1) * 128, :], ["yT"], [ysk + "_0", ysk + "_1"])
            for tt in range(2):
                sl = slice(tt * 512, (tt + 1) * 512)
                self.stt(ys[:, sl], ys[:, sl], self.gpost[:, l * KC + kc:l * KC + kc + 1], self.rstd[tt][:],
                         ALU.mult, ALU.mult, [ysk + f"_{tt}", "gpost", f"rstd{tt}"], [ysk + f"_{tt}"])
                self.tt("dve", self.xT[:, kc, sl], self.xT[:, kc, sl], ys[:, sl], ALU.add,
                        [f"x{kc}_{tt}", ysk + f"_{tt}"], [f"x{kc}_{tt}"])
                self.act(self.hT[:, kc, sl], self.xT[:, kc, sl], AF.Copy, [f"x{kc}_{tt}"], [f"h{kc}_{tt}"])
        self.dma("pool", self.pTb[:], self.pT_d[l].ap().rearrange("(c p) n -> p c n", p=128), (), ["pTb"])
        ev = 0
        for s in range(D // SLABW):
            slab, skey = self.next_slab()
            pes = self.peslab[s % 2]
            pek = f"peslab{s % 2}"
            self.dma("pool", pes[:], self.pe_proj_d[l][:, s * SLABW:(s + 1) * SLABW].rearrange("(c p) n -> p c n", p=128),
                     (), [pek])
            for mi in range(SLABW // 128):
                m = s * (SLABW // 128) + mi
                for tt in range(2):
                    sl = slice(tt * 512, (tt + 1) * 512)
                    pi = (ev * 2 + tt) % 4
                    ps = self.ps[pi]
                    pp = self.ps[4 + (ev * 2 + tt) % 2]
                    ppk = f"ps{4 + (ev * 2 + tt) % 2}"
                    for kc in range(KC):
                        self.mm(ps[:], slab[:, kc, mi * 128:(mi + 1) * 128], self.hT[:, kc, sl],
                                kc == 0, kc == KC - 1, [skey, f"h{kc}_{tt}"], [f"ps{pi}"])
                    for c in range(2):
                        self.mm(pp[:], pes[:, c, mi * 128:(mi + 1) * 128], self.pTb[:, c, sl], c == 0, c == 1,
                                [pek, "pTb"], [ppk])
                    gs = self.gate_st[(ev * 2 + tt) % 2]
                    gk = f"gate{(ev * 2 + tt) % 2}"
                    self.act(gs[:], ps[:], AF.Sigmoid, [f"ps{pi}"], [gk])
                    self.tt("dve", gs[:], gs[:], pp[:], ALU.mult, [gk, ppk], [gk])
                    self.tt("dve", self.xT[:, m, sl], self.xT[:, m, sl], gs[:], ALU.add,
                            [f"x{m}_{tt}", gk], [f"x{m}_{tt}"])
                ev += 1


def _const_tables():
    sl = alibi(16)
    r = np.arange(128)[:, None]
    c = np.arange(128)[None, :]
    tabA = np.zeros((4, 128, 3, 4, 128), np.float32)
    for kvh in range(4):
        for j in range(3):
            rel = c - r + (1 - j) * 128
            ok = np.abs(rel) <= 128
            for g in range(4):
                h = kvh * 4 + g
                tabA[kvh, :, j, g, :] = np.where(ok, -sl[h] * np.abs(rel), -BIG) / SCALE
    tabA = tabA.reshape(4, 128, 1536)
    kx = np.arange(64)[:, None]
    qx = np.arange(64)[None, :]
    cs = np.clip(qx - 8, 0, 48)
    okc = (kx >= cs) & (kx < cs + 16)
    cm = np.where(okc, 0.0, -BIG / SCALE).astype(np.float32)
    colmask = np.broadcast_to(cm[None, :, None, :], (2, 64, 30, 64)).reshape(128, 1920).copy()
    rowind = np.zeros((2, 128), np.float32)
    rowind[0, :64] = 1
    rowind[1, 64:] = 1
    u = np.arange(1920)[None, :]
    rr = np.arange(128)[:, None]
    Town = (-np.abs(u - 896 - rr) / SCALE).astype(np.float32)
    Toth = [(-np.abs(dl + u - 896 - rr) / SCALE).astype(np.float32) for dl in (-1024, 1024)]
    return tabA, colmask, rowind.astype(ml_dtypes.bfloat16), Town, Toth


def _rowmask(half):
    rm = np.full((2, 16, 8), -BIG, np.float32)
    for tt in range(2):
        if tt == 0:
            blks = [("own", kb) for kb in range(6)] + [("oth", kb) for kb in (6, 7)]
        else:
            blks = [("own", kb) for kb in range(2, 8)] + [("oth", kb) for kb in (0, 1)]
        for i, (who, kb) in enumerate(blks):
            base = half * 16 if who == "own" else (1 - half) * 16
            for a in range(2):
                ky = base + 2 * kb + a
                for e in range(8):
                    qy = half * 16 + tt * 8 + e
                    rs = min(max(qy - 4, 0), 24)
                    if rs <= ky < rs + 8:
                        rm[a, tt * 8 + i, e] = 0.0
    return rm.reshape(2, 128).astype(ml_dtypes.bfloat16)


def _core_inputs(c, inputs, x_state):
    b, half = c // 2, c % 2
    tabA, colmask, rowind, Town, Toth = _CONST
    ts = slice(half * T, (half + 1) * T)
    rpb = inputs["b_rpb"][0]
    rpr = np.zeros((16, 15, 127), np.float32)
    rpr[:, :, 48:79] = rpb[:, ::-1, ::-1]
    rpr = np.ascontiguousarray(np.broadcast_to(rpr.reshape(16, 1, 1905), (16, 64, 1905)))
    edge = np.zeros((128, 2), np.float32)
    edge[:, half] = -BIG
    m = {
        "x_in": x_state[c],
        "gpre": np.ascontiguousarray(inputs["norm_pre"].reshape(DEPTH, KC, 128).transpose(2, 0, 1).reshape(128, DEPTH * KC)),
        "gpost": np.ascontiguousarray(inputs["norm_post"].reshape(DEPTH, KC, 128).transpose(2, 0, 1).reshape(128, DEPTH * KC)),
    }
    for l in range(DEPTH):
        kind, j = l % 3, l // 3
        m[f"w_in{l}"] = inputs[("a_w_in", "b_w_in", "c_w_in")[kind]][j]
        m[f"pT{l}"] = np.ascontiguousarray(inputs["p"][l, b, ts, :].T)
        m[f"w_out{l}"] = inputs["w_out"][l]
        m[f"pe_proj{l}"] = inputs["pe_proj"][l]
        m[f"pe_gate{l}"] = inputs["pe_gate"][l]
    m.update({
        "a_sink_bc": np.ascontiguousarray(np.broadcast_to(inputs["a_sink"].reshape(1, 32), (128, 32))),
        "tabA": tabA, "edgeA": edge, "rpr": rpr, "colmaskB": colmask,
        "rowmaskB": _rowmask(half), "rowindB": rowind,
        "c_lambda_bc": np.ascontiguousarray(np.broadcast_to(inputs["c_lambda"][0].reshape(1, 512), (128, 512))),
        "c_subln_t": np.ascontiguousarray(inputs["c_subln"][0].reshape(2, 128).T),
        "TownC": Town, "TothC": Toth[half],
    })
    return m


_CONST = _const_tables()
_PROG_CACHE = {}


def _get_prog(phases, fused):
    key = (tuple(phases), fused)
    if key not in _PROG_CACHE:
        p = Prog(list(phases), fused)
        p.build()
        _PROG_CACHE[key] = p
    return _PROG_CACHE[key]


SPLIT_LAUNCHES = [[("P", 0)], [("AE", 0), ("P", 1)], [("AE", 1), ("P", 2)], [("AE", 2), ("P", 3)], [("AE", 3)]]
FUSED_LAUNCH = [("P", 0), ("AE", 0), ("P", 1), ("AE", 1), ("P", 2), ("AE", 2), ("P", 3), ("AE", 3)]
MODE = "fused"


def run_launch(phases, fused, inputs, x_state, extra):
    prog = _get_prog(phases, fused)
    in_maps = []
    for c in range(NCORES):
        m = _core_inputs(c, inputs, x_state)
        m.update(extra[c])
        in_maps.append({k: v for k, v in m.items() if k in prog.ext_in})
    res = run_bass_kernel_spmd(prog.nc, in_maps, core_ids=list(range(NCORES)))
    return [{k: np.asarray(r[k]) for k in prog.ext_out} for r in res.results]


def kernel(**inputs):
    inputs = {k: np.asarray(v) for k, v in inputs.items()}
    x = inputs["x"]
    x_state = [np.ascontiguousarray(x[c // 2, (c % 2) * T:(c % 2 + 1) * T, :].T) for c in range(NCORES)]
    if MODE == "fused":
        outs = run_launch(FUSED_LAUNCH, True, inputs, x_state, [{} for _ in range(NCORES)])
        x_state = [o["x_out"] for o in outs]
    else:
        extra = [{} for _ in range(NCORES)]
        for phases in SPLIT_LAUNCHES:
            outs = run_launch(phases, False, inputs, x_state, extra)
            x_state = [o["x_out"] for o in outs]
            extra = [{} for _ in range(NCORES)]
            for ph, l in phases:
                if ph == "P":
                    for c in range(NCORES):
                        o, oo = outs[c], outs[c ^ 1]
                        extra[c] = {f"q{l}_i": o[f"q{l}_o"], f"g{l}_i": o[f"g{l}_o"],
                                    f"k{l}_i": o[f"k{l}_o"], f"v{l}_i": o[f"v{l}_o"],
                                    f"ko{l}_i": oo[f"k{l}_o"], f"vo{l}_i": oo[f"v{l}_o"]}
    out = np.empty((4, 2048, D), np.float32)
    for c in range(NCORES):
        out[c // 2, (c % 2) * T:(c % 2 + 1) * T, :] = x_state[c].T
    return out
```

```python
import math
import numpy as np
import ml_dtypes
import concourse.bass as bass
import concourse.mybir as mybir
from concourse.bass_utils import run_bass_kernel_spmd

F32 = mybir.dt.float32
BF16 = mybir.dt.bfloat16
ALU = mybir.AluOpType
AF = mybir.ActivationFunctionType
AX = mybir.AxisListType

D = 2048
T = 1024
KC = 16
DEPTH = 4
SCALE = 128.0 ** -0.5
EPS = 1e-6
BIG = 30000.0
SLABW = 256
N_DMA_SEMS = 12
NCORES = 8

IN_W = {0: (2048, 512, 512, 2048), 1: (2048, 2048, 2048, 2048), 2: (2048, 2048, 2048, 2048)}


def alibi(n):
    return [2.0 ** (-8.0 * (h + 1) / n) for h in range(n)]


class Op:
    __slots__ = ("eng", "fn", "deps", "is_dma", "signal", "ticket", "idx", "inc")

    def __init__(self, eng, fn, is_dma):
        self.eng = eng
        self.fn = fn
        self.deps = []
        self.is_dma = is_dma
        self.signal = False
        self.ticket = None
        self.idx = -1
        self.inc = 16


class Sched:
    def __init__(self, nc, sync_same_engine=True):
        self.nc = nc
        self.ops = []
        self.last_writer = {}
        self.readers = {}
        self.sync_same_engine = sync_same_engine
        self.engs = {"pe": nc.tensor, "act": nc.scalar, "dve": nc.vector,
                     "pool": nc.gpsimd, "sp": nc.sync}

    def add(self, eng, fn, reads=(), writes=(), dma=False, inc=16):
        op = Op(eng, fn, dma)
        op.inc = inc
        op.idx = len(self.ops)
        deps = {}
        for k in reads:
            w = self.last_writer.get(k)
            if w is not None:
                deps[w.idx] = w
            if isinstance(k, str) and k.startswith("ps"):
                for r in self.readers.get(k, ()):
                    if r.eng != eng:
                        deps[r.idx] = r
        for k in writes:
            w = self.last_writer.get(k)
            if w is not None:
                deps[w.idx] = w
            for r in self.readers.get(k, ()):
                deps[r.idx] = r
        op.deps = list(deps.values())
        for k in writes:
            self.last_writer[k] = op
            self.readers[k] = []
        for k in reads:
            lst = self.readers.setdefault(k, [])
            if not dma:
                lst[:] = [r for r in lst if r.is_dma or r.eng != eng]
            lst.append(op)
        self.ops.append(op)
        return op

    def _skip(self, d, op):
        return (d.eng == op.eng and not op.is_dma and not d.is_dma
                and (d.eng == "pe" or not self.sync_same_engine))

    def emit(self):
        nc = self.nc
        for op in self.ops:
            for d in op.deps:
                if d.is_dma or self._skip(d, op):
                    continue
                d.signal = True
        sems = {e: nc.alloc_semaphore(name=f"s_{e}") for e in self.engs}
        dma_sems = {q: [nc.alloc_semaphore(name=f"d_{q}_{i}") for i in range(N_DMA_SEMS)]
                    for q in ("sp", "pool", "act")}
        dma_cnt = {q: [0] * N_DMA_SEMS for q in dma_sems}
        dma_rr = {q: 0 for q in dma_sems}
        cnt = {e: 0 for e in self.engs}
        waited = {e: {} for e in self.engs}

        def wait(eng, sem, val):
            key = id(sem)
            if waited[eng].get(key, 0) >= val:
                return
            waited[eng][key] = val
            self.engs[eng].wait_ge(sem, val)

        for op in self.ops:
            e = op.eng
            for d in op.deps:
                if d.is_dma:
                    wait(e, d.ticket[0], d.ticket[1])
                elif not self._skip(d, op):
                    wait(e, d.ticket[0], d.ticket[1])
            if op.is_dma:
                i = dma_rr[e]
                dma_rr[e] = (i + 1) % N_DMA_SEMS
                sem = dma_sems[e][i]
                if dma_cnt[e][i] > 0:
                    wait(e, sem, dma_cnt[e][i])
                ins = op.fn()
                dma_cnt[e][i] += op.inc
                ins.then_inc(sem, op.inc)
                op.ticket = (sem, dma_cnt[e][i])
            else:
                ins = op.fn()
                if op.signal:
                    cnt[e] += 1
                    ins.then_inc(sems[e], 1)
                    op.ticket = (sems[e], cnt[e])
        for q in dma_sems:
            for i in range(N_DMA_SEMS):
                if dma_cnt[q][i] > 0:
                    wait("sp", dma_sems[q][i], dma_cnt[q][i])
        for e2 in self.engs:
            if cnt[e2] > 0:
                wait("sp", sems[e2], cnt[e2])
        return len(self.ops)


class Prog:
    def __init__(self, phases, fused, dbg=False):
        self.phases = phases
        self.fused = fused
        self.nc = nc = bass.Bass("TRN2", target_bir_lowering=False)
        self.S = Sched(nc)
        self.ext_in = []
        self.ext_out = []
        self.dram = {}
        self._alloc()

    def din(self, name, shape, dt=F32):
        t = self.nc.dram_tensor(name, list(shape), dt, kind="ExternalInput")
        self.dram[name] = t
        self.ext_in.append(name)
        return t

    def dout(self, name, shape, dt=F32):
        t = self.nc.dram_tensor(name, list(shape), dt, kind="ExternalOutput")
        self.dram[name] = t
        self.ext_out.append(name)
        return t

    def dint(self, name, shape, dt=F32):
        t = self.nc.dram_tensor(name, list(shape), dt)
        self.dram[name] = t
        return t

    def sb(self, name, shape, dt):
        return self.nc.alloc_sbuf_tensor(name, list(shape), dt)

    def _alloc(self):
        nc = self.nc
        layers = sorted({l for _, l in self.phases})
        self.layers = layers
        kinds = {l % 3 for l in layers}
        self.x_in = self.din("x_in", [D, T])
        self.x_out = self.dout("x_out", [D, T])
        self.gpre_d = self.din("gpre", [128, DEPTH * KC])
        self.gpost_d = self.din("gpost", [128, DEPTH * KC])
        self.pT_d, self.w_out_d, self.pe_proj_d, self.pe_gate_d, self.w_in_d = {}, {}, {}, {}, {}
        for ph, l in self.phases:
            if ph == "P":
                self.w_in_d[l] = self.din(f"w_in{l}", [D, sum(IN_W[l % 3])])
            else:
                self.pT_d[l] = self.din(f"pT{l}", [256, T])
                self.w_out_d[l] = self.din(f"w_out{l}", [D, D])
                self.pe_proj_d[l] = self.din(f"pe_proj{l}", [256, D])
                self.pe_gate_d[l] = self.din(f"pe_gate{l}", [D, D])
        self.sink_d = self.din("a_sink_bc", [128, 32])
        self.tabA_d = self.din("tabA", [4, 128, 1536])
        self.edge_d = self.din("edgeA", [128, 2])
        self.rpr_d = self.din("rpr", [16, 64, 15 * 127])
        self.colmask_d = self.din("colmaskB", [128, 64])
        self.rowmask_d = self.din("rowmaskB", [2, 128], BF16)
        self.rowind_d = self.din("rowindB", [2, 128], BF16)
        self.lam_d = self.din("c_lambda_bc", [128, 512])
        self.subln_d = self.din("c_subln_t", [128, 2])
        self.Town_d = self.din("TownC", [128, 1920])
        self.Toth_d = self.din("TothC", [128, 1920])
        self.q_w, self.q_r, self.g_w, self.g_r = {}, {}, {}, {}
        self.k_w, self.k_r, self.v_w, self.v_r = {}, {}, {}, {}
        self.k_oth, self.v_oth, self.k_all, self.v_all = {}, {}, {}, {}
        for l in layers:
            _, dk, dv, _ = IN_W[l % 3]
            hasP = ("P", l) in self.phases
            hasA = ("AE", l) in self.phases
            if hasP and hasA:
                assert self.fused
                self.q_w[l] = self.q_r[l] = self.dint(f"q{l}", [D, T], BF16)
                self.g_w[l] = self.g_r[l] = self.dint(f"g{l}", [D, T], BF16)
                self.k_w[l] = self.k_r[l] = self.dint(f"k{l}", [dk, T], BF16)
                self.v_w[l] = self.v_r[l] = self.dint(f"v{l}", [T, dv], BF16)
                self.k_all[l] = self.dint(f"kall{l}", [2 * dk, T], BF16)
                self.v_all[l] = self.dint(f"vall{l}", [2 * T, dv], BF16)
                self.k_oth[l] = self.dint(f"koth{l}", [dk, T], BF16)
                self.v_oth[l] = self.dint(f"voth{l}", [T, dv], BF16)
            else:
                if hasP:
                    self.q_w[l] = self.dout(f"q{l}_o", [D, T], BF16)
                    self.g_w[l] = self.dout(f"g{l}_o", [D, T], BF16)
                    self.k_w[l] = self.dout(f"k{l}_o", [dk, T], BF16)
                    self.v_w[l] = self.dout(f"v{l}_o", [T, dv], BF16)
                if hasA:
                    self.q_r[l] = self.din(f"q{l}_i", [D, T], BF16)
                    self.g_r[l] = self.din(f"g{l}_i", [D, T], BF16)
                    self.k_r[l] = self.din(f"k{l}_i", [dk, T], BF16)
                    self.v_r[l] = self.din(f"v{l}_i", [T, dv], BF16)
                    self.k_oth[l] = self.din(f"ko{l}_i", [dk, T], BF16)
                    self.v_oth[l] = self.din(f"vo{l}_i", [T, dv], BF16)
        self.yT = self.dint("yT", [D, T], F32)
        self.xT = self.sb("xT", [128, KC, T], F32)
        self.hT = self.sb("hT", [128, KC, T], BF16)
        self.wslab = [self.sb(f"wslab{i}", [128, KC, SLABW], BF16) for i in range(2)]
        self.ones = self.sb("ones", [128, 128], BF16)
        self.zeros = self.sb("zeros", [128, 16], F32)
        self.gpre = self.sb("gpre_s", [128, DEPTH * KC], F32)
        self.gpost = self.sb("gpost_s", [128, DEPTH * KC], F32)
        self.sq = [self.sb(f"sq{i}", [128, 2, 512], BF16) for i in range(2)]
        self.rstd = [self.sb(f"rstd{i}", [128, 512], F32) for i in range(2)]
        self.stage = [self.sb(f"stage{i}", [128, T], BF16) for i in range(2)]
        self.ystage = [self.sb(f"ystage{i}", [128, T], F32) for i in range(2)]
        self.gate_st = [self.sb(f"gate{i}", [128, 512], F32) for i in range(2)]
        self.peslab = [self.sb(f"peslab{i}", [128, 2, SLABW], BF16) for i in range(2)]
        self.pTb = self.sb("pTb", [128, 2, T], BF16)
        self.kt_own = self.sb("kt_own", [128, T], BF16)
        self.kt_oth = self.sb("kt_oth", [128, T], BF16)
        self.qb = [self.sb(f"qb{i}", [128, T], BF16) for i in range(2)]
        self.gb = self.sb("gb", [128, 2, T], BF16)
        self.v_own_s = self.sb("v_own", [128, 8, 256], BF16)
        self.v_oth_s = self.sb("v_oth", [128, 8, 256], BF16)
        self.tab = self.sb("tab", [128, 3840], F32)
        self.tmp = self.sb("tmp", [128, 3, 512], F32)
        self.pt = self.sb("pt", [128, 3, 512], BF16)
        self.on0 = self.sb("on0", [128, 2, 512], F32)
        self.o_s = self.sb("o_s", [128, 2, 512], F32)
        self.rz = self.sb("rz", [128, 512], F32)
        self.zt = self.sb("zt", [128, 512], F32)
        self.es = self.sb("es", [128, 32], F32)
        self.sink_s = self.sb("sink_s", [128, 32], F32)
        self.edge = self.sb("edge", [128, 2], F32)
        self.lam_s = self.sb("lam_s", [128, 512], F32)
        self.lam_t = self.sb("lam_t", [128, 8], F32)
        self.subln_s = self.sb("subln_s", [128, 2], F32)
        self.rowmask = self.sb("rowmask", [2, 128], BF16)
        self.rowind = self.sb("rowind", [2, 128], BF16)
        self.cm64 = self.sb("cm64", [128, 64], F32)
        self.ps = [nc.alloc_psum_tensor(f"ps{i}", [128, 512], F32) for i in range(8)]
        self.wi = 0
        self.par = None

    def dma(self, q, out, in_, reads, writes):
        eng = {"sp": self.nc.sync, "pool": self.nc.gpsimd, "act": self.nc.scalar}[q]
        self.S.add(q, lambda: eng.dma_start(out=out, in_=in_), reads, writes, dma=True)

    def act(self, out, in_, func, reads, writes, bias=0.0, scale=1.0):
        nc = self.nc
        self.S.add("act", lambda: nc.scalar.activation(out=out, in_=in_, func=func, bias=bias, scale=scale),
                   reads, writes)

    def tt(self, eng, out, in0, in1, op, reads, writes):
        e = {"dve": self.nc.vector, "pool": self.nc.gpsimd}[eng]
        self.S.add(eng, lambda: e.tensor_tensor(out=out, in0=in0, in1=in1, op=op), reads, writes)

    def stt(self, out, in0, scalar, in1, op0, op1, reads, writes, eng="dve"):
        e = {"dve": self.nc.vector, "pool": self.nc.gpsimd}[eng]
        self.S.add(eng, lambda: e.scalar_tensor_tensor(out=out, in0=in0, scalar=scalar, in1=in1,
                                                        op0=op0, op1=op1), reads, writes)

    def ts(self, out, in0, s1, s2, op0, op1, reads, writes):
        nc = self.nc
        if s2 is None:
            self.S.add("dve", lambda: nc.vector.tensor_scalar(out=out, in0=in0, scalar1=s1, scalar2=None,
                                                              op0=op0), reads, writes)
        else:
            self.S.add("dve", lambda: nc.vector.tensor_scalar(out=out, in0=in0, scalar1=s1, scalar2=s2,
                                                              op0=op0, op1=op1), reads, writes)

    def recip(self, out, in_, reads, writes):
        nc = self.nc
        self.S.add("dve", lambda: nc.vector.reciprocal(out=out, in_=in_), reads, writes)

    def mm(self, out, lhsT, rhs, start, stop, reads, writes):
        nc = self.nc
        self.S.add("pe", lambda: nc.tensor.matmul(out, lhsT, rhs, start=start, stop=stop), reads, writes)

    def memset(self, ap, val, writes):
        nc = self.nc
        self.S.add("dve", lambda: nc.vector.memset(ap, val), (), writes)

    @staticmethod
    def slab_order(kind):
        wq, wk, wv, wg = IN_W[kind]
        b = [0, wq // SLABW, (wq + wk) // SLABW, (wq + wk + wv) // SLABW, (wq + wk + wv + wg) // SLABW]
        q, k, v, g = (list(range(b[i], b[i + 1])) for i in range(4))
        return k + v + q + g

    def plan_slabs(self):
        slabs = []
        for ph, l in self.phases:
            if ph == "P":
                for s in self.slab_order(l % 3):
                    slabs.append((self.w_in_d[l], s))
            elif ph == "AE":
                for s in range(D // SLABW):
                    slabs.append((self.w_out_d[l], s))
                for s in range(D // SLABW):
                    slabs.append((self.pe_gate_d[l], s))
        self.slabs = slabs
        self.slab_issued = 0
        self.slab_used = 0

    def _issue_slab(self):
        i = self.slab_issued
        if i >= len(self.slabs):
            return
        w, s = self.slabs[i]
        src = w[:, s * SLABW:(s + 1) * SLABW].rearrange("(kc p) n -> p kc n", p=128)
        self.dma("pool", self.wslab[i % 2][:], src, (), [f"wslab{i % 2}"])
        self.slab_issued += 1

    def next_slab(self):
        i = self.slab_used
        while self.slab_issued <= min(i + 1, len(self.slabs) - 1):
            self._issue_slab()
        self.slab_used += 1
        return self.wslab[i % 2], f"wslab{i % 2}"

    def build(self):
        nc = self.nc
        self.plan_slabs()
        if self.fused:
            self.par = nc.sync.partition_id() % 2
        self.memset(self.ones[:], 1.0, ["ones"])
        self.memset(self.zeros[:], 0.0, ["zeros"])
        self.dma("sp", self.gpre[:], self.gpre_d.ap(), (), ["gpre"])
        self.dma("sp", self.gpost[:], self.gpost_d.ap(), (), ["gpost"])
        xv = self.x_in.ap().rearrange("(kc p) n -> p kc n", p=128)
        for kc in range(KC):
            self.dma("sp", self.xT[:, kc, :], xv[:, kc, :], (), [f"x{kc}_0", f"x{kc}_1"])
        for ph, l in self.phases:
            if ph == "P":
                self.phase_P(l)
            else:
                kind = l % 3
                if kind == 0:
                    self.attn_A(l)
                elif kind == 1:
                    self.attn_B(l)
                else:
                    self.attn_C(l)
                self.phase_E(l)
        ov = self.x_out.ap().rearrange("(kc p) n -> p kc n", p=128)
        for kc in range(KC):
            self.dma("sp", ov[:, kc, :], self.xT[:, kc, :], [f"x{kc}_0", f"x{kc}_1"], [])
        n = self.S.emit()
        return n

    def stats_rstd(self, tt, src_fn, src_keys, nchunks, inv_n, rstd_i):
        ps = self.ps[7]
        for kc in range(nchunks):
            sqb = self.sq[kc % 2]
            self.act(sqb[:, 0, :], src_fn(kc), AF.Square, src_keys(kc), [f"sq{kc % 2}"])
            self.mm(ps[:], self.ones[:], sqb[:, 0, :], kc == 0, kc == nchunks - 1,
                    ["ones", f"sq{kc % 2}"], ["ps7"])
        r = self.rstd[rstd_i]
        self.recip_act(r[:], ps[:], ["ps7"], [f"rstd{rstd_i}"], in_scale=inv_n, in_bias=EPS, power=-0.5)

    def phase_P(self, l):
        kind = l % 3
        wq, wk, wv, wg = IN_W[kind]
        for tt in range(2):
            sl = slice(tt * 512, (tt + 1) * 512)
            self.stats_rstd(tt, lambda kc: self.xT[:, kc, sl], lambda kc: [f"x{kc}_{tt}"], KC, 1.0 / D, tt)
            for kc in range(KC):
                self.stt(self.hT[:, kc, sl], self.xT[:, kc, sl], self.gpre[:, l * KC + kc:l * KC + kc + 1],
                         self.rstd[tt][:], ALU.mult, ALU.mult,
                         [f"x{kc}_{tt}", "gpre", f"rstd{tt}"], [f"h{kc}_{tt}"])
        order = self.slab_order(kind)
        last_k = (wq + wk) // SLABW - 1
        last_v = (wq + wk + wv) // SLABW - 1
        ev = 0
        for s in order:
            c0 = s * SLABW
            slab, skey = self.next_slab()
            if c0 < wq:
                seg, dst, r0 = "q", self.q_w[l], c0
            elif c0 < wq + wk:
                seg, dst, r0 = "k", self.k_w[l], c0 - wq
            elif c0 < wq + wk + wv:
                seg, dst, r0 = "v", self.v_w[l], c0 - wq - wk
            else:
                seg, dst, r0 = "g", self.g_w[l], c0 - wq - wk - wv
            if seg != "v":
                for mi in range(SLABW // 128):
                    st = self.stage[ev % 2]
                    stk = f"stage{ev % 2}"
                    for tt in range(2):
                        sl = slice(tt * 512, (tt + 1) * 512)
                        pi = (ev * 2 + tt) % 4
                        ps = self.ps[pi]
                        for kc in range(KC):
                            self.mm(ps[:], slab[:, kc, mi * 128:(mi + 1) * 128], self.hT[:, kc, sl],
                                    kc == 0, kc == KC - 1, [skey, f"h{kc}_{tt}"], [f"ps{pi}"])
                        if seg == "g":
                            self.act(st[:, sl], ps[:], AF.Silu, [f"ps{pi}"], [stk + f"_{tt}"])
                        elif (ev + tt) % 2 == 0:
                            self.act(st[:, sl], ps[:], AF.Copy, [f"ps{pi}"], [stk + f"_{tt}"])
                        else:
                            nc = self.nc
                            self.S.add("dve", (lambda o=st[:, sl], i=ps[:]: nc.vector.tensor_copy(out=o, in_=i)),
                                       [f"ps{pi}"], [stk + f"_{tt}"])
                    row = r0 + mi * 128
                    self.dma("sp", dst[row:row + 128, :], st[:], [stk + "_0", stk + "_1"], [f"{seg}{l}"])
                    ev += 1
            else:
                for tb in range(8):
                    st = self.stage[ev % 2]
                    stk = f"stage{ev % 2}"
                    pi = ev % 4
                    ps = self.ps[pi]
                    for kc in range(KC):
                        self.mm(ps[:, 0:SLABW], self.hT[:, kc, tb * 128:(tb + 1) * 128], slab[:, kc, :],
                                kc == 0, kc == KC - 1, [skey, f"h{kc}_{tb // 4}"], [f"ps{pi}"])
                    nc = self.nc
                    self.S.add("dve", (lambda o=st[:, 0:SLABW], i=ps[:, 0:SLABW]: nc.vector.tensor_copy(out=o, in_=i)),
                               [f"ps{pi}"], [stk + "_0", stk + "_1"])
                    self.dma("sp", dst[tb * 128:(tb + 1) * 128, r0:r0 + SLABW], st[:, 0:SLABW],
                             [stk + "_0", stk + "_1"], [f"v{l}"])
                    ev += 1
            if self.fused and s == last_k:
                self.exchange_gather(l, "k")
            if self.fused and s == last_v:
                self.exchange_gather(l, "v")
        if self.fused:
            self.exchange_copy(l)

    def _xchg(self, l, nm):
        src, allt, otht = ((self.k_w[l], self.k_all[l], self.k_oth[l]) if nm == "k"
                           else (self.v_w[l], self.v_all[l], self.v_oth[l]))
        rows, cols = src.shape
        rc = min(rows, (2 << 20) // (cols * 2))
        return src, allt, otht, rc, rows // rc

    def exchange_gather(self, l, nm):
        nc = self.nc
        groups = [[2 * i, 2 * i + 1] for i in range(NCORES // 2)]
        src, allt, otht, rc, nchunk = self._xchg(l, nm)
        for c in range(nchunk):
            self.S.add("pool", (lambda s=src[c * rc:(c + 1) * rc, :], d=allt[c * 2 * rc:(c + 1) * 2 * rc, :]:
                                nc.gpsimd.collective_compute(
                "AllGather", mybir.AluOpType.bypass, replica_groups=groups,
                ins=[s.opt()], outs=[d.opt()])), [f"{nm}{l}"], [f"{nm}all{l}"], dma=True, inc=1)

    def exchange_copy(self, l):
        for nm in ("k", "v"):
            src, allt, otht, rc, nchunk = self._xchg(l, nm)
            gv = allt.ap().rearrange("(c r f) n -> c r f n", c=nchunk, r=2)
            self.dma("sp", otht.ap().rearrange("(c f) n -> c f n", c=nchunk),
                     gv[:, bass.ds(1 - self.par, 1), :, :].rearrange("c 1 f n -> c f n"),
                     [f"{nm}all{l}"], [f"{nm}oth{l}"])

    def oth_k(self, l, f0, c0, c1):
        return self.k_oth[l][f0:f0 + 128, c0:c1], ([f"koth{l}"] if self.fused else [])

    def oth_v(self, l, kb0, nkb, d0, dn):
        return (self.v_oth[l][kb0 * 128:(kb0 + nkb) * 128, d0:d0 + dn].rearrange("(kb p) d -> p kb d", p=128),
                [f"voth{l}"] if self.fused else [])

    def run_pipeline(self, groups, LA):
        flat = []
        for gi, g in enumerate(groups):
            nt = len(g["tasks"])
            for ti, (s1, s2) in enumerate(g["tasks"]):
                flat.append((gi, s1, s2, ti == nt - 1))
        for gi in range(min(2, len(groups))):
            if groups[gi]["loads"]:
                groups[gi]["loads"]()
        n = len(flat)
        for i in range(n + LA):
            if i < n:
                flat[i][1](i)
            j = i - LA
            if j >= 0:
                gj, _, s2, last = flat[j]
                s2(j)
                if last:
                    if groups[gj]["post"]:
                        groups[gj]["post"]()
                    if gj + 2 < len(groups) and groups[gj + 2]["loads"]:
                        groups[gj + 2]["loads"]()

    def recip_act(self, out, in_, rk, wk, in_scale=1.0, in_bias=0.0, power=-1.0):
        self.act(out, in_, AF.Ln, rk, wk, bias=in_bias, scale=in_scale)
        self.act(out, out, AF.Exp, wk, wk, scale=power)

    def attn_A(self, l):
        j = l // 3
        self.dma("sp", self.sink_s[:], self.sink_d.ap(), (), ["sink_s"])
        self.dma("sp", self.edge[:], self.edge_d.ap(), (), ["edge"])
        for h in range(16):
            c = j * 16 + h
            self.act(self.es[:, c:c + 1], self.zeros[:, 0:1], AF.Exp, ["zeros", "sink_s"], ["es"],
                     bias=self.sink_s[:, c:c + 1], scale=1.0)
        tabv = self.tab[:, 0:1536].rearrange("p (j n) -> p j n", j=3)
        for kvh in range(4):
            f0 = kvh * 128
            self.dma("sp", self.kt_own[:], self.k_r[l][f0:f0 + 128, :], [f"k{l}"], ["kt_own"])
            ap, rk = self.oth_k(l, f0, 896, 1024)
            self.dma("sp", self.kt_oth[:, 0:128], ap, rk, ["kt_oth"])
            ap, rk = self.oth_k(l, f0, 0, 128)
            self.dma("sp", self.kt_oth[:, 128:256], ap, rk, ["kt_oth"])
            self.dma("sp", self.v_own_s[:, :, 0:128],
                     self.v_r[l][:, f0:f0 + 128].rearrange("(kb p) d -> p kb d", p=128), [f"v{l}"], ["v_own0"])
            ap, rk = self.oth_v(l, 7, 1, f0, 128)
            self.dma("sp", self.v_oth_s[:, 0:1, 0:128], ap, rk, ["v_oth0"])
            ap, rk = self.oth_v(l, 0, 1, f0, 128)
            self.dma("sp", self.v_oth_s[:, 1:2, 0:128], ap, rk, ["v_oth0"])
            self.dma("sp", self.tab[:, 0:1536], self.tabA_d[kvh], (), ["tabL"])
            groups = []
            for qi in range(8):
                par = qi % 2
                qs = slice(qi * 128, (qi + 1) * 128)
                q4 = self.qb[0][:, par * 512:(par + 1) * 512]
                g4 = self.gb[:, par, 0:512]
                qk, gk = f"qb0_{par}", f"gb{par}"
                po, pz = (4, 5) if par == 0 else (6, 7)

                def loads(qs=qs, q4=q4, g4=g4, qk=qk, gk=gk, kvh=kvh):
                    qsrc = self.q_r[l][kvh * 512:(kvh + 1) * 512, qs].rearrange("(g p) n -> p g n", p=128)
                    self.dma("sp", q4.rearrange("p (g n) -> p g n", g=4), qsrc, [f"q{l}"], [qk])
                    gsrc = self.g_r[l][kvh * 512:(kvh + 1) * 512, qs].rearrange("(g p) n -> p g n", p=128)
                    self.dma("sp", g4.rearrange("p (g n) -> p g n", g=4), gsrc, [f"g{l}"], [gk])

                tasks = []
                for jj in range(3):
                    kb = qi + jj - 1
                    if kb < 0:
                        blk = (self.kt_oth[:, 0:128], "kt_oth", self.v_oth_s[:, 0, 0:128], "v_oth0", 0)
                    elif kb > 7:
                        blk = (self.kt_oth[:, 128:256], "kt_oth", self.v_oth_s[:, 1, 0:128], "v_oth0", 1)
                    else:
                        blk = (self.kt_own[:, kb * 128:(kb + 1) * 128], "kt_own", self.v_own_s[:, kb, 0:128], "v_own0", None)

                    def s1(i, blk=blk, jj=jj, q4=q4, qk=qk):
                        kap, kk, vap, vk, edge = blk
                        b = i % 3
                        self.mm(self.ps[b][:], kap, q4, True, True, [kk, qk], [f"ps{b}"])
                        self.tt("dve", self.tmp[:, b, :], self.ps[b][:], tabv[:, jj, :], ALU.add,
                                [f"ps{b}", "tabL"], [f"tmp{b}"])
                        if edge is None:
                            self.act(self.pt[:, b, :], self.tmp[:, b, :], AF.Exp, [f"tmp{b}"], [f"pt{b}"], scale=SCALE)
                        else:
                            self.act(self.pt[:, b, :], self.tmp[:, b, :], AF.Exp, [f"tmp{b}", "edge"], [f"pt{b}"],
                                     bias=self.edge[:, edge:edge + 1], scale=SCALE)

                    def s2(i, blk=blk, jj=jj, po=po, pz=pz):
                        kap, kk, vap, vk, edge = blk
                        b = i % 3
                        self.mm(self.ps[po][:], vap, self.pt[:, b, :], jj == 0, jj == 2, [vk, f"pt{b}"], [f"ps{po}"])
                        self.mm(self.ps[pz][:], self.ones[:], self.pt[:, b, :], jj == 0, jj == 2,
                                ["ones", f"pt{b}"], [f"ps{pz}"])
                    tasks.append((s1, s2))

                def post(qs=qs, g4=g4, gk=gk, po=po, pz=pz, kvh=kvh, qi=qi):
                    for g in range(4):
                        c = j * 16 + kvh * 4 + g
                        gs = slice(g * 128, (g + 1) * 128)
                        self.act(self.rz[:, gs], self.ps[pz][:, gs], AF.Ln, [f"ps{pz}", "es"], ["rz"],
                                 bias=self.es[:, c:c + 1], scale=1.0)
                    self.act(self.rz[:], self.rz[:], AF.Exp, ["rz"], ["rz"], scale=-1.0)
                    self.tt("dve", self.o_s[:, 0, :], self.ps[po][:], self.rz[:], ALU.mult, [f"ps{po}", "rz"], ["o_s"])
                    okeys = [f"h{kvh * 4 + g}_{qi // 4}" for g in range(4)]
                    self.tt("pool", self.hT[:, kvh * 4:(kvh + 1) * 4, qs],
                            self.o_s[:, 0, :].rearrange("p (g n) -> p g n", g=4),
                            g4.rearrange("p (g n) -> p g n", g=4), ALU.mult, ["o_s", gk], okeys)
                groups.append(dict(loads=loads, tasks=tasks, post=post))
            self.run_pipeline(groups, 2)

    def attn_B(self, l):
        self.dma("sp", self.rowmask[:], self.rowmask_d.ap(), (), ["rowmask"])
        self.dma("sp", self.rowind[:], self.rowind_d.ap(), (), ["rowind"])
        self.dma("sp", self.cm64[:], self.colmask_d.ap(), (), ["cm64"])
        self.memset(self.tab[:, 0:1920], 0.0, ["tabL"])
        self.memset(self.tab[:, 1920:3840], 0.0, ["tabH"])
        ktown = [(self.kt_own, ["kt_own"]), (self.qb[1], ["qb1"])]
        ktoth = [(self.kt_oth, ["kt_oth"]), (self.stage[0], ["stage0_0", "stage0_1"])]
        qbuf = [(self.qb[0], ["qb0_0", "qb0_1"]), (self.stage[1], ["stage1_0", "stage1_1"])]
        groups = []
        gidx = 0
        for h in range(16):
            hp = h % 2
            f0 = h * 128
            kto, ktok = ktown[hp]
            ktt, kttk = ktoth[hp]
            qq, qqk = qbuf[hp]
            gg, ggk = self.gb[:, hp, :], f"gb{hp}"
            vo, vok = self.v_own_s[:, :, hp * 128:(hp + 1) * 128], f"v_own{hp}"
            vt, vtk = self.v_oth_s[:, :, hp * 128:(hp + 1) * 128], f"v_oth{hp}"
            tkey = "tabL" if hp == 0 else "tabH"
            TB = self.tab[:, hp * 1920:(hp + 1) * 1920].rearrange("p (j n) -> p j n", j=30)

            def loads(h=h, f0=f0, kto=kto, ktok=ktok, ktt=ktt, kttk=kttk, qq=qq, qqk=qqk, gg=gg, ggk=ggk,
                      vo=vo, vok=vok, vt=vt, vtk=vtk, tkey=tkey, TB=TB):
                self.dma("sp", kto[:], self.k_r[l][f0:f0 + 128, :], [f"k{l}"], ktok)
                ap, rk = self.oth_k(l, f0, 0, 256)
                self.dma("sp", ktt[:, 0:256], ap, rk, kttk)
                ap, rk = self.oth_k(l, f0, 768, 1024)
                self.dma("sp", ktt[:, 768:1024], ap, rk, kttk)
                self.dma("sp", vo, self.v_r[l][:, f0:f0 + 128].rearrange("(kb p) d -> p kb d", p=128), [f"v{l}"], [vok])
                ap, rk = self.oth_v(l, 0, 2, f0, 128)
                self.dma("sp", vt[:, 0:2, :], ap, rk, [vtk])
                ap, rk = self.oth_v(l, 6, 2, f0, 128)
                self.dma("sp", vt[:, 6:8, :], ap, rk, [vtk])
                self.dma("sp", qq[:], self.q_r[l][f0:f0 + 128, :], [f"q{l}"], qqk)
                self.dma("sp", gg, self.g_r[l][f0:f0 + 128, :], [f"g{l}"], [ggk])
                for a in range(2):
                    src = bass.AP(self.rpr_d, h * 64 * 1905 + 63, [[1904, 64], [127, 15], [1, 64]])
                    self.dma("sp", TB[a * 64:(a + 1) * 64, a + 7:a + 22, :], src, (), [tkey])
                for a in range(2):
                    cmb = bass.AP(self.cm64, a * 64 * 64, [[64, 64], [0, 15], [1, 64]])
                    self.stt(TB[a * 64:(a + 1) * 64, a + 7:a + 22, :], TB[a * 64:(a + 1) * 64, a + 7:a + 22, :],
                             1.0 / SCALE, cmb, ALU.mult, ALU.add, [tkey, "cm64"], [tkey])

            for tt in range(2):
                sl = slice(tt * 512, (tt + 1) * 512)
                if tt == 0:
                    blks = [("own", kb, 2 * kb + 7) for kb in range(6)] + [("oth", kb, 2 * kb - 9) for kb in (6, 7)]
                else:
                    blks = [("own", kb, 2 * kb - 1) for kb in range(2, 8)] + [("oth", kb, 15 + 2 * kb) for kb in (0, 1)]
                nb = len(blks)
                po, pz = (4, 5) if gidx % 2 == 0 else (6, 7)
                tasks = []
                for i_b, (who, kb, d0) in enumerate(blks):
                    pidx = tt * 8 + i_b
                    if who == "own":
                        kap, kk, vap, vk = kto[:, kb * 128:(kb + 1) * 128], ktok, vo[:, kb, :], vok
                    else:
                        kap, kk, vap, vk = ktt[:, kb * 128:(kb + 1) * 128], kttk, vt[:, kb, :], vtk

                    def s1(i, kap=kap, kk=kk, pidx=pidx, d0=d0, qq=qq, qqk=qqk, sl=sl, TB=TB, tkey=tkey):
                        b = i % 3
                        self.mm(self.ps[b][:], kap, qq[:, sl], True, False, list(kk) + list(qqk), [f"ps{b}"])
                        self.mm(self.ps[b][:], self.rowind[:],
                                bass.AP(self.rowmask, pidx * 8, [[128, 2], [1, 8], [0, 64]]), False, True,
                                ["rowind", "rowmask"], [f"ps{b}"])
                        j0 = 21 - d0
                        self.tt("dve", self.tmp[:, b, :], self.ps[b][:],
                                TB[:, j0:j0 + 8, :].rearrange("p j n -> p (j n)"), ALU.add,
                                [f"ps{b}", tkey], [f"tmp{b}"])
                        self.act(self.pt[:, b, :], self.tmp[:, b, :], AF.Exp, [f"tmp{b}"], [f"pt{b}"], scale=SCALE)

                    def s2(i, vap=vap, vk=vk, i_b=i_b, nb=nb, po=po, pz=pz):
                        b = i % 3
                        self.mm(self.ps[po][:], vap, self.pt[:, b, :], i_b == 0, i_b == nb - 1, [vk, f"pt{b}"], [f"ps{po}"])
                        self.mm(self.ps[pz][:], self.ones[:], self.pt[:, b, :], i_b == 0, i_b == nb - 1,
                                ["ones", f"pt{b}"], [f"ps{pz}"])
                    tasks.append((s1, s2))

                def post(h=h, sl=sl, tt=tt, gg=gg, ggk=ggk, po=po, pz=pz):
                    self.recip_act(self.rz[:], self.ps[pz][:], [f"ps{pz}"], ["rz"])
                    self.tt("dve", self.o_s[:, 0, :], self.ps[po][:], self.rz[:], ALU.mult, [f"ps{po}", "rz"], ["o_s"])
                    self.tt("pool", self.hT[:, h, sl], self.o_s[:, 0, :], gg[:, sl], ALU.mult,
                            ["o_s", ggk], [f"h{h}_{tt}"])
                groups.append(dict(loads=loads if tt == 0 else None, tasks=tasks, post=post))
                gidx += 1
        self.run_pipeline(groups, 2)

    def attn_C(self, l):
        nc = self.nc
        lam_init = 0.8 - 0.6 * math.exp(-0.3 * l)
        slopes = alibi(8)
        self.dma("sp", self.lam_s[:], self.lam_d.ap(), (), ["lam_s"])
        self.dma("sp", self.subln_s[:], self.subln_d.ap(), (), ["subln_s"])
        self.dma("sp", self.tab[:, 0:1920], self.Town_d.ap(), (), ["tabL"])
        self.dma("sp", self.tab[:, 1920:3840], self.Toth_d.ap(), (), ["tabH"])
        lt = self.lam_t
        for i in range(2):
            self.tt("dve", self.zt[:, 0:128], self.lam_s[:, (2 * i) * 128:(2 * i + 1) * 128],
                    self.lam_s[:, (2 * i + 1) * 128:(2 * i + 2) * 128], ALU.mult, ["lam_s"], ["zt"])
            self.S.add("dve", (lambda i=i: nc.vector.reduce_sum(out=lt[:, i:i + 1], in_=self.zt[:, 0:128], axis=AX.X)),
                       ["zt"], ["lam_t"])
        self.act(lt[:, 2:4], lt[:, 0:2], AF.Exp, ["lam_t"], ["lam_t"])
        self.tt("dve", lt[:, 4:5], lt[:, 3:4], lt[:, 2:3], ALU.subtract, ["lam_t"], ["lam_t"])
        self.ts(lt[:, 5:6], lt[:, 4:5], -lam_init, None, ALU.add, None, ["lam_t"], ["neglam"])
        self.ts(self.subln_s[:], self.subln_s[:], 1.0 - lam_init, None, ALU.mult, None, ["subln_s"], ["subln_s"])
        neglam = lt[:, 5:6]
        Town = self.tab[:, 0:1920]
        Toth = self.tab[:, 1920:3840]
        ktown = [(self.kt_own, ["kt_own"]), (self.qb[1], ["qb1"])]
        ktoth = [(self.kt_oth, ["kt_oth"]), (self.stage[0], ["stage0_0", "stage0_1"])]
        for h in range(8):
            self.dma("sp", self.v_own_s[:], self.v_r[l][:, h * 256:(h + 1) * 256].rearrange("(kb p) d -> p kb d", p=128),
                     [f"v{l}"], ["v_own0", "v_own1"])
            ap, rk = self.oth_v(l, 0, 8, h * 256, 256)
            self.dma("sp", self.v_oth_s[:], ap, rk, ["v_oth0", "v_oth1"])
            self.dma("sp", self.gb[:], self.g_r[l][h * 256:(h + 1) * 256, :].rearrange("(c p) n -> p c n", p=128),
                     [f"g{l}"], ["gb0", "gb1"])
            for m in range(2):
                f0 = h * 256 + m * 128
                self.dma("sp", ktown[m][0][:], self.k_r[l][f0:f0 + 128, :], [f"k{l}"], ktown[m][1])
                ap, rk = self.oth_k(l, f0, 0, T)
                self.dma("sp", ktoth[m][0][:], ap, rk, ktoth[m][1])
            groups = []
            gidx = 0
            for tt in range(2):
                sl = slice(tt * 512, (tt + 1) * 512)
                for m in range(2):
                    f0 = h * 256 + m * 128
                    gp = gidx % 2
                    q1 = self.qb[0][:, gp * 512:(gp + 1) * 512]
                    qk = f"qb0_{gp}"
                    po = 2 + 3 * m

                    def loads(f0=f0, sl=sl, q1=q1, qk=qk):
                        self.dma("sp", q1, self.q_r[l][f0:f0 + 128, sl], [f"q{l}"], [qk])

                    tasks = []
                    for i_b in range(16):
                        kb = i_b % 8
                        if i_b < 8:
                            kap, kk, vap, vk, Tt, tk = ktown[m][0][:, kb * 128:(kb + 1) * 128], ktown[m][1], self.v_own_s, ["v_own0", "v_own1"], Town, "tabL"
                        else:
                            kap, kk, vap, vk, Tt, tk = ktoth[m][0][:, kb * 128:(kb + 1) * 128], ktoth[m][1], self.v_oth_s, ["v_oth0", "v_oth1"], Toth, "tabH"
                        off = tt * 512 - kb * 128 + 896

                        def s1(i, kap=kap, kk=kk, Tt=Tt, tk=tk, off=off, q1=q1, qk=qk):
                            b = i % 2
                            self.mm(self.ps[b][:], kap, q1, True, True, list(kk) + [qk], [f"ps{b}"])
                            self.stt(self.tmp[:, b, :], Tt[:, off:off + 512], slopes[h], self.ps[b][:],
                                     ALU.mult, ALU.add, [f"ps{b}", tk], [f"tmp{b}"])
                            self.act(self.pt[:, b, :], self.tmp[:, b, :], AF.Exp, [f"tmp{b}"], [f"pt{b}"], scale=SCALE)

                        def s2(i, vap=vap, vk=vk, kb=kb, i_b=i_b, po=po):
                            b = i % 2
                            for hf in range(2):
                                self.mm(self.ps[po + hf][:], vap[:, kb, hf * 128:(hf + 1) * 128], self.pt[:, b, :],
                                        i_b == 0, i_b == 15, list(vk) + [f"pt{b}"], [f"ps{po + hf}"])
                            self.mm(self.ps[po + 2][:], self.ones[:], self.pt[:, b, :], i_b == 0, i_b == 15,
                                    ["ones", f"pt{b}"], [f"ps{po + 2}"])
                        tasks.append((s1, s2))

                    def post(m=m, po=po, sl=sl, tt=tt):
                        self.recip_act(self.rz[:], self.ps[po + 2][:], [f"ps{po + 2}"], ["rz"])
                        if m == 0:
                            for hf in range(2):
                                self.tt("dve", self.on0[:, hf, :], self.ps[po + hf][:], self.rz[:], ALU.mult,
                                        [f"ps{po + hf}", "rz"], ["on0"])
                            return
                        for hf in range(2):
                            self.tt("dve", self.o_s[:, hf, :], self.ps[po + hf][:], self.rz[:], ALU.mult,
                                    [f"ps{po + hf}", "rz"], ["o_s"])
                        self.stt(self.on0[:], self.o_s[:], neglam, self.on0[:], ALU.mult, ALU.add,
                                 ["o_s", "on0", "neglam"], ["on0"])
                        sqb = self.sq[0]
                        self.act(sqb[:], self.on0[:], AF.Square, ["on0"], ["sq0"])
                        for hf in range(2):
                            self.mm(self.ps[7][:], self.ones[:], sqb[:, hf, :], hf == 0, hf == 1, ["ones", "sq0"], ["ps7"])
                        self.recip_act(self.rz[:], self.ps[7][:], ["ps7"], ["rz"], in_scale=1.0 / 256, in_bias=EPS, power=-0.5)
                        for hf in range(2):
                            self.stt(self.on0[:, hf, :], self.on0[:, hf, :], self.subln_s[:, hf:hf + 1], self.rz[:],
                                     ALU.mult, ALU.mult, ["on0", "subln_s", "rz"], ["on0"])
                        self.tt("pool", self.hT[:, 2 * h:2 * h + 2, sl], self.on0[:], self.gb[:, :, sl], ALU.mult,
                                ["on0", "gb0", "gb1"], [f"h{2 * h}_{tt}", f"h{2 * h + 1}_{tt}"])
                    groups.append(dict(loads=loads, tasks=tasks, post=post))
                    gidx += 1
            self.run_pipeline(groups, 1)

    def phase_E(self, l):
        nc = self.nc
        ev = 0
        for s in range(D // SLABW):
            slab, skey = self.next_slab()
            for mi in range(SLABW // 128):
                m = s * (SLABW // 128) + mi
                ys = self.ystage[ev % 2]
                ysk = f"ystage{ev % 2}"
                for tt in range(2):
                    sl = slice(tt * 512, (tt + 1) * 512)
                    pi = (ev * 2 + tt) % 4
                    ps = self.ps[pi]
                    for kc in range(KC):
                        self.mm(ps[:], slab[:, kc, mi * 128:(mi + 1) * 128], self.hT[:, kc, sl],
                                kc == 0, kc == KC - 1, [skey, f"h{kc}_{tt}"], [f"ps{pi}"])
                    self.S.add("dve", (lambda o=ys[:, sl], i=ps[:]: nc.vector.tensor_copy(out=o, in_=i)),
                               [f"ps{pi}"], [ysk + f"_{tt}"])
                    sqb = self.sq[tt]
                    self.act(sqb[:, 0, :], ys[:, sl], AF.Square, [ysk + f"_{tt}"], [f"sq{tt}"])
                    self.mm(self.ps[6 + tt][:], self.ones[:], sqb[:, 0, :], m == 0, m == KC - 1,
                            ["ones", f"sq{tt}"], [f"ps{6 + tt}"])
                self.dma("sp", self.yT[m * 128:(m + 1) * 128, :], ys[:], [ysk + "_0", ysk + "_1"], ["yT"])
                ev += 1
        for tt in range(2):
            r = self.rstd[tt]
            self.recip_act(r[:], self.ps[6 + tt][:], [f"ps{6 + tt}"], [f"rstd{tt}"], in_scale=1.0 / D, in_bias=EPS, power=-0.5)
        for kc in range(KC):
            ys = self.ystage[kc % 2]
            ysk = f"ystage{kc % 2}"
            self.dma("sp", ys[:], self.yT[kc * 128:(kc + 1) * 128, :], ["yT"], [ysk + "_0", ysk + "_1"])
            for tt in range(2):
                sl = slice(tt * 512, (tt + 1) * 512)
                self.stt(ys[:, sl], ys[:, sl], self.gpost[:, l * KC + kc:l * KC + kc + 1], self.rstd[tt][:],
                         ALU.mult, ALU.mult, [ysk + f"_{tt}", "gpost", f"rstd{tt}"], [ysk + f"_{tt}"])
                self.tt("dve", self.xT[:, kc, sl], self.xT[:, kc, sl], ys[:, sl], ALU.add,
                        [f"x{kc}_{tt}", ysk + f"_{tt}"], [f"x{kc}_{tt}"])
                self.act(self.hT[:, kc, sl], self.xT[:, kc, sl], AF.Copy, [f"x{kc}_{tt}"], [f"h{kc}_{tt}"])
        self.dma("pool", self.pTb[:], self.pT_d[l].ap().rearrange("(c p) n -> p c n", p=128), (), ["pTb"])
        ev = 0
        for s in range(D // SLABW):
            slab, skey = self.next_slab()
            pes = self.peslab[s % 2]
            pek = f"peslab{s % 2}"
            self.dma("pool", pes[:], self.pe_proj_d[l][:, s * SLABW:(s + 1) * SLABW].rearrange("(c p) n -> p c n", p=128),
                     (), [pek])
            for mi in range(SLABW // 128):
                m = s * (SLABW // 128) + mi
                for tt in range(2):
                    sl = slice(tt * 512, (tt + 1) * 512)
                    pi = (ev * 2 + tt) % 4
                    ps = self.ps[pi]
                    pp = self.ps[4 + (ev * 2 + tt) % 2]
                    ppk = f"ps{4 + (ev * 2 + tt) % 2}"
                    for kc in range(KC):
                        self.mm(ps[:], slab[:, kc, mi * 128:(mi + 1) * 128], self.hT[:, kc, sl],
                                kc == 0, kc == KC - 1, [skey, f"h{kc}_{tt}"], [f"ps{pi}"])
                    for c in range(2):
                        self.mm(pp[:], pes[:, c, mi * 128:(mi + 1) * 128], self.pTb[:, c, sl], c == 0, c == 1,
                                [pek, "pTb"], [ppk])
                    gs = self.gate_st[(ev * 2 + tt) % 2]
                    gk = f"gate{(ev * 2 + tt) % 2}"
                    self.act(gs[:], ps[:], AF.Sigmoid, [f"ps{pi}"], [gk])
                    self.tt("dve", gs[:], gs[:], pp[:], ALU.mult, [gk, ppk], [gk])
                    self.tt("dve", self.xT[:, m, sl], self.xT[:, m, sl], gs[:], ALU.add,
                            [f"x{m}_{tt}", gk], [f"x{m}_{tt}"])
                ev += 1


def _const_tables():
    sl = alibi(16)
    r = np.arange(128)[:, None]
    c = np.arange(128)[None, :]
    tabA = np.zeros((4, 128, 3, 4, 128), np.float32)
    for kvh in range(4):
        for j in range(3):
            rel = c - r + (1 - j) * 128
            ok = np.abs(rel) <= 128
            for g in range(4):
                h = kvh * 4 + g
                tabA[kvh, :, j, g, :] = np.where(ok, -sl[h] * np.abs(rel), -BIG) / SCALE
    tabA = tabA.reshape(4, 128, 1536)
    kx = np.arange(64)[:, None]
    qx = np.arange(64)[None, :]
    cs = np.clip(qx - 8, 0, 48)
    okc = (kx >= cs) & (kx < cs + 16)
    cm = np.where(okc, 0.0, -BIG / SCALE).astype(np.float32)
    colmask = np.ascontiguousarray(np.broadcast_to(cm[None, :, :], (2, 64, 64)).reshape(128, 64))
    rowind = np.zeros((2, 128), np.float32)
    rowind[0, :64] = 1
    rowind[1, 64:] = 1
    u = np.arange(1920)[None, :]
    rr = np.arange(128)[:, None]
    Town = (-np.abs(u - 896 - rr) / SCALE).astype(np.float32)
    Toth = [(-np.abs(dl + u - 896 - rr) / SCALE).astype(np.float32) for dl in (-1024, 1024)]
    return tabA, colmask, rowind.astype(ml_dtypes.bfloat16), Town, Toth


def _rowmask(half):
    rm = np.full((2, 16, 8), -BIG, np.float32)
    for tt in range(2):
        if tt == 0:
            blks = [("own", kb) for kb in range(6)] + [("oth", kb) for kb in (6, 7)]
        else:
            blks = [("own", kb) for kb in range(2, 8)] + [("oth", kb) for kb in (0, 1)]
        for i, (who, kb) in enumerate(blks):
            base = half * 16 if who == "own" else (1 - half) * 16
            for a in range(2):
                ky = base + 2 * kb + a
                for e in range(8):
                    qy = half * 16 + tt * 8 + e
                    rs = min(max(qy - 4, 0), 24)
                    if rs <= ky < rs + 8:
                        rm[a, tt * 8 + i, e] = 0.0
    return rm.reshape(2, 128).astype(ml_dtypes.bfloat16)


def _core_inputs(c, inputs, x_state):
    b, half = c // 2, c % 2
    tabA, colmask, rowind, Town, Toth = _CONST
    ts = slice(half * T, (half + 1) * T)
    rpb = inputs["b_rpb"][0]
    rpr = np.zeros((16, 15, 127), np.float32)
    rpr[:, :, 48:79] = rpb[:, ::-1, ::-1]
    rpr = np.ascontiguousarray(np.broadcast_to(rpr.reshape(16, 1, 1905), (16, 64, 1905)))
    edge = np.zeros((128, 2), np.float32)
    edge[:, half] = -BIG
    m = {
        "x_in": x_state[c],
        "gpre": np.ascontiguousarray(inputs["norm_pre"].reshape(DEPTH, KC, 128).transpose(2, 0, 1).reshape(128, DEPTH * KC)),
        "gpost": np.ascontiguousarray(inputs["norm_post"].reshape(DEPTH, KC, 128).transpose(2, 0, 1).reshape(128, DEPTH * KC)),
    }
    for l in range(DEPTH):
        kind, j = l % 3, l // 3
        m[f"w_in{l}"] = inputs[("a_w_in", "b_w_in", "c_w_in")[kind]][j]
        m[f"pT{l}"] = np.ascontiguousarray(inputs["p"][l, b, ts, :].T)
        m[f"w_out{l}"] = inputs["w_out"][l]
        m[f"pe_proj{l}"] = inputs["pe_proj"][l]
        m[f"pe_gate{l}"] = inputs["pe_gate"][l]
    m.update({
        "a_sink_bc": np.ascontiguousarray(np.broadcast_to(inputs["a_sink"].reshape(1, 32), (128, 32))),
        "tabA": tabA, "edgeA": edge, "rpr": rpr, "colmaskB": colmask,
        "rowmaskB": _rowmask(half), "rowindB": rowind,
        "c_lambda_bc": np.ascontiguousarray(np.broadcast_to(inputs["c_lambda"][0].reshape(1, 512), (128, 512))),
        "c_subln_t": np.ascontiguousarray(inputs["c_subln"][0].reshape(2, 128).T),
        "TownC": Town, "TothC": Toth[half],
    })
    return m


_CONST = _const_tables()
_PROG_CACHE = {}


def _get_prog(phases, fused):
    key = (tuple(phases), fused)
    if key not in _PROG_CACHE:
        p = Prog(list(phases), fused)
        p.build()
        _PROG_CACHE[key] = p
    return _PROG_CACHE[key]


SPLIT_LAUNCHES = [[("P", 0)], [("AE", 0), ("P", 1)], [("AE", 1), ("P", 2)], [("AE", 2), ("P", 3)], [("AE", 3)]]
FUSED_LAUNCH = [("P", 0), ("AE", 0), ("P", 1), ("AE", 1), ("P", 2), ("AE", 2), ("P", 3), ("AE", 3)]
MODE = "fused"


def run_launch(phases, fused, inputs, x_state, extra):
    prog = _get_prog(phases, fused)
    in_maps = []
    for c in range(NCORES):
        m = _core_inputs(c, inputs, x_state)
        m.update(extra[c])
        in_maps.append({k: v for k, v in m.items() if k in prog.ext_in})
    res = run_bass_kernel_spmd(prog.nc, in_maps, core_ids=list(range(NCORES)))
    return [{k: np.asarray(r[k]) for k in prog.ext_out} for r in res.results]


def kernel(**inputs):
    inputs = {k: np.asarray(v) for k, v in inputs.items()}
    x = inputs["x"]
    x_state = [np.ascontiguousarray(x[c // 2, (c % 2) * T:(c % 2 + 1) * T, :].T) for c in range(NCORES)]
    if MODE == "fused":
        outs = run_launch(FUSED_LAUNCH, True, inputs, x_state, [{} for _ in range(NCORES)])
        x_state = [o["x_out"] for o in outs]
    else:
        extra = [{} for _ in range(NCORES)]
        for phases in SPLIT_LAUNCHES:
            outs = run_launch(phases, False, inputs, x_state, extra)
            x_state = [o["x_out"] for o in outs]
            extra = [{} for _ in range(NCORES)]
            for ph, l in phases:
                if ph == "P":
                    for c in range(NCORES):
                        o, oo = outs[c], outs[c ^ 1]
                        extra[c] = {f"q{l}_i": o[f"q{l}_o"], f"g{l}_i": o[f"g{l}_o"],
                                    f"k{l}_i": o[f"k{l}_o"], f"v{l}_i": o[f"v{l}_o"],
                                    f"ko{l}_i": oo[f"k{l}_o"], f"vo{l}_i": oo[f"v{l}_o"]}
    out = np.empty((4, 2048, D), np.float32)
    for c in range(NCORES):
        out[c // 2, (c % 2) * T:(c % 2 + 1) * T, :] = x_state[c].T
    return out
```

```python
import math
import numpy as np
import ml_dtypes
import concourse.bass as bass
import concourse.mybir as mybir
from concourse.bass_utils import run_bass_kernel_spmd

F32 = mybir.dt.float32
BF16 = mybir.dt.bfloat16
ALU = mybir.AluOpType
AF = mybir.ActivationFunctionType
AX = mybir.AxisListType

D = 2048
T = 1024
KC = 16
DEPTH = 4
SCALE = 128.0 ** -0.5
EPS = 1e-6
BIG = 30000.0
SLABW = 256
N_DMA_SEMS = 12
NCORES = 8

IN_W = {0: (2048, 512, 512, 2048), 1: (2048, 2048, 2048, 2048), 2: (2048, 2048, 2048, 2048)}


def alibi(n):
    return [2.0 ** (-8.0 * (h + 1) / n) for h in range(n)]


class Op:
    __slots__ = ("eng", "fn", "deps", "is_dma", "signal", "ticket", "idx", "inc")

    def __init__(self, eng, fn, is_dma):
        self.eng = eng
        self.fn = fn
        self.deps = []
        self.is_dma = is_dma
        self.signal = False
        self.ticket = None
        self.idx = -1
        self.inc = 16


class Sched:
    def __init__(self, nc, sync_same_engine=True):
        self.nc = nc
        self.ops = []
        self.last_writer = {}
        self.readers = {}
        self.sync_same_engine = sync_same_engine
        self.engs = {"pe": nc.tensor, "act": nc.scalar, "dve": nc.vector,
                     "pool": nc.gpsimd, "sp": nc.sync}

    def add(self, eng, fn, reads=(), writes=(), dma=False, inc=16):
        op = Op(eng, fn, dma)
        op.inc = inc
        op.idx = len(self.ops)
        deps = {}
        for k in reads:
            w = self.last_writer.get(k)
            if w is not None:
                deps[w.idx] = w
            if isinstance(k, str) and k.startswith("ps"):
                for r in self.readers.get(k, ()):
                    if r.eng != eng:
                        deps[r.idx] = r
        for k in writes:
            w = self.last_writer.get(k)
            if w is not None:
                deps[w.idx] = w
            for r in self.readers.get(k, ()):
                deps[r.idx] = r
        op.deps = list(deps.values())
        for k in writes:
            self.last_writer[k] = op
            self.readers[k] = []
        for k in reads:
            lst = self.readers.setdefault(k, [])
            if not dma:
                lst[:] = [r for r in lst if r.is_dma or r.eng != eng]
            lst.append(op)
        self.ops.append(op)
        return op

    def _skip(self, d, op):
        return (d.eng == op.eng and not op.is_dma and not d.is_dma
                and (d.eng == "pe" or not self.sync_same_engine))

    def emit(self):
        nc = self.nc
        for op in self.ops:
            for d in op.deps:
                if d.is_dma or self._skip(d, op):
                    continue
                d.signal = True
        sems = {e: nc.alloc_semaphore(name=f"s_{e}") for e in self.engs}
        dma_sems = {q: [nc.alloc_semaphore(name=f"d_{q}_{i}") for i in range(N_DMA_SEMS)]
                    for q in ("sp", "pool", "act")}
        dma_cnt = {q: [0] * N_DMA_SEMS for q in dma_sems}
        dma_rr = {q: 0 for q in dma_sems}
        cnt = {e: 0 for e in self.engs}
        waited = {e: {} for e in self.engs}

        def wait(eng, sem, val):
            key = id(sem)
            if waited[eng].get(key, 0) >= val:
                return
            waited[eng][key] = val
            self.engs[eng].wait_ge(sem, val)

        for op in self.ops:
            e = op.eng
            for d in op.deps:
                if d.is_dma:
                    wait(e, d.ticket[0], d.ticket[1])
                elif not self._skip(d, op):
                    wait(e, d.ticket[0], d.ticket[1])
            if op.is_dma:
                i = dma_rr[e]
                dma_rr[e] = (i + 1) % N_DMA_SEMS
                sem = dma_sems[e][i]
                if dma_cnt[e][i] > 0:
                    wait(e, sem, dma_cnt[e][i])
                ins = op.fn()
                dma_cnt[e][i] += op.inc
                ins.then_inc(sem, op.inc)
                op.ticket = (sem, dma_cnt[e][i])
            else:
                ins = op.fn()
                if op.signal:
                    cnt[e] += 1
                    ins.then_inc(sems[e], 1)
                    op.ticket = (sems[e], cnt[e])
        for q in dma_sems:
            for i in range(N_DMA_SEMS):
                if dma_cnt[q][i] > 0:
                    wait("sp", dma_sems[q][i], dma_cnt[q][i])
        for e2 in self.engs:
            if cnt[e2] > 0:
                wait("sp", sems[e2], cnt[e2])
        return len(self.ops)


class Prog:
    def __init__(self, phases, fused, dbg=False):
        self.phases = phases
        self.fused = fused
        self.nc = nc = bass.Bass("TRN2", target_bir_lowering=False)
        self.S = Sched(nc)
        self.ext_in = []
        self.ext_out = []
        self.dram = {}
        self._alloc()

    def din(self, name, shape, dt=F32):
        t = self.nc.dram_tensor(name, list(shape), dt, kind="ExternalInput")
        self.dram[name] = t
        self.ext_in.append(name)
        return t

    def dout(self, name, shape, dt=F32):
        t = self.nc.dram_tensor(name, list(shape), dt, kind="ExternalOutput")
        self.dram[name] = t
        self.ext_out.append(name)
        return t

    def dint(self, name, shape, dt=F32):
        t = self.nc.dram_tensor(name, list(shape), dt)
        self.dram[name] = t
        return t

    def sb(self, name, shape, dt):
        return self.nc.alloc_sbuf_tensor(name, list(shape), dt)

    def _alloc(self):
        nc = self.nc
        layers = sorted({l for _, l in self.phases})
        self.layers = layers
        kinds = {l % 3 for l in layers}
        self.x_in = self.din("x_in", [D, T])
        self.x_out = self.dout("x_out", [D, T])
        self.gpre_d = self.din("gpre", [128, DEPTH * KC])
        self.gpost_d = self.din("gpost", [128, DEPTH * KC])
        self.pT_d, self.w_out_d, self.pe_proj_d, self.pe_gate_d, self.w_in_d = {}, {}, {}, {}, {}
        for ph, l in self.phases:
            if ph == "P":
                self.w_in_d[l] = self.din(f"w_in{l}", [D, sum(IN_W[l % 3])])
            else:
                self.pT_d[l] = self.din(f"pT{l}", [256, T])
                self.w_out_d[l] = self.din(f"w_out{l}", [D, D])
                self.pe_proj_d[l] = self.din(f"pe_proj{l}", [256, D])
                self.pe_gate_d[l] = self.din(f"pe_gate{l}", [D, D])
        self.sink_d = self.din("a_sink_bc", [128, 32])
        self.tabA_d = self.din("tabA", [4, 128, 1536])
        self.edge_d = self.din("edgeA", [128, 2])
        self.rpr_d = self.din("rpr", [16, 64, 15 * 127])
        self.colmask_d = self.din("colmaskB", [128, 64])
        self.rowmask_d = self.din("rowmaskB", [2, 128], BF16)
        self.rowind_d = self.din("rowindB", [2, 128], BF16)
        self.lam_d = self.din("c_lambda_bc", [128, 512])
        self.subln_d = self.din("c_subln_t", [128, 2])
        self.Town_d = self.din("TownC", [128, 1920])
        self.Toth_d = self.din("TothC", [128, 1920])
        self.q_w, self.q_r, self.g_w, self.g_r = {}, {}, {}, {}
        self.k_w, self.k_r, self.v_w, self.v_r = {}, {}, {}, {}
        self.k_oth, self.v_oth, self.k_all, self.v_all = {}, {}, {}, {}
        for l in layers:
            _, dk, dv, _ = IN_W[l % 3]
            hasP = ("P", l) in self.phases
            hasA = ("AE", l) in self.phases
            if hasP and hasA:
                assert self.fused
                self.q_w[l] = self.q_r[l] = self.dint(f"q{l}", [D, T], BF16)
                self.g_w[l] = self.g_r[l] = self.dint(f"g{l}", [D, T], BF16)
                self.k_w[l] = self.k_r[l] = self.dint(f"k{l}", [dk, T], BF16)
                self.v_w[l] = self.v_r[l] = self.dint(f"v{l}", [T, dv], BF16)
                self.k_all[l] = self.dint(f"kall{l}", [2 * dk, T], BF16)
                self.v_all[l] = self.dint(f"vall{l}", [2 * T, dv], BF16)
                self.k_oth[l] = self.dint(f"koth{l}", [dk, T], BF16)
                self.v_oth[l] = self.dint(f"voth{l}", [T, dv], BF16)
            else:
                if hasP:
                    self.q_w[l] = self.dout(f"q{l}_o", [D, T], BF16)
                    self.g_w[l] = self.dout(f"g{l}_o", [D, T], BF16)
                    self.k_w[l] = self.dout(f"k{l}_o", [dk, T], BF16)
                    self.v_w[l] = self.dout(f"v{l}_o", [T, dv], BF16)
                if hasA:
                    self.q_r[l] = self.din(f"q{l}_i", [D, T], BF16)
                    self.g_r[l] = self.din(f"g{l}_i", [D, T], BF16)
                    self.k_r[l] = self.din(f"k{l}_i", [dk, T], BF16)
                    self.v_r[l] = self.din(f"v{l}_i", [T, dv], BF16)
                    self.k_oth[l] = self.din(f"ko{l}_i", [dk, T], BF16)
                    self.v_oth[l] = self.din(f"vo{l}_i", [T, dv], BF16)
        self.yT = self.dint("yT", [D, T], F32)
        self.xT = self.sb("xT", [128, KC, T], F32)
        self.hT = self.sb("hT", [128, KC, T], BF16)
        self.wslab = [self.sb(f"wslab{i}", [128, KC, SLABW], BF16) for i in range(2)]
        self.ones = self.sb("ones", [128, 128], BF16)
        self.zeros = self.sb("zeros", [128, 16], F32)
        self.zeros128 = self.sb("zeros128", [128, 128], F32)
        self.esk = self.sb("esk", [128, 512], F32)
        self.gpre = self.sb("gpre_s", [128, DEPTH * KC], F32)
        self.gpost = self.sb("gpost_s", [128, DEPTH * KC], F32)
        self.sq = [self.sb(f"sq{i}", [128, 2, 512], BF16) for i in range(2)]
        self.rstd = [self.sb(f"rstd{i}", [128, 512], F32) for i in range(2)]
        self.stage = [self.sb(f"stage{i}", [128, T], BF16) for i in range(2)]
        self.ystage = [self.sb(f"ystage{i}", [128, T], F32) for i in range(2)]
        self.gate_st = [self.sb(f"gate{i}", [128, 512], F32) for i in range(2)]
        self.peslab = [self.sb(f"peslab{i}", [128, 2, SLABW], BF16) for i in range(2)]
        self.pTb = self.sb("pTb", [128, 2, T], BF16)
        self.kt_own = self.sb("kt_own", [128, T], BF16)
        self.kt_oth = self.sb("kt_oth", [128, T], BF16)
        self.qb = [self.sb(f"qb{i}", [128, T], BF16) for i in range(2)]
        self.gb = self.sb("gb", [128, 2, T], BF16)
        self.v_own_s = self.sb("v_own", [128, 8, 256], BF16)
        self.v_oth_s = self.sb("v_oth", [128, 8, 256], BF16)
        self.tab = self.sb("tab", [128, 3840], F32)
        self.tmp = self.sb("tmp", [128, 3, 512], F32)
        self.pt = self.sb("pt", [128, 3, 512], BF16)
        self.on0 = self.sb("on0", [128, 2, 512], F32)
        self.o_s = self.sb("o_s", [128, 2, 512], F32)
        self.rz = self.sb("rz", [128, 512], F32)
        self.zt = self.sb("zt", [128, 512], F32)
        self.es = self.sb("es", [128, 32], F32)
        self.sink_s = self.sb("sink_s", [128, 32], F32)
        self.edge = self.sb("edge", [128, 2], F32)
        self.lam_s = self.sb("lam_s", [128, 512], F32)
        self.lam_t = self.sb("lam_t", [128, 8], F32)
        self.subln_s = self.sb("subln_s", [128, 2], F32)
        self.rowmask = self.sb("rowmask", [2, 128], BF16)
        self.rowind = self.sb("rowind", [2, 128], BF16)
        self.cm64 = self.sb("cm64", [128, 64], F32)
        self.ps = [nc.alloc_psum_tensor(f"ps{i}", [128, 512], F32) for i in range(8)]
        self.wi = 0
        self.par = None

    def dma(self, q, out, in_, reads, writes):
        eng = {"sp": self.nc.sync, "pool": self.nc.gpsimd, "act": self.nc.scalar}[q]
        self.S.add(q, lambda: eng.dma_start(out=out, in_=in_), reads, writes, dma=True)

    def act(self, out, in_, func, reads, writes, bias=0.0, scale=1.0):
        nc = self.nc
        self.S.add("act", lambda: nc.scalar.activation(out=out, in_=in_, func=func, bias=bias, scale=scale),
                   reads, writes)

    def tt(self, eng, out, in0, in1, op, reads, writes):
        e = {"dve": self.nc.vector, "pool": self.nc.gpsimd}[eng]
        self.S.add(eng, lambda: e.tensor_tensor(out=out, in0=in0, in1=in1, op=op), reads, writes)

    def stt(self, out, in0, scalar, in1, op0, op1, reads, writes, eng="dve"):
        e = {"dve": self.nc.vector, "pool": self.nc.gpsimd}[eng]
        self.S.add(eng, lambda: e.scalar_tensor_tensor(out=out, in0=in0, scalar=scalar, in1=in1,
                                                        op0=op0, op1=op1), reads, writes)

    def ts(self, out, in0, s1, s2, op0, op1, reads, writes):
        nc = self.nc
        if s2 is None:
            self.S.add("dve", lambda: nc.vector.tensor_scalar(out=out, in0=in0, scalar1=s1, scalar2=None,
                                                              op0=op0), reads, writes)
        else:
            self.S.add("dve", lambda: nc.vector.tensor_scalar(out=out, in0=in0, scalar1=s1, scalar2=s2,
                                                              op0=op0, op1=op1), reads, writes)

    def recip(self, out, in_, reads, writes):
        nc = self.nc
        self.S.add("dve", lambda: nc.vector.reciprocal(out=out, in_=in_), reads, writes)

    def mm(self, out, lhsT, rhs, start, stop, reads, writes):
        nc = self.nc
        self.S.add("pe", lambda: nc.tensor.matmul(out, lhsT, rhs, start=start, stop=stop), reads, writes)

    def memset(self, ap, val, writes):
        nc = self.nc
        self.S.add("dve", lambda: nc.vector.memset(ap, val), (), writes)

    @staticmethod
    def slab_order(kind):
        wq, wk, wv, wg = IN_W[kind]
        b = [0, wq // SLABW, (wq + wk) // SLABW, (wq + wk + wv) // SLABW, (wq + wk + wv + wg) // SLABW]
        q, k, v, g = (list(range(b[i], b[i + 1])) for i in range(4))
        return k + v + q + g

    def plan_slabs(self):
        slabs = []
        for ph, l in self.phases:
            if ph == "P":
                for s in self.slab_order(l % 3):
                    slabs.append((self.w_in_d[l], s))
            elif ph == "AE":
                for s in range(D // SLABW):
                    slabs.append((self.w_out_d[l], s))
                for s in range(D // SLABW):
                    slabs.append((self.pe_gate_d[l], s))
        self.slabs = slabs
        self.slab_issued = 0
        self.slab_used = 0

    def _issue_slab(self):
        i = self.slab_issued
        if i >= len(self.slabs):
            return
        w, s = self.slabs[i]
        src = w[:, s * SLABW:(s + 1) * SLABW].rearrange("(kc p) n -> p kc n", p=128)
        self.dma("pool", self.wslab[i % 2][:], src, (), [f"wslab{i % 2}"])
        self.slab_issued += 1

    def next_slab(self):
        i = self.slab_used
        while self.slab_issued <= min(i + 1, len(self.slabs) - 1):
            self._issue_slab()
        self.slab_used += 1
        return self.wslab[i % 2], f"wslab{i % 2}"

    def build(self):
        nc = self.nc
        self.plan_slabs()
        if self.fused:
            self.par = nc.sync.partition_id() % 2
        self.memset(self.ones[:], 1.0, ["ones"])
        self.memset(self.zeros[:], 0.0, ["zeros"])
        self.memset(self.zeros128[:], 0.0, ["zeros128"])
        self.dma("sp", self.gpre[:], self.gpre_d.ap(), (), ["gpre"])
        self.dma("sp", self.gpost[:], self.gpost_d.ap(), (), ["gpost"])
        xv = self.x_in.ap().rearrange("(kc p) n -> p kc n", p=128)
        for kc in range(KC):
            self.dma("sp", self.xT[:, kc, :], xv[:, kc, :], (), [f"x{kc}_0", f"x{kc}_1"])
        for ph, l in self.phases:
            if ph == "P":
                self.phase_P(l)
            else:
                kind = l % 3
                if kind == 0:
                    self.attn_A(l)
                elif kind == 1:
                    self.attn_B(l)
                else:
                    self.attn_C(l)
                self.phase_E(l)
        ov = self.x_out.ap().rearrange("(kc p) n -> p kc n", p=128)
        for kc in range(KC):
            self.dma("sp", ov[:, kc, :], self.xT[:, kc, :], [f"x{kc}_0", f"x{kc}_1"], [])
        n = self.S.emit()
        return n

    def stats_rstd(self, tt, src_fn, src_keys, nchunks, inv_n, rstd_i):
        ps = self.ps[7]
        for kc in range(nchunks):
            sqb = self.sq[kc % 2]
            self.act(sqb[:, 0, :], src_fn(kc), AF.Square, src_keys(kc), [f"sq{kc % 2}"])
            self.mm(ps[:], self.ones[:], sqb[:, 0, :], kc == 0, kc == nchunks - 1,
                    ["ones", f"sq{kc % 2}"], ["ps7"])
        r = self.rstd[rstd_i]
        self.recip_act(r[:], ps[:], ["ps7"], [f"rstd{rstd_i}"], in_scale=inv_n, in_bias=EPS, power=-0.5)

    def phase_P(self, l):
        kind = l % 3
        wq, wk, wv, wg = IN_W[kind]
        for tt in range(2):
            sl = slice(tt * 512, (tt + 1) * 512)
            self.stats_rstd(tt, lambda kc: self.xT[:, kc, sl], lambda kc: [f"x{kc}_{tt}"], KC, 1.0 / D, tt)
            for kc in range(KC):
                self.stt(self.hT[:, kc, sl], self.xT[:, kc, sl], self.gpre[:, l * KC + kc:l * KC + kc + 1],
                         self.rstd[tt][:], ALU.mult, ALU.mult,
                         [f"x{kc}_{tt}", "gpre", f"rstd{tt}"], [f"h{kc}_{tt}"])
        order = self.slab_order(kind)
        last_k = (wq + wk) // SLABW - 1
        last_v = (wq + wk + wv) // SLABW - 1
        ev = 0
        for s in order:
            c0 = s * SLABW
            slab, skey = self.next_slab()
            if c0 < wq:
                seg, dst, r0 = "q", self.q_w[l], c0
            elif c0 < wq + wk:
                seg, dst, r0 = "k", self.k_w[l], c0 - wq
            elif c0 < wq + wk + wv:
                seg, dst, r0 = "v", self.v_w[l], c0 - wq - wk
            else:
                seg, dst, r0 = "g", self.g_w[l], c0 - wq - wk - wv
            if seg != "v":
                for mi in range(SLABW // 128):
                    st = self.stage[ev % 2]
                    stk = f"stage{ev % 2}"
                    for tt in range(2):
                        sl = slice(tt * 512, (tt + 1) * 512)
                        pi = (ev * 2 + tt) % 4
                        ps = self.ps[pi]
                        for kc in range(KC):
                            self.mm(ps[:], slab[:, kc, mi * 128:(mi + 1) * 128], self.hT[:, kc, sl],
                                    kc == 0, kc == KC - 1, [skey, f"h{kc}_{tt}"], [f"ps{pi}"])
                        if seg == "g":
                            self.act(st[:, sl], ps[:], AF.Silu, [f"ps{pi}"], [stk + f"_{tt}"])
                        elif (ev + tt) % 2 == 0:
                            self.act(st[:, sl], ps[:], AF.Copy, [f"ps{pi}"], [stk + f"_{tt}"])
                        else:
                            nc = self.nc
                            self.S.add("dve", (lambda o=st[:, sl], i=ps[:]: nc.vector.tensor_copy(out=o, in_=i)),
                                       [f"ps{pi}"], [stk + f"_{tt}"])
                    row = r0 + mi * 128
                    self.dma("sp", dst[row:row + 128, :], st[:], [stk + "_0", stk + "_1"], [f"{seg}{l}"])
                    ev += 1
            else:
                for tb in range(8):
                    st = self.stage[ev % 2]
                    stk = f"stage{ev % 2}"
                    pi = ev % 4
                    ps = self.ps[pi]
                    for kc in range(KC):
                        self.mm(ps[:, 0:SLABW], self.hT[:, kc, tb * 128:(tb + 1) * 128], slab[:, kc, :],
                                kc == 0, kc == KC - 1, [skey, f"h{kc}_{tb // 4}"], [f"ps{pi}"])
                    nc = self.nc
                    self.S.add("dve", (lambda o=st[:, 0:SLABW], i=ps[:, 0:SLABW]: nc.vector.tensor_copy(out=o, in_=i)),
                               [f"ps{pi}"], [stk + "_0", stk + "_1"])
                    self.dma("sp", dst[tb * 128:(tb + 1) * 128, r0:r0 + SLABW], st[:, 0:SLABW],
                             [stk + "_0", stk + "_1"], [f"v{l}"])
                    ev += 1
            if self.fused and s == last_k:
                self.exchange_gather(l, "k")
            if self.fused and s == last_v:
                self.exchange_gather(l, "v")
        if self.fused:
            self.exchange_copy(l)

    def _xchg(self, l, nm):
        src, allt, otht = ((self.k_w[l], self.k_all[l], self.k_oth[l]) if nm == "k"
                           else (self.v_w[l], self.v_all[l], self.v_oth[l]))
        rows, cols = src.shape
        rc = min(rows, (2 << 20) // (cols * 2))
        return src, allt, otht, rc, rows // rc

    def exchange_gather(self, l, nm):
        nc = self.nc
        groups = [[2 * i, 2 * i + 1] for i in range(NCORES // 2)]
        src, allt, otht, rc, nchunk = self._xchg(l, nm)
        for c in range(nchunk):
            self.S.add("pool", (lambda s=src[c * rc:(c + 1) * rc, :], d=allt[c * 2 * rc:(c + 1) * 2 * rc, :]:
                                nc.gpsimd.collective_compute(
                "AllGather", mybir.AluOpType.bypass, replica_groups=groups,
                ins=[s.opt()], outs=[d.opt()])), [f"{nm}{l}"], [f"{nm}all{l}"], dma=True, inc=1)

    def exchange_copy(self, l):
        for nm in ("k", "v"):
            src, allt, otht, rc, nchunk = self._xchg(l, nm)
            gv = allt.ap().rearrange("(c r f) n -> c r f n", c=nchunk, r=2)
            self.dma("sp", otht.ap().rearrange("(c f) n -> f c n", c=nchunk),
                     gv[:, bass.ds(1 - self.par, 1), :, :].rearrange("c 1 f n -> f c n"),
                     [f"{nm}all{l}"], [f"{nm}oth{l}"])

    def oth_k(self, l, f0, c0, c1):
        return self.k_oth[l][f0:f0 + 128, c0:c1], ([f"koth{l}"] if self.fused else [])

    def oth_v(self, l, kb0, nkb, d0, dn):
        return (self.v_oth[l][kb0 * 128:(kb0 + nkb) * 128, d0:d0 + dn].rearrange("(kb p) d -> p kb d", p=128),
                [f"voth{l}"] if self.fused else [])

    def run_pipeline(self, groups, LA):
        flat = []
        for gi, g in enumerate(groups):
            nt = len(g["tasks"])
            for ti, (s1, s2) in enumerate(g["tasks"]):
                flat.append((gi, s1, s2, ti == nt - 1))
        for gi in range(min(2, len(groups))):
            if groups[gi]["loads"]:
                groups[gi]["loads"]()
        n = len(flat)
        for i in range(n + LA):
            if i < n:
                flat[i][1](i)
            j = i - LA
            if j >= 0:
                gj, _, s2, last = flat[j]
                s2(j)
                if last:
                    if groups[gj]["post"]:
                        groups[gj]["post"]()
                    if gj + 2 < len(groups) and groups[gj + 2]["loads"]:
                        groups[gj + 2]["loads"]()

    def recip_act(self, out, in_, rk, wk, in_scale=1.0, in_bias=0.0, power=-1.0):
        self.act(out, in_, AF.Ln, rk, wk, bias=in_bias, scale=in_scale)
        self.act(out, out, AF.Exp, wk, wk, scale=power)

    def attn_A(self, l):
        j = l // 3
        self.dma("sp", self.sink_s[:], self.sink_d.ap(), (), ["sink_s"])
        self.dma("sp", self.edge[:], self.edge_d.ap(), (), ["edge"])
        tabv = self.tab[:, 0:1536].rearrange("p (j n) -> p j n", j=3)
        for kvh in range(4):
            f0 = kvh * 128
            self.dma("sp", self.kt_own[:], self.k_r[l][f0:f0 + 128, :], [f"k{l}"], ["kt_own"])
            ap, rk = self.oth_k(l, f0, 896, 1024)
            self.dma("sp", self.kt_oth[:, 0:128], ap, rk, ["kt_oth"])
            ap, rk = self.oth_k(l, f0, 0, 128)
            self.dma("sp", self.kt_oth[:, 128:256], ap, rk, ["kt_oth"])
            self.dma("sp", self.v_own_s[:, :, 0:128],
                     self.v_r[l][:, f0:f0 + 128].rearrange("(kb p) d -> p kb d", p=128), [f"v{l}"], ["v_own0"])
            ap, rk = self.oth_v(l, 7, 1, f0, 128)
            self.dma("sp", self.v_oth_s[:, 0:1, 0:128], ap, rk, ["v_oth0"])
            ap, rk = self.oth_v(l, 0, 1, f0, 128)
            self.dma("sp", self.v_oth_s[:, 1:2, 0:128], ap, rk, ["v_oth0"])
            self.dma("sp", self.tab[:, 0:1536], self.tabA_d[kvh], (), ["tabL"])
            for g in range(4):
                c = j * 16 + kvh * 4 + g
                self.act(self.esk[:, g * 128:(g + 1) * 128], self.zeros128[:], AF.Exp, ["zeros128", "sink_s"], ["esk"],
                         bias=self.sink_s[:, c:c + 1], scale=1.0)
            groups = []
            for qi in range(8):
                par = qi % 2
                qs = slice(qi * 128, (qi + 1) * 128)
                q4 = self.qb[0][:, par * 512:(par + 1) * 512]
                g4 = self.gb[:, par, 0:512]
                qk, gk = f"qb0_{par}", f"gb{par}"
                po, pz = (4, 5) if par == 0 else (6, 7)

                def loads(qs=qs, q4=q4, g4=g4, qk=qk, gk=gk, kvh=kvh):
                    qsrc = self.q_r[l][kvh * 512:(kvh + 1) * 512, qs].rearrange("(g p) n -> p g n", p=128)
                    self.dma("sp", q4.rearrange("p (g n) -> p g n", g=4), qsrc, [f"q{l}"], [qk])
                    gsrc = self.g_r[l][kvh * 512:(kvh + 1) * 512, qs].rearrange("(g p) n -> p g n", p=128)
                    self.dma("sp", g4.rearrange("p (g n) -> p g n", g=4), gsrc, [f"g{l}"], [gk])

                tasks = []
                for jj in range(3):
                    kb = qi + jj - 1
                    if kb < 0:
                        blk = (self.kt_oth[:, 0:128], "kt_oth", self.v_oth_s[:, 0, 0:128], "v_oth0", 0)
                    elif kb > 7:
                        blk = (self.kt_oth[:, 128:256], "kt_oth", self.v_oth_s[:, 1, 0:128], "v_oth0", 1)
                    else:
                        blk = (self.kt_own[:, kb * 128:(kb + 1) * 128], "kt_own", self.v_own_s[:, kb, 0:128], "v_own0", None)

                    def s1(i, blk=blk, jj=jj, q4=q4, qk=qk):
                        kap, kk, vap, vk, edge = blk
                        b = i % 3
                        self.mm(self.ps[b][:], kap, q4, True, True, [kk, qk], [f"ps{b}"])
                        self.tt("dve", self.tmp[:, b, :], self.ps[b][:], tabv[:, jj, :], ALU.add,
                                [f"ps{b}", "tabL"], [f"tmp{b}"])
                        if edge is None:
                            self.act(self.pt[:, b, :], self.tmp[:, b, :], AF.Exp, [f"tmp{b}"], [f"pt{b}"], scale=SCALE)
                        else:
                            self.act(self.pt[:, b, :], self.tmp[:, b, :], AF.Exp, [f"tmp{b}", "edge"], [f"pt{b}"],
                                     bias=self.edge[:, edge:edge + 1], scale=SCALE)

                    def s2(i, blk=blk, jj=jj, po=po, pz=pz):
                        kap, kk, vap, vk, edge = blk
                        b = i % 3
                        self.mm(self.ps[po][:], vap, self.pt[:, b, :], jj == 0, jj == 2, [vk, f"pt{b}"], [f"ps{po}"])
                        self.mm(self.ps[pz][:], self.ones[:], self.pt[:, b, :], jj == 0, jj == 2,
                                ["ones", f"pt{b}"], [f"ps{pz}"])
                    tasks.append((s1, s2))

                def post(qs=qs, g4=g4, gk=gk, po=po, pz=pz, kvh=kvh, qi=qi):
                    self.tt("dve", self.zt[:], self.ps[pz][:], self.esk[:], ALU.add, [f"ps{pz}", "esk"], ["zt"])
                    self.recip_act(self.rz[:], self.zt[:], ["zt"], ["rz"])
                    self.tt("dve", self.o_s[:, 0, :], self.ps[po][:], self.rz[:], ALU.mult, [f"ps{po}", "rz"], ["o_s"])
                    okeys = [f"h{kvh * 4 + g}_{qi // 4}" for g in range(4)]
                    self.tt("pool", self.hT[:, kvh * 4:(kvh + 1) * 4, qs],
                            self.o_s[:, 0, :].rearrange("p (g n) -> p g n", g=4),
                            g4.rearrange("p (g n) -> p g n", g=4), ALU.mult, ["o_s", gk], okeys)
                groups.append(dict(loads=loads, tasks=tasks, post=post))
            self.run_pipeline(groups, 2)

    def attn_B(self, l):
        self.dma("sp", self.rowmask[:], self.rowmask_d.ap(), (), ["rowmask"])
        self.dma("sp", self.rowind[:], self.rowind_d.ap(), (), ["rowind"])
        self.dma("sp", self.cm64[:], self.colmask_d.ap(), (), ["cm64"])
        self.memset(self.tab[:, 0:1920], 0.0, ["tabL"])
        self.memset(self.tab[:, 1920:3840], 0.0, ["tabH"])
        ktown = [(self.kt_own, ["kt_own"]), (self.qb[1], ["qb1"])]
        ktoth = [(self.kt_oth, ["kt_oth"]), (self.stage[0], ["stage0_0", "stage0_1"])]
        qbuf = [(self.qb[0], ["qb0_0", "qb0_1"]), (self.stage[1], ["stage1_0", "stage1_1"])]
        groups = []
        gidx = 0
        for h in range(16):
            hp = h % 2
            f0 = h * 128
            kto, ktok = ktown[hp]
            ktt, kttk = ktoth[hp]
            qq, qqk = qbuf[hp]
            gg, ggk = self.gb[:, hp, :], f"gb{hp}"
            vo, vok = self.v_own_s[:, :, hp * 128:(hp + 1) * 128], f"v_own{hp}"
            vt, vtk = self.v_oth_s[:, :, hp * 128:(hp + 1) * 128], f"v_oth{hp}"
            tkey = "tabL" if hp == 0 else "tabH"
            TB = self.tab[:, hp * 1920:(hp + 1) * 1920].rearrange("p (j n) -> p j n", j=30)

            def loads(h=h, f0=f0, kto=kto, ktok=ktok, ktt=ktt, kttk=kttk, qq=qq, qqk=qqk, gg=gg, ggk=ggk,
                      vo=vo, vok=vok, vt=vt, vtk=vtk, tkey=tkey, TB=TB):
                self.dma("sp", kto[:], self.k_r[l][f0:f0 + 128, :], [f"k{l}"], ktok)
                ap, rk = self.oth_k(l, f0, 0, 256)
                self.dma("sp", ktt[:, 0:256], ap, rk, kttk)
                ap, rk = self.oth_k(l, f0, 768, 1024)
                self.dma("sp", ktt[:, 768:1024], ap, rk, kttk)
                self.dma("sp", vo, self.v_r[l][:, f0:f0 + 128].rearrange("(kb p) d -> p kb d", p=128), [f"v{l}"], [vok])
                ap, rk = self.oth_v(l, 0, 2, f0, 128)
                self.dma("sp", vt[:, 0:2, :], ap, rk, [vtk])
                ap, rk = self.oth_v(l, 6, 2, f0, 128)
                self.dma("sp", vt[:, 6:8, :], ap, rk, [vtk])
                self.dma("sp", qq[:], self.q_r[l][f0:f0 + 128, :], [f"q{l}"], qqk)
                self.dma("sp", gg, self.g_r[l][f0:f0 + 128, :], [f"g{l}"], [ggk])
                for a in range(2):
                    src = bass.AP(self.rpr_d, h * 64 * 1905 + 63, [[1904, 64], [127, 15], [1, 64]])
                    self.dma("sp", TB[a * 64:(a + 1) * 64, a + 7:a + 22, :], src, (), [tkey])
                for a in range(2):
                    cmb = bass.AP(self.cm64, a * 64 * 64, [[64, 64], [0, 15], [1, 64]])
                    self.stt(TB[a * 64:(a + 1) * 64, a + 7:a + 22, :], TB[a * 64:(a + 1) * 64, a + 7:a + 22, :],
                             1.0 / SCALE, cmb, ALU.mult, ALU.add, [tkey, "cm64"], [tkey])

            for tt in range(2):
                sl = slice(tt * 512, (tt + 1) * 512)
                if tt == 0:
                    blks = [("own", kb, 2 * kb + 7) for kb in range(6)] + [("oth", kb, 2 * kb - 9) for kb in (6, 7)]
                else:
                    blks = [("own", kb, 2 * kb - 1) for kb in range(2, 8)] + [("oth", kb, 15 + 2 * kb) for kb in (0, 1)]
                nb = len(blks)
                po, pz = (4, 5) if gidx % 2 == 0 else (6, 7)
                tasks = []
                for i_b, (who, kb, d0) in enumerate(blks):
                    pidx = tt * 8 + i_b
                    if who == "own":
                        kap, kk, vap, vk = kto[:, kb * 128:(kb + 1) * 128], ktok, vo[:, kb, :], vok
                    else:
                        kap, kk, vap, vk = ktt[:, kb * 128:(kb + 1) * 128], kttk, vt[:, kb, :], vtk

                    def s1(i, kap=kap, kk=kk, pidx=pidx, d0=d0, qq=qq, qqk=qqk, sl=sl, TB=TB, tkey=tkey):
                        b = i % 3
                        self.mm(self.ps[b][:], kap, qq[:, sl], True, False, list(kk) + list(qqk), [f"ps{b}"])
                        self.mm(self.ps[b][:], self.rowind[:],
                                bass.AP(self.rowmask, pidx * 8, [[128, 2], [1, 8], [0, 64]]), False, True,
                                ["rowind", "rowmask"], [f"ps{b}"])
                        j0 = 21 - d0
                        self.tt("dve", self.tmp[:, b, :], self.ps[b][:],
                                TB[:, j0:j0 + 8, :].rearrange("p j n -> p (j n)"), ALU.add,
                                [f"ps{b}", tkey], [f"tmp{b}"])
                        self.act(self.pt[:, b, :], self.tmp[:, b, :], AF.Exp, [f"tmp{b}"], [f"pt{b}"], scale=SCALE)

                    def s2(i, vap=vap, vk=vk, i_b=i_b, nb=nb, po=po, pz=pz):
                        b = i % 3
                        self.mm(self.ps[po][:], vap, self.pt[:, b, :], i_b == 0, i_b == nb - 1, [vk, f"pt{b}"], [f"ps{po}"])
                        self.mm(self.ps[pz][:], self.ones[:], self.pt[:, b, :], i_b == 0, i_b == nb - 1,
                                ["ones", f"pt{b}"], [f"ps{pz}"])
                    tasks.append((s1, s2))

                def post(h=h, sl=sl, tt=tt, gg=gg, ggk=ggk, po=po, pz=pz):
                    self.recip_act(self.rz[:], self.ps[pz][:], [f"ps{pz}"], ["rz"])
                    self.tt("dve", self.o_s[:, 0, :], self.ps[po][:], self.rz[:], ALU.mult, [f"ps{po}", "rz"], ["o_s"])
                    self.tt("pool", self.hT[:, h, sl], self.o_s[:, 0, :], gg[:, sl], ALU.mult,
                            ["o_s", ggk], [f"h{h}_{tt}"])
                groups.append(dict(loads=loads if tt == 0 else None, tasks=tasks, post=post))
                gidx += 1
        self.run_pipeline(groups, 2)

    def attn_C(self, l):
        nc = self.nc
        lam_init = 0.8 - 0.6 * math.exp(-0.3 * l)
        slopes = alibi(8)
        self.dma("sp", self.lam_s[:], self.lam_d.ap(), (), ["lam_s"])
        self.dma("sp", self.subln_s[:], self.subln_d.ap(), (), ["subln_s"])
        self.dma("sp", self.tab[:, 0:1920], self.Town_d.ap(), (), ["tabL"])
        self.dma("sp", self.tab[:, 1920:3840], self.Toth_d.ap(), (), ["tabH"])
        lt = self.lam_t
        for i in range(2):
            self.tt("dve", self.zt[:, 0:128], self.lam_s[:, (2 * i) * 128:(2 * i + 1) * 128],
                    self.lam_s[:, (2 * i + 1) * 128:(2 * i + 2) * 128], ALU.mult, ["lam_s"], ["zt"])
            self.S.add("dve", (lambda i=i: nc.vector.reduce_sum(out=lt[:, i:i + 1], in_=self.zt[:, 0:128], axis=AX.X)),
                       ["zt"], ["lam_t"])
        self.act(lt[:, 2:4], lt[:, 0:2], AF.Exp, ["lam_t"], ["lam_t"])
        self.tt("dve", lt[:, 4:5], lt[:, 3:4], lt[:, 2:3], ALU.subtract, ["lam_t"], ["lam_t"])
        self.ts(lt[:, 5:6], lt[:, 4:5], -lam_init, None, ALU.add, None, ["lam_t"], ["neglam"])
        self.ts(self.subln_s[:], self.subln_s[:], 1.0 - lam_init, None, ALU.mult, None, ["subln_s"], ["subln_s"])
        neglam = lt[:, 5:6]
        Town = self.tab[:, 0:1920]
        Toth = self.tab[:, 1920:3840]
        ktown = [(self.kt_own, ["kt_own"]), (self.qb[1], ["qb1"])]
        ktoth = [(self.kt_oth, ["kt_oth"]), (self.stage[0], ["stage0_0", "stage0_1"])]
        for h in range(8):
            self.dma("sp", self.v_own_s[:], self.v_r[l][:, h * 256:(h + 1) * 256].rearrange("(kb p) d -> p kb d", p=128),
                     [f"v{l}"], ["v_own0", "v_own1"])
            ap, rk = self.oth_v(l, 0, 8, h * 256, 256)
            self.dma("sp", self.v_oth_s[:], ap, rk, ["v_oth0", "v_oth1"])
            self.dma("sp", self.gb[:], self.g_r[l][h * 256:(h + 1) * 256, :].rearrange("(c p) n -> p c n", p=128),
                     [f"g{l}"], ["gb0", "gb1"])
            for m in range(2):
                f0 = h * 256 + m * 128
                self.dma("sp", ktown[m][0][:], self.k_r[l][f0:f0 + 128, :], [f"k{l}"], ktown[m][1])
                ap, rk = self.oth_k(l, f0, 0, T)
                self.dma("sp", ktoth[m][0][:], ap, rk, ktoth[m][1])
            groups = []
            gidx = 0
            for tt in range(2):
                sl = slice(tt * 512, (tt + 1) * 512)
                for m in range(2):
                    f0 = h * 256 + m * 128
                    gp = gidx % 2
                    q1 = self.qb[0][:, gp * 512:(gp + 1) * 512]
                    qk = f"qb0_{gp}"
                    po = 3

                    def loads(f0=f0, sl=sl, q1=q1, qk=qk):
                        self.dma("sp", q1, self.q_r[l][f0:f0 + 128, sl], [f"q{l}"], [qk])

                    tasks = []
                    for i_b in range(16):
                        kb = i_b % 8
                        if i_b < 8:
                            kap, kk, vap, vk, Tt, tk = ktown[m][0][:, kb * 128:(kb + 1) * 128], ktown[m][1], self.v_own_s, ["v_own0", "v_own1"], Town, "tabL"
                        else:
                            kap, kk, vap, vk, Tt, tk = ktoth[m][0][:, kb * 128:(kb + 1) * 128], ktoth[m][1], self.v_oth_s, ["v_oth0", "v_oth1"], Toth, "tabH"
                        off = tt * 512 - kb * 128 + 896

                        def s1(i, kap=kap, kk=kk, Tt=Tt, tk=tk, off=off, q1=q1, qk=qk):
                            b = i % 3
                            self.mm(self.ps[b][:], kap, q1, True, True, list(kk) + [qk], [f"ps{b}"])
                            self.stt(self.tmp[:, b, :], Tt[:, off:off + 512], slopes[h], self.ps[b][:],
                                     ALU.mult, ALU.add, [f"ps{b}", tk], [f"tmp{b}"])
                            self.act(self.pt[:, b, :], self.tmp[:, b, :], AF.Exp, [f"tmp{b}"], [f"pt{b}"], scale=SCALE)

                        def s2(i, vap=vap, vk=vk, kb=kb, i_b=i_b, po=po):
                            b = i % 3
                            for hf in range(2):
                                self.mm(self.ps[po + hf][:], vap[:, kb, hf * 128:(hf + 1) * 128], self.pt[:, b, :],
                                        i_b == 0, i_b == 15, list(vk) + [f"pt{b}"], [f"ps{po + hf}"])
                            self.mm(self.ps[po + 2][:], self.ones[:], self.pt[:, b, :], i_b == 0, i_b == 15,
                                    ["ones", f"pt{b}"], [f"ps{po + 2}"])
                        tasks.append((s1, s2))

                    def post(m=m, po=po, sl=sl, tt=tt):
                        self.recip_act(self.rz[:], self.ps[po + 2][:], [f"ps{po + 2}"], ["rz"])
                        if m == 0:
                            for hf in range(2):
                                self.tt("dve", self.on0[:, hf, :], self.ps[po + hf][:], self.rz[:], ALU.mult,
                                        [f"ps{po + hf}", "rz"], ["on0"])
                            return
                        for hf in range(2):
                            self.tt("dve", self.o_s[:, hf, :], self.ps[po + hf][:], self.rz[:], ALU.mult,
                                    [f"ps{po + hf}", "rz"], ["o_s"])
                        self.stt(self.on0[:], self.o_s[:], neglam, self.on0[:], ALU.mult, ALU.add,
                                 ["o_s", "on0", "neglam"], ["on0"])
                        sqb = self.sq[0]
                        self.act(sqb[:], self.on0[:], AF.Square, ["on0"], ["sq0"])
                        for hf in range(2):
                            self.mm(self.ps[6][:], self.ones[:], sqb[:, hf, :], hf == 0, hf == 1, ["ones", "sq0"], ["ps6"])
                        self.recip_act(self.rz[:], self.ps[6][:], ["ps6"], ["rz"], in_scale=1.0 / 256, in_bias=EPS, power=-0.5)
                        for hf in range(2):
                            self.stt(self.on0[:, hf, :], self.on0[:, hf, :], self.subln_s[:, hf:hf + 1], self.rz[:],
                                     ALU.mult, ALU.mult, ["on0", "subln_s", "rz"], ["on0"])
                        self.tt("pool", self.hT[:, 2 * h:2 * h + 2, sl], self.on0[:], self.gb[:, :, sl], ALU.mult,
                                ["on0", "gb0", "gb1"], [f"h{2 * h}_{tt}", f"h{2 * h + 1}_{tt}"])
                    groups.append(dict(loads=loads, tasks=tasks, post=post))
                    gidx += 1
            self.run_pipeline(groups, 2)

    def phase_E(self, l):
        nc = self.nc
        ev = 0
        pend = None
        for s in range(D // SLABW):
            slab, skey = self.next_slab()
            for mi in range(SLABW // 128):
                m = s * (SLABW // 128) + mi
                ys = self.ystage[ev % 2]
                ysk = f"ystage{ev % 2}"
                for tt in range(2):
                    sl = slice(tt * 512, (tt + 1) * 512)
                    pi = (ev * 2 + tt) % 4
                    ps = self.ps[pi]
                    for kc in range(KC):
                        self.mm(ps[:], slab[:, kc, mi * 128:(mi + 1) * 128], self.hT[:, kc, sl],
                                kc == 0, kc == KC - 1, [skey, f"h{kc}_{tt}"], [f"ps{pi}"])
                    self.S.add("dve", (lambda o=ys[:, sl], i=ps[:]: nc.vector.tensor_copy(out=o, in_=i)),
                               [f"ps{pi}"], [ysk + f"_{tt}"])
                    sqb = self.sq[tt]
                    self.act(sqb[:, 0, :], ys[:, sl], AF.Square, [ysk + f"_{tt}"], [f"sq{tt}"])
                    if pend is not None:
                        pend()
                    pend = (lambda tt=tt, m=m, sqb=sqb: self.mm(self.ps[6 + tt][:], self.ones[:], sqb[:, 0, :], m == 0,
                                                               m == KC - 1, ["ones", f"sq{tt}"], [f"ps{6 + tt}"]))
                self.dma("sp", self.yT[m * 128:(m + 1) * 128, :], ys[:], [ysk + "_0", ysk + "_1"], ["yT"])
                ev += 1
        pend()
        for tt in range(2):
            r = self.rstd[tt]
            self.recip_act(r[:], self.ps[6 + tt][:], [f"ps{6 + tt}"], [f"rstd{tt}"], in_scale=1.0 / D, in_bias=EPS, power=-0.5)
        for kc in range(KC):
            ys = self.ystage[kc % 2]
            ysk = f"ystage{kc % 2}"
            self.dma("sp", ys[:], self.yT[kc * 128:(kc + 1) * 128, :], ["yT"], [ysk + "_0", ysk + "_1"])
            for tt in range(2):
                sl = slice(tt * 512, (tt + 1) * 512)
                self.stt(ys[:, sl], ys[:, sl], self.gpost[:, l * KC + kc:l * KC + kc + 1], self.rstd[tt][:],
                         ALU.mult, ALU.mult, [ysk + f"_{tt}", "gpost", f"rstd{tt}"], [ysk + f"_{tt}"])
                self.tt("dve", self.xT[:, kc, sl], self.xT[:, kc, sl], ys[:, sl], ALU.add,
                        [f"x{kc}_{tt}", ysk + f"_{tt}"], [f"x{kc}_{tt}"])
                self.act(self.hT[:, kc, sl], self.xT[:, kc, sl], AF.Copy, [f"x{kc}_{tt}"], [f"h{kc}_{tt}"])
        self.dma("pool", self.pTb[:], self.pT_d[l].ap().rearrange("(c p) n -> p c n", p=128), (), ["pTb"])
        ev = 0
        for s in range(D // SLABW):
            slab, skey = self.next_slab()
            pes = self.peslab[s % 2]
            pek = f"peslab{s % 2}"
            self.dma("pool", pes[:], self.pe_proj_d[l][:, s * SLABW:(s + 1) * SLABW].rearrange("(c p) n -> p c n", p=128),
                     (), [pek])
            for mi in range(SLABW // 128):
                m = s * (SLABW // 128) + mi
                for tt in range(2):
                    sl = slice(tt * 512, (tt + 1) * 512)
                    pi = (ev * 2 + tt) % 4
                    ps = self.ps[pi]
                    pp = self.ps[4 + (ev * 2 + tt) % 2]
                    ppk = f"ps{4 + (ev * 2 + tt) % 2}"
                    for kc in range(KC):
                        self.mm(ps[:], slab[:, kc, mi * 128:(mi + 1) * 128], self.hT[:, kc, sl],
                                kc == 0, kc == KC - 1, [skey, f"h{kc}_{tt}"], [f"ps{pi}"])
                    for c in range(2):
                        self.mm(pp[:], pes[:, c, mi * 128:(mi + 1) * 128], self.pTb[:, c, sl], c == 0, c == 1,
                                [pek, "pTb"], [ppk])
                    gs = self.gate_st[(ev * 2 + tt) % 2]
                    gk = f"gate{(ev * 2 + tt) % 2}"
                    self.act(gs[:], ps[:], AF.Sigmoid, [f"ps{pi}"], [gk])
                    self.tt("dve", gs[:], gs[:], pp[:], ALU.mult, [gk, ppk], [gk])
                    self.tt("dve", self.xT[:, m, sl], self.xT[:, m, sl], gs[:], ALU.add,
                            [f"x{m}_{tt}", gk], [f"x{m}_{tt}"])
                ev += 1


def _const_tables():
    sl = alibi(16)
    r = np.arange(128)[:, None]
    c = np.arange(128)[None, :]
    tabA = np.zeros((4, 128, 3, 4, 128), np.float32)
    for kvh in range(4):
        for j in range(3):
            rel = c - r + (1 - j) * 128
            ok = np.abs(rel) <= 128
            for g in range(4):
                h = kvh * 4 + g
                tabA[kvh, :, j, g, :] = np.where(ok, -sl[h] * np.abs(rel), -BIG) / SCALE
    tabA = tabA.reshape(4, 128, 1536)
    kx = np.arange(64)[:, None]
    qx = np.arange(64)[None, :]
    cs = np.clip(qx - 8, 0, 48)
    okc = (kx >= cs) & (kx < cs + 16)
    cm = np.where(okc, 0.0, -BIG / SCALE).astype(np.float32)
    colmask = np.ascontiguousarray(np.broadcast_to(cm[None, :, :], (2, 64, 64)).reshape(128, 64))
    rowind = np.zeros((2, 128), np.float32)
    rowind[0, :64] = 1
    rowind[1, 64:] = 1
    u = np.arange(1920)[None, :]
    rr = np.arange(128)[:, None]
    Town = (-np.abs(u - 896 - rr) / SCALE).astype(np.float32)
    Toth = [(-np.abs(dl + u - 896 - rr) / SCALE).astype(np.float32) for dl in (-1024, 1024)]
    return tabA, colmask, rowind.astype(ml_dtypes.bfloat16), Town, Toth


def _rowmask(half):
    rm = np.full((2, 16, 8), -BIG, np.float32)
    for tt in range(2):
        if tt == 0:
            blks = [("own", kb) for kb in range(6)] + [("oth", kb) for kb in (6, 7)]
        else:
            blks = [("own", kb) for kb in range(2, 8)] + [("oth", kb) for kb in (0, 1)]
        for i, (who, kb) in enumerate(blks):
            base = half * 16 if who == "own" else (1 - half) * 16
            for a in range(2):
                ky = base + 2 * kb + a
                for e in range(8):
                    qy = half * 16 + tt * 8 + e
                    rs = min(max(qy - 4, 0), 24)
                    if rs <= ky < rs + 8:
                        rm[a, tt * 8 + i, e] = 0.0
    return rm.reshape(2, 128).astype(ml_dtypes.bfloat16)


def _core_inputs(c, inputs, x_state):
    b, half = c // 2, c % 2
    tabA, colmask, rowind, Town, Toth = _CONST
    ts = slice(half * T, (half + 1) * T)
    rpb = inputs["b_rpb"][0]
    rpr = np.zeros((16, 15, 127), np.float32)
    rpr[:, :, 48:79] = rpb[:, ::-1, ::-1]
    rpr = np.ascontiguousarray(np.broadcast_to(rpr.reshape(16, 1, 1905), (16, 64, 1905)))
    edge = np.zeros((128, 2), np.float32)
    edge[:, half] = -BIG
    m = {
        "x_in": x_state[c],
        "gpre": np.ascontiguousarray(inputs["norm_pre"].reshape(DEPTH, KC, 128).transpose(2, 0, 1).reshape(128, DEPTH * KC)),
        "gpost": np.ascontiguousarray(inputs["norm_post"].reshape(DEPTH, KC, 128).transpose(2, 0, 1).reshape(128, DEPTH * KC)),
    }
    for l in range(DEPTH):
        kind, j = l % 3, l // 3
        m[f"w_in{l}"] = inputs[("a_w_in", "b_w_in", "c_w_in")[kind]][j]
        m[f"pT{l}"] = np.ascontiguousarray(inputs["p"][l, b, ts, :].T)
        m[f"w_out{l}"] = inputs["w_out"][l]
        m[f"pe_proj{l}"] = inputs["pe_proj"][l]
        m[f"pe_gate{l}"] = inputs["pe_gate"][l]
    m.update({
        "a_sink_bc": np.ascontiguousarray(np.broadcast_to(inputs["a_sink"].reshape(1, 32), (128, 32))),
        "tabA": tabA, "edgeA": edge, "rpr": rpr, "colmaskB": colmask,
        "rowmaskB": _rowmask(half), "rowindB": rowind,
        "c_lambda_bc": np.ascontiguousarray(np.broadcast_to(inputs["c_lambda"][0].reshape(1, 512), (128, 512))),
        "c_subln_t": np.ascontiguousarray(inputs["c_subln"][0].reshape(2, 128).T),
        "TownC": Town, "TothC": Toth[half],
    })
    return m


_CONST = _const_tables()
_PROG_CACHE = {}


def _get_prog(phases, fused):
    key = (tuple(phases), fused)
    if key not in _PROG_CACHE:
        p = Prog(list(phases), fused)
        p.build()
        _PROG_CACHE[key] = p
    return _PROG_CACHE[key]


SPLIT_LAUNCHES = [[("P", 0)], [("AE", 0), ("P", 1)], [("AE", 1), ("P", 2)], [("AE", 2), ("P", 3)], [("AE", 3)]]
FUSED_LAUNCH = [("P", 0), ("AE", 0), ("P", 1), ("AE", 1), ("P", 2), ("AE", 2), ("P", 3), ("AE", 3)]
MODE = "fused"


def run_launch(phases, fused, inputs, x_state, extra):
    prog = _get_prog(phases, fused)
    in_maps = []
    for c in range(NCORES):
        m = _core_inputs(c, inputs, x_state)
        m.update(extra[c])
        in_maps.append({k: v for k, v in m.items() if k in prog.ext_in})
    res = run_bass_kernel_spmd(prog.nc, in_maps, core_ids=list(range(NCORES)))
    return [{k: np.asarray(r[k]) for k in prog.ext_out} for r in res.results]


def kernel(**inputs):
    inputs = {k: np.asarray(v) for k, v in inputs.items()}
    x = inputs["x"]
    x_state = [np.ascontiguousarray(x[c // 2, (c % 2) * T:(c % 2 + 1) * T, :].T) for c in range(NCORES)]
    if MODE == "fused":
        outs = run_launch(FUSED_LAUNCH, True, inputs, x_state, [{} for _ in range(NCORES)])
        x_state = [o["x_out"] for o in outs]
    else:
        extra = [{} for _ in range(NCORES)]
        for phases in SPLIT_LAUNCHES:
            outs = run_launch(phases, False, inputs, x_state, extra)
            x_state = [o["x_out"] for o in outs]
            extra = [{} for _ in range(NCORES)]
            for ph, l in phases:
                if ph == "P":
                    for c in range(NCORES):
                        o, oo = outs[c], outs[c ^ 1]
                        extra[c] = {f"q{l}_i": o[f"q{l}_o"], f"g{l}_i": o[f"g{l}_o"],
                                    f"k{l}_i": o[f"k{l}_o"], f"v{l}_i": o[f"v{l}_o"],
                                    f"ko{l}_i": oo[f"k{l}_o"], f"vo{l}_i": oo[f"v{l}_o"]}
    out = np.empty((4, 2048, D), np.float32)
    for c in range(NCORES):
        out[c // 2, (c % 2) * T:(c % 2 + 1) * T, :] = x_state[c].T
    return out
```

```python
import math
import numpy as np
import ml_dtypes
import concourse.bass as bass
import concourse.mybir as mybir
from concourse.bass_utils import run_bass_kernel_spmd

F32 = mybir.dt.float32
BF16 = mybir.dt.bfloat16
ALU = mybir.AluOpType
AF = mybir.ActivationFunctionType
AX = mybir.AxisListType

D = 2048
T = 1024
KC = 16
DEPTH = 4
SCALE = 128.0 ** -0.5
EPS = 1e-6
BIG = 30000.0
SLABW = 256
LAGS_A = (1, 2, 3)
N_DMA_SEMS = 12
NCORES = 8

IN_W = {0: (2048, 512, 512, 2048), 1: (2048, 2048, 2048, 2048), 2: (2048, 2048, 2048, 2048)}


def alibi(n):
    return [2.0 ** (-8.0 * (h + 1) / n) for h in range(n)]


class Op:
    __slots__ = ("eng", "fn", "deps", "is_dma", "signal", "ticket", "idx", "inc")

    def __init__(self, eng, fn, is_dma):
        self.eng = eng
        self.fn = fn
        self.deps = []
        self.is_dma = is_dma
        self.signal = False
        self.ticket = None
        self.idx = -1
        self.inc = 16


class Sched:
    def __init__(self, nc, sync_same_engine=True):
        self.nc = nc
        self.ops = []
        self.last_writer = {}
        self.readers = {}
        self.sync_same_engine = sync_same_engine
        self.engs = {"pe": nc.tensor, "act": nc.scalar, "dve": nc.vector,
                     "pool": nc.gpsimd, "sp": nc.sync}

    def add(self, eng, fn, reads=(), writes=(), dma=False, inc=16):
        op = Op(eng, fn, dma)
        op.inc = inc
        op.idx = len(self.ops)
        deps = {}
        for k in reads:
            w = self.last_writer.get(k)
            if w is not None:
                deps[w.idx] = w
            if isinstance(k, str) and k.startswith("ps"):
                for r in self.readers.get(k, ()):
                    if r.eng != eng:
                        deps[r.idx] = r
        for k in writes:
            w = self.last_writer.get(k)
            if w is not None:
                deps[w.idx] = w
            for r in self.readers.get(k, ()):
                deps[r.idx] = r
        op.deps = list(deps.values())
        for k in writes:
            self.last_writer[k] = op
            self.readers[k] = []
        for k in reads:
            lst = self.readers.setdefault(k, [])
            if not dma:
                lst[:] = [r for r in lst if r.is_dma or r.eng != eng]
            lst.append(op)
        self.ops.append(op)
        return op

    def _skip(self, d, op):
        return (d.eng == op.eng and not op.is_dma and not d.is_dma
                and (d.eng == "pe" or not self.sync_same_engine))

    def emit(self):
        nc = self.nc
        for op in self.ops:
            for d in op.deps:
                if d.is_dma or self._skip(d, op):
                    continue
                d.signal = True
        sems = {e: nc.alloc_semaphore(name=f"s_{e}") for e in self.engs}
        dma_sems = {q: [nc.alloc_semaphore(name=f"d_{q}_{i}") for i in range(N_DMA_SEMS)]
                    for q in ("sp", "pool", "act")}
        dma_cnt = {q: [0] * N_DMA_SEMS for q in dma_sems}
        dma_rr = {q: 0 for q in dma_sems}
        cnt = {e: 0 for e in self.engs}
        waited = {e: {} for e in self.engs}

        def wait(eng, sem, val):
            key = id(sem)
            if waited[eng].get(key, 0) >= val:
                return
            waited[eng][key] = val
            self.engs[eng].wait_ge(sem, val)

        for op in self.ops:
            e = op.eng
            for d in op.deps:
                if d.is_dma:
                    wait(e, d.ticket[0], d.ticket[1])
                elif not self._skip(d, op):
                    wait(e, d.ticket[0], d.ticket[1])
            if op.is_dma:
                i = dma_rr[e]
                dma_rr[e] = (i + 1) % N_DMA_SEMS
                sem = dma_sems[e][i]
                if dma_cnt[e][i] > 0:
                    wait(e, sem, dma_cnt[e][i])
                ins = op.fn()
                dma_cnt[e][i] += op.inc
                ins.then_inc(sem, op.inc)
                op.ticket = (sem, dma_cnt[e][i])
            else:
                ins = op.fn()
                if op.signal:
                    cnt[e] += 1
                    ins.then_inc(sems[e], 1)
                    op.ticket = (sems[e], cnt[e])
        for q in dma_sems:
            for i in range(N_DMA_SEMS):
                if dma_cnt[q][i] > 0:
                    wait("sp", dma_sems[q][i], dma_cnt[q][i])
        for e2 in self.engs:
            if cnt[e2] > 0:
                wait("sp", sems[e2], cnt[e2])
        return len(self.ops)


class Prog:
    def __init__(self, phases, fused, dbg=False):
        self.phases = phases
        self.fused = fused
        self.nc = nc = bass.Bass("TRN2", target_bir_lowering=False)
        self.S = Sched(nc)
        self.ext_in = []
        self.ext_out = []
        self.dram = {}
        self._alloc()

    def din(self, name, shape, dt=F32):
        t = self.nc.dram_tensor(name, list(shape), dt, kind="ExternalInput")
        self.dram[name] = t
        self.ext_in.append(name)
        return t

    def dout(self, name, shape, dt=F32):
        t = self.nc.dram_tensor(name, list(shape), dt, kind="ExternalOutput")
        self.dram[name] = t
        self.ext_out.append(name)
        return t

    def dint(self, name, shape, dt=F32):
        t = self.nc.dram_tensor(name, list(shape), dt)
        self.dram[name] = t
        return t

    def sb(self, name, shape, dt):
        return self.nc.alloc_sbuf_tensor(name, list(shape), dt)

    def _alloc(self):
        nc = self.nc
        layers = sorted({l for _, l in self.phases})
        self.layers = layers
        kinds = {l % 3 for l in layers}
        self.x_in = self.din("x_in", [D, T])
        self.x_out = self.dout("x_out", [D, T])
        self.gpre_d = self.din("gpre", [128, DEPTH * KC])
        self.gpost_d = self.din("gpost", [128, DEPTH * KC])
        self.pT_d, self.w_out_d, self.pe_proj_d, self.pe_gate_d, self.w_in_d = {}, {}, {}, {}, {}
        for ph, l in self.phases:
            if ph == "P":
                self.w_in_d[l] = self.din(f"w_in{l}", [D, sum(IN_W[l % 3])])
            else:
                self.pT_d[l] = self.din(f"pT{l}", [256, T])
                self.w_out_d[l] = self.din(f"w_out{l}", [D, D])
                self.pe_proj_d[l] = self.din(f"pe_proj{l}", [256, D])
                self.pe_gate_d[l] = self.din(f"pe_gate{l}", [D, D])
        self.sink_d = self.din("a_sink_bc", [128, 32])
        self.tabA_d = self.din("tabA", [4, 128, 1536])
        self.edge_d = self.din("edgeA", [128, 2])
        self.rpr_d = self.din("rpr", [16, 64, 15 * 127])
        self.colmask_d = self.din("colmaskB", [128, 64])
        self.rowmask_d = self.din("rowmaskB", [2, 128], BF16)
        self.rowind_d = self.din("rowindB", [2, 128], BF16)
        self.lam_d = self.din("c_lambda_bc", [128, 512])
        self.subln_d = self.din("c_subln_t", [128, 2])
        self.Town_d = self.din("TownC", [128, 1920])
        self.Toth_d = self.din("TothC", [128, 1920])
        self.q_w, self.q_r, self.g_w, self.g_r = {}, {}, {}, {}
        self.k_w, self.k_r, self.v_w, self.v_r = {}, {}, {}, {}
        self.k_oth, self.v_oth, self.k_all, self.v_all = {}, {}, {}, {}
        for l in layers:
            _, dk, dv, _ = IN_W[l % 3]
            hasP = ("P", l) in self.phases
            hasA = ("AE", l) in self.phases
            if hasP and hasA:
                assert self.fused
                self.q_w[l] = self.q_r[l] = self.dint(f"q{l}", [D, T], BF16)
                self.g_w[l] = self.g_r[l] = self.dint(f"g{l}", [D, T], BF16)
                self.k_w[l] = self.k_r[l] = self.dint(f"k{l}", [dk, T], BF16)
                self.v_w[l] = self.v_r[l] = self.dint(f"v{l}", [T, dv], BF16)
                self.k_all[l] = self.dint(f"kall{l}", [2 * dk, T], BF16)
                self.v_all[l] = self.dint(f"vall{l}", [2 * T, dv], BF16)
                self.k_oth[l] = self.dint(f"koth{l}", [dk, T], BF16)
                self.v_oth[l] = self.dint(f"voth{l}", [T, dv], BF16)
            else:
                if hasP:
                    self.q_w[l] = self.dout(f"q{l}_o", [D, T], BF16)
                    self.g_w[l] = self.dout(f"g{l}_o", [D, T], BF16)
                    self.k_w[l] = self.dout(f"k{l}_o", [dk, T], BF16)
                    self.v_w[l] = self.dout(f"v{l}_o", [T, dv], BF16)
                if hasA:
                    self.q_r[l] = self.din(f"q{l}_i", [D, T], BF16)
                    self.g_r[l] = self.din(f"g{l}_i", [D, T], BF16)
                    self.k_r[l] = self.din(f"k{l}_i", [dk, T], BF16)
                    self.v_r[l] = self.din(f"v{l}_i", [T, dv], BF16)
                    self.k_oth[l] = self.din(f"ko{l}_i", [dk, T], BF16)
                    self.v_oth[l] = self.din(f"vo{l}_i", [T, dv], BF16)
        self.yT = self.dint("yT", [D, T], F32)
        self.xT = self.sb("xT", [128, KC, T], F32)
        self.hT = self.sb("hT", [128, KC, T], BF16)
        self.wslab = [self.sb(f"wslab{i}", [128, KC, SLABW], BF16) for i in range(2)]
        self.ones = self.sb("ones", [128, 128], BF16)
        self.zeros = self.sb("zeros", [128, 16], F32)
        self.zeros128 = self.sb("zeros128", [128, 128], F32)
        self.esk = self.sb("esk", [128, 512], F32)
        self.gpre = self.sb("gpre_s", [128, DEPTH * KC], F32)
        self.gpost = self.sb("gpost_s", [128, DEPTH * KC], F32)
        self.sq = [self.sb(f"sq{i}", [128, 2, 512], BF16) for i in range(2)]
        self.rstd = [self.sb(f"rstd{i}", [128, 512], F32) for i in range(2)]
        self.stage = [self.sb(f"stage{i}", [128, T], BF16) for i in range(2)]
        self.ystage = [self.sb(f"ystage{i}", [128, T], F32) for i in range(2)]
        self.gate_st = [self.sb(f"gate{i}", [128, 512], F32) for i in range(2)]
        self.peslab = [self.sb(f"peslab{i}", [128, 2, SLABW], BF16) for i in range(2)]
        self.pTb = self.sb("pTb", [128, 2, T], BF16)
        self.kt_own = self.sb("kt_own", [128, T], BF16)
        self.kt_oth = self.sb("kt_oth", [128, T], BF16)
        self.qb = [self.sb(f"qb{i}", [128, T], BF16) for i in range(2)]
        self.gb = self.sb("gb", [128, 2, T], BF16)
        self.v_own_s = self.sb("v_own", [128, 8, 256], BF16)
        self.v_oth_s = self.sb("v_oth", [128, 8, 256], BF16)
        self.tab = self.sb("tab", [128, 3840], F32)
        self.tmp = self.sb("tmp", [128, 4, 512], F32)
        self.pt = self.sb("pt", [128, 4, 512], BF16)
        self.on0 = self.sb("on0", [128, 2, 512], F32)
        self.o_s = self.sb("o_s", [128, 2, 512], F32)
        self.rz = self.sb("rz", [128, 512], F32)
        self.zt = self.sb("zt", [128, 512], F32)
        self.es = self.sb("es", [128, 32], F32)
        self.sink_s = self.sb("sink_s", [128, 32], F32)
        self.edge = self.sb("edge", [128, 2], F32)
        self.lam_s = self.sb("lam_s", [128, 512], F32)
        self.lam_t = self.sb("lam_t", [128, 8], F32)
        self.subln_s = self.sb("subln_s", [128, 2], F32)
        self.rowmask = self.sb("rowmask", [2, 128], BF16)
        self.rowind = self.sb("rowind", [2, 128], BF16)
        self.cm64 = self.sb("cm64", [128, 64], F32)
        self.ps = [nc.alloc_psum_tensor(f"ps{i}", [128, 512], F32) for i in range(8)]
        self.wi = 0
        self.par = None

    def dma(self, q, out, in_, reads, writes):
        eng = {"sp": self.nc.sync, "pool": self.nc.gpsimd, "act": self.nc.scalar}[q]
        self.S.add(q, lambda: eng.dma_start(out=out, in_=in_), reads, writes, dma=True)

    def act(self, out, in_, func, reads, writes, bias=0.0, scale=1.0):
        nc = self.nc
        self.S.add("act", lambda: nc.scalar.activation(out=out, in_=in_, func=func, bias=bias, scale=scale),
                   reads, writes)

    def tt(self, eng, out, in0, in1, op, reads, writes):
        e = {"dve": self.nc.vector, "pool": self.nc.gpsimd}[eng]
        self.S.add(eng, lambda: e.tensor_tensor(out=out, in0=in0, in1=in1, op=op), reads, writes)

    def stt(self, out, in0, scalar, in1, op0, op1, reads, writes, eng="dve"):
        e = {"dve": self.nc.vector, "pool": self.nc.gpsimd}[eng]
        self.S.add(eng, lambda: e.scalar_tensor_tensor(out=out, in0=in0, scalar=scalar, in1=in1,
                                                        op0=op0, op1=op1), reads, writes)

    def ts(self, out, in0, s1, s2, op0, op1, reads, writes):
        nc = self.nc
        if s2 is None:
            self.S.add("dve", lambda: nc.vector.tensor_scalar(out=out, in0=in0, scalar1=s1, scalar2=None,
                                                              op0=op0), reads, writes)
        else:
            self.S.add("dve", lambda: nc.vector.tensor_scalar(out=out, in0=in0, scalar1=s1, scalar2=s2,
                                                              op0=op0, op1=op1), reads, writes)

    def recip(self, out, in_, reads, writes):
        nc = self.nc
        self.S.add("dve", lambda: nc.vector.reciprocal(out=out, in_=in_), reads, writes)

    def mm(self, out, lhsT, rhs, start, stop, reads, writes):
        nc = self.nc
        self.S.add("pe", lambda: nc.tensor.matmul(out, lhsT, rhs, start=start, stop=stop), reads, writes)

    def memset(self, ap, val, writes):
        nc = self.nc
        self.S.add("dve", lambda: nc.vector.memset(ap, val), (), writes)

    @staticmethod
    def slab_order(kind):
        wq, wk, wv, wg = IN_W[kind]
        b = [0, wq // SLABW, (wq + wk) // SLABW, (wq + wk + wv) // SLABW, (wq + wk + wv + wg) // SLABW]
        q, k, v, g = (list(range(b[i], b[i + 1])) for i in range(4))
        return k + v + q + g

    def plan_slabs(self):
        slabs = []
        for ph, l in self.phases:
            if ph == "P":
                for s in self.slab_order(l % 3):
                    slabs.append((self.w_in_d[l], s))
            elif ph == "AE":
                for s in range(D // SLABW):
                    slabs.append((self.w_out_d[l], s))
                for s in range(D // SLABW):
                    slabs.append((self.pe_gate_d[l], s))
        self.slabs = slabs
        self.slab_issued = 0
        self.slab_used = 0

    def _issue_slab(self):
        i = self.slab_issued
        if i >= len(self.slabs):
            return
        w, s = self.slabs[i]
        src = w[:, s * SLABW:(s + 1) * SLABW].rearrange("(kc p) n -> p kc n", p=128)
        self.dma("pool", self.wslab[i % 2][:], src, (), [f"wslab{i % 2}"])
        self.slab_issued += 1

    def next_slab(self):
        i = self.slab_used
        while self.slab_issued <= min(i + 1, len(self.slabs) - 1):
            self._issue_slab()
        self.slab_used += 1
        return self.wslab[i % 2], f"wslab{i % 2}"

    def build(self):
        nc = self.nc
        self.plan_slabs()
        if self.fused:
            self.par = nc.sync.partition_id() % 2
        self.memset(self.ones[:], 1.0, ["ones"])
        self.memset(self.zeros[:], 0.0, ["zeros"])
        self.memset(self.zeros128[:], 0.0, ["zeros128"])
        self.dma("sp", self.gpre[:], self.gpre_d.ap(), (), ["gpre"])
        self.dma("sp", self.gpost[:], self.gpost_d.ap(), (), ["gpost"])
        xv = self.x_in.ap().rearrange("(kc p) n -> p kc n", p=128)
        for kc in range(KC):
            self.dma("sp", self.xT[:, kc, :], xv[:, kc, :], (), [f"x{kc}_0", f"x{kc}_1"])
        for ph, l in self.phases:
            if ph == "P":
                self.phase_P(l)
            else:
                kind = l % 3
                if kind == 0:
                    self.attn_A(l)
                elif kind == 1:
                    self.attn_B(l)
                else:
                    self.attn_C(l)
                self.phase_E(l)
        ov = self.x_out.ap().rearrange("(kc p) n -> p kc n", p=128)
        for kc in range(KC):
            self.dma("sp", ov[:, kc, :], self.xT[:, kc, :], [f"x{kc}_0", f"x{kc}_1"], [])
        n = self.S.emit()
        return n

    def stats_rstd(self, tt, src_fn, src_keys, nchunks, inv_n, rstd_i):
        ps = self.ps[7]
        for kc in range(nchunks):
            sqb = self.sq[kc % 2]
            self.act(sqb[:, 0, :], src_fn(kc), AF.Square, src_keys(kc), [f"sq{kc % 2}"])
            self.mm(ps[:], self.ones[:], sqb[:, 0, :], kc == 0, kc == nchunks - 1,
                    ["ones", f"sq{kc % 2}"], ["ps7"])
        r = self.rstd[rstd_i]
        self.recip_act(r[:], ps[:], ["ps7"], [f"rstd{rstd_i}"], in_scale=inv_n, in_bias=EPS, power=-0.5)

    def phase_P(self, l):
        kind = l % 3
        wq, wk, wv, wg = IN_W[kind]
        for tt in range(2):
            sl = slice(tt * 512, (tt + 1) * 512)
            self.stats_rstd(tt, lambda kc: self.xT[:, kc, sl], lambda kc: [f"x{kc}_{tt}"], KC, 1.0 / D, tt)
            for kc in range(KC):
                self.stt(self.hT[:, kc, sl], self.xT[:, kc, sl], self.gpre[:, l * KC + kc:l * KC + kc + 1],
                         self.rstd[tt][:], ALU.mult, ALU.mult,
                         [f"x{kc}_{tt}", "gpre", f"rstd{tt}"], [f"h{kc}_{tt}"])
        order = self.slab_order(kind)
        last_k = (wq + wk) // SLABW - 1
        last_v = (wq + wk + wv) // SLABW - 1
        ev = 0
        for s in order:
            c0 = s * SLABW
            slab, skey = self.next_slab()
            if c0 < wq:
                seg, dst, r0 = "q", self.q_w[l], c0
            elif c0 < wq + wk:
                seg, dst, r0 = "k", self.k_w[l], c0 - wq
            elif c0 < wq + wk + wv:
                seg, dst, r0 = "v", self.v_w[l], c0 - wq - wk
            else:
                seg, dst, r0 = "g", self.g_w[l], c0 - wq - wk - wv
            if seg != "v":
                for mi in range(SLABW // 128):
                    st = self.stage[ev % 2]
                    stk = f"stage{ev % 2}"
                    for tt in range(2):
                        sl = slice(tt * 512, (tt + 1) * 512)
                        pi = (ev * 2 + tt) % 4
                        ps = self.ps[pi]
                        for kc in range(KC):
                            self.mm(ps[:], slab[:, kc, mi * 128:(mi + 1) * 128], self.hT[:, kc, sl],
                                    kc == 0, kc == KC - 1, [skey, f"h{kc}_{tt}"], [f"ps{pi}"])
                        if seg == "g":
                            self.act(st[:, sl], ps[:], AF.Silu, [f"ps{pi}"], [stk + f"_{tt}"])
                        elif (ev + tt) % 2 == 0:
                            self.act(st[:, sl], ps[:], AF.Copy, [f"ps{pi}"], [stk + f"_{tt}"])
                        else:
                            nc = self.nc
                            self.S.add("dve", (lambda o=st[:, sl], i=ps[:]: nc.vector.tensor_copy(out=o, in_=i)),
                                       [f"ps{pi}"], [stk + f"_{tt}"])
                    row = r0 + mi * 128
                    self.dma("sp", dst[row:row + 128, :], st[:], [stk + "_0", stk + "_1"], [f"{seg}{l}"])
                    ev += 1
            else:
                for tb in range(8):
                    st = self.stage[ev % 2]
                    stk = f"stage{ev % 2}"
                    pi = ev % 4
                    ps = self.ps[pi]
                    for kc in range(KC):
                        self.mm(ps[:, 0:SLABW], self.hT[:, kc, tb * 128:(tb + 1) * 128], slab[:, kc, :],
                                kc == 0, kc == KC - 1, [skey, f"h{kc}_{tb // 4}"], [f"ps{pi}"])
                    nc = self.nc
                    self.S.add("dve", (lambda o=st[:, 0:SLABW], i=ps[:, 0:SLABW]: nc.vector.tensor_copy(out=o, in_=i)),
                               [f"ps{pi}"], [stk + "_0", stk + "_1"])
                    self.dma("sp", dst[tb * 128:(tb + 1) * 128, r0:r0 + SLABW], st[:, 0:SLABW],
                             [stk + "_0", stk + "_1"], [f"v{l}"])
                    ev += 1
            if self.fused and s == last_k:
                self.exchange_gather(l, "k")
            if self.fused and s == last_v:
                self.exchange_gather(l, "v")
        if self.fused:
            self.exchange_copy(l)

    def _xchg(self, l, nm):
        src, allt, otht = ((self.k_w[l], self.k_all[l], self.k_oth[l]) if nm == "k"
                           else (self.v_w[l], self.v_all[l], self.v_oth[l]))
        rows, cols = src.shape
        rc = min(rows, (2 << 20) // (cols * 2))
        return src, allt, otht, rc, rows // rc

    def exchange_gather(self, l, nm):
        nc = self.nc
        groups = [[2 * i, 2 * i + 1] for i in range(NCORES // 2)]
        src, allt, otht, rc, nchunk = self._xchg(l, nm)
        for c in range(nchunk):
            self.S.add("pool", (lambda s=src[c * rc:(c + 1) * rc, :], d=allt[c * 2 * rc:(c + 1) * 2 * rc, :]:
                                nc.gpsimd.collective_compute(
                "AllGather", mybir.AluOpType.bypass, replica_groups=groups,
                ins=[s.opt()], outs=[d.opt()])), [f"{nm}{l}"], [f"{nm}all{l}"], dma=True, inc=1)

    def exchange_copy(self, l):
        for nm in ("k", "v"):
            src, allt, otht, rc, nchunk = self._xchg(l, nm)
            gv = allt.ap().rearrange("(c r f) n -> c r f n", c=nchunk, r=2)
            self.dma("sp", otht.ap().rearrange("(c f) n -> f c n", c=nchunk),
                     gv[:, bass.ds(1 - self.par, 1), :, :].rearrange("c 1 f n -> f c n"),
                     [f"{nm}all{l}"], [f"{nm}oth{l}"])

    def oth_k(self, l, f0, c0, c1):
        return self.k_oth[l][f0:f0 + 128, c0:c1], ([f"koth{l}"] if self.fused else [])

    def oth_v(self, l, kb0, nkb, d0, dn):
        return (self.v_oth[l][kb0 * 128:(kb0 + nkb) * 128, d0:d0 + dn].rearrange("(kb p) d -> p kb d", p=128),
                [f"voth{l}"] if self.fused else [])

    def run_pipeline(self, groups, LA, NBUF=2):
        flat = []
        for gi, g in enumerate(groups):
            nt = len(g["tasks"])
            for ti, (s1, s2) in enumerate(g["tasks"]):
                flat.append((gi, s1, s2, ti == nt - 1))
        for gi in range(min(NBUF, len(groups))):
            if groups[gi]["loads"]:
                groups[gi]["loads"]()
        n = len(flat)
        pending = []
        for i in range(n + LA):
            if i < n:
                flat[i][1](i)
            j = i - LA
            if j >= 0:
                gj, _, s2, last = flat[j]
                s2(j)
                if last:
                    stages = groups[gj]["post"] or []
                    for k, (lag, fn) in enumerate(stages):
                        pending.append((j + lag, fn, gj if k == len(stages) - 1 else None))
                still = []
                for due, fn, gdone in pending:
                    if due <= j:
                        fn()
                        if gdone is not None and gdone + NBUF < len(groups) and groups[gdone + NBUF]["loads"]:
                            groups[gdone + NBUF]["loads"]()
                    else:
                        still.append((due, fn, gdone))
                pending = still
        for due, fn, gdone in pending:
            fn()

    def recip_act(self, out, in_, rk, wk, in_scale=1.0, in_bias=0.0, power=-1.0):
        self.act(out, in_, AF.Ln, rk, wk, bias=in_bias, scale=in_scale)
        self.act(out, out, AF.Exp, wk, wk, scale=power)

    def attn_A(self, l):
        j = l // 3
        self.dma("sp", self.sink_s[:], self.sink_d.ap(), (), ["sink_s"])
        self.dma("sp", self.edge[:], self.edge_d.ap(), (), ["edge"])
        tabv = self.tab[:, 0:1536].rearrange("p (j n) -> p j n", j=3)
        for kvh in range(4):
            f0 = kvh * 128
            self.dma("sp", self.kt_own[:], self.k_r[l][f0:f0 + 128, :], [f"k{l}"], ["kt_own"])
            ap, rk = self.oth_k(l, f0, 896, 1024)
            self.dma("sp", self.kt_oth[:, 0:128], ap, rk, ["kt_oth"])
            ap, rk = self.oth_k(l, f0, 0, 128)
            self.dma("sp", self.kt_oth[:, 128:256], ap, rk, ["kt_oth"])
            self.dma("sp", self.v_own_s[:, :, 0:128],
                     self.v_r[l][:, f0:f0 + 128].rearrange("(kb p) d -> p kb d", p=128), [f"v{l}"], ["v_own0"])
            ap, rk = self.oth_v(l, 7, 1, f0, 128)
            self.dma("sp", self.v_oth_s[:, 0:1, 0:128], ap, rk, ["v_oth0"])
            ap, rk = self.oth_v(l, 0, 1, f0, 128)
            self.dma("sp", self.v_oth_s[:, 1:2, 0:128], ap, rk, ["v_oth0"])
            self.dma("sp", self.tab[:, 0:1536], self.tabA_d[kvh], (), ["tabL"])
            for g in range(4):
                c = j * 16 + kvh * 4 + g
                self.act(self.esk[:, g * 128:(g + 1) * 128], self.zeros128[:], AF.Exp, ["zeros128", "sink_s"], ["esk"],
                         bias=self.sink_s[:, c:c + 1], scale=1.0)
            groups = []
            for qi in range(8):
                par = qi % 2
                slot = qi % 3
                qs = slice(qi * 128, (qi + 1) * 128)
                q4 = (self.qb[0][:, 0:512], self.qb[0][:, 512:1024], self.qb[1][:, 0:512])[slot]
                g4 = (self.gb[:, 0, 0:512], self.gb[:, 0, 512:1024], self.gb[:, 1, 0:512])[slot]
                qk, gk = ("qb0_0", "qb0_1", "qb1")[slot], ("gb0", "gb0", "gb1")[slot]
                po, pz = (4, 5) if par == 0 else (6, 7)

                def loads(qs=qs, q4=q4, g4=g4, qk=qk, gk=gk, kvh=kvh):
                    qsrc = self.q_r[l][kvh * 512:(kvh + 1) * 512, qs].rearrange("(g p) n -> p g n", p=128)
                    self.dma("sp", q4.rearrange("p (g n) -> p g n", g=4), qsrc, [f"q{l}"], [qk])
                    gsrc = self.g_r[l][kvh * 512:(kvh + 1) * 512, qs].rearrange("(g p) n -> p g n", p=128)
                    self.dma("sp", g4.rearrange("p (g n) -> p g n", g=4), gsrc, [f"g{l}"], [gk])

                tasks = []
                for jj in range(3):
                    kb = qi + jj - 1
                    if kb < 0:
                        blk = (self.kt_oth[:, 0:128], "kt_oth", self.v_oth_s[:, 0, 0:128], "v_oth0", 0)
                    elif kb > 7:
                        blk = (self.kt_oth[:, 128:256], "kt_oth", self.v_oth_s[:, 1, 0:128], "v_oth0", 1)
                    else:
                        blk = (self.kt_own[:, kb * 128:(kb + 1) * 128], "kt_own", self.v_own_s[:, kb, 0:128], "v_own0", None)

                    def s1(i, blk=blk, jj=jj, q4=q4, qk=qk):
                        kap, kk, vap, vk, edge = blk
                        b = i % 3
                        self.mm(self.ps[b][:], kap, q4, True, True, [kk, qk], [f"ps{b}"])
                        self.tt("dve", self.tmp[:, b, :], self.ps[b][:], tabv[:, jj, :], ALU.add,
                                [f"ps{b}", "tabL"], [f"tmp{b}"])
                        if edge is None:
                            self.act(self.pt[:, b, :], self.tmp[:, b, :], AF.Exp, [f"tmp{b}"], [f"pt{b}"], scale=SCALE)
                        else:
                            self.act(self.pt[:, b, :], self.tmp[:, b, :], AF.Exp, [f"tmp{b}", "edge"], [f"pt{b}"],
                                     bias=self.edge[:, edge:edge + 1], scale=SCALE)

                    def s2(i, blk=blk, jj=jj, po=po, pz=pz):
                        kap, kk, vap, vk, edge = blk
                        b = i % 3
                        self.mm(self.ps[po][:], vap, self.pt[:, b, :], jj == 0, jj == 2, [vk, f"pt{b}"], [f"ps{po}"])
                        self.mm(self.ps[pz][:], self.ones[:], self.pt[:, b, :], jj == 0, jj == 2,
                                ["ones", f"pt{b}"], [f"ps{pz}"])
                    tasks.append((s1, s2))

                def post_a(pz=pz):
                    self.tt("dve", self.zt[:], self.ps[pz][:], self.esk[:], ALU.add, [f"ps{pz}", "esk"], ["zt"])

                def post_b():
                    self.recip_act(self.rz[:], self.zt[:], ["zt"], ["rz"])

                def post_c(qs=qs, g4=g4, gk=gk, po=po, kvh=kvh, qi=qi):
                    self.tt("dve", self.o_s[:, 0, :], self.ps[po][:], self.rz[:], ALU.mult, [f"ps{po}", "rz"], ["o_s"])
                    okeys = [f"h{kvh * 4 + g}_{qi // 4}" for g in range(4)]
                    self.tt("pool", self.hT[:, kvh * 4:(kvh + 1) * 4, qs],
                            self.o_s[:, 0, :].rearrange("p (g n) -> p g n", g=4),
                            g4.rearrange("p (g n) -> p g n", g=4), ALU.mult, ["o_s", gk], okeys)
                groups.append(dict(loads=loads, tasks=tasks, post=[(LAGS_A[0], post_a), (LAGS_A[1], post_b), (LAGS_A[2], post_c)]))
            self.run_pipeline(groups, 2, NBUF=3)

    def attn_B(self, l):
        self.dma("sp", self.rowmask[:], self.rowmask_d.ap(), (), ["rowmask"])
        self.dma("sp", self.rowind[:], self.rowind_d.ap(), (), ["rowind"])
        self.dma("sp", self.cm64[:], self.colmask_d.ap(), (), ["cm64"])
        self.memset(self.tab[:, 0:1920], 0.0, ["tabL"])
        self.memset(self.tab[:, 1920:3840], 0.0, ["tabH"])
        ktown = [(self.kt_own, ["kt_own"]), (self.qb[1], ["qb1"])]
        ktoth = [(self.kt_oth, ["kt_oth"]), (self.stage[0], ["stage0_0", "stage0_1"])]
        qbuf = [(self.qb[0], ["qb0_0", "qb0_1"]), (self.stage[1], ["stage1_0", "stage1_1"])]
        groups = []
        gidx = 0
        for h in range(16):
            hp = h % 2
            f0 = h * 128
            kto, ktok = ktown[hp]
            ktt, kttk = ktoth[hp]
            qq, qqk = qbuf[hp]
            gg, ggk = self.gb[:, hp, :], f"gb{hp}"
            vo, vok = self.v_own_s[:, :, hp * 128:(hp + 1) * 128], f"v_own{hp}"
            vt, vtk = self.v_oth_s[:, :, hp * 128:(hp + 1) * 128], f"v_oth{hp}"
            tkey = "tabL" if hp == 0 else "tabH"
            TB = self.tab[:, hp * 1920:(hp + 1) * 1920].rearrange("p (j n) -> p j n", j=30)

            def loads(h=h, f0=f0, kto=kto, ktok=ktok, ktt=ktt, kttk=kttk, qq=qq, qqk=qqk, gg=gg, ggk=ggk,
                      vo=vo, vok=vok, vt=vt, vtk=vtk, tkey=tkey, TB=TB):
                self.dma("sp", kto[:], self.k_r[l][f0:f0 + 128, :], [f"k{l}"], ktok)
                ap, rk = self.oth_k(l, f0, 0, 256)
                self.dma("sp", ktt[:, 0:256], ap, rk, kttk)
                ap, rk = self.oth_k(l, f0, 768, 1024)
                self.dma("sp", ktt[:, 768:1024], ap, rk, kttk)
                self.dma("sp", vo, self.v_r[l][:, f0:f0 + 128].rearrange("(kb p) d -> p kb d", p=128), [f"v{l}"], [vok])
                ap, rk = self.oth_v(l, 0, 2, f0, 128)
                self.dma("sp", vt[:, 0:2, :], ap, rk, [vtk])
                ap, rk = self.oth_v(l, 6, 2, f0, 128)
                self.dma("sp", vt[:, 6:8, :], ap, rk, [vtk])
                self.dma("sp", qq[:], self.q_r[l][f0:f0 + 128, :], [f"q{l}"], qqk)
                self.dma("sp", gg, self.g_r[l][f0:f0 + 128, :], [f"g{l}"], [ggk])
                for a in range(2):
                    src = bass.AP(self.rpr_d, h * 64 * 1905 + 63, [[1904, 64], [127, 15], [1, 64]])
                    self.dma("sp", TB[a * 64:(a + 1) * 64, a + 7:a + 22, :], src, (), [tkey])
                for a in range(2):
                    cmb = bass.AP(self.cm64, a * 64 * 64, [[64, 64], [0, 15], [1, 64]])
                    self.stt(TB[a * 64:(a + 1) * 64, a + 7:a + 22, :], TB[a * 64:(a + 1) * 64, a + 7:a + 22, :],
                             1.0 / SCALE, cmb, ALU.mult, ALU.add, [tkey, "cm64"], [tkey])

            for tt in range(2):
                sl = slice(tt * 512, (tt + 1) * 512)
                if tt == 0:
                    blks = [("own", kb, 2 * kb + 7) for kb in range(6)] + [("oth", kb, 2 * kb - 9) for kb in (6, 7)]
                else:
                    blks = [("own", kb, 2 * kb - 1) for kb in range(2, 8)] + [("oth", kb, 15 + 2 * kb) for kb in (0, 1)]
                nb = len(blks)
                po, pz = (4, 5) if gidx % 2 == 0 else (6, 7)
                tasks = []
                for i_b, (who, kb, d0) in enumerate(blks):
                    pidx = tt * 8 + i_b
                    if who == "own":
                        kap, kk, vap, vk = kto[:, kb * 128:(kb + 1) * 128], ktok, vo[:, kb, :], vok
                    else:
                        kap, kk, vap, vk = ktt[:, kb * 128:(kb + 1) * 128], kttk, vt[:, kb, :], vtk

                    def s1(i, kap=kap, kk=kk, pidx=pidx, d0=d0, qq=qq, qqk=qqk, sl=sl, TB=TB, tkey=tkey):
                        b = i % 4
                        self.mm(self.ps[b][:], kap, qq[:, sl], True, False, list(kk) + list(qqk), [f"ps{b}"])
                        self.mm(self.ps[b][:], self.rowind[:],
                                bass.AP(self.rowmask, pidx * 8, [[128, 2], [1, 8], [0, 64]]), False, True,
                                ["rowind", "rowmask"], [f"ps{b}"])
                        j0 = 21 - d0
                        self.tt("dve", self.tmp[:, b, :], self.ps[b][:],
                                TB[:, j0:j0 + 8, :].rearrange("p j n -> p (j n)"), ALU.add,
                                [f"ps{b}", tkey], [f"tmp{b}"])
                        self.act(self.pt[:, b, :], self.tmp[:, b, :], AF.Exp, [f"tmp{b}"], [f"pt{b}"], scale=SCALE)

                    def s2(i, vap=vap, vk=vk, i_b=i_b, nb=nb, po=po, pz=pz):
                        b = i % 4
                        self.mm(self.ps[po][:], vap, self.pt[:, b, :], i_b == 0, i_b == nb - 1, [vk, f"pt{b}"], [f"ps{po}"])
                        self.mm(self.ps[pz][:], self.ones[:], self.pt[:, b, :], i_b == 0, i_b == nb - 1,
                                ["ones", f"pt{b}"], [f"ps{pz}"])
                    tasks.append((s1, s2))

                def post_a(pz=pz):
                    self.recip_act(self.rz[:], self.ps[pz][:], [f"ps{pz}"], ["rz"])

                def post_b(h=h, sl=sl, tt=tt, gg=gg, ggk=ggk, po=po):
                    self.tt("dve", self.o_s[:, 0, :], self.ps[po][:], self.rz[:], ALU.mult, [f"ps{po}", "rz"], ["o_s"])
                    self.tt("pool", self.hT[:, h, sl], self.o_s[:, 0, :], gg[:, sl], ALU.mult,
                            ["o_s", ggk], [f"h{h}_{tt}"])
                groups.append(dict(loads=loads if tt == 0 else None, tasks=tasks, post=[(1, post_a), (3, post_b)]))
                gidx += 1
        self.run_pipeline(groups, 3)

    def attn_C(self, l):
        nc = self.nc
        lam_init = 0.8 - 0.6 * math.exp(-0.3 * l)
        slopes = alibi(8)
        self.dma("sp", self.lam_s[:], self.lam_d.ap(), (), ["lam_s"])
        self.dma("sp", self.subln_s[:], self.subln_d.ap(), (), ["subln_s"])
        self.dma("sp", self.tab[:, 0:1920], self.Town_d.ap(), (), ["tabL"])
        self.dma("sp", self.tab[:, 1920:3840], self.Toth_d.ap(), (), ["tabH"])
        lt = self.lam_t
        for i in range(2):
            self.tt("dve", self.zt[:, 0:128], self.lam_s[:, (2 * i) * 128:(2 * i + 1) * 128],
                    self.lam_s[:, (2 * i + 1) * 128:(2 * i + 2) * 128], ALU.mult, ["lam_s"], ["zt"])
            self.S.add("dve", (lambda i=i: nc.vector.reduce_sum(out=lt[:, i:i + 1], in_=self.zt[:, 0:128], axis=AX.X)),
                       ["zt"], ["lam_t"])
        self.act(lt[:, 2:4], lt[:, 0:2], AF.Exp, ["lam_t"], ["lam_t"])
        self.tt("dve", lt[:, 4:5], lt[:, 3:4], lt[:, 2:3], ALU.subtract, ["lam_t"], ["lam_t"])
        self.ts(lt[:, 5:6], lt[:, 4:5], -lam_init, None, ALU.add, None, ["lam_t"], ["neglam"])
        self.ts(self.subln_s[:], self.subln_s[:], 1.0 - lam_init, None, ALU.mult, None, ["subln_s"], ["subln_s"])
        neglam = lt[:, 5:6]
        Town = self.tab[:, 0:1920]
        Toth = self.tab[:, 1920:3840]
        ktown = [(self.kt_own, ["kt_own"]), (self.qb[1], ["qb1"])]
        ktoth = [(self.kt_oth, ["kt_oth"]), (self.stage[0], ["stage0_0", "stage0_1"])]
        for h in range(8):
            self.dma("sp", self.v_own_s[:], self.v_r[l][:, h * 256:(h + 1) * 256].rearrange("(kb p) d -> p kb d", p=128),
                     [f"v{l}"], ["v_own0", "v_own1"])
            ap, rk = self.oth_v(l, 0, 8, h * 256, 256)
            self.dma("sp", self.v_oth_s[:], ap, rk, ["v_oth0", "v_oth1"])
            self.dma("sp", self.gb[:], self.g_r[l][h * 256:(h + 1) * 256, :].rearrange("(c p) n -> p c n", p=128),
                     [f"g{l}"], ["gb0", "gb1"])
            for m in range(2):
                f0 = h * 256 + m * 128
                self.dma("sp", ktown[m][0][:], self.k_r[l][f0:f0 + 128, :], [f"k{l}"], ktown[m][1])
                ap, rk = self.oth_k(l, f0, 0, T)
                self.dma("sp", ktoth[m][0][:], ap, rk, ktoth[m][1])
            groups = []
            gidx = 0
            for tt in range(2):
                sl = slice(tt * 512, (tt + 1) * 512)
                for m in range(2):
                    f0 = h * 256 + m * 128
                    gp = gidx % 2
                    q1 = self.qb[0][:, gp * 512:(gp + 1) * 512]
                    qk = f"qb0_{gp}"
                    po = 4

                    def loads(f0=f0, sl=sl, q1=q1, qk=qk):
                        self.dma("sp", q1, self.q_r[l][f0:f0 + 128, sl], [f"q{l}"], [qk])

                    tasks = []
                    for i_b in range(16):
                        kb = i_b % 8
                        if i_b < 8:
                            kap, kk, vap, vk, Tt, tk = ktown[m][0][:, kb * 128:(kb + 1) * 128], ktown[m][1], self.v_own_s, ["v_own0", "v_own1"], Town, "tabL"
                        else:
                            kap, kk, vap, vk, Tt, tk = ktoth[m][0][:, kb * 128:(kb + 1) * 128], ktoth[m][1], self.v_oth_s, ["v_oth0", "v_oth1"], Toth, "tabH"
                        off = tt * 512 - kb * 128 + 896

                        def s1(i, kap=kap, kk=kk, Tt=Tt, tk=tk, off=off, q1=q1, qk=qk):
                            b = i % 4
                            self.mm(self.ps[b][:], kap, q1, True, True, list(kk) + [qk], [f"ps{b}"])
                            self.stt(self.tmp[:, b, :], Tt[:, off:off + 512], slopes[h], self.ps[b][:],
                                     ALU.mult, ALU.add, [f"ps{b}", tk], [f"tmp{b}"])
                            self.act(self.pt[:, b, :], self.tmp[:, b, :], AF.Exp, [f"tmp{b}"], [f"pt{b}"], scale=SCALE)

                        def s2(i, vap=vap, vk=vk, kb=kb, i_b=i_b, po=po):
                            b = i % 4
                            for hf in range(2):
                                self.mm(self.ps[po + hf][:], vap[:, kb, hf * 128:(hf + 1) * 128], self.pt[:, b, :],
                                        i_b == 0, i_b == 15, list(vk) + [f"pt{b}"], [f"ps{po + hf}"])
                            self.mm(self.ps[po + 2][:], self.ones[:], self.pt[:, b, :], i_b == 0, i_b == 15,
                                    ["ones", f"pt{b}"], [f"ps{po + 2}"])
                        tasks.append((s1, s2))

                    def post_a(po=po):
                        self.recip_act(self.rz[:], self.ps[po + 2][:], [f"ps{po + 2}"], ["rz"])

                    def post(m=m, po=po, sl=sl, tt=tt):
                        if m == 0:
                            for hf in range(2):
                                self.tt("dve", self.on0[:, hf, :], self.ps[po + hf][:], self.rz[:], ALU.mult,
                                        [f"ps{po + hf}", "rz"], ["on0"])
                            return
                        for hf in range(2):
                            self.tt("dve", self.o_s[:, hf, :], self.ps[po + hf][:], self.rz[:], ALU.mult,
                                    [f"ps{po + hf}", "rz"], ["o_s"])
                        self.stt(self.on0[:], self.o_s[:], neglam, self.on0[:], ALU.mult, ALU.add,
                                 ["o_s", "on0", "neglam"], ["on0"])
                        sqb = self.sq[0]
                        self.act(sqb[:], self.on0[:], AF.Square, ["on0"], ["sq0"])
                        for hf in range(2):
                            self.mm(self.ps[7][:], self.ones[:], sqb[:, hf, :], hf == 0, hf == 1, ["ones", "sq0"], ["ps7"])
                        self.recip_act(self.rz[:], self.ps[7][:], ["ps7"], ["rz"], in_scale=1.0 / 256, in_bias=EPS, power=-0.5)
                        for hf in range(2):
                            self.stt(self.on0[:, hf, :], self.on0[:, hf, :], self.subln_s[:, hf:hf + 1], self.rz[:],
                                     ALU.mult, ALU.mult, ["on0", "subln_s", "rz"], ["on0"])
                        self.tt("pool", self.hT[:, 2 * h:2 * h + 2, sl], self.on0[:], self.gb[:, :, sl], ALU.mult,
                                ["on0", "gb0", "gb1"], [f"h{2 * h}_{tt}", f"h{2 * h + 1}_{tt}"])
                    groups.append(dict(loads=loads, tasks=tasks, post=[(0, post_a), (0, post)]))
                    gidx += 1
            self.run_pipeline(groups, 3)

    def phase_E(self, l):
        nc = self.nc
        ev = 0
        pend = None
        for s in range(D // SLABW):
            slab, skey = self.next_slab()
            for mi in range(SLABW // 128):
                m = s * (SLABW // 128) + mi
                ys = self.ystage[ev % 2]
                ysk = f"ystage{ev % 2}"
                for tt in range(2):
                    sl = slice(tt * 512, (tt + 1) * 512)
                    pi = (ev * 2 + tt) % 4
                    ps = self.ps[pi]
                    for kc in range(KC):
                        self.mm(ps[:], slab[:, kc, mi * 128:(mi + 1) * 128], self.hT[:, kc, sl],
                                kc == 0, kc == KC - 1, [skey, f"h{kc}_{tt}"], [f"ps{pi}"])
                    self.S.add("dve", (lambda o=ys[:, sl], i=ps[:]: nc.vector.tensor_copy(out=o, in_=i)),
                               [f"ps{pi}"], [ysk + f"_{tt}"])
                    sqb = self.sq[tt]
                    self.act(sqb[:, 0, :], ys[:, sl], AF.Square, [ysk + f"_{tt}"], [f"sq{tt}"])
                    if pend is not None:
                        pend()
                    pend = (lambda tt=tt, m=m, sqb=sqb: self.mm(self.ps[6 + tt][:], self.ones[:], sqb[:, 0, :], m == 0,
                                                               m == KC - 1, ["ones", f"sq{tt}"], [f"ps{6 + tt}"]))
                self.dma("sp", self.yT[m * 128:(m + 1) * 128, :], ys[:], [ysk + "_0", ysk + "_1"], ["yT"])
                ev += 1
        pend()
        for tt in range(2):
            r = self.rstd[tt]
            self.recip_act(r[:], self.ps[6 + tt][:], [f"ps{6 + tt}"], [f"rstd{tt}"], in_scale=1.0 / D, in_bias=EPS, power=-0.5)
        for kc in range(KC):
            ys = self.ystage[kc % 2]
            ysk = f"ystage{kc % 2}"
            self.dma("sp", ys[:], self.yT[kc * 128:(kc + 1) * 128, :], ["yT"], [ysk + "_0", ysk + "_1"])
            for tt in range(2):
                sl = slice(tt * 512, (tt + 1) * 512)
                self.stt(ys[:, sl], ys[:, sl], self.gpost[:, l * KC + kc:l * KC + kc + 1], self.rstd[tt][:],
                         ALU.mult, ALU.mult, [ysk + f"_{tt}", "gpost", f"rstd{tt}"], [ysk + f"_{tt}"])
                self.tt("dve", self.xT[:, kc, sl], self.xT[:, kc, sl], ys[:, sl], ALU.add,
                        [f"x{kc}_{tt}", ysk + f"_{tt}"], [f"x{kc}_{tt}"])
                self.act(self.hT[:, kc, sl], self.xT[:, kc, sl], AF.Copy, [f"x{kc}_{tt}"], [f"h{kc}_{tt}"])
        self.dma("pool", self.pTb[:], self.pT_d[l].ap().rearrange("(c p) n -> p c n", p=128), (), ["pTb"])
        ev = 0
        for s in range(D // SLABW):
            slab, skey = self.next_slab()
            pes = self.peslab[s % 2]
            pek = f"peslab{s % 2}"
            if s == 0:
                self.dma("pool", pes[:], self.pe_proj_d[l][:, 0:SLABW].rearrange("(c p) n -> p c n", p=128), (), [pek])
            if s + 1 < D // SLABW:
                s1_ = s + 1
                self.dma("pool", self.peslab[s1_ % 2][:],
                         self.pe_proj_d[l][:, s1_ * SLABW:(s1_ + 1) * SLABW].rearrange("(c p) n -> p c n", p=128),
                         (), [f"peslab{s1_ % 2}"])
            for mi in range(SLABW // 128):
                m = s * (SLABW // 128) + mi
                for tt in range(2):
                    sl = slice(tt * 512, (tt + 1) * 512)
                    pi = (ev * 2 + tt) % 4
                    ps = self.ps[pi]
                    pp = self.ps[4 + (ev * 2 + tt) % 2]
                    ppk = f"ps{4 + (ev * 2 + tt) % 2}"
                    for kc in range(KC):
                        self.mm(ps[:], slab[:, kc, mi * 128:(mi + 1) * 128], self.hT[:, kc, sl],
                                kc == 0, kc == KC - 1, [skey, f"h{kc}_{tt}"], [f"ps{pi}"])
                    for c in range(2):
                        self.mm(pp[:], pes[:, c, mi * 128:(mi + 1) * 128], self.pTb[:, c, sl], c == 0, c == 1,
                                [pek, "pTb"], [ppk])
                    gs = self.gate_st[(ev * 2 + tt) % 2]
                    gk = f"gate{(ev * 2 + tt) % 2}"
                    self.act(gs[:], ps[:], AF.Sigmoid, [f"ps{pi}"], [gk])
                    self.tt("dve", gs[:], gs[:], pp[:], ALU.mult, [gk, ppk], [gk])
                    self.tt("dve", self.xT[:, m, sl], self.xT[:, m, sl], gs[:], ALU.add,
                            [f"x{m}_{tt}", gk], [f"x{m}_{tt}"])
                ev += 1


def _const_tables():
    sl = alibi(16)
    r = np.arange(128)[:, None]
    c = np.arange(128)[None, :]
    tabA = np.zeros((4, 128, 3, 4, 128), np.float32)
    for kvh in range(4):
        for j in range(3):
            rel = c - r + (1 - j) * 128
            ok = np.abs(rel) <= 128
            for g in range(4):
                h = kvh * 4 + g
                tabA[kvh, :, j, g, :] = np.where(ok, -sl[h] * np.abs(rel), -BIG) / SCALE
    tabA = tabA.reshape(4, 128, 1536)
    kx = np.arange(64)[:, None]
    qx = np.arange(64)[None, :]
    cs = np.clip(qx - 8, 0, 48)
    okc = (kx >= cs) & (kx < cs + 16)
    cm = np.where(okc, 0.0, -BIG / SCALE).astype(np.float32)
    colmask = np.ascontiguousarray(np.broadcast_to(cm[None, :, :], (2, 64, 64)).reshape(128, 64))
    rowind = np.zeros((2, 128), np.float32)
    rowind[0, :64] = 1
    rowind[1, 64:] = 1
    u = np.arange(1920)[None, :]
    rr = np.arange(128)[:, None]
    Town = (-np.abs(u - 896 - rr) / SCALE).astype(np.float32)
    Toth = [(-np.abs(dl + u - 896 - rr) / SCALE).astype(np.float32) for dl in (-1024, 1024)]
    return tabA, colmask, rowind.astype(ml_dtypes.bfloat16), Town, Toth


def _rowmask(half):
    rm = np.full((2, 16, 8), -BIG, np.float32)
    for tt in range(2):
        if tt == 0:
            blks = [("own", kb) for kb in range(6)] + [("oth", kb) for kb in (6, 7)]
        else:
            blks = [("own", kb) for kb in range(2, 8)] + [("oth", kb) for kb in (0, 1)]
        for i, (who, kb) in enumerate(blks):
            base = half * 16 if who == "own" else (1 - half) * 16
            for a in range(2):
                ky = base + 2 * kb + a
                for e in range(8):
                    qy = half * 16 + tt * 8 + e
                    rs = min(max(qy - 4, 0), 24)
                    if rs <= ky < rs + 8:
                        rm[a, tt * 8 + i, e] = 0.0
    return rm.reshape(2, 128).astype(ml_dtypes.bfloat16)


def _core_inputs(c, inputs, x_state):
    b, half = c // 2, c % 2
    tabA, colmask, rowind, Town, Toth = _CONST
    ts = slice(half * T, (half + 1) * T)
    rpb = inputs["b_rpb"][0]
    rpr = np.zeros((16, 15, 127), np.float32)
    rpr[:, :, 48:79] = rpb[:, ::-1, ::-1]
    rpr = np.ascontiguousarray(np.broadcast_to(rpr.reshape(16, 1, 1905), (16, 64, 1905)))
    edge = np.zeros((128, 2), np.float32)
    edge[:, half] = -BIG
    m = {
        "x_in": x_state[c],
        "gpre": np.ascontiguousarray(inputs["norm_pre"].reshape(DEPTH, KC, 128).transpose(2, 0, 1).reshape(128, DEPTH * KC)),
        "gpost": np.ascontiguousarray(inputs["norm_post"].reshape(DEPTH, KC, 128).transpose(2, 0, 1).reshape(128, DEPTH * KC)),
    }
    for l in range(DEPTH):
        kind, j = l % 3, l // 3
        m[f"w_in{l}"] = inputs[("a_w_in", "b_w_in", "c_w_in")[kind]][j]
        m[f"pT{l}"] = np.ascontiguousarray(inputs["p"][l, b, ts, :].T)
        m[f"w_out{l}"] = inputs["w_out"][l]
        m[f"pe_proj{l}"] = inputs["pe_proj"][l]
        m[f"pe_gate{l}"] = inputs["pe_gate"][l]
    m.update({
        "a_sink_bc": np.ascontiguousarray(np.broadcast_to(inputs["a_sink"].reshape(1, 32), (128, 32))),
        "tabA": tabA, "edgeA": edge, "rpr": rpr, "colmaskB": colmask,
        "rowmaskB": _rowmask(half), "rowindB": rowind,
        "c_lambda_bc": np.ascontiguousarray(np.broadcast_to(inputs["c_lambda"][0].reshape(1, 512), (128, 512))),
        "c_subln_t": np.ascontiguousarray(inputs["c_subln"][0].reshape(2, 128).T),
        "TownC": Town, "TothC": Toth[half],
    })
    return m


_CONST = _const_tables()
_PROG_CACHE = {}


def _get_prog(phases, fused):
    key = (tuple(phases), fused)
    if key not in _PROG_CACHE:
        p = Prog(list(phases), fused)
        p.build()
        _PROG_CACHE[key] = p
    return _PROG_CACHE[key]


SPLIT_LAUNCHES = [[("P", 0)], [("AE", 0), ("P", 1)], [("AE", 1), ("P", 2)], [("AE", 2), ("P", 3)], [("AE", 3)]]
FUSED_LAUNCH = [("P", 0), ("AE", 0), ("P", 1), ("AE", 1), ("P", 2), ("AE", 2), ("P", 3), ("AE", 3)]
MODE = "fused"


def run_launch(phases, fused, inputs, x_state, extra):
    prog = _get_prog(phases, fused)
    in_maps = []
    for c in range(NCORES):
        m = _core_inputs(c, inputs, x_state)
        m.update(extra[c])
        in_maps.append({k: v for k, v in m.items() if k in prog.ext_in})
    res = run_bass_kernel_spmd(prog.nc, in_maps, core_ids=list(range(NCORES)))
    return [{k: np.asarray(r[k]) for k in prog.ext_out} for r in res.results]


def kernel(**inputs):
    inputs = {k: np.asarray(v) for k, v in inputs.items()}
    x = inputs["x"]
    x_state = [np.ascontiguousarray(x[c // 2, (c % 2) * T:(c % 2 + 1) * T, :].T) for c in range(NCORES)]
    if MODE == "fused":
        outs = run_launch(FUSED_LAUNCH, True, inputs, x_state, [{} for _ in range(NCORES)])
        x_state = [o["x_out"] for o in outs]
    else:
        extra = [{} for _ in range(NCORES)]
        for phases in SPLIT_LAUNCHES:
            outs = run_launch(phases, False, inputs, x_state, extra)
            x_state = [o["x_out"] for o in outs]
            extra = [{} for _ in range(NCORES)]
            for ph, l in phases:
                if ph == "P":
                    for c in range(NCORES):
                        o, oo = outs[c], outs[c ^ 1]
                        extra[c] = {f"q{l}_i": o[f"q{l}_o"], f"g{l}_i": o[f"g{l}_o"],
                                    f"k{l}_i": o[f"k{l}_o"], f"v{l}_i": o[f"v{l}_o"],
                                    f"ko{l}_i": oo[f"k{l}_o"], f"vo{l}_i": oo[f"v{l}_o"]}
    out = np.empty((4, 2048, D), np.float32)
    for c in range(NCORES):
        out[c // 2, (c % 2) * T:(c % 2 + 1) * T, :] = x_state[c].T
    return out
```

```python
import math
import numpy as np
import ml_dtypes
import concourse.bass as bass
import concourse.mybir as mybir
from concourse.bass_utils import run_bass_kernel_spmd

F32 = mybir.dt.float32
BF16 = mybir.dt.bfloat16
ALU = mybir.AluOpType
AF = mybir.ActivationFunctionType
AX = mybir.AxisListType

D = 2048
T = 1024
KC = 16
DEPTH = 4
SCALE = 128.0 ** -0.5
EPS = 1e-6
BIG = 30000.0
SLABW = 256
LAGS_A = (1, 2, 3)
N_DMA_SEMS = 12
NCORES = 8

IN_W = {0: (2048, 512, 512, 2048), 1: (2048, 2048, 2048, 2048), 2: (2048, 2048, 2048, 2048)}


def alibi(n):
    return [2.0 ** (-8.0 * (h + 1) / n) for h in range(n)]


class Op:
    __slots__ = ("eng", "fn", "deps", "is_dma", "signal", "ticket", "idx", "inc")

    def __init__(self, eng, fn, is_dma):
        self.eng = eng
        self.fn = fn
        self.deps = []
        self.is_dma = is_dma
        self.signal = False
        self.ticket = None
        self.idx = -1
        self.inc = 16


class Sched:
    def __init__(self, nc, sync_same_engine=True):
        self.nc = nc
        self.ops = []
        self.last_writer = {}
        self.readers = {}
        self.sync_same_engine = sync_same_engine
        self.engs = {"pe": nc.tensor, "act": nc.scalar, "dve": nc.vector,
                     "pool": nc.gpsimd, "sp": nc.sync}

    def add(self, eng, fn, reads=(), writes=(), dma=False, inc=16):
        op = Op(eng, fn, dma)
        op.inc = inc
        op.idx = len(self.ops)
        deps = {}
        for k in reads:
            w = self.last_writer.get(k)
            if w is not None:
                deps[w.idx] = w
            if isinstance(k, str) and k.startswith("ps"):
                for r in self.readers.get(k, ()):
                    if r.eng != eng:
                        deps[r.idx] = r
        for k in writes:
            w = self.last_writer.get(k)
            if w is not None:
                deps[w.idx] = w
            for r in self.readers.get(k, ()):
                deps[r.idx] = r
        op.deps = list(deps.values())
        for k in writes:
            self.last_writer[k] = op
            self.readers[k] = []
        for k in reads:
            lst = self.readers.setdefault(k, [])
            if not dma:
                lst[:] = [r for r in lst if r.is_dma or r.eng != eng]
            lst.append(op)
        self.ops.append(op)
        return op

    def _skip(self, d, op):
        return (d.eng == op.eng and not op.is_dma and not d.is_dma
                and (d.eng == "pe" or not self.sync_same_engine))

    def emit(self):
        nc = self.nc
        for op in self.ops:
            for d in op.deps:
                if d.is_dma or self._skip(d, op):
                    continue
                d.signal = True
        sems = {e: nc.alloc_semaphore(name=f"s_{e}") for e in self.engs}
        dma_sems = {q: [nc.alloc_semaphore(name=f"d_{q}_{i}") for i in range(N_DMA_SEMS)]
                    for q in ("sp", "pool", "act")}
        dma_cnt = {q: [0] * N_DMA_SEMS for q in dma_sems}
        dma_rr = {q: 0 for q in dma_sems}
        cnt = {e: 0 for e in self.engs}
        waited = {e: {} for e in self.engs}

        def wait(eng, sem, val):
            key = id(sem)
            if waited[eng].get(key, 0) >= val:
                return
            waited[eng][key] = val
            self.engs[eng].wait_ge(sem, val)

        for op in self.ops:
            e = op.eng
            for d in op.deps:
                if d.is_dma:
                    wait(e, d.ticket[0], d.ticket[1])
                elif not self._skip(d, op):
                    wait(e, d.ticket[0], d.ticket[1])
            if op.is_dma:
                i = dma_rr[e]
                dma_rr[e] = (i + 1) % N_DMA_SEMS
                sem = dma_sems[e][i]
                if dma_cnt[e][i] > 0:
                    wait(e, sem, dma_cnt[e][i])
                ins = op.fn()
                dma_cnt[e][i] += op.inc
                ins.then_inc(sem, op.inc)
                op.ticket = (sem, dma_cnt[e][i])
            else:
                ins = op.fn()
                if op.signal:
                    cnt[e] += 1
                    ins.then_inc(sems[e], 1)
                    op.ticket = (sems[e], cnt[e])
        for q in dma_sems:
            for i in range(N_DMA_SEMS):
                if dma_cnt[q][i] > 0:
                    wait("sp", dma_sems[q][i], dma_cnt[q][i])
        for e2 in self.engs:
            if cnt[e2] > 0:
                wait("sp", sems[e2], cnt[e2])
        return len(self.ops)


class Prog:
    def __init__(self, phases, fused, dbg=False):
        self.phases = phases
        self.fused = fused
        self.nc = nc = bass.Bass("TRN2", target_bir_lowering=False)
        self.S = Sched(nc)
        self.ext_in = []
        self.ext_out = []
        self.dram = {}
        self._alloc()

    def din(self, name, shape, dt=F32):
        t = self.nc.dram_tensor(name, list(shape), dt, kind="ExternalInput")
        self.dram[name] = t
        self.ext_in.append(name)
        return t

    def dout(self, name, shape, dt=F32):
        t = self.nc.dram_tensor(name, list(shape), dt, kind="ExternalOutput")
        self.dram[name] = t
        self.ext_out.append(name)
        return t

    def dint(self, name, shape, dt=F32):
        t = self.nc.dram_tensor(name, list(shape), dt)
        self.dram[name] = t
        return t

    def sb(self, name, shape, dt):
        return self.nc.alloc_sbuf_tensor(name, list(shape), dt)

    def _alloc(self):
        nc = self.nc
        layers = sorted({l for _, l in self.phases})
        self.layers = layers
        kinds = {l % 3 for l in layers}
        self.x_in = self.din("x_in", [D, T])
        self.x_out = self.dout("x_out", [D, T])
        self.gpre_d = self.din("gpre", [128, DEPTH * KC])
        self.gpost_d = self.din("gpost", [128, DEPTH * KC])
        self.pT_d, self.w_out_d, self.pe_proj_d, self.pe_gate_d, self.w_in_d = {}, {}, {}, {}, {}
        for ph, l in self.phases:
            if ph == "P":
                self.w_in_d[l] = self.din(f"w_in{l}", [D, sum(IN_W[l % 3])])
            else:
                self.pT_d[l] = self.din(f"pT{l}", [256, T])
                self.w_out_d[l] = self.din(f"w_out{l}", [D, D])
                self.pe_proj_d[l] = self.din(f"pe_proj{l}", [256, D])
                self.pe_gate_d[l] = self.din(f"pe_gate{l}", [D, D])
        self.sink_d = self.din("a_sink_bc", [128, 32])
        self.tabA_d = self.din("tabA", [4, 128, 1536])
        self.edge_d = self.din("edgeA", [128, 2])
        self.rpr_d = self.din("rpr", [16, 64, 15 * 127])
        self.colmask_d = self.din("colmaskB", [128, 64])
        self.rowmask_d = self.din("rowmaskB", [2, 128], BF16)
        self.rowind_d = self.din("rowindB", [2, 128], BF16)
        self.lam_d = self.din("c_lambda_bc", [128, 512])
        self.subln_d = self.din("c_subln_t", [128, 2])
        self.Town_d = self.din("TownC", [128, 1920])
        self.Toth_d = self.din("TothC", [128, 1920])
        self.q_w, self.q_r, self.g_w, self.g_r = {}, {}, {}, {}
        self.k_w, self.k_r, self.v_w, self.v_r = {}, {}, {}, {}
        self.k_oth, self.v_oth, self.k_all, self.v_all = {}, {}, {}, {}
        for l in layers:
            _, dk, dv, _ = IN_W[l % 3]
            hasP = ("P", l) in self.phases
            hasA = ("AE", l) in self.phases
            if hasP and hasA:
                assert self.fused
                self.q_w[l] = self.q_r[l] = self.dint(f"q{l}", [D, T], BF16)
                self.g_w[l] = self.g_r[l] = self.dint(f"g{l}", [D, T], BF16)
                self.k_w[l] = self.k_r[l] = self.dint(f"k{l}", [dk, T], BF16)
                self.v_w[l] = self.v_r[l] = self.dint(f"v{l}", [T, dv], BF16)
                self.k_all[l] = self.dint(f"kall{l}", [2 * dk, T], BF16)
                self.v_all[l] = self.dint(f"vall{l}", [2 * T, dv], BF16)
                self.k_oth[l] = self.dint(f"koth{l}", [dk, T], BF16)
                self.v_oth[l] = self.dint(f"voth{l}", [T, dv], BF16)
            else:
                if hasP:
                    self.q_w[l] = self.dout(f"q{l}_o", [D, T], BF16)
                    self.g_w[l] = self.dout(f"g{l}_o", [D, T], BF16)
                    self.k_w[l] = self.dout(f"k{l}_o", [dk, T], BF16)
                    self.v_w[l] = self.dout(f"v{l}_o", [T, dv], BF16)
                if hasA:
                    self.q_r[l] = self.din(f"q{l}_i", [D, T], BF16)
                    self.g_r[l] = self.din(f"g{l}_i", [D, T], BF16)
                    self.k_r[l] = self.din(f"k{l}_i", [dk, T], BF16)
                    self.v_r[l] = self.din(f"v{l}_i", [T, dv], BF16)
                    self.k_oth[l] = self.din(f"ko{l}_i", [dk, T], BF16)
                    self.v_oth[l] = self.din(f"vo{l}_i", [T, dv], BF16)
        self.yT = self.dint("yT", [D, T], F32)
        self.xT = self.sb("xT", [128, KC, T], F32)
        self.hT = self.sb("hT", [128, KC, T], BF16)
        self.wslab = [self.sb(f"wslab{i}", [128, KC, SLABW], BF16) for i in range(2)]
        self.ones = self.sb("ones", [128, 128], BF16)
        self.zeros = self.sb("zeros", [128, 16], F32)
        self.zeros128 = self.sb("zeros128", [128, 128], F32)
        self.esk = self.sb("esk", [128, 512], F32)
        self.gpre = self.sb("gpre_s", [128, DEPTH * KC], F32)
        self.gpost = self.sb("gpost_s", [128, DEPTH * KC], F32)
        self.sq = [self.sb(f"sq{i}", [128, 2, 512], BF16) for i in range(2)]
        self.rstd = [self.sb(f"rstd{i}", [128, 512], F32) for i in range(2)]
        self.stage = [self.sb(f"stage{i}", [128, T], BF16) for i in range(2)]
        self.ystage = [self.sb(f"ystage{i}", [128, T], F32) for i in range(2)]
        self.gate_st = [self.sb(f"gate{i}", [128, 512], F32) for i in range(2)]
        self.peslab = [self.sb(f"peslab{i}", [128, 2, SLABW], BF16) for i in range(2)]
        self.pTb = self.sb("pTb", [128, 2, T], BF16)
        self.kt_own = self.sb("kt_own", [128, T], BF16)
        self.kt_oth = self.sb("kt_oth", [128, T], BF16)
        self.qb = [self.sb(f"qb{i}", [128, T], BF16) for i in range(2)]
        self.gb = self.sb("gb", [128, 2, T], BF16)
        self.v_own_s = self.sb("v_own", [128, 8, 256], BF16)
        self.v_oth_s = self.sb("v_oth", [128, 8, 256], BF16)
        self.tab = self.sb("tab", [128, 3840], F32)
        self.tmp = self.sb("tmp", [128, 4, 512], F32)
        self.pt = self.sb("pt", [128, 4, 512], BF16)
        self.on0 = self.sb("on0", [128, 2, 512], F32)
        self.o_s = self.sb("o_s", [128, 2, 512], F32)
        self.rz = self.sb("rz", [128, 512], F32)
        self.zt = self.sb("zt", [128, 512], F32)
        self.es = self.sb("es", [128, 32], F32)
        self.sink_s = self.sb("sink_s", [128, 32], F32)
        self.edge = self.sb("edge", [128, 2], F32)
        self.lam_s = self.sb("lam_s", [128, 512], F32)
        self.lam_t = self.sb("lam_t", [128, 8], F32)
        self.subln_s = self.sb("subln_s", [128, 2], F32)
        self.rowmask = self.sb("rowmask", [2, 128], BF16)
        self.rowind = self.sb("rowind", [2, 128], BF16)
        self.cm64 = self.sb("cm64", [128, 64], F32)
        self.ps = [nc.alloc_psum_tensor(f"ps{i}", [128, 512], F32) for i in range(8)]
        self.wi = 0
        self.par = None

    def dma(self, q, out, in_, reads, writes):
        eng = {"sp": self.nc.sync, "pool": self.nc.gpsimd, "act": self.nc.scalar}[q]
        self.S.add(q, lambda: eng.dma_start(out=out, in_=in_), reads, writes, dma=True)

    def act(self, out, in_, func, reads, writes, bias=0.0, scale=1.0):
        nc = self.nc
        self.S.add("act", lambda: nc.scalar.activation(out=out, in_=in_, func=func, bias=bias, scale=scale),
                   reads, writes)

    def tt(self, eng, out, in0, in1, op, reads, writes):
        e = {"dve": self.nc.vector, "pool": self.nc.gpsimd}[eng]
        self.S.add(eng, lambda: e.tensor_tensor(out=out, in0=in0, in1=in1, op=op), reads, writes)

    def stt(self, out, in0, scalar, in1, op0, op1, reads, writes, eng="dve"):
        e = {"dve": self.nc.vector, "pool": self.nc.gpsimd}[eng]
        self.S.add(eng, lambda: e.scalar_tensor_tensor(out=out, in0=in0, scalar=scalar, in1=in1,
                                                        op0=op0, op1=op1), reads, writes)

    def ts(self, out, in0, s1, s2, op0, op1, reads, writes):
        nc = self.nc
        if s2 is None:
            self.S.add("dve", lambda: nc.vector.tensor_scalar(out=out, in0=in0, scalar1=s1, scalar2=None,
                                                              op0=op0), reads, writes)
        else:
            self.S.add("dve", lambda: nc.vector.tensor_scalar(out=out, in0=in0, scalar1=s1, scalar2=s2,
                                                              op0=op0, op1=op1), reads, writes)

    def recip(self, out, in_, reads, writes):
        nc = self.nc
        self.S.add("dve", lambda: nc.vector.reciprocal(out=out, in_=in_), reads, writes)

    def mm(self, out, lhsT, rhs, start, stop, reads, writes):
        nc = self.nc
        self.S.add("pe", lambda: nc.tensor.matmul(out, lhsT, rhs, start=start, stop=stop), reads, writes)

    def memset(self, ap, val, writes):
        nc = self.nc
        self.S.add("dve", lambda: nc.vector.memset(ap, val), (), writes)

    @staticmethod
    def slab_order(kind):
        wq, wk, wv, wg = IN_W[kind]
        b = [0, wq // SLABW, (wq + wk) // SLABW, (wq + wk + wv) // SLABW, (wq + wk + wv + wg) // SLABW]
        q, k, v, g = (list(range(b[i], b[i + 1])) for i in range(4))
        return k + v + q + g

    def plan_slabs(self):
        slabs = []
        for ph, l in self.phases:
            if ph == "P":
                for s in self.slab_order(l % 3):
                    slabs.append((self.w_in_d[l], s))
            elif ph == "AE":
                for s in range(D // SLABW):
                    slabs.append((self.w_out_d[l], s))
                for s in range(D // SLABW):
                    slabs.append((self.pe_gate_d[l], s))
        self.slabs = slabs
        self.slab_issued = 0
        self.slab_used = 0

    def _issue_slab(self):
        i = self.slab_issued
        if i >= len(self.slabs):
            return
        w, s = self.slabs[i]
        src = w[:, s * SLABW:(s + 1) * SLABW].rearrange("(kc p) n -> p kc n", p=128)
        self.dma("pool", self.wslab[i % 2][:], src, (), [f"wslab{i % 2}"])
        self.slab_issued += 1

    def next_slab(self):
        i = self.slab_used
        while self.slab_issued <= min(i + 1, len(self.slabs) - 1):
            self._issue_slab()
        self.slab_used += 1
        return self.wslab[i % 2], f"wslab{i % 2}"

    def build(self):
        nc = self.nc
        self.plan_slabs()
        if self.fused:
            self.par = nc.sync.partition_id() % 2
        self.memset(self.ones[:], 1.0, ["ones"])
        self.memset(self.zeros[:], 0.0, ["zeros"])
        self.memset(self.zeros128[:], 0.0, ["zeros128"])
        self.dma("sp", self.gpre[:], self.gpre_d.ap(), (), ["gpre"])
        self.dma("sp", self.gpost[:], self.gpost_d.ap(), (), ["gpost"])
        xv = self.x_in.ap().rearrange("(kc p) n -> p kc n", p=128)
        for kc in range(KC):
            self.dma("sp", self.xT[:, kc, :], xv[:, kc, :], (), [f"x{kc}_0", f"x{kc}_1"])
        for ph, l in self.phases:
            if ph == "P":
                self.phase_P(l)
            else:
                kind = l % 3
                if kind == 0:
                    self.attn_A(l)
                elif kind == 1:
                    self.attn_B(l)
                else:
                    self.attn_C(l)
                self.phase_E(l)
        ov = self.x_out.ap().rearrange("(kc p) n -> p kc n", p=128)
        for kc in range(KC):
            self.dma("sp", ov[:, kc, :], self.xT[:, kc, :], [f"x{kc}_0", f"x{kc}_1"], [])
        n = self.S.emit()
        return n

    def stats_rstd(self, tt, src_fn, src_keys, nchunks, inv_n, rstd_i):
        ps = self.ps[7]
        for kc in range(nchunks):
            sqb = self.sq[kc % 2]
            self.act(sqb[:, 0, :], src_fn(kc), AF.Square, src_keys(kc), [f"sq{kc % 2}"])
            self.mm(ps[:], self.ones[:], sqb[:, 0, :], kc == 0, kc == nchunks - 1,
                    ["ones", f"sq{kc % 2}"], ["ps7"])
        r = self.rstd[rstd_i]
        self.recip_act(r[:], ps[:], ["ps7"], [f"rstd{rstd_i}"], in_scale=inv_n, in_bias=EPS, power=-0.5)

    def phase_P(self, l):
        kind = l % 3
        wq, wk, wv, wg = IN_W[kind]
        for tt in range(2):
            sl = slice(tt * 512, (tt + 1) * 512)
            self.stats_rstd(tt, lambda kc: self.xT[:, kc, sl], lambda kc: [f"x{kc}_{tt}"], KC, 1.0 / D, tt)
            for kc in range(KC):
                self.stt(self.hT[:, kc, sl], self.xT[:, kc, sl], self.gpre[:, l * KC + kc:l * KC + kc + 1],
                         self.rstd[tt][:], ALU.mult, ALU.mult,
                         [f"x{kc}_{tt}", "gpre", f"rstd{tt}"], [f"h{kc}_{tt}"])
        order = self.slab_order(kind)
        last_k = (wq + wk) // SLABW - 1
        last_v = (wq + wk + wv) // SLABW - 1
        ev = 0
        for s in order:
            c0 = s * SLABW
            slab, skey = self.next_slab()
            if c0 < wq:
                seg, dst, r0 = "q", self.q_w[l], c0
            elif c0 < wq + wk:
                seg, dst, r0 = "k", self.k_w[l], c0 - wq
            elif c0 < wq + wk + wv:
                seg, dst, r0 = "v", self.v_w[l], c0 - wq - wk
            else:
                seg, dst, r0 = "g", self.g_w[l], c0 - wq - wk - wv
            if seg != "v":
                for mi in range(SLABW // 128):
                    st = self.stage[ev % 2]
                    stk = f"stage{ev % 2}"
                    for tt in range(2):
                        sl = slice(tt * 512, (tt + 1) * 512)
                        pi = (ev * 2 + tt) % 4
                        ps = self.ps[pi]
                        for kc in range(KC):
                            self.mm(ps[:], slab[:, kc, mi * 128:(mi + 1) * 128], self.hT[:, kc, sl],
                                    kc == 0, kc == KC - 1, [skey, f"h{kc}_{tt}"], [f"ps{pi}"])
                        if seg == "g":
                            self.act(st[:, sl], ps[:], AF.Silu, [f"ps{pi}"], [stk + f"_{tt}"])
                        elif (ev + tt) % 2 == 0:
                            self.act(st[:, sl], ps[:], AF.Copy, [f"ps{pi}"], [stk + f"_{tt}"])
                        else:
                            nc = self.nc
                            self.S.add("dve", (lambda o=st[:, sl], i=ps[:]: nc.vector.tensor_copy(out=o, in_=i)),
                                       [f"ps{pi}"], [stk + f"_{tt}"])
                    row = r0 + mi * 128
                    self.dma("sp", dst[row:row + 128, :], st[:], [stk + "_0", stk + "_1"], [f"{seg}{l}"])
                    ev += 1
            else:
                for tb in range(8):
                    st = self.stage[ev % 2]
                    stk = f"stage{ev % 2}"
                    pi = ev % 4
                    ps = self.ps[pi]
                    for kc in range(KC):
                        self.mm(ps[:, 0:SLABW], self.hT[:, kc, tb * 128:(tb + 1) * 128], slab[:, kc, :],
                                kc == 0, kc == KC - 1, [skey, f"h{kc}_{tb // 4}"], [f"ps{pi}"])
                    nc = self.nc
                    self.S.add("dve", (lambda o=st[:, 0:SLABW], i=ps[:, 0:SLABW]: nc.vector.tensor_copy(out=o, in_=i)),
                               [f"ps{pi}"], [stk + "_0", stk + "_1"])
                    self.dma("sp", dst[tb * 128:(tb + 1) * 128, r0:r0 + SLABW], st[:, 0:SLABW],
                             [stk + "_0", stk + "_1"], [f"v{l}"])
                    ev += 1
            if self.fused and s == last_k:
                self.exchange_gather(l, "k")
            if self.fused and s == last_v:
                self.exchange_gather(l, "v")
        if self.fused:
            self.exchange_copy(l)

    def _xchg(self, l, nm):
        src, allt, otht = ((self.k_w[l], self.k_all[l], self.k_oth[l]) if nm == "k"
                           else (self.v_w[l], self.v_all[l], self.v_oth[l]))
        rows, cols = src.shape
        rc = min(rows, (2 << 20) // (cols * 2))
        return src, allt, otht, rc, rows // rc

    def exchange_gather(self, l, nm):
        nc = self.nc
        groups = [[2 * i, 2 * i + 1] for i in range(NCORES // 2)]
        src, allt, otht, rc, nchunk = self._xchg(l, nm)
        for c in range(nchunk):
            self.S.add("pool", (lambda s=src[c * rc:(c + 1) * rc, :], d=allt[c * 2 * rc:(c + 1) * 2 * rc, :]:
                                nc.gpsimd.collective_compute(
                "AllGather", mybir.AluOpType.bypass, replica_groups=groups,
                ins=[s.opt()], outs=[d.opt()])), [f"{nm}{l}"], [f"{nm}all{l}"], dma=True, inc=1)

    def exchange_copy(self, l):
        for nm in ("k", "v"):
            src, allt, otht, rc, nchunk = self._xchg(l, nm)
            gv = allt.ap().rearrange("(c r f) n -> c r f n", c=nchunk, r=2)
            self.dma("sp", otht.ap().rearrange("(c f) n -> f c n", c=nchunk),
                     gv[:, bass.ds(1 - self.par, 1), :, :].rearrange("c 1 f n -> f c n"),
                     [f"{nm}all{l}"], [f"{nm}oth{l}"])

    def oth_k(self, l, f0, c0, c1):
        return self.k_oth[l][f0:f0 + 128, c0:c1], ([f"koth{l}"] if self.fused else [])

    def oth_v(self, l, kb0, nkb, d0, dn):
        return (self.v_oth[l][kb0 * 128:(kb0 + nkb) * 128, d0:d0 + dn].rearrange("(kb p) d -> p kb d", p=128),
                [f"voth{l}"] if self.fused else [])

    def run_pipeline(self, groups, LA, NBUF=2):
        flat = []
        for gi, g in enumerate(groups):
            nt = len(g["tasks"])
            for ti, (s1, s2) in enumerate(g["tasks"]):
                flat.append((gi, s1, s2, ti == nt - 1))
        for gi in range(min(NBUF, len(groups))):
            if groups[gi]["loads"]:
                groups[gi]["loads"]()
        n = len(flat)
        pending = []
        for i in range(n + LA):
            if i < n:
                flat[i][1](i)
            j = i - LA
            if j >= 0:
                gj, _, s2, last = flat[j]
                s2(j)
                if last:
                    stages = groups[gj]["post"] or []
                    for k, (lag, fn) in enumerate(stages):
                        pending.append((j + lag, fn, gj if k == len(stages) - 1 else None))
                still = []
                for due, fn, gdone in pending:
                    if due <= j:
                        fn()
                        if gdone is not None and gdone + NBUF < len(groups) and groups[gdone + NBUF]["loads"]:
                            groups[gdone + NBUF]["loads"]()
                    else:
                        still.append((due, fn, gdone))
                pending = still
        for due, fn, gdone in pending:
            fn()

    def recip_act(self, out, in_, rk, wk, in_scale=1.0, in_bias=0.0, power=-1.0):
        self.act(out, in_, AF.Ln, rk, wk, bias=in_bias, scale=in_scale)
        self.act(out, out, AF.Exp, wk, wk, scale=power)

    def attn_A(self, l):
        j = l // 3
        self.dma("sp", self.sink_s[:], self.sink_d.ap(), (), ["sink_s"])
        self.dma("sp", self.edge[:], self.edge_d.ap(), (), ["edge"])
        tabv = self.tab[:, 0:1536].rearrange("p (j n) -> p j n", j=3)
        for kvh in range(4):
            f0 = kvh * 128
            self.dma("sp", self.kt_own[:], self.k_r[l][f0:f0 + 128, :], [f"k{l}"], ["kt_own"])
            ap, rk = self.oth_k(l, f0, 896, 1024)
            self.dma("sp", self.kt_oth[:, 0:128], ap, rk, ["kt_oth"])
            ap, rk = self.oth_k(l, f0, 0, 128)
            self.dma("sp", self.kt_oth[:, 128:256], ap, rk, ["kt_oth"])
            self.dma("sp", self.v_own_s[:, :, 0:128],
                     self.v_r[l][:, f0:f0 + 128].rearrange("(kb p) d -> p kb d", p=128), [f"v{l}"], ["v_own0"])
            ap, rk = self.oth_v(l, 7, 1, f0, 128)
            self.dma("sp", self.v_oth_s[:, 0:1, 0:128], ap, rk, ["v_oth0"])
            ap, rk = self.oth_v(l, 0, 1, f0, 128)
            self.dma("sp", self.v_oth_s[:, 1:2, 0:128], ap, rk, ["v_oth0"])
            self.dma("sp", self.tab[:, 0:1536], self.tabA_d[kvh], (), ["tabL"])
            for g in range(4):
                c = j * 16 + kvh * 4 + g
                self.act(self.esk[:, g * 128:(g + 1) * 128], self.zeros128[:], AF.Exp, ["zeros128", "sink_s"], ["esk"],
                         bias=self.sink_s[:, c:c + 1], scale=1.0)
            groups = []
            for qi in range(8):
                par = qi % 2
                slot = qi % 3
                qs = slice(qi * 128, (qi + 1) * 128)
                q4 = (self.qb[0][:, 0:512], self.qb[0][:, 512:1024], self.qb[1][:, 0:512])[slot]
                g4 = (self.gb[:, 0, 0:512], self.gb[:, 0, 512:1024], self.gb[:, 1, 0:512])[slot]
                qk, gk = ("qb0_0", "qb0_1", "qb1")[slot], ("gb0", "gb0", "gb1")[slot]
                po, pz = (4, 5) if par == 0 else (6, 7)

                def loads(qs=qs, q4=q4, g4=g4, qk=qk, gk=gk, kvh=kvh):
                    qsrc = self.q_r[l][kvh * 512:(kvh + 1) * 512, qs].rearrange("(g p) n -> p g n", p=128)
                    self.dma("sp", q4.rearrange("p (g n) -> p g n", g=4), qsrc, [f"q{l}"], [qk])
                    gsrc = self.g_r[l][kvh * 512:(kvh + 1) * 512, qs].rearrange("(g p) n -> p g n", p=128)
                    self.dma("sp", g4.rearrange("p (g n) -> p g n", g=4), gsrc, [f"g{l}"], [gk])

                tasks = []
                for jj in range(3):
                    kb = qi + jj - 1
                    if kb < 0:
                        blk = (self.kt_oth[:, 0:128], "kt_oth", self.v_oth_s[:, 0, 0:128], "v_oth0", 0)
                    elif kb > 7:
                        blk = (self.kt_oth[:, 128:256], "kt_oth", self.v_oth_s[:, 1, 0:128], "v_oth0", 1)
                    else:
                        blk = (self.kt_own[:, kb * 128:(kb + 1) * 128], "kt_own", self.v_own_s[:, kb, 0:128], "v_own0", None)

                    def s1(i, blk=blk, jj=jj, q4=q4, qk=qk):
                        kap, kk, vap, vk, edge = blk
                        b = i % 3
                        self.mm(self.ps[b][:], kap, q4, True, True, [kk, qk], [f"ps{b}"])
                        self.tt("dve", self.tmp[:, b, :], self.ps[b][:], tabv[:, jj, :], ALU.add,
                                [f"ps{b}", "tabL"], [f"tmp{b}"])
                        if edge is None:
                            self.act(self.pt[:, b, :], self.tmp[:, b, :], AF.Exp, [f"tmp{b}"], [f"pt{b}"], scale=SCALE)
                        else:
                            self.act(self.pt[:, b, :], self.tmp[:, b, :], AF.Exp, [f"tmp{b}", "edge"], [f"pt{b}"],
                                     bias=self.edge[:, edge:edge + 1], scale=SCALE)

                    def s2(i, blk=blk, jj=jj, po=po, pz=pz):
                        kap, kk, vap, vk, edge = blk
                        b = i % 3
                        self.mm(self.ps[po][:], vap, self.pt[:, b, :], jj == 0, jj == 2, [vk, f"pt{b}"], [f"ps{po}"])
                        self.mm(self.ps[pz][:], self.ones[:], self.pt[:, b, :], jj == 0, jj == 2,
                                ["ones", f"pt{b}"], [f"ps{pz}"])
                    tasks.append((s1, s2))

                def post_a(pz=pz):
                    self.tt("dve", self.zt[:], self.ps[pz][:], self.esk[:], ALU.add, [f"ps{pz}", "esk"], ["zt"])

                def post_b():
                    self.recip_act(self.rz[:], self.zt[:], ["zt"], ["rz"])

                def post_c(qs=qs, g4=g4, gk=gk, po=po, kvh=kvh, qi=qi):
                    self.tt("dve", self.o_s[:, 0, :], self.ps[po][:], self.rz[:], ALU.mult, [f"ps{po}", "rz"], ["o_s"])
                    okeys = [f"h{kvh * 4 + g}_{qi // 4}" for g in range(4)]
                    self.tt("pool", self.hT[:, kvh * 4:(kvh + 1) * 4, qs],
                            self.o_s[:, 0, :].rearrange("p (g n) -> p g n", g=4),
                            g4.rearrange("p (g n) -> p g n", g=4), ALU.mult, ["o_s", gk], okeys)
                groups.append(dict(loads=loads, tasks=tasks, post=[(LAGS_A[0], post_a), (LAGS_A[1], post_b), (LAGS_A[2], post_c)]))
            self.run_pipeline(groups, 2, NBUF=3)

    def attn_B(self, l):
        self.dma("sp", self.rowmask[:], self.rowmask_d.ap(), (), ["rowmask"])
        self.dma("sp", self.rowind[:], self.rowind_d.ap(), (), ["rowind"])
        self.dma("sp", self.cm64[:], self.colmask_d.ap(), (), ["cm64"])
        self.memset(self.tab[:, 0:1920], 0.0, ["tabL"])
        self.memset(self.tab[:, 1920:3840], 0.0, ["tabH"])
        ktown = [(self.kt_own, ["kt_own"]), (self.qb[1], ["qb1"])]
        ktoth = [(self.kt_oth, ["kt_oth"]), (self.stage[0], ["stage0_0", "stage0_1"])]
        qbuf = [(self.qb[0], ["qb0_0", "qb0_1"]), (self.stage[1], ["stage1_0", "stage1_1"])]
        groups = []
        gidx = 0
        for h in range(16):
            hp = h % 2
            f0 = h * 128
            kto, ktok = ktown[hp]
            ktt, kttk = ktoth[hp]
            qq, qqk = qbuf[hp]
            gg, ggk = self.gb[:, hp, :], f"gb{hp}"
            vo, vok = self.v_own_s[:, :, hp * 128:(hp + 1) * 128], f"v_own{hp}"
            vt, vtk = self.v_oth_s[:, :, hp * 128:(hp + 1) * 128], f"v_oth{hp}"
            tkey = "tabL" if hp == 0 else "tabH"
            TB = self.tab[:, hp * 1920:(hp + 1) * 1920].rearrange("p (j n) -> p j n", j=30)

            def loads(h=h, f0=f0, kto=kto, ktok=ktok, ktt=ktt, kttk=kttk, qq=qq, qqk=qqk, gg=gg, ggk=ggk,
                      vo=vo, vok=vok, vt=vt, vtk=vtk, tkey=tkey, TB=TB):
                self.dma("sp", kto[:], self.k_r[l][f0:f0 + 128, :], [f"k{l}"], ktok)
                ap, rk = self.oth_k(l, f0, 0, 256)
                self.dma("sp", ktt[:, 0:256], ap, rk, kttk)
                ap, rk = self.oth_k(l, f0, 768, 1024)
                self.dma("sp", ktt[:, 768:1024], ap, rk, kttk)
                self.dma("sp", vo, self.v_r[l][:, f0:f0 + 128].rearrange("(kb p) d -> p kb d", p=128), [f"v{l}"], [vok])
                ap, rk = self.oth_v(l, 0, 2, f0, 128)
                self.dma("sp", vt[:, 0:2, :], ap, rk, [vtk])
                ap, rk = self.oth_v(l, 6, 2, f0, 128)
                self.dma("sp", vt[:, 6:8, :], ap, rk, [vtk])
                self.dma("sp", qq[:], self.q_r[l][f0:f0 + 128, :], [f"q{l}"], qqk)
                self.dma("sp", gg, self.g_r[l][f0:f0 + 128, :], [f"g{l}"], [ggk])
                for a in range(2):
                    src = bass.AP(self.rpr_d, h * 64 * 1905 + 63, [[1904, 64], [127, 15], [1, 64]])
                    self.dma("sp", TB[a * 64:(a + 1) * 64, a + 7:a + 22, :], src, (), [tkey])
                for a in range(2):
                    cmb = bass.AP(self.cm64, a * 64 * 64, [[64, 64], [0, 15], [1, 64]])
                    self.stt(TB[a * 64:(a + 1) * 64, a + 7:a + 22, :], TB[a * 64:(a + 1) * 64, a + 7:a + 22, :],
                             1.0 / SCALE, cmb, ALU.mult, ALU.add, [tkey, "cm64"], [tkey])

            for tt in range(2):
                sl = slice(tt * 512, (tt + 1) * 512)
                if tt == 0:
                    blks = [("own", kb, 2 * kb + 7) for kb in range(6)] + [("oth", kb, 2 * kb - 9) for kb in (6, 7)]
                else:
                    blks = [("own", kb, 2 * kb - 1) for kb in range(2, 8)] + [("oth", kb, 15 + 2 * kb) for kb in (0, 1)]
                nb = len(blks)
                po, pz = (4, 5) if gidx % 2 == 0 else (6, 7)
                tasks = []
                for i_b, (who, kb, d0) in enumerate(blks):
                    pidx = tt * 8 + i_b
                    if who == "own":
                        kap, kk, vap, vk = kto[:, kb * 128:(kb + 1) * 128], ktok, vo[:, kb, :], vok
                    else:
                        kap, kk, vap, vk = ktt[:, kb * 128:(kb + 1) * 128], kttk, vt[:, kb, :], vtk

                    def s1(i, kap=kap, kk=kk, pidx=pidx, d0=d0, qq=qq, qqk=qqk, sl=sl, TB=TB, tkey=tkey):
                        b = i % 4
                        self.mm(self.ps[b][:], kap, qq[:, sl], True, False, list(kk) + list(qqk), [f"ps{b}"])
                        self.mm(self.ps[b][:], self.rowind[:],
                                bass.AP(self.rowmask, pidx * 8, [[128, 2], [1, 8], [0, 64]]), False, True,
                                ["rowind", "rowmask"], [f"ps{b}"])
                        j0 = 21 - d0
                        self.tt("dve", self.tmp[:, b, :], self.ps[b][:],
                                TB[:, j0:j0 + 8, :].rearrange("p j n -> p (j n)"), ALU.add,
                                [f"ps{b}", tkey], [f"tmp{b}"])
                        self.act(self.pt[:, b, :], self.tmp[:, b, :], AF.Exp, [f"tmp{b}"], [f"pt{b}"], scale=SCALE)

                    def s2(i, vap=vap, vk=vk, i_b=i_b, nb=nb, po=po, pz=pz):
                        b = i % 4
                        self.mm(self.ps[po][:], vap, self.pt[:, b, :], i_b == 0, i_b == nb - 1, [vk, f"pt{b}"], [f"ps{po}"])
                        self.mm(self.ps[pz][:], self.ones[:], self.pt[:, b, :], i_b == 0, i_b == nb - 1,
                                ["ones", f"pt{b}"], [f"ps{pz}"])
                    tasks.append((s1, s2))

                def post_a(pz=pz):
                    self.recip_act(self.rz[:], self.ps[pz][:], [f"ps{pz}"], ["rz"])

                def post_b(h=h, sl=sl, tt=tt, gg=gg, ggk=ggk, po=po):
                    self.tt("dve", self.o_s[:, 0, :], self.ps[po][:], self.rz[:], ALU.mult, [f"ps{po}", "rz"], ["o_s"])
                    self.tt("pool", self.hT[:, h, sl], self.o_s[:, 0, :], gg[:, sl], ALU.mult,
                            ["o_s", ggk], [f"h{h}_{tt}"])
                groups.append(dict(loads=loads if tt == 0 else None, tasks=tasks, post=[(1, post_a), (3, post_b)]))
                gidx += 1
        self.run_pipeline(groups, 3)

    def attn_C(self, l):
        nc = self.nc
        lam_init = 0.8 - 0.6 * math.exp(-0.3 * l)
        slopes = alibi(8)
        self.dma("sp", self.lam_s[:], self.lam_d.ap(), (), ["lam_s"])
        self.dma("sp", self.subln_s[:], self.subln_d.ap(), (), ["subln_s"])
        self.dma("sp", self.tab[:, 0:1920], self.Town_d.ap(), (), ["tabL"])
        self.dma("sp", self.tab[:, 1920:3840], self.Toth_d.ap(), (), ["tabH"])
        lt = self.lam_t
        for i in range(2):
            self.tt("dve", self.zt[:, 0:128], self.lam_s[:, (2 * i) * 128:(2 * i + 1) * 128],
                    self.lam_s[:, (2 * i + 1) * 128:(2 * i + 2) * 128], ALU.mult, ["lam_s"], ["zt"])
            self.S.add("dve", (lambda i=i: nc.vector.reduce_sum(out=lt[:, i:i + 1], in_=self.zt[:, 0:128], axis=AX.X)),
                       ["zt"], ["lam_t"])
        self.act(lt[:, 2:4], lt[:, 0:2], AF.Exp, ["lam_t"], ["lam_t"])
        self.tt("dve", lt[:, 4:5], lt[:, 3:4], lt[:, 2:3], ALU.subtract, ["lam_t"], ["lam_t"])
        self.ts(lt[:, 5:6], lt[:, 4:5], -lam_init, None, ALU.add, None, ["lam_t"], ["neglam"])
        self.ts(self.subln_s[:], self.subln_s[:], 1.0 - lam_init, None, ALU.mult, None, ["subln_s"], ["subln_s"])
        neglam = lt[:, 5:6]
        Town = self.tab[:, 0:1920]
        Toth = self.tab[:, 1920:3840]
        ktown = [(self.kt_own, ["kt_own"]), (self.qb[1], ["qb1"])]
        ktoth = [(self.kt_oth, ["kt_oth"]), (self.stage[0], ["stage0_0", "stage0_1"])]
        for h in range(8):
            def late_loads(h=h):
                ap, rk = self.oth_k(l, h * 256, 0, T)
                self.dma("sp", ktoth[0][0][:], ap, rk, ktoth[0][1])
                ap, rk = self.oth_v(l, 0, 8, h * 256, 256)
                self.dma("sp", self.v_oth_s[:], ap, rk, ["v_oth0", "v_oth1"])
                for m in (1,):
                    f0 = h * 256 + m * 128
                    self.dma("sp", ktown[m][0][:], self.k_r[l][f0:f0 + 128, :], [f"k{l}"], ktown[m][1])
                    ap, rk = self.oth_k(l, f0, 0, T)
                    self.dma("sp", ktoth[m][0][:], ap, rk, ktoth[m][1])
                self.dma("sp", self.gb[:], self.g_r[l][h * 256:(h + 1) * 256, :].rearrange("(c p) n -> p c n", p=128),
                         [f"g{l}"], ["gb0", "gb1"])
            self.dma("sp", ktown[0][0][:], self.k_r[l][h * 256:h * 256 + 128, :], [f"k{l}"], ktown[0][1])
            self.dma("sp", self.v_own_s[:], self.v_r[l][:, h * 256:(h + 1) * 256].rearrange("(kb p) d -> p kb d", p=128),
                     [f"v{l}"], ["v_own0", "v_own1"])
            groups = []
            gidx = 0
            for tt in range(2):
                sl = slice(tt * 512, (tt + 1) * 512)
                for m in range(2):
                    f0 = h * 256 + m * 128
                    gp = gidx % 2
                    q1 = self.qb[0][:, gp * 512:(gp + 1) * 512]
                    qk = f"qb0_{gp}"
                    po = 4

                    def loads(f0=f0, sl=sl, q1=q1, qk=qk, first=(gidx == 0)):
                        self.dma("sp", q1, self.q_r[l][f0:f0 + 128, sl], [f"q{l}"], [qk])
                        if first:
                            late_loads()

                    tasks = []
                    for i_b in range(16):
                        kb = i_b % 8
                        if i_b < 8:
                            kap, kk, vap, vk, Tt, tk = ktown[m][0][:, kb * 128:(kb + 1) * 128], ktown[m][1], self.v_own_s, ["v_own0", "v_own1"], Town, "tabL"
                        else:
                            kap, kk, vap, vk, Tt, tk = ktoth[m][0][:, kb * 128:(kb + 1) * 128], ktoth[m][1], self.v_oth_s, ["v_oth0", "v_oth1"], Toth, "tabH"
                        off = tt * 512 - kb * 128 + 896

                        def s1(i, kap=kap, kk=kk, Tt=Tt, tk=tk, off=off, q1=q1, qk=qk):
                            b = i % 4
                            self.mm(self.ps[b][:], kap, q1, True, True, list(kk) + [qk], [f"ps{b}"])
                            self.stt(self.tmp[:, b, :], Tt[:, off:off + 512], slopes[h], self.ps[b][:],
                                     ALU.mult, ALU.add, [f"ps{b}", tk], [f"tmp{b}"])
                            self.act(self.pt[:, b, :], self.tmp[:, b, :], AF.Exp, [f"tmp{b}"], [f"pt{b}"], scale=SCALE)

                        def s2(i, vap=vap, vk=vk, kb=kb, i_b=i_b, po=po):
                            b = i % 4
                            for hf in range(2):
                                self.mm(self.ps[po + hf][:], vap[:, kb, hf * 128:(hf + 1) * 128], self.pt[:, b, :],
                                        i_b == 0, i_b == 15, list(vk) + [f"pt{b}"], [f"ps{po + hf}"])
                            self.mm(self.ps[po + 2][:], self.ones[:], self.pt[:, b, :], i_b == 0, i_b == 15,
                                    ["ones", f"pt{b}"], [f"ps{po + 2}"])
                        tasks.append((s1, s2))

                    def post_a(po=po):
                        self.recip_act(self.rz[:], self.ps[po + 2][:], [f"ps{po + 2}"], ["rz"])

                    def post(m=m, po=po, sl=sl, tt=tt):
                        if m == 0:
                            for hf in range(2):
                                self.tt("dve", self.on0[:, hf, :], self.ps[po + hf][:], self.rz[:], ALU.mult,
                                        [f"ps{po + hf}", "rz"], ["on0"])
                            return
                        for hf in range(2):
                            self.tt("dve", self.o_s[:, hf, :], self.ps[po + hf][:], self.rz[:], ALU.mult,
                                    [f"ps{po + hf}", "rz"], ["o_s"])
                        self.stt(self.on0[:], self.o_s[:], neglam, self.on0[:], ALU.mult, ALU.add,
                                 ["o_s", "on0", "neglam"], ["on0"])
                        sqb = self.sq[0]
                        self.act(sqb[:], self.on0[:], AF.Square, ["on0"], ["sq0"])
                        for hf in range(2):
                            self.mm(self.ps[7][:], self.ones[:], sqb[:, hf, :], hf == 0, hf == 1, ["ones", "sq0"], ["ps7"])
                        self.recip_act(self.rz[:], self.ps[7][:], ["ps7"], ["rz"], in_scale=1.0 / 256, in_bias=EPS, power=-0.5)
                        for hf in range(2):
                            self.stt(self.on0[:, hf, :], self.on0[:, hf, :], self.subln_s[:, hf:hf + 1], self.rz[:],
                                     ALU.mult, ALU.mult, ["on0", "subln_s", "rz"], ["on0"])
                        self.tt("pool", self.hT[:, 2 * h:2 * h + 2, sl], self.on0[:], self.gb[:, :, sl], ALU.mult,
                                ["on0", "gb0", "gb1"], [f"h{2 * h}_{tt}", f"h{2 * h + 1}_{tt}"])
                    groups.append(dict(loads=loads, tasks=tasks, post=[(0, post_a), (0, post)]))
                    gidx += 1
            self.run_pipeline(groups, 3)

    def phase_E(self, l):
        nc = self.nc
        ev = 0
        pend = None
        for s in range(D // SLABW):
            slab, skey = self.next_slab()
            for mi in range(SLABW // 128):
                m = s * (SLABW // 128) + mi
                ys = self.ystage[ev % 2]
                ysk = f"ystage{ev % 2}"
                for tt in range(2):
                    sl = slice(tt * 512, (tt + 1) * 512)
                    pi = (ev * 2 + tt) % 4
                    ps = self.ps[pi]
                    for kc in range(KC):
                        self.mm(ps[:], slab[:, kc, mi * 128:(mi + 1) * 128], self.hT[:, kc, sl],
                                kc == 0, kc == KC - 1, [skey, f"h{kc}_{tt}"], [f"ps{pi}"])
                    self.S.add("dve", (lambda o=ys[:, sl], i=ps[:]: nc.vector.tensor_copy(out=o, in_=i)),
                               [f"ps{pi}"], [ysk + f"_{tt}"])
                    sqb = self.sq[tt]
                    self.act(sqb[:, 0, :], ys[:, sl], AF.Square, [ysk + f"_{tt}"], [f"sq{tt}"])
                    if pend is not None:
                        pend()
                    pend = (lambda tt=tt, m=m, sqb=sqb: self.mm(self.ps[6 + tt][:], self.ones[:], sqb[:, 0, :], m == 0,
                                                               m == KC - 1, ["ones", f"sq{tt}"], [f"ps{6 + tt}"]))
                self.dma("sp", self.yT[m * 128:(m + 1) * 128, :], ys[:], [ysk + "_0", ysk + "_1"], ["yT"])
                ev += 1
        pend()
        for tt in range(2):
            r = self.rstd[tt]
            self.recip_act(r[:], self.ps[6 + tt][:], [f"ps{6 + tt}"], [f"rstd{tt}"], in_scale=1.0 / D, in_bias=EPS, power=-0.5)
        for kc in range(KC):
            ys = self.ystage[kc % 2]
            ysk = f"ystage{kc % 2}"
            self.dma("sp", ys[:], self.yT[kc * 128:(kc + 1) * 128, :], ["yT"], [ysk + "_0", ysk + "_1"])
            for tt in range(2):
                sl = slice(tt * 512, (tt + 1) * 512)
                self.stt(ys[:, sl], ys[:, sl], self.gpost[:, l * KC + kc:l * KC + kc + 1], self.rstd[tt][:],
                         ALU.mult, ALU.mult, [ysk + f"_{tt}", "gpost", f"rstd{tt}"], [ysk + f"_{tt}"])
                self.tt("dve", self.xT[:, kc, sl], self.xT[:, kc, sl], ys[:, sl], ALU.add,
                        [f"x{kc}_{tt}", ysk + f"_{tt}"], [f"x{kc}_{tt}"])
                self.act(self.hT[:, kc, sl], self.xT[:, kc, sl], AF.Copy, [f"x{kc}_{tt}"], [f"h{kc}_{tt}"])
        self.dma("pool", self.pTb[:], self.pT_d[l].ap().rearrange("(c p) n -> p c n", p=128), (), ["pTb"])
        ev = 0
        for s in range(D // SLABW):
            slab, skey = self.next_slab()
            pes = self.peslab[s % 2]
            pek = f"peslab{s % 2}"
            if s == 0:
                self.dma("pool", pes[:], self.pe_proj_d[l][:, 0:SLABW].rearrange("(c p) n -> p c n", p=128), (), [pek])
            if s + 1 < D // SLABW:
                s1_ = s + 1
                self.dma("pool", self.peslab[s1_ % 2][:],
                         self.pe_proj_d[l][:, s1_ * SLABW:(s1_ + 1) * SLABW].rearrange("(c p) n -> p c n", p=128),
                         (), [f"peslab{s1_ % 2}"])
            for mi in range(SLABW // 128):
                m = s * (SLABW // 128) + mi
                for tt in range(2):
                    sl = slice(tt * 512, (tt + 1) * 512)
                    pi = (ev * 2 + tt) % 4
                    ps = self.ps[pi]
                    pp = self.ps[4 + (ev * 2 + tt) % 2]
                    ppk = f"ps{4 + (ev * 2 + tt) % 2}"
                    for kc in range(KC):
                        self.mm(ps[:], slab[:, kc, mi * 128:(mi + 1) * 128], self.hT[:, kc, sl],
                                kc == 0, kc == KC - 1, [skey, f"h{kc}_{tt}"], [f"ps{pi}"])
                    for c in range(2):
                        self.mm(pp[:], pes[:, c, mi * 128:(mi + 1) * 128], self.pTb[:, c, sl], c == 0, c == 1,
                                [pek, "pTb"], [ppk])
                    gs = self.gate_st[(ev * 2 + tt) % 2]
                    gk = f"gate{(ev * 2 + tt) % 2}"
                    self.act(gs[:], ps[:], AF.Sigmoid, [f"ps{pi}"], [gk])
                    self.tt("dve", gs[:], gs[:], pp[:], ALU.mult, [gk, ppk], [gk])
                    self.tt("dve", self.xT[:, m, sl], self.xT[:, m, sl], gs[:], ALU.add,
                            [f"x{m}_{tt}", gk], [f"x{m}_{tt}"])
                ev += 1


def _const_tables():
    sl = alibi(16)
    r = np.arange(128)[:, None]
    c = np.arange(128)[None, :]
    tabA = np.zeros((4, 128, 3, 4, 128), np.float32)
    for kvh in range(4):
        for j in range(3):
            rel = c - r + (1 - j) * 128
            ok = np.abs(rel) <= 128
            for g in range(4):
                h = kvh * 4 + g
                tabA[kvh, :, j, g, :] = np.where(ok, -sl[h] * np.abs(rel), -BIG) / SCALE
    tabA = tabA.reshape(4, 128, 1536)
    kx = np.arange(64)[:, None]
    qx = np.arange(64)[None, :]
    cs = np.clip(qx - 8, 0, 48)
    okc = (kx >= cs) & (kx < cs + 16)
    cm = np.where(okc, 0.0, -BIG / SCALE).astype(np.float32)
    colmask = np.ascontiguousarray(np.broadcast_to(cm[None, :, :], (2, 64, 64)).reshape(128, 64))
    rowind = np.zeros((2, 128), np.float32)
    rowind[0, :64] = 1
    rowind[1, 64:] = 1
    u = np.arange(1920)[None, :]
    rr = np.arange(128)[:, None]
    Town = (-np.abs(u - 896 - rr) / SCALE).astype(np.float32)
    Toth = [(-np.abs(dl + u - 896 - rr) / SCALE).astype(np.float32) for dl in (-1024, 1024)]
    return tabA, colmask, rowind.astype(ml_dtypes.bfloat16), Town, Toth


def _rowmask(half):
    rm = np.full((2, 16, 8), -BIG, np.float32)
    for tt in range(2):
        if tt == 0:
            blks = [("own", kb) for kb in range(6)] + [("oth", kb) for kb in (6, 7)]
        else:
            blks = [("own", kb) for kb in range(2, 8)] + [("oth", kb) for kb in (0, 1)]
        for i, (who, kb) in enumerate(blks):
            base = half * 16 if who == "own" else (1 - half) * 16
            for a in range(2):
                ky = base + 2 * kb + a
                for e in range(8):
                    qy = half * 16 + tt * 8 + e
                    rs = min(max(qy - 4, 0), 24)
                    if rs <= ky < rs + 8:
                        rm[a, tt * 8 + i, e] = 0.0
    return rm.reshape(2, 128).astype(ml_dtypes.bfloat16)


def _core_inputs(c, inputs, x_state):
    b, half = c // 2, c % 2
    tabA, colmask, rowind, Town, Toth = _CONST
    ts = slice(half * T, (half + 1) * T)
    rpb = inputs["b_rpb"][0]
    rpr = np.zeros((16, 15, 127), np.float32)
    rpr[:, :, 48:79] = rpb[:, ::-1, ::-1]
    rpr = np.ascontiguousarray(np.broadcast_to(rpr.reshape(16, 1, 1905), (16, 64, 1905)))
    edge = np.zeros((128, 2), np.float32)
    edge[:, half] = -BIG
    m = {
        "x_in": x_state[c],
        "gpre": np.ascontiguousarray(inputs["norm_pre"].reshape(DEPTH, KC, 128).transpose(2, 0, 1).reshape(128, DEPTH * KC)),
        "gpost": np.ascontiguousarray(inputs["norm_post"].reshape(DEPTH, KC, 128).transpose(2, 0, 1).reshape(128, DEPTH * KC)),
    }
    for l in range(DEPTH):
        kind, j = l % 3, l // 3
        m[f"w_in{l}"] = inputs[("a_w_in", "b_w_in", "c_w_in")[kind]][j]
        m[f"pT{l}"] = np.ascontiguousarray(inputs["p"][l, b, ts, :].T)
        m[f"w_out{l}"] = inputs["w_out"][l]
        m[f"pe_proj{l}"] = inputs["pe_proj"][l]
        m[f"pe_gate{l}"] = inputs["pe_gate"][l]
    m.update({
        "a_sink_bc": np.ascontiguousarray(np.broadcast_to(inputs["a_sink"].reshape(1, 32), (128, 32))),
        "tabA": tabA, "edgeA": edge, "rpr": rpr, "colmaskB": colmask,
        "rowmaskB": _rowmask(half), "rowindB": rowind,
        "c_lambda_bc": np.ascontiguousarray(np.broadcast_to(inputs["c_lambda"][0].reshape(1, 512), (128, 512))),
        "c_subln_t": np.ascontiguousarray(inputs["c_subln"][0].reshape(2, 128).T),
        "TownC": Town, "TothC": Toth[half],
    })
    return m


_CONST = _const_tables()
_PROG_CACHE = {}


def _get_prog(phases, fused):
    key = (tuple(phases), fused)
    if key not in _PROG_CACHE:
        p = Prog(list(phases), fused)
        p.build()
        _PROG_CACHE[key] = p
    return _PROG_CACHE[key]


SPLIT_LAUNCHES = [[("P", 0)], [("AE", 0), ("P", 1)], [("AE", 1), ("P", 2)], [("AE", 2), ("P", 3)], [("AE", 3)]]
FUSED_LAUNCH = [("P", 0), ("AE", 0), ("P", 1), ("AE", 1), ("P", 2), ("AE", 2), ("P", 3), ("AE", 3)]
MODE = "fused"


def run_launch(phases, fused, inputs, x_state, extra):
    prog = _get_prog(phases, fused)
    in_maps = []
    for c in range(NCORES):
        m = _core_inputs(c, inputs, x_state)
        m.update(extra[c])
        in_maps.append({k: v for k, v in m.items() if k in prog.ext_in})
    res = run_bass_kernel_spmd(prog.nc, in_maps, core_ids=list(range(NCORES)))
    return [{k: np.asarray(r[k]) for k in prog.ext_out} for r in res.results]


def kernel(**inputs):
    inputs = {k: np.asarray(v) for k, v in inputs.items()}
    x = inputs["x"]
    x_state = [np.ascontiguousarray(x[c // 2, (c % 2) * T:(c % 2 + 1) * T, :].T) for c in range(NCORES)]
    if MODE == "fused":
        outs = run_launch(FUSED_LAUNCH, True, inputs, x_state, [{} for _ in range(NCORES)])
        x_state = [o["x_out"] for o in outs]
    else:
        extra = [{} for _ in range(NCORES)]
        for phases in SPLIT_LAUNCHES:
            outs = run_launch(phases, False, inputs, x_state, extra)
            x_state = [o["x_out"] for o in outs]
            extra = [{} for _ in range(NCORES)]
            for ph, l in phases:
                if ph == "P":
                    for c in range(NCORES):
                        o, oo = outs[c], outs[c ^ 1]
                        extra[c] = {f"q{l}_i": o[f"q{l}_o"], f"g{l}_i": o[f"g{l}_o"],
                                    f"k{l}_i": o[f"k{l}_o"], f"v{l}_i": o[f"v{l}_o"],
                                    f"ko{l}_i": oo[f"k{l}_o"], f"vo{l}_i": oo[f"v{l}_o"]}
    out = np.empty((4, 2048, D), np.float32)
    for c in range(NCORES):
        out[c // 2, (c % 2) * T:(c % 2 + 1) * T, :] = x_state[c].T
    return out
```

```python
import math
import numpy as np
import ml_dtypes
import concourse.bass as bass
import concourse.mybir as mybir
from concourse.bass_utils import run_bass_kernel_spmd

F32 = mybir.dt.float32
BF16 = mybir.dt.bfloat16
ALU = mybir.AluOpType
AF = mybir.ActivationFunctionType
AX = mybir.AxisListType

D = 2048
T = 1024
KC = 16
DEPTH = 4
SCALE = 128.0 ** -0.5
EPS = 1e-6
BIG = 30000.0
SLABW = 256
LAGS_A = (1, 2, 3)
N_DMA_SEMS = 12
NCORES = 8

IN_W = {0: (2048, 512, 512, 2048), 1: (2048, 2048, 2048, 2048), 2: (2048, 2048, 2048, 2048)}


def alibi(n):
    return [2.0 ** (-8.0 * (h + 1) / n) for h in range(n)]


class Op:
    __slots__ = ("eng", "fn", "deps", "is_dma", "signal", "ticket", "idx", "inc")

    def __init__(self, eng, fn, is_dma):
        self.eng = eng
        self.fn = fn
        self.deps = []
        self.is_dma = is_dma
        self.signal = False
        self.ticket = None
        self.idx = -1
        self.inc = 16


class Sched:
    def __init__(self, nc, sync_same_engine=True):
        self.nc = nc
        self.ops = []
        self.last_writer = {}
        self.readers = {}
        self.sync_same_engine = sync_same_engine
        self.engs = {"pe": nc.tensor, "act": nc.scalar, "dve": nc.vector,
                     "pool": nc.gpsimd, "sp": nc.sync}

    def add(self, eng, fn, reads=(), writes=(), dma=False, inc=16):
        op = Op(eng, fn, dma)
        op.inc = inc
        op.idx = len(self.ops)
        deps = {}
        for k in reads:
            w = self.last_writer.get(k)
            if w is not None:
                deps[w.idx] = w
            if isinstance(k, str) and k.startswith("ps"):
                for r in self.readers.get(k, ()):
                    if r.eng != eng:
                        deps[r.idx] = r
        for k in writes:
            w = self.last_writer.get(k)
            if w is not None:
                deps[w.idx] = w
            for r in self.readers.get(k, ()):
                deps[r.idx] = r
        op.deps = list(deps.values())
        for k in writes:
            self.last_writer[k] = op
            self.readers[k] = []
        for k in reads:
            lst = self.readers.setdefault(k, [])
            if not dma:
                lst[:] = [r for r in lst if r.is_dma or r.eng != eng]
            lst.append(op)
        self.ops.append(op)
        return op

    def _skip(self, d, op):
        return (d.eng == op.eng and not op.is_dma and not d.is_dma
                and (d.eng == "pe" or not self.sync_same_engine))

    def emit(self):
        nc = self.nc
        for op in self.ops:
            for d in op.deps:
                if d.is_dma or self._skip(d, op):
                    continue
                d.signal = True
        sems = {e: nc.alloc_semaphore(name=f"s_{e}") for e in self.engs}
        dma_sems = {q: [nc.alloc_semaphore(name=f"d_{q}_{i}") for i in range(N_DMA_SEMS)]
                    for q in ("sp", "pool", "act")}
        dma_cnt = {q: [0] * N_DMA_SEMS for q in dma_sems}
        dma_rr = {q: 0 for q in dma_sems}
        cnt = {e: 0 for e in self.engs}
        waited = {e: {} for e in self.engs}

        def wait(eng, sem, val):
            key = id(sem)
            if waited[eng].get(key, 0) >= val:
                return
            waited[eng][key] = val
            self.engs[eng].wait_ge(sem, val)

        for op in self.ops:
            e = op.eng
            for d in op.deps:
                if d.is_dma:
                    wait(e, d.ticket[0], d.ticket[1])
                elif not self._skip(d, op):
                    wait(e, d.ticket[0], d.ticket[1])
            if op.is_dma:
                i = dma_rr[e]
                dma_rr[e] = (i + 1) % N_DMA_SEMS
                sem = dma_sems[e][i]
                if dma_cnt[e][i] > 0:
                    wait(e, sem, dma_cnt[e][i])
                ins = op.fn()
                dma_cnt[e][i] += op.inc
                ins.then_inc(sem, op.inc)
                op.ticket = (sem, dma_cnt[e][i])
            else:
                ins = op.fn()
                if op.signal:
                    cnt[e] += 1
                    ins.then_inc(sems[e], 1)
                    op.ticket = (sems[e], cnt[e])
        for q in dma_sems:
            for i in range(N_DMA_SEMS):
                if dma_cnt[q][i] > 0:
                    wait("sp", dma_sems[q][i], dma_cnt[q][i])
        for e2 in self.engs:
            if cnt[e2] > 0:
                wait("sp", sems[e2], cnt[e2])
        return len(self.ops)


class Prog:
    def __init__(self, phases, fused, dbg=False):
        self.phases = phases
        self.fused = fused
        self.nc = nc = bass.Bass("TRN2", target_bir_lowering=False)
        self.S = Sched(nc)
        self.ext_in = []
        self.ext_out = []
        self.dram = {}
        self._alloc()

    def din(self, name, shape, dt=F32):
        t = self.nc.dram_tensor(name, list(shape), dt, kind="ExternalInput")
        self.dram[name] = t
        self.ext_in.append(name)
        return t

    def dout(self, name, shape, dt=F32):
        t = self.nc.dram_tensor(name, list(shape), dt, kind="ExternalOutput")
        self.dram[name] = t
        self.ext_out.append(name)
        return t

    def dint(self, name, shape, dt=F32):
        t = self.nc.dram_tensor(name, list(shape), dt)
        self.dram[name] = t
        return t

    def sb(self, name, shape, dt):
        return self.nc.alloc_sbuf_tensor(name, list(shape), dt)

    def _alloc(self):
        nc = self.nc
        layers = sorted({l for _, l in self.phases})
        self.layers = layers
        kinds = {l % 3 for l in layers}
        self.x_in = self.din("x_in", [D, T])
        self.x_out = self.dout("x_out", [D, T])
        self.gpre_d = self.din("gpre", [128, DEPTH * KC])
        self.gpost_d = self.din("gpost", [128, DEPTH * KC])
        self.pT_d, self.w_out_d, self.pe_proj_d, self.pe_gate_d, self.w_in_d = {}, {}, {}, {}, {}
        for ph, l in self.phases:
            if ph == "P":
                self.w_in_d[l] = self.din(f"w_in{l}", [D, sum(IN_W[l % 3])])
            else:
                self.pT_d[l] = self.din(f"pT{l}", [256, T])
                self.w_out_d[l] = self.din(f"w_out{l}", [D, D])
                self.pe_proj_d[l] = self.din(f"pe_proj{l}", [256, D])
                self.pe_gate_d[l] = self.din(f"pe_gate{l}", [D, D])
        self.sink_d = self.din("a_sink_bc", [128, 32])
        self.tabA_d = self.din("tabA", [4, 128, 1536])
        self.edge_d = self.din("edgeA", [128, 2])
        self.rpr_d = self.din("rpr", [16, 64, 15 * 127])
        self.colmask_d = self.din("colmaskB", [128, 64])
        self.rowmask_d = self.din("rowmaskB", [2, 128], BF16)
        self.rowind_d = self.din("rowindB", [2, 128], BF16)
        self.lam_d = self.din("c_lambda_bc", [128, 512])
        self.subln_d = self.din("c_subln_t", [128, 2])
        self.Town_d = self.din("TR0C", [128, 1920])
        self.Toth_d = self.din("TR1C", [128, 1920])
        self.q_w, self.q_r, self.g_w, self.g_r = {}, {}, {}, {}
        self.k_w, self.k_r, self.v_w, self.v_r = {}, {}, {}, {}
        self.k_oth, self.v_oth, self.k_all, self.v_all = {}, {}, {}, {}
        for l in layers:
            _, dk, dv, _ = IN_W[l % 3]
            hasP = ("P", l) in self.phases
            hasA = ("AE", l) in self.phases
            if hasP and hasA:
                assert self.fused
                self.q_w[l] = self.q_r[l] = self.dint(f"q{l}", [D, T], BF16)
                self.g_w[l] = self.g_r[l] = self.dint(f"g{l}", [D, T], BF16)
                self.k_w[l] = self.k_r[l] = self.dint(f"k{l}", [dk, T], BF16)
                self.v_w[l] = self.v_r[l] = self.dint(f"v{l}", [T, dv], BF16)
                self.k_all[l] = self.dint(f"kall{l}", [2 * dk, T], BF16)
                self.v_all[l] = self.dint(f"vall{l}", [2 * T, dv], BF16)
            else:
                if hasP:
                    self.q_w[l] = self.dout(f"q{l}_o", [D, T], BF16)
                    self.g_w[l] = self.dout(f"g{l}_o", [D, T], BF16)
                    self.k_w[l] = self.dout(f"k{l}_o", [dk, T], BF16)
                    self.v_w[l] = self.dout(f"v{l}_o", [T, dv], BF16)
                if hasA:
                    self.q_r[l] = self.din(f"q{l}_i", [D, T], BF16)
                    self.g_r[l] = self.din(f"g{l}_i", [D, T], BF16)
                    self.k_r[l] = self.din(f"k{l}_i", [dk, T], BF16)
                    self.v_r[l] = self.din(f"v{l}_i", [T, dv], BF16)
                    self.k_all[l] = self.din(f"kall{l}_i", [2 * dk, T], BF16)
                    self.v_all[l] = self.din(f"vall{l}_i", [2 * T, dv], BF16)
        self.yT = self.dint("yT", [D, T], F32)
        self.xT = self.sb("xT", [128, KC, T], F32)
        self.hT = self.sb("hT", [128, KC, T], BF16)
        self.wslab = [self.sb(f"wslab{i}", [128, KC, SLABW], BF16) for i in range(2)]
        self.ones = self.sb("ones", [128, 128], BF16)
        self.zeros = self.sb("zeros", [128, 16], F32)
        self.zeros128 = self.sb("zeros128", [128, 128], F32)
        self.esk = self.sb("esk", [128, 512], F32)
        self.gpre = self.sb("gpre_s", [128, DEPTH * KC], F32)
        self.gpost = self.sb("gpost_s", [128, DEPTH * KC], F32)
        self.sq = [self.sb(f"sq{i}", [128, 2, 512], BF16) for i in range(2)]
        self.rstd = [self.sb(f"rstd{i}", [128, 512], F32) for i in range(2)]
        self.stage = [self.sb(f"stage{i}", [128, T], BF16) for i in range(2)]
        self.ystage = [self.sb(f"ystage{i}", [128, T], F32) for i in range(2)]
        self.gate_st = [self.sb(f"gate{i}", [128, 512], F32) for i in range(2)]
        self.peslab = [self.sb(f"peslab{i}", [128, 2, SLABW], BF16) for i in range(2)]
        self.pTb = self.sb("pTb", [128, 2, T], BF16)
        self.kt_own = self.sb("kt_own", [128, T], BF16)
        self.kt_oth = self.sb("kt_oth", [128, T], BF16)
        self.qb = [self.sb(f"qb{i}", [128, T], BF16) for i in range(2)]
        self.gb = self.sb("gb", [128, 2, T], BF16)
        self.v_own_s = self.sb("v_own", [128, 8, 256], BF16)
        self.v_oth_s = self.sb("v_oth", [128, 8, 256], BF16)
        self.tab = self.sb("tab", [128, 3840], F32)
        self.tmp = self.sb("tmp", [128, 4, 512], F32)
        self.pt = self.sb("pt", [128, 4, 512], BF16)
        self.on0 = self.sb("on0", [128, 2, 512], F32)
        self.o_s = self.sb("o_s", [128, 2, 512], F32)
        self.rz = self.sb("rz", [128, 512], F32)
        self.zt = self.sb("zt", [128, 512], F32)
        self.es = self.sb("es", [128, 32], F32)
        self.sink_s = self.sb("sink_s", [128, 32], F32)
        self.edge = self.sb("edge", [128, 2], F32)
        self.lam_s = self.sb("lam_s", [128, 512], F32)
        self.lam_t = self.sb("lam_t", [128, 8], F32)
        self.subln_s = self.sb("subln_s", [128, 2], F32)
        self.rowmask = self.sb("rowmask", [2, 128], BF16)
        self.rowind = self.sb("rowind", [2, 128], BF16)
        self.cm64 = self.sb("cm64", [128, 64], F32)
        self.ps = [nc.alloc_psum_tensor(f"ps{i}", [128, 512], F32) for i in range(8)]
        self.wi = 0
        self.par = None

    def dma(self, q, out, in_, reads, writes):
        eng = {"sp": self.nc.sync, "pool": self.nc.gpsimd, "act": self.nc.scalar}[q]
        self.S.add(q, lambda: eng.dma_start(out=out, in_=in_), reads, writes, dma=True)

    def act(self, out, in_, func, reads, writes, bias=0.0, scale=1.0):
        nc = self.nc
        self.S.add("act", lambda: nc.scalar.activation(out=out, in_=in_, func=func, bias=bias, scale=scale),
                   reads, writes)

    def tt(self, eng, out, in0, in1, op, reads, writes):
        e = {"dve": self.nc.vector, "pool": self.nc.gpsimd}[eng]
        self.S.add(eng, lambda: e.tensor_tensor(out=out, in0=in0, in1=in1, op=op), reads, writes)

    def stt(self, out, in0, scalar, in1, op0, op1, reads, writes, eng="dve"):
        e = {"dve": self.nc.vector, "pool": self.nc.gpsimd}[eng]
        self.S.add(eng, lambda: e.scalar_tensor_tensor(out=out, in0=in0, scalar=scalar, in1=in1,
                                                        op0=op0, op1=op1), reads, writes)

    def ts(self, out, in0, s1, s2, op0, op1, reads, writes):
        nc = self.nc
        if s2 is None:
            self.S.add("dve", lambda: nc.vector.tensor_scalar(out=out, in0=in0, scalar1=s1, scalar2=None,
                                                              op0=op0), reads, writes)
        else:
            self.S.add("dve", lambda: nc.vector.tensor_scalar(out=out, in0=in0, scalar1=s1, scalar2=s2,
                                                              op0=op0, op1=op1), reads, writes)

    def recip(self, out, in_, reads, writes):
        nc = self.nc
        self.S.add("dve", lambda: nc.vector.reciprocal(out=out, in_=in_), reads, writes)

    def mm(self, out, lhsT, rhs, start, stop, reads, writes):
        nc = self.nc
        self.S.add("pe", lambda: nc.tensor.matmul(out, lhsT, rhs, start=start, stop=stop), reads, writes)

    def memset(self, ap, val, writes):
        nc = self.nc
        self.S.add("dve", lambda: nc.vector.memset(ap, val), (), writes)

    @staticmethod
    def slab_order(kind):
        wq, wk, wv, wg = IN_W[kind]
        b = [0, wq // SLABW, (wq + wk) // SLABW, (wq + wk + wv) // SLABW, (wq + wk + wv + wg) // SLABW]
        q, k, v, g = (list(range(b[i], b[i + 1])) for i in range(4))
        return k + v + q + g

    def plan_slabs(self):
        slabs = []
        for ph, l in self.phases:
            if ph == "P":
                for s in self.slab_order(l % 3):
                    slabs.append((self.w_in_d[l], s))
            elif ph == "AE":
                for s in range(D // SLABW):
                    slabs.append((self.w_out_d[l], s))
                for s in range(D // SLABW):
                    slabs.append((self.pe_gate_d[l], s))
        self.slabs = slabs
        self.slab_issued = 0
        self.slab_used = 0

    def _issue_slab(self):
        i = self.slab_issued
        if i >= len(self.slabs):
            return
        w, s = self.slabs[i]
        src = w[:, s * SLABW:(s + 1) * SLABW].rearrange("(kc p) n -> p kc n", p=128)
        self.dma("pool", self.wslab[i % 2][:], src, (), [f"wslab{i % 2}"])
        self.slab_issued += 1

    def next_slab(self):
        i = self.slab_used
        while self.slab_issued <= min(i + 1, len(self.slabs) - 1):
            self._issue_slab()
        self.slab_used += 1
        return self.wslab[i % 2], f"wslab{i % 2}"

    def build(self):
        nc = self.nc
        self.plan_slabs()
        self.memset(self.ones[:], 1.0, ["ones"])
        self.memset(self.zeros[:], 0.0, ["zeros"])
        self.memset(self.zeros128[:], 0.0, ["zeros128"])
        self.dma("sp", self.gpre[:], self.gpre_d.ap(), (), ["gpre"])
        self.dma("sp", self.gpost[:], self.gpost_d.ap(), (), ["gpost"])
        xv = self.x_in.ap().rearrange("(kc p) n -> p kc n", p=128)
        for kc in range(KC):
            self.dma("sp", self.xT[:, kc, :], xv[:, kc, :], (), [f"x{kc}_0", f"x{kc}_1"])
        for ph, l in self.phases:
            if ph == "P":
                self.phase_P(l)
            else:
                kind = l % 3
                if kind == 0:
                    self.attn_A(l)
                elif kind == 1:
                    self.attn_B(l)
                else:
                    self.attn_C(l)
                self.phase_E(l)
        ov = self.x_out.ap().rearrange("(kc p) n -> p kc n", p=128)
        for kc in range(KC):
            self.dma("sp", ov[:, kc, :], self.xT[:, kc, :], [f"x{kc}_0", f"x{kc}_1"], [])
        n = self.S.emit()
        return n

    def stats_rstd(self, tt, src_fn, src_keys, nchunks, inv_n, rstd_i):
        ps = self.ps[7]
        for kc in range(nchunks):
            sqb = self.sq[kc % 2]
            self.act(sqb[:, 0, :], src_fn(kc), AF.Square, src_keys(kc), [f"sq{kc % 2}"])
            self.mm(ps[:], self.ones[:], sqb[:, 0, :], kc == 0, kc == nchunks - 1,
                    ["ones", f"sq{kc % 2}"], ["ps7"])
        r = self.rstd[rstd_i]
        self.recip_act(r[:], ps[:], ["ps7"], [f"rstd{rstd_i}"], in_scale=inv_n, in_bias=EPS, power=-0.5)

    def phase_P(self, l):
        kind = l % 3
        wq, wk, wv, wg = IN_W[kind]
        for tt in range(2):
            sl = slice(tt * 512, (tt + 1) * 512)
            self.stats_rstd(tt, lambda kc: self.xT[:, kc, sl], lambda kc: [f"x{kc}_{tt}"], KC, 1.0 / D, tt)
            for kc in range(KC):
                self.stt(self.hT[:, kc, sl], self.xT[:, kc, sl], self.gpre[:, l * KC + kc:l * KC + kc + 1],
                         self.rstd[tt][:], ALU.mult, ALU.mult,
                         [f"x{kc}_{tt}", "gpre", f"rstd{tt}"], [f"h{kc}_{tt}"])
        order = self.slab_order(kind)
        last_k = (wq + wk) // SLABW - 1
        last_v = (wq + wk + wv) // SLABW - 1
        ev = 0
        for s in order:
            c0 = s * SLABW
            slab, skey = self.next_slab()
            if c0 < wq:
                seg, dst, r0 = "q", self.q_w[l], c0
            elif c0 < wq + wk:
                seg, dst, r0 = "k", self.k_w[l], c0 - wq
            elif c0 < wq + wk + wv:
                seg, dst, r0 = "v", self.v_w[l], c0 - wq - wk
            else:
                seg, dst, r0 = "g", self.g_w[l], c0 - wq - wk - wv
            if seg != "v":
                for mi in range(SLABW // 128):
                    st = self.stage[ev % 2]
                    stk = f"stage{ev % 2}"
                    for tt in range(2):
                        sl = slice(tt * 512, (tt + 1) * 512)
                        pi = (ev * 2 + tt) % 4
                        ps = self.ps[pi]
                        for kc in range(KC):
                            self.mm(ps[:], slab[:, kc, mi * 128:(mi + 1) * 128], self.hT[:, kc, sl],
                                    kc == 0, kc == KC - 1, [skey, f"h{kc}_{tt}"], [f"ps{pi}"])
                        if seg == "g":
                            self.act(st[:, sl], ps[:], AF.Silu, [f"ps{pi}"], [stk + f"_{tt}"])
                        elif (ev + tt) % 2 == 0:
                            self.act(st[:, sl], ps[:], AF.Copy, [f"ps{pi}"], [stk + f"_{tt}"])
                        else:
                            nc = self.nc
                            self.S.add("dve", (lambda o=st[:, sl], i=ps[:]: nc.vector.tensor_copy(out=o, in_=i)),
                                       [f"ps{pi}"], [stk + f"_{tt}"])
                    row = r0 + mi * 128
                    self.dma("sp", dst[row:row + 128, :], st[:], [stk + "_0", stk + "_1"], [f"{seg}{l}"])
                    ev += 1
            else:
                for tb in range(8):
                    st = self.stage[ev % 2]
                    stk = f"stage{ev % 2}"
                    pi = ev % 4
                    ps = self.ps[pi]
                    for kc in range(KC):
                        self.mm(ps[:, 0:SLABW], self.hT[:, kc, tb * 128:(tb + 1) * 128], slab[:, kc, :],
                                kc == 0, kc == KC - 1, [skey, f"h{kc}_{tb // 4}"], [f"ps{pi}"])
                    nc = self.nc
                    self.S.add("dve", (lambda o=st[:, 0:SLABW], i=ps[:, 0:SLABW]: nc.vector.tensor_copy(out=o, in_=i)),
                               [f"ps{pi}"], [stk + "_0", stk + "_1"])
                    self.dma("sp", dst[tb * 128:(tb + 1) * 128, r0:r0 + SLABW], st[:, 0:SLABW],
                             [stk + "_0", stk + "_1"], [f"v{l}"])
                    ev += 1
            if self.fused and s == last_k:
                self.exchange_gather(l, "k")
            if self.fused and s == last_v:
                self.exchange_gather(l, "v")

    def exchange_gather(self, l, nm):
        nc = self.nc
        groups = [[2 * i, 2 * i + 1] for i in range(NCORES // 2)]
        src, allt = (self.k_w[l], self.k_all[l]) if nm == "k" else (self.v_w[l], self.v_all[l])
        rows, cols = src.shape
        rc = min(rows, (2 << 20) // (cols * 2))
        for c in range(rows // rc):
            self.S.add("pool", (lambda s=src[c * rc:(c + 1) * rc, :], d=allt[c * 2 * rc:(c + 1) * 2 * rc, :]:
                                nc.gpsimd.collective_compute(
                "AllGather", mybir.AluOpType.bypass, replica_groups=groups,
                ins=[s.opt()], outs=[d.opt()])), [f"{nm}{l}"], [f"{nm}all{l}"], dma=True, inc=1)

    def gk(self, l, R, f0, c0, c1):
        dk = IN_W[l % 3][1]
        rc = min(dk, (2 << 20) // (T * 2))
        row = ((f0 // rc) * 2 + R) * rc + f0 % rc
        return self.k_all[l][row:row + 128, c0:c1], [f"kall{l}"] if self.fused else []

    def gv(self, l, R, kb0, nkb, d0, dn):
        dv = IN_W[l % 3][2]
        rc = min(T, (2 << 20) // (dv * 2))
        t0 = kb0 * 128
        assert t0 % rc + nkb * 128 <= rc
        row = ((t0 // rc) * 2 + R) * rc + t0 % rc
        return (self.v_all[l][row:row + nkb * 128, d0:d0 + dn].rearrange("(kb p) d -> p kb d", p=128),
                [f"vall{l}"] if self.fused else [])

    def run_pipeline(self, groups, LA, NBUF=2):
        flat = []
        for gi, g in enumerate(groups):
            nt = len(g["tasks"])
            for ti, (s1, s2) in enumerate(g["tasks"]):
                flat.append((gi, s1, s2, ti == nt - 1))
        for gi in range(min(NBUF, len(groups))):
            if groups[gi]["loads"]:
                groups[gi]["loads"]()
        n = len(flat)
        pending = []
        for i in range(n + LA):
            if i < n:
                flat[i][1](i)
            j = i - LA
            if j >= 0:
                gj, _, s2, last = flat[j]
                s2(j)
                if last:
                    stages = groups[gj]["post"] or []
                    for k, (lag, fn) in enumerate(stages):
                        pending.append((j + lag, fn, gj if k == len(stages) - 1 else None))
                still = []
                for due, fn, gdone in pending:
                    if due <= j:
                        fn()
                        if gdone is not None and gdone + NBUF < len(groups) and groups[gdone + NBUF]["loads"]:
                            groups[gdone + NBUF]["loads"]()
                    else:
                        still.append((due, fn, gdone))
                pending = still
        for due, fn, gdone in pending:
            fn()

    def recip_act(self, out, in_, rk, wk, in_scale=1.0, in_bias=0.0, power=-1.0):
        self.act(out, in_, AF.Ln, rk, wk, bias=in_bias, scale=in_scale)
        self.act(out, out, AF.Exp, wk, wk, scale=power)

    def attn_A(self, l):
        j = l // 3
        self.dma("sp", self.sink_s[:], self.sink_d.ap(), (), ["sink_s"])
        self.dma("sp", self.edge[:], self.edge_d.ap(), (), ["edge"])
        tabv = self.tab[:, 0:1536].rearrange("p (j n) -> p j n", j=3)
        for kvh in range(4):
            f0 = kvh * 128
            self.dma("sp", self.kt_own[:], self.k_r[l][f0:f0 + 128, :], [f"k{l}"], ["kt_own"])
            ap, rk = self.gk(l, 0, f0, 896, 1024)
            self.dma("sp", self.kt_oth[:, 0:128], ap, rk, ["kt_oth"])
            ap, rk = self.gk(l, 1, f0, 0, 128)
            self.dma("sp", self.kt_oth[:, 128:256], ap, rk, ["kt_oth"])
            self.dma("sp", self.v_own_s[:, :, 0:128],
                     self.v_r[l][:, f0:f0 + 128].rearrange("(kb p) d -> p kb d", p=128), [f"v{l}"], ["v_own0"])
            ap, rk = self.gv(l, 0, 7, 1, f0, 128)
            self.dma("sp", self.v_oth_s[:, 0:1, 0:128], ap, rk, ["v_oth0"])
            ap, rk = self.gv(l, 1, 0, 1, f0, 128)
            self.dma("sp", self.v_oth_s[:, 1:2, 0:128], ap, rk, ["v_oth0"])
            self.dma("sp", self.tab[:, 0:1536], self.tabA_d[kvh], (), ["tabL"])
            for g in range(4):
                c = j * 16 + kvh * 4 + g
                self.act(self.esk[:, g * 128:(g + 1) * 128], self.zeros128[:], AF.Exp, ["zeros128", "sink_s"], ["esk"],
                         bias=self.sink_s[:, c:c + 1], scale=1.0)
            groups = []
            for qi in range(8):
                par = qi % 2
                slot = qi % 3
                qs = slice(qi * 128, (qi + 1) * 128)
                q4 = (self.qb[0][:, 0:512], self.qb[0][:, 512:1024], self.qb[1][:, 0:512])[slot]
                g4 = (self.gb[:, 0, 0:512], self.gb[:, 0, 512:1024], self.gb[:, 1, 0:512])[slot]
                qk, gk = ("qb0_0", "qb0_1", "qb1")[slot], ("gb0", "gb0", "gb1")[slot]
                po, pz = (4, 5) if par == 0 else (6, 7)

                def loads(qs=qs, q4=q4, g4=g4, qk=qk, gk=gk, kvh=kvh):
                    qsrc = self.q_r[l][kvh * 512:(kvh + 1) * 512, qs].rearrange("(g p) n -> p g n", p=128)
                    self.dma("sp", q4.rearrange("p (g n) -> p g n", g=4), qsrc, [f"q{l}"], [qk])
                    gsrc = self.g_r[l][kvh * 512:(kvh + 1) * 512, qs].rearrange("(g p) n -> p g n", p=128)
                    self.dma("sp", g4.rearrange("p (g n) -> p g n", g=4), gsrc, [f"g{l}"], [gk])

                tasks = []
                for jj in range(3):
                    kb = qi + jj - 1
                    if kb < 0:
                        blk = (self.kt_oth[:, 0:128], "kt_oth", self.v_oth_s[:, 0, 0:128], "v_oth0", 0)
                    elif kb > 7:
                        blk = (self.kt_oth[:, 128:256], "kt_oth", self.v_oth_s[:, 1, 0:128], "v_oth0", 1)
                    else:
                        blk = (self.kt_own[:, kb * 128:(kb + 1) * 128], "kt_own", self.v_own_s[:, kb, 0:128], "v_own0", None)

                    def s1(i, blk=blk, jj=jj, q4=q4, qk=qk):
                        kap, kk, vap, vk, edge = blk
                        b = i % 3
                        self.mm(self.ps[b][:], kap, q4, True, True, [kk, qk], [f"ps{b}"])
                        self.tt("dve", self.tmp[:, b, :], self.ps[b][:], tabv[:, jj, :], ALU.add,
                                [f"ps{b}", "tabL"], [f"tmp{b}"])
                        if edge is None:
                            self.act(self.pt[:, b, :], self.tmp[:, b, :], AF.Exp, [f"tmp{b}"], [f"pt{b}"], scale=SCALE)
                        else:
                            self.act(self.pt[:, b, :], self.tmp[:, b, :], AF.Exp, [f"tmp{b}", "edge"], [f"pt{b}"],
                                     bias=self.edge[:, edge:edge + 1], scale=SCALE)

                    def s2(i, blk=blk, jj=jj, po=po, pz=pz):
                        kap, kk, vap, vk, edge = blk
                        b = i % 3
                        self.mm(self.ps[po][:], vap, self.pt[:, b, :], jj == 0, jj == 2, [vk, f"pt{b}"], [f"ps{po}"])
                        self.mm(self.ps[pz][:], self.ones[:], self.pt[:, b, :], jj == 0, jj == 2,
                                ["ones", f"pt{b}"], [f"ps{pz}"])
                    tasks.append((s1, s2))

                def post_a(pz=pz):
                    self.tt("dve", self.zt[:], self.ps[pz][:], self.esk[:], ALU.add, [f"ps{pz}", "esk"], ["zt"])

                def post_b():
                    self.recip_act(self.rz[:], self.zt[:], ["zt"], ["rz"])

                def post_c(qs=qs, g4=g4, gk=gk, po=po, kvh=kvh, qi=qi):
                    self.tt("dve", self.o_s[:, 0, :], self.ps[po][:], self.rz[:], ALU.mult, [f"ps{po}", "rz"], ["o_s"])
                    okeys = [f"h{kvh * 4 + g}_{qi // 4}" for g in range(4)]
                    self.tt("pool", self.hT[:, kvh * 4:(kvh + 1) * 4, qs],
                            self.o_s[:, 0, :].rearrange("p (g n) -> p g n", g=4),
                            g4.rearrange("p (g n) -> p g n", g=4), ALU.mult, ["o_s", gk], okeys)
                groups.append(dict(loads=loads, tasks=tasks, post=[(LAGS_A[0], post_a), (LAGS_A[1], post_b), (LAGS_A[2], post_c)]))
            self.run_pipeline(groups, 2, NBUF=3)

    def attn_B(self, l):
        self.dma("sp", self.rowmask[:], self.rowmask_d.ap(), (), ["rowmask"])
        self.dma("sp", self.rowind[:], self.rowind_d.ap(), (), ["rowind"])
        self.dma("sp", self.cm64[:], self.colmask_d.ap(), (), ["cm64"])
        self.memset(self.tab[:, 0:1920], 0.0, ["tabL"])
        self.memset(self.tab[:, 1920:3840], 0.0, ["tabH"])
        ktown = [(self.kt_own, ["kt_own"]), (self.qb[1], ["qb1"])]
        ktoth = [(self.kt_oth, ["kt_oth"]), (self.stage[0], ["stage0_0", "stage0_1"])]
        qbuf = [(self.qb[0], ["qb0_0", "qb0_1"]), (self.stage[1], ["stage1_0", "stage1_1"])]
        groups = []
        gidx = 0
        for h in range(16):
            hp = h % 2
            f0 = h * 128
            kto, ktok = ktown[hp]
            ktt, kttk = ktoth[hp]
            qq, qqk = qbuf[hp]
            gg, ggk = self.gb[:, hp, :], f"gb{hp}"
            vo, vok = self.v_own_s[:, :, hp * 128:(hp + 1) * 128], f"v_own{hp}"
            vt, vtk = self.v_oth_s[:, :, hp * 128:(hp + 1) * 128], f"v_oth{hp}"
            tkey = "tabL" if hp == 0 else "tabH"
            TB = self.tab[:, hp * 1920:(hp + 1) * 1920].rearrange("p (j n) -> p j n", j=30)

            def loads(h=h, f0=f0, kto=kto, ktok=ktok, ktt=ktt, kttk=kttk, qq=qq, qqk=qqk, gg=gg, ggk=ggk,
                      vo=vo, vok=vok, vt=vt, vtk=vtk, tkey=tkey, TB=TB):
                self.dma("sp", kto[:], self.k_r[l][f0:f0 + 128, :], [f"k{l}"], ktok)
                ap, rk = self.gk(l, 1, f0, 0, 256)
                self.dma("sp", ktt[:, 0:256], ap, rk, kttk)
                ap, rk = self.gk(l, 0, f0, 768, 1024)
                self.dma("sp", ktt[:, 768:1024], ap, rk, kttk)
                self.dma("sp", vo, self.v_r[l][:, f0:f0 + 128].rearrange("(kb p) d -> p kb d", p=128), [f"v{l}"], [vok])
                ap, rk = self.gv(l, 1, 0, 2, f0, 128)
                self.dma("sp", vt[:, 0:2, :], ap, rk, [vtk])
                ap, rk = self.gv(l, 0, 6, 2, f0, 128)
                self.dma("sp", vt[:, 6:8, :], ap, rk, [vtk])
                self.dma("sp", qq[:], self.q_r[l][f0:f0 + 128, :], [f"q{l}"], qqk)
                self.dma("sp", gg, self.g_r[l][f0:f0 + 128, :], [f"g{l}"], [ggk])
                for a in range(2):
                    src = bass.AP(self.rpr_d, h * 64 * 1905 + 63, [[1904, 64], [127, 15], [1, 64]])
                    self.dma("sp", TB[a * 64:(a + 1) * 64, a + 7:a + 22, :], src, (), [tkey])
                for a in range(2):
                    cmb = bass.AP(self.cm64, a * 64 * 64, [[64, 64], [0, 15], [1, 64]])
                    self.stt(TB[a * 64:(a + 1) * 64, a + 7:a + 22, :], TB[a * 64:(a + 1) * 64, a + 7:a + 22, :],
                             1.0 / SCALE, cmb, ALU.mult, ALU.add, [tkey, "cm64"], [tkey])

            for tt in range(2):
                sl = slice(tt * 512, (tt + 1) * 512)
                if tt == 0:
                    blks = [("own", kb, 2 * kb + 7) for kb in range(6)] + [("oth", kb, 2 * kb - 9) for kb in (6, 7)]
                else:
                    blks = [("own", kb, 2 * kb - 1) for kb in range(2, 8)] + [("oth", kb, 15 + 2 * kb) for kb in (0, 1)]
                nb = len(blks)
                po, pz = (4, 5) if gidx % 2 == 0 else (6, 7)
                tasks = []
                for i_b, (who, kb, d0) in enumerate(blks):
                    pidx = tt * 8 + i_b
                    if who == "own":
                        kap, kk, vap, vk = kto[:, kb * 128:(kb + 1) * 128], ktok, vo[:, kb, :], vok
                    else:
                        kap, kk, vap, vk = ktt[:, kb * 128:(kb + 1) * 128], kttk, vt[:, kb, :], vtk

                    def s1(i, kap=kap, kk=kk, pidx=pidx, d0=d0, qq=qq, qqk=qqk, sl=sl, TB=TB, tkey=tkey):
                        b = i % 4
                        self.mm(self.ps[b][:], kap, qq[:, sl], True, False, list(kk) + list(qqk), [f"ps{b}"])
                        self.mm(self.ps[b][:], self.rowind[:],
                                bass.AP(self.rowmask, pidx * 8, [[128, 2], [1, 8], [0, 64]]), False, True,
                                ["rowind", "rowmask"], [f"ps{b}"])
                        j0 = 21 - d0
                        self.tt("dve", self.tmp[:, b, :], self.ps[b][:],
                                TB[:, j0:j0 + 8, :].rearrange("p j n -> p (j n)"), ALU.add,
                                [f"ps{b}", tkey], [f"tmp{b}"])
                        self.act(self.pt[:, b, :], self.tmp[:, b, :], AF.Exp, [f"tmp{b}"], [f"pt{b}"], scale=SCALE)

                    def s2(i, vap=vap, vk=vk, i_b=i_b, nb=nb, po=po, pz=pz):
                        b = i % 4
                        self.mm(self.ps[po][:], vap, self.pt[:, b, :], i_b == 0, i_b == nb - 1, [vk, f"pt{b}"], [f"ps{po}"])
                        self.mm(self.ps[pz][:], self.ones[:], self.pt[:, b, :], i_b == 0, i_b == nb - 1,
                                ["ones", f"pt{b}"], [f"ps{pz}"])
                    tasks.append((s1, s2))

                def post_a(pz=pz):
                    self.recip_act(self.rz[:], self.ps[pz][:], [f"ps{pz}"], ["rz"])

                def post_b(h=h, sl=sl, tt=tt, gg=gg, ggk=ggk, po=po):
                    self.tt("dve", self.o_s[:, 0, :], self.ps[po][:], self.rz[:], ALU.mult, [f"ps{po}", "rz"], ["o_s"])
                    self.tt("pool", self.hT[:, h, sl], self.o_s[:, 0, :], gg[:, sl], ALU.mult,
                            ["o_s", ggk], [f"h{h}_{tt}"])
                groups.append(dict(loads=loads if tt == 0 else None, tasks=tasks, post=[(1, post_a), (3, post_b)]))
                gidx += 1
        self.run_pipeline(groups, 3)

    def attn_C(self, l):
        nc = self.nc
        lam_init = 0.8 - 0.6 * math.exp(-0.3 * l)
        slopes = alibi(8)
        self.dma("sp", self.lam_s[:], self.lam_d.ap(), (), ["lam_s"])
        self.dma("sp", self.subln_s[:], self.subln_d.ap(), (), ["subln_s"])
        self.dma("sp", self.tab[:, 0:1920], self.Town_d.ap(), (), ["tabL"])
        self.dma("sp", self.tab[:, 1920:3840], self.Toth_d.ap(), (), ["tabH"])
        lt = self.lam_t
        for i in range(2):
            self.tt("dve", self.zt[:, 0:128], self.lam_s[:, (2 * i) * 128:(2 * i + 1) * 128],
                    self.lam_s[:, (2 * i + 1) * 128:(2 * i + 2) * 128], ALU.mult, ["lam_s"], ["zt"])
            self.S.add("dve", (lambda i=i: nc.vector.reduce_sum(out=lt[:, i:i + 1], in_=self.zt[:, 0:128], axis=AX.X)),
                       ["zt"], ["lam_t"])
        self.act(lt[:, 2:4], lt[:, 0:2], AF.Exp, ["lam_t"], ["lam_t"])
        self.tt("dve", lt[:, 4:5], lt[:, 3:4], lt[:, 2:3], ALU.subtract, ["lam_t"], ["lam_t"])
        self.ts(lt[:, 5:6], lt[:, 4:5], -lam_init, None, ALU.add, None, ["lam_t"], ["neglam"])
        self.ts(self.subln_s[:], self.subln_s[:], 1.0 - lam_init, None, ALU.mult, None, ["subln_s"], ["subln_s"])
        neglam = lt[:, 5:6]
        Town = self.tab[:, 0:1920]
        Toth = self.tab[:, 1920:3840]
        ktown = [(self.kt_own, ["kt_own"]), (self.qb[1], ["qb1"])]
        ktoth = [(self.kt_oth, ["kt_oth"]), (self.stage[0], ["stage0_0", "stage0_1"])]
        for h in range(8):
            def late_loads(h=h):
                ap, rk = self.gk(l, 1, h * 256, 0, T)
                self.dma("sp", ktoth[0][0][:], ap, rk, ktoth[0][1])
                for c4 in range(2):
                    ap, rk = self.gv(l, 1, 4 * c4, 4, h * 256, 256)
                    self.dma("sp", self.v_oth_s[:, 4 * c4:4 * c4 + 4, :], ap, rk, ["v_oth0", "v_oth1"])
                f1 = h * 256 + 128
                ap, rk = self.gk(l, 0, f1, 0, T)
                self.dma("sp", ktown[1][0][:], ap, rk, ktown[1][1])
                ap, rk = self.gk(l, 1, f1, 0, T)
                self.dma("sp", ktoth[1][0][:], ap, rk, ktoth[1][1])
                self.dma("sp", self.gb[:], self.g_r[l][h * 256:(h + 1) * 256, :].rearrange("(c p) n -> p c n", p=128),
                         [f"g{l}"], ["gb0", "gb1"])
            ap, rk = self.gk(l, 0, h * 256, 0, T)
            self.dma("sp", ktown[0][0][:], ap, rk, ktown[0][1])
            for c4 in range(2):
                ap, rk = self.gv(l, 0, 4 * c4, 4, h * 256, 256)
                self.dma("sp", self.v_own_s[:, 4 * c4:4 * c4 + 4, :], ap, rk, ["v_own0", "v_own1"])
            groups = []
            gidx = 0
            for tt in range(2):
                sl = slice(tt * 512, (tt + 1) * 512)
                for m in range(2):
                    f0 = h * 256 + m * 128
                    gp = gidx % 2
                    q1 = self.qb[0][:, gp * 512:(gp + 1) * 512]
                    qk = f"qb0_{gp}"
                    po = 4

                    def loads(f0=f0, sl=sl, q1=q1, qk=qk, first=(gidx == 0)):
                        self.dma("sp", q1, self.q_r[l][f0:f0 + 128, sl], [f"q{l}"], [qk])
                        if first:
                            late_loads()

                    tasks = []
                    for i_b in range(16):
                        kb = i_b % 8
                        if i_b < 8:
                            kap, kk, vap, vk, Tt, tk = ktown[m][0][:, kb * 128:(kb + 1) * 128], ktown[m][1], self.v_own_s, ["v_own0", "v_own1"], Town, "tabL"
                        else:
                            kap, kk, vap, vk, Tt, tk = ktoth[m][0][:, kb * 128:(kb + 1) * 128], ktoth[m][1], self.v_oth_s, ["v_oth0", "v_oth1"], Toth, "tabH"
                        off = tt * 512 - kb * 128 + 896

                        def s1(i, kap=kap, kk=kk, Tt=Tt, tk=tk, off=off, q1=q1, qk=qk):
                            b = i % 4
                            self.mm(self.ps[b][:], kap, q1, True, True, list(kk) + [qk], [f"ps{b}"])
                            self.stt(self.tmp[:, b, :], Tt[:, off:off + 512], slopes[h], self.ps[b][:],
                                     ALU.mult, ALU.add, [f"ps{b}", tk], [f"tmp{b}"])
                            self.act(self.pt[:, b, :], self.tmp[:, b, :], AF.Exp, [f"tmp{b}"], [f"pt{b}"], scale=SCALE)

                        def s2(i, vap=vap, vk=vk, kb=kb, i_b=i_b, po=po):
                            b = i % 4
                            for hf in range(2):
                                self.mm(self.ps[po + hf][:], vap[:, kb, hf * 128:(hf + 1) * 128], self.pt[:, b, :],
                                        i_b == 0, i_b == 15, list(vk) + [f"pt{b}"], [f"ps{po + hf}"])
                            self.mm(self.ps[po + 2][:], self.ones[:], self.pt[:, b, :], i_b == 0, i_b == 15,
                                    ["ones", f"pt{b}"], [f"ps{po + 2}"])
                        tasks.append((s1, s2))

                    def post_a(po=po):
                        self.recip_act(self.rz[:], self.ps[po + 2][:], [f"ps{po + 2}"], ["rz"])

                    def post(m=m, po=po, sl=sl, tt=tt):
                        if m == 0:
                            for hf in range(2):
                                self.tt("dve", self.on0[:, hf, :], self.ps[po + hf][:], self.rz[:], ALU.mult,
                                        [f"ps{po + hf}", "rz"], ["on0"])
                            return
                        for hf in range(2):
                            self.tt("dve", self.o_s[:, hf, :], self.ps[po + hf][:], self.rz[:], ALU.mult,
                                    [f"ps{po + hf}", "rz"], ["o_s"])
                        self.stt(self.on0[:], self.o_s[:], neglam, self.on0[:], ALU.mult, ALU.add,
                                 ["o_s", "on0", "neglam"], ["on0"])
                        sqb = self.sq[0]
                        self.act(sqb[:], self.on0[:], AF.Square, ["on0"], ["sq0"])
                        for hf in range(2):
                            self.mm(self.ps[7][:], self.ones[:], sqb[:, hf, :], hf == 0, hf == 1, ["ones", "sq0"], ["ps7"])
                        self.recip_act(self.rz[:], self.ps[7][:], ["ps7"], ["rz"], in_scale=1.0 / 256, in_bias=EPS, power=-0.5)
                        for hf in range(2):
                            self.stt(self.on0[:, hf, :], self.on0[:, hf, :], self.subln_s[:, hf:hf + 1], self.rz[:],
                                     ALU.mult, ALU.mult, ["on0", "subln_s", "rz"], ["on0"])
                        self.tt("pool", self.hT[:, 2 * h:2 * h + 2, sl], self.on0[:], self.gb[:, :, sl], ALU.mult,
                                ["on0", "gb0", "gb1"], [f"h{2 * h}_{tt}", f"h{2 * h + 1}_{tt}"])
                    groups.append(dict(loads=loads, tasks=tasks, post=[(0, post_a), (0, post)]))
                    gidx += 1
            self.run_pipeline(groups, 3)

    def phase_E(self, l):
        nc = self.nc
        ev = 0
        pend = None
        for s in range(D // SLABW):
            slab, skey = self.next_slab()
            for mi in range(SLABW // 128):
                m = s * (SLABW // 128) + mi
                ys = self.ystage[ev % 2]
                ysk = f"ystage{ev % 2}"
                for tt in range(2):
                    sl = slice(tt * 512, (tt + 1) * 512)
                    pi = (ev * 2 + tt) % 4
                    ps = self.ps[pi]
                    for kc in range(KC):
                        self.mm(ps[:], slab[:, kc, mi * 128:(mi + 1) * 128], self.hT[:, kc, sl],
                                kc == 0, kc == KC - 1, [skey, f"h{kc}_{tt}"], [f"ps{pi}"])
                    self.S.add("dve", (lambda o=ys[:, sl], i=ps[:]: nc.vector.tensor_copy(out=o, in_=i)),
                               [f"ps{pi}"], [ysk + f"_{tt}"])
                    sqb = self.sq[tt]
                    self.act(sqb[:, 0, :], ys[:, sl], AF.Square, [ysk + f"_{tt}"], [f"sq{tt}"])
                    if pend is not None:
                        pend()
                    pend = (lambda tt=tt, m=m, sqb=sqb: self.mm(self.ps[6 + tt][:], self.ones[:], sqb[:, 0, :], m == 0,
                                                               m == KC - 1, ["ones", f"sq{tt}"], [f"ps{6 + tt}"]))
                self.dma("sp", self.yT[m * 128:(m + 1) * 128, :], ys[:], [ysk + "_0", ysk + "_1"], ["yT"])
                ev += 1
        pend()
        for tt in range(2):
            r = self.rstd[tt]
            self.recip_act(r[:], self.ps[6 + tt][:], [f"ps{6 + tt}"], [f"rstd{tt}"], in_scale=1.0 / D, in_bias=EPS, power=-0.5)
        for kc in range(KC):
            ys = self.ystage[kc % 2]
            ysk = f"ystage{kc % 2}"
            self.dma("sp", ys[:], self.yT[kc * 128:(kc + 1) * 128, :], ["yT"], [ysk + "_0", ysk + "_1"])
            for tt in range(2):
                sl = slice(tt * 512, (tt + 1) * 512)
                self.stt(ys[:, sl], ys[:, sl], self.gpost[:, l * KC + kc:l * KC + kc + 1], self.rstd[tt][:],
                         ALU.mult, ALU.mult, [ysk + f"_{tt}", "gpost", f"rstd{tt}"], [ysk + f"_{tt}"])
                self.tt("dve", self.xT[:, kc, sl], self.xT[:, kc, sl], ys[:, sl], ALU.add,
                        [f"x{kc}_{tt}", ysk + f"_{tt}"], [f"x{kc}_{tt}"])
                self.act(self.hT[:, kc, sl], self.xT[:, kc, sl], AF.Copy, [f"x{kc}_{tt}"], [f"h{kc}_{tt}"])
        self.dma("pool", self.pTb[:], self.pT_d[l].ap().rearrange("(c p) n -> p c n", p=128), (), ["pTb"])
        ev = 0
        for s in range(D // SLABW):
            slab, skey = self.next_slab()
            pes = self.peslab[s % 2]
            pek = f"peslab{s % 2}"
            if s == 0:
                self.dma("pool", pes[:], self.pe_proj_d[l][:, 0:SLABW].rearrange("(c p) n -> p c n", p=128), (), [pek])
            if s + 1 < D // SLABW:
                s1_ = s + 1
                self.dma("pool", self.peslab[s1_ % 2][:],
                         self.pe_proj_d[l][:, s1_ * SLABW:(s1_ + 1) * SLABW].rearrange("(c p) n -> p c n", p=128),
                         (), [f"peslab{s1_ % 2}"])
            for mi in range(SLABW // 128):
                m = s * (SLABW // 128) + mi
                for tt in range(2):
                    sl = slice(tt * 512, (tt + 1) * 512)
                    pi = (ev * 2 + tt) % 4
                    ps = self.ps[pi]
                    pp = self.ps[4 + (ev * 2 + tt) % 2]
                    ppk = f"ps{4 + (ev * 2 + tt) % 2}"
                    for kc in range(KC):
                        self.mm(ps[:], slab[:, kc, mi * 128:(mi + 1) * 128], self.hT[:, kc, sl],
                                kc == 0, kc == KC - 1, [skey, f"h{kc}_{tt}"], [f"ps{pi}"])
                    for c in range(2):
                        self.mm(pp[:], pes[:, c, mi * 128:(mi + 1) * 128], self.pTb[:, c, sl], c == 0, c == 1,
                                [pek, "pTb"], [ppk])
                    gs = self.gate_st[(ev * 2 + tt) % 2]
                    gk = f"gate{(ev * 2 + tt) % 2}"
                    self.act(gs[:], ps[:], AF.Sigmoid, [f"ps{pi}"], [gk])
                    self.tt("dve", gs[:], gs[:], pp[:], ALU.mult, [gk, ppk], [gk])
                    self.tt("dve", self.xT[:, m, sl], self.xT[:, m, sl], gs[:], ALU.add,
                            [f"x{m}_{tt}", gk], [f"x{m}_{tt}"])
                ev += 1


def _const_tables():
    sl = alibi(16)
    r = np.arange(128)[:, None]
    c = np.arange(128)[None, :]
    tabA = np.zeros((4, 128, 3, 4, 128), np.float32)
    for kvh in range(4):
        for j in range(3):
            rel = c - r + (1 - j) * 128
            ok = np.abs(rel) <= 128
            for g in range(4):
                h = kvh * 4 + g
                tabA[kvh, :, j, g, :] = np.where(ok, -sl[h] * np.abs(rel), -BIG) / SCALE
    tabA = tabA.reshape(4, 128, 1536)
    kx = np.arange(64)[:, None]
    qx = np.arange(64)[None, :]
    cs = np.clip(qx - 8, 0, 48)
    okc = (kx >= cs) & (kx < cs + 16)
    cm = np.where(okc, 0.0, -BIG / SCALE).astype(np.float32)
    colmask = np.ascontiguousarray(np.broadcast_to(cm[None, :, :], (2, 64, 64)).reshape(128, 64))
    rowind = np.zeros((2, 128), np.float32)
    rowind[0, :64] = 1
    rowind[1, 64:] = 1
    u = np.arange(1920)[None, :]
    rr = np.arange(128)[:, None]
    Town = (-np.abs(u - 896 - rr) / SCALE).astype(np.float32)
    Toth = [(-np.abs(dl + u - 896 - rr) / SCALE).astype(np.float32) for dl in (-1024, 1024)]
    return tabA, colmask, rowind.astype(ml_dtypes.bfloat16), Town, Toth


def _rowmask(half):
    rm = np.full((2, 16, 8), -BIG, np.float32)
    for tt in range(2):
        if tt == 0:
            blks = [("own", kb) for kb in range(6)] + [("oth", kb) for kb in (6, 7)]
        else:
            blks = [("own", kb) for kb in range(2, 8)] + [("oth", kb) for kb in (0, 1)]
        for i, (who, kb) in enumerate(blks):
            base = half * 16 if who == "own" else (1 - half) * 16
            for a in range(2):
                ky = base + 2 * kb + a
                for e in range(8):
                    qy = half * 16 + tt * 8 + e
                    rs = min(max(qy - 4, 0), 24)
                    if rs <= ky < rs + 8:
                        rm[a, tt * 8 + i, e] = 0.0
    return rm.reshape(2, 128).astype(ml_dtypes.bfloat16)


def _core_inputs(c, inputs, x_state):
    b, half = c // 2, c % 2
    tabA, colmask, rowind, Town, Toth = _CONST
    ts = slice(half * T, (half + 1) * T)
    rpb = inputs["b_rpb"][0]
    rpr = np.zeros((16, 15, 127), np.float32)
    rpr[:, :, 48:79] = rpb[:, ::-1, ::-1]
    rpr = np.ascontiguousarray(np.broadcast_to(rpr.reshape(16, 1, 1905), (16, 64, 1905)))
    edge = np.zeros((128, 2), np.float32)
    edge[:, half] = -BIG
    m = {
        "x_in": x_state[c],
        "gpre": np.ascontiguousarray(inputs["norm_pre"].reshape(DEPTH, KC, 128).transpose(2, 0, 1).reshape(128, DEPTH * KC)),
        "gpost": np.ascontiguousarray(inputs["norm_post"].reshape(DEPTH, KC, 128).transpose(2, 0, 1).reshape(128, DEPTH * KC)),
    }
    for l in range(DEPTH):
        kind, j = l % 3, l // 3
        m[f"w_in{l}"] = inputs[("a_w_in", "b_w_in", "c_w_in")[kind]][j]
        m[f"pT{l}"] = np.ascontiguousarray(inputs["p"][l, b, ts, :].T)
        m[f"w_out{l}"] = inputs["w_out"][l]
        m[f"pe_proj{l}"] = inputs["pe_proj"][l]
        m[f"pe_gate{l}"] = inputs["pe_gate"][l]
    m.update({
        "a_sink_bc": np.ascontiguousarray(np.broadcast_to(inputs["a_sink"].reshape(1, 32), (128, 32))),
        "tabA": tabA, "edgeA": edge, "rpr": rpr, "colmaskB": colmask,
        "rowmaskB": _rowmask(half), "rowindB": rowind,
        "c_lambda_bc": np.ascontiguousarray(np.broadcast_to(inputs["c_lambda"][0].reshape(1, 512), (128, 512))),
        "c_subln_t": np.ascontiguousarray(inputs["c_subln"][0].reshape(2, 128).T),
        "TR0C": Town if half == 0 else Toth[1], "TR1C": Toth[0] if half == 0 else Town,
    })
    return m


_CONST = _const_tables()
_PROG_CACHE = {}


def _get_prog(phases, fused):
    key = (tuple(phases), fused)
    if key not in _PROG_CACHE:
        p = Prog(list(phases), fused)
        p.build()
        _PROG_CACHE[key] = p
    return _PROG_CACHE[key]


SPLIT_LAUNCHES = [[("P", 0)], [("AE", 0), ("P", 1)], [("AE", 1), ("P", 2)], [("AE", 2), ("P", 3)], [("AE", 3)]]
FUSED_LAUNCH = [("P", 0), ("AE", 0), ("P", 1), ("AE", 1), ("P", 2), ("AE", 2), ("P", 3), ("AE", 3)]
MODE = "fused"


def _gathered(a0, a1):
    rows, cols = a0.shape
    rc = min(rows, (2 << 20) // (cols * 2))
    n = rows // rc
    return np.ascontiguousarray(np.stack([a0.reshape(n, rc, cols), a1.reshape(n, rc, cols)], axis=1).reshape(2 * rows, cols))


def run_launch(phases, fused, inputs, x_state, extra):
    prog = _get_prog(phases, fused)
    in_maps = []
    for c in range(NCORES):
        m = _core_inputs(c, inputs, x_state)
        m.update(extra[c])
        in_maps.append({k: v for k, v in m.items() if k in prog.ext_in})
    res = run_bass_kernel_spmd(prog.nc, in_maps, core_ids=list(range(NCORES)))
    return [{k: np.asarray(r[k]) for k in prog.ext_out} for r in res.results]


def kernel(**inputs):
    inputs = {k: np.asarray(v) for k, v in inputs.items()}
    x = inputs["x"]
    x_state = [np.ascontiguousarray(x[c // 2, (c % 2) * T:(c % 2 + 1) * T, :].T) for c in range(NCORES)]
    if MODE == "fused":
        outs = run_launch(FUSED_LAUNCH, True, inputs, x_state, [{} for _ in range(NCORES)])
        x_state = [o["x_out"] for o in outs]
    else:
        extra = [{} for _ in range(NCORES)]
        for phases in SPLIT_LAUNCHES:
            outs = run_launch(phases, False, inputs, x_state, extra)
            x_state = [o["x_out"] for o in outs]
            extra = [{} for _ in range(NCORES)]
            for ph, l in phases:
                if ph == "P":
                    for c in range(NCORES):
                        o, o0, o1 = outs[c], outs[c & ~1], outs[c | 1]
                        extra[c] = {f"q{l}_i": o[f"q{l}_o"], f"g{l}_i": o[f"g{l}_o"],
                                    f"k{l}_i": o[f"k{l}_o"], f"v{l}_i": o[f"v{l}_o"],
                                    f"kall{l}_i": _gathered(o0[f"k{l}_o"], o1[f"k{l}_o"]),
                                    f"vall{l}_i": _gathered(o0[f"v{l}_o"], o1[f"v{l}_o"])}
    out = np.empty((4, 2048, D), np.float32)
    for c in range(NCORES):
        out[c // 2, (c % 2) * T:(c % 2 + 1) * T, :] = x_state[c].T
    return out
```

```python
import math
import numpy as np
import ml_dtypes
import concourse.bass as bass
import concourse.mybir as mybir
from concourse.bass_utils import run_bass_kernel_spmd

F32 = mybir.dt.float32
BF16 = mybir.dt.bfloat16
ALU = mybir.AluOpType
AF = mybir.ActivationFunctionType
AX = mybir.AxisListType

D = 2048
T = 1024
KC = 16
DEPTH = 4
SCALE = 128.0 ** -0.5
EPS = 1e-6
BIG = 30000.0
SLABW = 256
LAGS_A = (1, 2, 3)
N_DMA_SEMS = 12
NCORES = 8

IN_W = {0: (2048, 512, 512, 2048), 1: (2048, 2048, 2048, 2048), 2: (2048, 2048, 2048, 2048)}


def alibi(n):
    return [2.0 ** (-8.0 * (h + 1) / n) for h in range(n)]


class Op:
    __slots__ = ("eng", "fn", "deps", "is_dma", "signal", "ticket", "idx", "inc")

    def __init__(self, eng, fn, is_dma):
        self.eng = eng
        self.fn = fn
        self.deps = []
        self.is_dma = is_dma
        self.signal = False
        self.ticket = None
        self.idx = -1
        self.inc = 16


class Sched:
    def __init__(self, nc, sync_same_engine=True):
        self.nc = nc
        self.ops = []
        self.last_writer = {}
        self.readers = {}
        self.sync_same_engine = sync_same_engine
        self.engs = {"pe": nc.tensor, "act": nc.scalar, "dve": nc.vector,
                     "pool": nc.gpsimd, "sp": nc.sync}

    def add(self, eng, fn, reads=(), writes=(), dma=False, inc=16):
        op = Op(eng, fn, dma)
        op.inc = inc
        op.idx = len(self.ops)
        deps = {}
        for k in reads:
            w = self.last_writer.get(k)
            if w is not None:
                deps[w.idx] = w
            if isinstance(k, str) and k.startswith("ps"):
                for r in self.readers.get(k, ()):
                    if r.eng != eng:
                        deps[r.idx] = r
        for k in writes:
            w = self.last_writer.get(k)
            if w is not None:
                deps[w.idx] = w
            for r in self.readers.get(k, ()):
                deps[r.idx] = r
        op.deps = list(deps.values())
        for k in writes:
            self.last_writer[k] = op
            self.readers[k] = []
        for k in reads:
            lst = self.readers.setdefault(k, [])
            if not dma:
                lst[:] = [r for r in lst if r.is_dma or r.eng != eng]
            lst.append(op)
        self.ops.append(op)
        return op

    def _skip(self, d, op):
        return (d.eng == op.eng and not op.is_dma and not d.is_dma
                and (d.eng == "pe" or not self.sync_same_engine))

    def emit(self):
        nc = self.nc
        for op in self.ops:
            for d in op.deps:
                if d.is_dma or self._skip(d, op):
                    continue
                d.signal = True
        sems = {e: nc.alloc_semaphore(name=f"s_{e}") for e in self.engs}
        dma_sems = {q: [nc.alloc_semaphore(name=f"d_{q}_{i}") for i in range(N_DMA_SEMS)]
                    for q in ("sp", "pool", "act")}
        dma_cnt = {q: [0] * N_DMA_SEMS for q in dma_sems}
        dma_rr = {q: 0 for q in dma_sems}
        cnt = {e: 0 for e in self.engs}
        waited = {e: {} for e in self.engs}

        def wait(eng, sem, val):
            key = id(sem)
            if waited[eng].get(key, 0) >= val:
                return
            waited[eng][key] = val
            self.engs[eng].wait_ge(sem, val)

        for op in self.ops:
            e = op.eng
            for d in op.deps:
                if d.is_dma:
                    wait(e, d.ticket[0], d.ticket[1])
                elif not self._skip(d, op):
                    wait(e, d.ticket[0], d.ticket[1])
            if op.is_dma:
                i = dma_rr[e]
                dma_rr[e] = (i + 1) % N_DMA_SEMS
                sem = dma_sems[e][i]
                if dma_cnt[e][i] > 0:
                    wait(e, sem, dma_cnt[e][i])
                ins = op.fn()
                dma_cnt[e][i] += op.inc
                ins.then_inc(sem, op.inc)
                op.ticket = (sem, dma_cnt[e][i])
            else:
                ins = op.fn()
                if op.signal:
                    cnt[e] += 1
                    ins.then_inc(sems[e], 1)
                    op.ticket = (sems[e], cnt[e])
        for q in dma_sems:
            for i in range(N_DMA_SEMS):
                if dma_cnt[q][i] > 0:
                    wait("sp", dma_sems[q][i], dma_cnt[q][i])
        for e2 in self.engs:
            if cnt[e2] > 0:
                wait("sp", sems[e2], cnt[e2])
        return len(self.ops)


class Prog:
    def __init__(self, phases, fused, dbg=False):
        self.phases = phases
        self.fused = fused
        self.nc = nc = bass.Bass("TRN2", target_bir_lowering=False)
        self.S = Sched(nc)
        self.ext_in = []
        self.ext_out = []
        self.dram = {}
        self._alloc()

    def din(self, name, shape, dt=F32):
        t = self.nc.dram_tensor(name, list(shape), dt, kind="ExternalInput")
        self.dram[name] = t
        self.ext_in.append(name)
        return t

    def dout(self, name, shape, dt=F32):
        t = self.nc.dram_tensor(name, list(shape), dt, kind="ExternalOutput")
        self.dram[name] = t
        self.ext_out.append(name)
        return t

    def dint(self, name, shape, dt=F32):
        t = self.nc.dram_tensor(name, list(shape), dt)
        self.dram[name] = t
        return t

    def sb(self, name, shape, dt):
        return self.nc.alloc_sbuf_tensor(name, list(shape), dt)

    def _alloc(self):
        nc = self.nc
        layers = sorted({l for _, l in self.phases})
        self.layers = layers
        kinds = {l % 3 for l in layers}
        self.x_in = self.din("x_in", [D, T])
        self.x_out = self.dout("x_out", [D, T])
        self.gpre_d = self.din("gpre", [128, DEPTH * KC])
        self.gpost_d = self.din("gpost", [128, DEPTH * KC])
        self.pT_d, self.w_out_d, self.pe_proj_d, self.pe_gate_d, self.w_in_d = {}, {}, {}, {}, {}
        for ph, l in self.phases:
            if ph == "P":
                self.w_in_d[l] = self.din(f"w_in{l}", [D, sum(IN_W[l % 3])])
            else:
                self.pT_d[l] = self.din(f"pT{l}", [256, T])
                self.w_out_d[l] = self.din(f"w_out{l}", [D, D])
                self.pe_proj_d[l] = self.din(f"pe_proj{l}", [256, D])
                self.pe_gate_d[l] = self.din(f"pe_gate{l}", [D, D])
        self.sink_d = self.din("a_sink_bc", [128, 32])
        self.tabA_d = self.din("tabA", [4, 128, 1536])
        self.edge_d = self.din("edgeA", [128, 2])
        self.rpr_d = self.din("rpr", [16, 64, 15 * 127])
        self.colmask_d = self.din("colmaskB", [128, 64])
        self.rowmask_d = self.din("rowmaskB", [2, 128], BF16)
        self.rowind_d = self.din("rowindB", [2, 128], BF16)
        self.lam_d = self.din("c_lambda_bc", [128, 512])
        self.subln_d = self.din("c_subln_t", [128, 2])
        self.Town_d = self.din("TR0C", [128, 1920])
        self.Toth_d = self.din("TR1C", [128, 1920])
        self.q_w, self.q_r, self.g_w, self.g_r = {}, {}, {}, {}
        self.k_w, self.k_r, self.v_w, self.v_r = {}, {}, {}, {}
        self.k_oth, self.v_oth, self.k_all, self.v_all = {}, {}, {}, {}
        for l in layers:
            _, dk, dv, _ = IN_W[l % 3]
            hasP = ("P", l) in self.phases
            hasA = ("AE", l) in self.phases
            if hasP and hasA:
                assert self.fused
                self.q_w[l] = self.q_r[l] = self.dint(f"q{l}", [D, T], BF16)
                self.g_w[l] = self.g_r[l] = self.dint(f"g{l}", [D, T], BF16)
                self.k_w[l] = self.k_r[l] = self.dint(f"k{l}", [dk, T], BF16)
                self.v_w[l] = self.v_r[l] = self.dint(f"v{l}", [T, dv], BF16)
                self.k_all[l] = self.dint(f"kall{l}", [2 * dk, T], BF16)
                self.v_all[l] = self.dint(f"vall{l}", [2 * T, dv], BF16)
            else:
                if hasP:
                    self.q_w[l] = self.dout(f"q{l}_o", [D, T], BF16)
                    self.g_w[l] = self.dout(f"g{l}_o", [D, T], BF16)
                    self.k_w[l] = self.dout(f"k{l}_o", [dk, T], BF16)
                    self.v_w[l] = self.dout(f"v{l}_o", [T, dv], BF16)
                if hasA:
                    self.q_r[l] = self.din(f"q{l}_i", [D, T], BF16)
                    self.g_r[l] = self.din(f"g{l}_i", [D, T], BF16)
                    self.k_r[l] = self.din(f"k{l}_i", [dk, T], BF16)
                    self.v_r[l] = self.din(f"v{l}_i", [T, dv], BF16)
                    self.k_all[l] = self.din(f"kall{l}_i", [2 * dk, T], BF16)
                    self.v_all[l] = self.din(f"vall{l}_i", [2 * T, dv], BF16)
        self.yT = self.dint("yT", [D, T], F32)
        self.xT = self.sb("xT", [128, KC, T], F32)
        self.hT = self.sb("hT", [128, KC, T], BF16)
        self.wslab = [self.sb(f"wslab{i}", [128, KC, SLABW], BF16) for i in range(2)]
        self.ones = self.sb("ones", [128, 128], BF16)
        self.zeros = self.sb("zeros", [128, 16], F32)
        self.zeros128 = self.sb("zeros128", [128, 128], F32)
        self.esk = self.sb("esk", [128, 512], F32)
        self.gpre = self.sb("gpre_s", [128, DEPTH * KC], F32)
        self.gpost = self.sb("gpost_s", [128, DEPTH * KC], F32)
        self.sq = [self.sb(f"sq{i}", [128, 2, 512], BF16) for i in range(2)]
        self.rstd = [self.sb(f"rstd{i}", [128, 512], F32) for i in range(2)]
        self.stage = [self.sb(f"stage{i}", [128, T], BF16) for i in range(2)]
        self.ystage = [self.sb(f"ystage{i}", [128, T], F32) for i in range(2)]
        self.gate_st = [self.sb(f"gate{i}", [128, 512], F32) for i in range(2)]
        self.peslab = [self.sb(f"peslab{i}", [128, 2, SLABW], BF16) for i in range(2)]
        self.pTb = self.sb("pTb", [128, 2, T], BF16)
        self.kt_own = self.sb("kt_own", [128, T], BF16)
        self.kt_oth = self.sb("kt_oth", [128, T], BF16)
        self.qb = [self.sb(f"qb{i}", [128, T], BF16) for i in range(2)]
        self.gb = self.sb("gb", [128, 2, T], BF16)
        self.v_own_s = self.sb("v_own", [128, 8, 256], BF16)
        self.v_oth_s = self.sb("v_oth", [128, 8, 256], BF16)
        self.tab = self.sb("tab", [128, 3840], F32)
        self.tmp = self.sb("tmp", [128, 4, 512], F32)
        self.pt = self.sb("pt", [128, 4, 512], BF16)
        self.on0 = self.sb("on0", [128, 2, 512], F32)
        self.o_s = self.sb("o_s", [128, 2, 512], F32)
        self.rz = self.sb("rz", [128, 512], F32)
        self.zt = self.sb("zt", [128, 512], F32)
        self.es = self.sb("es", [128, 32], F32)
        self.sink_s = self.sb("sink_s", [128, 32], F32)
        self.edge = self.sb("edge", [128, 2], F32)
        self.lam_s = self.sb("lam_s", [128, 512], F32)
        self.lam_t = self.sb("lam_t", [128, 8], F32)
        self.subln_s = self.sb("subln_s", [128, 2], F32)
        self.rowmask = self.sb("rowmask", [2, 128], BF16)
        self.rowind = self.sb("rowind", [2, 128], BF16)
        self.cm64 = self.sb("cm64", [128, 64], F32)
        self.ps = [nc.alloc_psum_tensor(f"ps{i}", [128, 512], F32) for i in range(8)]
        self.wi = 0
        self.par = None

    def dma(self, q, out, in_, reads, writes):
        eng = {"sp": self.nc.sync, "pool": self.nc.gpsimd, "act": self.nc.scalar}[q]
        self.S.add(q, lambda: eng.dma_start(out=out, in_=in_), reads, writes, dma=True)

    def act(self, out, in_, func, reads, writes, bias=0.0, scale=1.0):
        nc = self.nc
        self.S.add("act", lambda: nc.scalar.activation(out=out, in_=in_, func=func, bias=bias, scale=scale),
                   reads, writes)

    def tt(self, eng, out, in0, in1, op, reads, writes):
        e = {"dve": self.nc.vector, "pool": self.nc.gpsimd}[eng]
        self.S.add(eng, lambda: e.tensor_tensor(out=out, in0=in0, in1=in1, op=op), reads, writes)

    def stt(self, out, in0, scalar, in1, op0, op1, reads, writes, eng="dve"):
        e = {"dve": self.nc.vector, "pool": self.nc.gpsimd}[eng]
        self.S.add(eng, lambda: e.scalar_tensor_tensor(out=out, in0=in0, scalar=scalar, in1=in1,
                                                        op0=op0, op1=op1), reads, writes)

    def ts(self, out, in0, s1, s2, op0, op1, reads, writes):
        nc = self.nc
        if s2 is None:
            self.S.add("dve", lambda: nc.vector.tensor_scalar(out=out, in0=in0, scalar1=s1, scalar2=None,
                                                              op0=op0), reads, writes)
        else:
            self.S.add("dve", lambda: nc.vector.tensor_scalar(out=out, in0=in0, scalar1=s1, scalar2=s2,
                                                              op0=op0, op1=op1), reads, writes)

    def recip(self, out, in_, reads, writes):
        nc = self.nc
        self.S.add("dve", lambda: nc.vector.reciprocal(out=out, in_=in_), reads, writes)

    def mm(self, out, lhsT, rhs, start, stop, reads, writes):
        nc = self.nc
        self.S.add("pe", lambda: nc.tensor.matmul(out, lhsT, rhs, start=start, stop=stop), reads, writes)

    def memset(self, ap, val, writes):
        nc = self.nc
        self.S.add("dve", lambda: nc.vector.memset(ap, val), (), writes)

    @staticmethod
    def slab_order(kind):
        wq, wk, wv, wg = IN_W[kind]
        b = [0, wq // SLABW, (wq + wk) // SLABW, (wq + wk + wv) // SLABW, (wq + wk + wv + wg) // SLABW]
        q, k, v, g = (list(range(b[i], b[i + 1])) for i in range(4))
        return k + v + q + g

    def plan_slabs(self):
        slabs = []
        for ph, l in self.phases:
            if ph == "P":
                for s in self.slab_order(l % 3):
                    slabs.append((self.w_in_d[l], s))
            elif ph == "AE":
                for s in range(D // SLABW):
                    slabs.append((self.w_out_d[l], s))
                for s in range(D // SLABW):
                    slabs.append((self.pe_gate_d[l], s))
        self.slabs = slabs
        self.slab_issued = 0
        self.slab_used = 0

    def _issue_slab(self):
        i = self.slab_issued
        if i >= len(self.slabs):
            return
        w, s = self.slabs[i]
        src = w[:, s * SLABW:(s + 1) * SLABW].rearrange("(kc p) n -> p kc n", p=128)
        self.dma("pool", self.wslab[i % 2][:], src, (), [f"wslab{i % 2}"])
        self.slab_issued += 1

    def next_slab(self):
        i = self.slab_used
        while self.slab_issued <= min(i + 1, len(self.slabs) - 1):
            self._issue_slab()
        self.slab_used += 1
        return self.wslab[i % 2], f"wslab{i % 2}"

    def build(self):
        nc = self.nc
        self.plan_slabs()
        self.memset(self.ones[:], 1.0, ["ones"])
        self.memset(self.zeros[:], 0.0, ["zeros"])
        self.memset(self.zeros128[:], 0.0, ["zeros128"])
        self.dma("sp", self.gpre[:], self.gpre_d.ap(), (), ["gpre"])
        self.dma("sp", self.gpost[:], self.gpost_d.ap(), (), ["gpost"])
        xv = self.x_in.ap().rearrange("(kc p) n -> p kc n", p=128)
        for kc in range(KC):
            self.dma("sp", self.xT[:, kc, :], xv[:, kc, :], (), [f"x{kc}_0", f"x{kc}_1"])
        for ph, l in self.phases:
            if ph == "P":
                self.phase_P(l)
            else:
                kind = l % 3
                if kind == 0:
                    self.attn_A(l)
                elif kind == 1:
                    self.attn_B(l)
                else:
                    self.attn_C(l)
                self.phase_E(l)
        ov = self.x_out.ap().rearrange("(kc p) n -> p kc n", p=128)
        for kc in range(KC):
            self.dma("sp", ov[:, kc, :], self.xT[:, kc, :], [f"x{kc}_0", f"x{kc}_1"], [])
        n = self.S.emit()
        return n

    def stats_rstd(self, tt, src_fn, src_keys, nchunks, inv_n, rstd_i):
        ps = self.ps[7]
        for kc in range(nchunks):
            sqb = self.sq[kc % 2]
            self.act(sqb[:, 0, :], src_fn(kc), AF.Square, src_keys(kc), [f"sq{kc % 2}"])
            self.mm(ps[:], self.ones[:], sqb[:, 0, :], kc == 0, kc == nchunks - 1,
                    ["ones", f"sq{kc % 2}"], ["ps7"])
        r = self.rstd[rstd_i]
        self.recip_act(r[:], ps[:], ["ps7"], [f"rstd{rstd_i}"], in_scale=inv_n, in_bias=EPS, power=-0.5)

    def phase_P(self, l):
        kind = l % 3
        wq, wk, wv, wg = IN_W[kind]
        for tt in range(2):
            sl = slice(tt * 512, (tt + 1) * 512)
            self.stats_rstd(tt, lambda kc: self.xT[:, kc, sl], lambda kc: [f"x{kc}_{tt}"], KC, 1.0 / D, tt)
            for kc in range(KC):
                self.stt(self.hT[:, kc, sl], self.xT[:, kc, sl], self.gpre[:, l * KC + kc:l * KC + kc + 1],
                         self.rstd[tt][:], ALU.mult, ALU.mult,
                         [f"x{kc}_{tt}", "gpre", f"rstd{tt}"], [f"h{kc}_{tt}"])
        order = self.slab_order(kind)
        pending_x = []
        last_k = (wq + wk) // SLABW - 1
        last_v = (wq + wk + wv) // SLABW - 1
        ev = 0
        for s in order:
            c0 = s * SLABW
            slab, skey = self.next_slab()
            if c0 < wq:
                seg, dst, r0 = "q", self.q_w[l], c0
            elif c0 < wq + wk:
                seg, dst, r0 = "k", self.k_w[l], c0 - wq
            elif c0 < wq + wk + wv:
                seg, dst, r0 = "v", self.v_w[l], c0 - wq - wk
            else:
                seg, dst, r0 = "g", self.g_w[l], c0 - wq - wk - wv
            if seg != "v":
                for mi in range(SLABW // 128):
                    st = self.stage[ev % 2]
                    stk = f"stage{ev % 2}"
                    for tt in range(2):
                        sl = slice(tt * 512, (tt + 1) * 512)
                        pi = (ev * 2 + tt) % 4
                        ps = self.ps[pi]
                        for kc in range(KC):
                            self.mm(ps[:], slab[:, kc, mi * 128:(mi + 1) * 128], self.hT[:, kc, sl],
                                    kc == 0, kc == KC - 1, [skey, f"h{kc}_{tt}"], [f"ps{pi}"])
                        if seg == "g":
                            self.act(st[:, sl], ps[:], AF.Silu, [f"ps{pi}"], [stk + f"_{tt}"])
                        elif (ev + tt) % 2 == 0:
                            self.act(st[:, sl], ps[:], AF.Copy, [f"ps{pi}"], [stk + f"_{tt}"])
                        else:
                            nc = self.nc
                            self.S.add("dve", (lambda o=st[:, sl], i=ps[:]: nc.vector.tensor_copy(out=o, in_=i)),
                                       [f"ps{pi}"], [stk + f"_{tt}"])
                    row = r0 + mi * 128
                    self.dma("sp", dst[row:row + 128, :], st[:], [stk + "_0", stk + "_1"], [f"{seg}{l}"])
                    ev += 1
            else:
                for tb in range(8):
                    st = self.stage[ev % 2]
                    stk = f"stage{ev % 2}"
                    pi = ev % 4
                    ps = self.ps[pi]
                    for kc in range(KC):
                        self.mm(ps[:, 0:SLABW], self.hT[:, kc, tb * 128:(tb + 1) * 128], slab[:, kc, :],
                                kc == 0, kc == KC - 1, [skey, f"h{kc}_{tb // 4}"], [f"ps{pi}"])
                    nc = self.nc
                    self.S.add("dve", (lambda o=st[:, 0:SLABW], i=ps[:, 0:SLABW]: nc.vector.tensor_copy(out=o, in_=i)),
                               [f"ps{pi}"], [stk + "_0", stk + "_1"])
                    self.dma("sp", dst[tb * 128:(tb + 1) * 128, r0:r0 + SLABW], st[:, 0:SLABW],
                             [stk + "_0", stk + "_1"], [f"v{l}"])
                    ev += 1
            if self.fused:
                pos = order.index(s)
                if s == last_k:
                    for c in range(self.n_gather_chunks(l, "k")):
                        pending_x.append((pos + 2 * c, "k", c))
                if s == last_v:
                    t0 = max([p for p, _, _ in pending_x] + [pos - 2]) + 2
                    for c in range(self.n_gather_chunks(l, "v")):
                        pending_x.append((max(pos, t0) + 2 * c, "v", c))
                for item in [it for it in pending_x if it[0] <= pos]:
                    pending_x.remove(item)
                    self.exchange_gather(l, item[1], only=item[2])
        for item in pending_x:
            self.exchange_gather(l, item[1], only=item[2])

    def n_gather_chunks(self, l, nm):
        src = self.k_w[l] if nm == "k" else self.v_w[l]
        rows, cols = src.shape
        return rows // min(rows, (2 << 20) // (cols * 2))

    def exchange_gather(self, l, nm, only=None):
        nc = self.nc
        groups = [[2 * i, 2 * i + 1] for i in range(NCORES // 2)]
        src, allt = (self.k_w[l], self.k_all[l]) if nm == "k" else (self.v_w[l], self.v_all[l])
        rows, cols = src.shape
        rc = min(rows, (2 << 20) // (cols * 2))
        for c in range(rows // rc):
            if only is not None and c != only:
                continue
            self.S.add("pool", (lambda s=src[c * rc:(c + 1) * rc, :], d=allt[c * 2 * rc:(c + 1) * 2 * rc, :]:
                                nc.gpsimd.collective_compute(
                "AllGather", mybir.AluOpType.bypass, replica_groups=groups,
                ins=[s.opt()], outs=[d.opt()])), [f"{nm}{l}"], [f"{nm}all{l}"], dma=True, inc=1)

    def gk(self, l, R, f0, c0, c1):
        dk = IN_W[l % 3][1]
        rc = min(dk, (2 << 20) // (T * 2))
        row = ((f0 // rc) * 2 + R) * rc + f0 % rc
        return self.k_all[l][row:row + 128, c0:c1], [f"kall{l}"] if self.fused else []

    def gv(self, l, R, kb0, nkb, d0, dn):
        dv = IN_W[l % 3][2]
        rc = min(T, (2 << 20) // (dv * 2))
        t0 = kb0 * 128
        assert t0 % rc + nkb * 128 <= rc
        row = ((t0 // rc) * 2 + R) * rc + t0 % rc
        return (self.v_all[l][row:row + nkb * 128, d0:d0 + dn].rearrange("(kb p) d -> p kb d", p=128),
                [f"vall{l}"] if self.fused else [])

    def run_pipeline(self, groups, LA, NBUF=2):
        flat = []
        for gi, g in enumerate(groups):
            nt = len(g["tasks"])
            for ti, (s1, s2) in enumerate(g["tasks"]):
                flat.append((gi, s1, s2, ti == nt - 1))
        for gi in range(min(NBUF, len(groups))):
            if groups[gi]["loads"]:
                groups[gi]["loads"]()
        n = len(flat)
        pending = []
        for i in range(n + LA):
            if i < n:
                flat[i][1](i)
            j = i - LA
            if j >= 0:
                gj, _, s2, last = flat[j]
                s2(j)
                if last:
                    stages = groups[gj]["post"] or []
                    for k, (lag, fn) in enumerate(stages):
                        pending.append((j + lag, fn, gj if k == len(stages) - 1 else None))
                still = []
                for due, fn, gdone in pending:
                    if due <= j:
                        fn()
                        if gdone is not None and gdone + NBUF < len(groups) and groups[gdone + NBUF]["loads"]:
                            groups[gdone + NBUF]["loads"]()
                    else:
                        still.append((due, fn, gdone))
                pending = still
        for due, fn, gdone in pending:
            fn()

    def recip_act(self, out, in_, rk, wk, in_scale=1.0, in_bias=0.0, power=-1.0):
        self.act(out, in_, AF.Ln, rk, wk, bias=in_bias, scale=in_scale)
        self.act(out, out, AF.Exp, wk, wk, scale=power)

    def attn_A(self, l):
        j = l // 3
        self.dma("sp", self.sink_s[:], self.sink_d.ap(), (), ["sink_s"])
        self.dma("sp", self.edge[:], self.edge_d.ap(), (), ["edge"])
        tabv = self.tab[:, 0:1536].rearrange("p (j n) -> p j n", j=3)
        for kvh in range(4):
            f0 = kvh * 128
            self.dma("sp", self.kt_own[:], self.k_r[l][f0:f0 + 128, :], [f"k{l}"], ["kt_own"])
            ap, rk = self.gk(l, 0, f0, 896, 1024)
            self.dma("sp", self.kt_oth[:, 0:128], ap, rk, ["kt_oth"])
            ap, rk = self.gk(l, 1, f0, 0, 128)
            self.dma("sp", self.kt_oth[:, 128:256], ap, rk, ["kt_oth"])
            self.dma("sp", self.v_own_s[:, :, 0:128],
                     self.v_r[l][:, f0:f0 + 128].rearrange("(kb p) d -> p kb d", p=128), [f"v{l}"], ["v_own0"])
            ap, rk = self.gv(l, 0, 7, 1, f0, 128)
            self.dma("sp", self.v_oth_s[:, 0:1, 0:128], ap, rk, ["v_oth0"])
            ap, rk = self.gv(l, 1, 0, 1, f0, 128)
            self.dma("sp", self.v_oth_s[:, 1:2, 0:128], ap, rk, ["v_oth0"])
            self.dma("sp", self.tab[:, 0:1536], self.tabA_d[kvh], (), ["tabL"])
            for g in range(4):
                c = j * 16 + kvh * 4 + g
                self.act(self.esk[:, g * 128:(g + 1) * 128], self.zeros128[:], AF.Exp, ["zeros128", "sink_s"], ["esk"],
                         bias=self.sink_s[:, c:c + 1], scale=1.0)
            groups = []
            for qi in range(8):
                par = qi % 2
                slot = qi % 3
                qs = slice(qi * 128, (qi + 1) * 128)
                q4 = (self.qb[0][:, 0:512], self.qb[0][:, 512:1024], self.qb[1][:, 0:512])[slot]
                g4 = (self.gb[:, 0, 0:512], self.gb[:, 0, 512:1024], self.gb[:, 1, 0:512])[slot]
                qk, gk = ("qb0_0", "qb0_1", "qb1")[slot], ("gb0", "gb0", "gb1")[slot]
                po, pz = (4, 5) if par == 0 else (6, 7)

                def loads(qs=qs, q4=q4, g4=g4, qk=qk, gk=gk, kvh=kvh):
                    qsrc = self.q_r[l][kvh * 512:(kvh + 1) * 512, qs].rearrange("(g p) n -> p g n", p=128)
                    self.dma("sp", q4.rearrange("p (g n) -> p g n", g=4), qsrc, [f"q{l}"], [qk])
                    gsrc = self.g_r[l][kvh * 512:(kvh + 1) * 512, qs].rearrange("(g p) n -> p g n", p=128)
                    self.dma("sp", g4.rearrange("p (g n) -> p g n", g=4), gsrc, [f"g{l}"], [gk])

                tasks = []
                for jj in range(3):
                    kb = qi + jj - 1
                    if kb < 0:
                        blk = (self.kt_oth[:, 0:128], "kt_oth", self.v_oth_s[:, 0, 0:128], "v_oth0", 0)
                    elif kb > 7:
                        blk = (self.kt_oth[:, 128:256], "kt_oth", self.v_oth_s[:, 1, 0:128], "v_oth0", 1)
                    else:
                        blk = (self.kt_own[:, kb * 128:(kb + 1) * 128], "kt_own", self.v_own_s[:, kb, 0:128], "v_own0", None)

                    def s1(i, blk=blk, jj=jj, q4=q4, qk=qk):
                        kap, kk, vap, vk, edge = blk
                        b = i % 3
                        self.mm(self.ps[b][:], kap, q4, True, True, [kk, qk], [f"ps{b}"])
                        self.tt("dve", self.tmp[:, b, :], self.ps[b][:], tabv[:, jj, :], ALU.add,
                                [f"ps{b}", "tabL"], [f"tmp{b}"])
                        if edge is None:
                            self.act(self.pt[:, b, :], self.tmp[:, b, :], AF.Exp, [f"tmp{b}"], [f"pt{b}"], scale=SCALE)
                        else:
                            self.act(self.pt[:, b, :], self.tmp[:, b, :], AF.Exp, [f"tmp{b}", "edge"], [f"pt{b}"],
                                     bias=self.edge[:, edge:edge + 1], scale=SCALE)

                    def s2(i, blk=blk, jj=jj, po=po, pz=pz):
                        kap, kk, vap, vk, edge = blk
                        b = i % 3
                        self.mm(self.ps[po][:], vap, self.pt[:, b, :], jj == 0, jj == 2, [vk, f"pt{b}"], [f"ps{po}"])
                        self.mm(self.ps[pz][:], self.ones[:], self.pt[:, b, :], jj == 0, jj == 2,
                                ["ones", f"pt{b}"], [f"ps{pz}"])
                    tasks.append((s1, s2))

                def post_a(pz=pz):
                    self.tt("dve", self.zt[:], self.ps[pz][:], self.esk[:], ALU.add, [f"ps{pz}", "esk"], ["zt"])

                def post_b():
                    self.recip_act(self.rz[:], self.zt[:], ["zt"], ["rz"])

                def post_c(qs=qs, g4=g4, gk=gk, po=po, kvh=kvh, qi=qi):
                    self.tt("dve", self.o_s[:, 0, :], self.ps[po][:], self.rz[:], ALU.mult, [f"ps{po}", "rz"], ["o_s"])
                    okeys = [f"h{kvh * 4 + g}_{qi // 4}" for g in range(4)]
                    self.tt("pool", self.hT[:, kvh * 4:(kvh + 1) * 4, qs],
                            self.o_s[:, 0, :].rearrange("p (g n) -> p g n", g=4),
                            g4.rearrange("p (g n) -> p g n", g=4), ALU.mult, ["o_s", gk], okeys)
                groups.append(dict(loads=loads, tasks=tasks, post=[(LAGS_A[0], post_a), (LAGS_A[1], post_b), (LAGS_A[2], post_c)]))
            self.run_pipeline(groups, 2, NBUF=3)

    def attn_B(self, l):
        self.dma("sp", self.rowmask[:], self.rowmask_d.ap(), (), ["rowmask"])
        self.dma("sp", self.rowind[:], self.rowind_d.ap(), (), ["rowind"])
        self.dma("sp", self.cm64[:], self.colmask_d.ap(), (), ["cm64"])
        self.memset(self.tab[:, 0:1920], 0.0, ["tabL"])
        self.memset(self.tab[:, 1920:3840], 0.0, ["tabH"])
        ktown = [(self.kt_own, ["kt_own"]), (self.qb[1], ["qb1"])]
        ktoth = [(self.kt_oth, ["kt_oth"]), (self.stage[0], ["stage0_0", "stage0_1"])]
        qbuf = [(self.qb[0], ["qb0_0", "qb0_1"]), (self.stage[1], ["stage1_0", "stage1_1"])]
        groups = []
        gidx = 0
        for h in range(16):
            hp = h % 2
            f0 = h * 128
            kto, ktok = ktown[hp]
            ktt, kttk = ktoth[hp]
            qq, qqk = qbuf[hp]
            gg, ggk = self.gb[:, hp, :], f"gb{hp}"
            vo, vok = self.v_own_s[:, :, hp * 128:(hp + 1) * 128], f"v_own{hp}"
            vt, vtk = self.v_oth_s[:, :, hp * 128:(hp + 1) * 128], f"v_oth{hp}"
            tkey = "tabL" if hp == 0 else "tabH"
            TB = self.tab[:, hp * 1920:(hp + 1) * 1920].rearrange("p (j n) -> p j n", j=30)

            def loads(h=h, f0=f0, kto=kto, ktok=ktok, ktt=ktt, kttk=kttk, qq=qq, qqk=qqk, gg=gg, ggk=ggk,
                      vo=vo, vok=vok, vt=vt, vtk=vtk, tkey=tkey, TB=TB):
                self.dma("sp", kto[:], self.k_r[l][f0:f0 + 128, :], [f"k{l}"], ktok)
                ap, rk = self.gk(l, 1, f0, 0, 256)
                self.dma("sp", ktt[:, 0:256], ap, rk, kttk)
                ap, rk = self.gk(l, 0, f0, 768, 1024)
                self.dma("sp", ktt[:, 768:1024], ap, rk, kttk)
                self.dma("sp", vo, self.v_r[l][:, f0:f0 + 128].rearrange("(kb p) d -> p kb d", p=128), [f"v{l}"], [vok])
                ap, rk = self.gv(l, 1, 0, 2, f0, 128)
                self.dma("sp", vt[:, 0:2, :], ap, rk, [vtk])
                ap, rk = self.gv(l, 0, 6, 2, f0, 128)
                self.dma("sp", vt[:, 6:8, :], ap, rk, [vtk])
                self.dma("sp", qq[:], self.q_r[l][f0:f0 + 128, :], [f"q{l}"], qqk)
                self.dma("sp", gg, self.g_r[l][f0:f0 + 128, :], [f"g{l}"], [ggk])
                for a in range(2):
                    src = bass.AP(self.rpr_d, h * 64 * 1905 + 63, [[1904, 64], [127, 15], [1, 64]])
                    self.dma("sp", TB[a * 64:(a + 1) * 64, a + 7:a + 22, :], src, (), [tkey])
                for a in range(2):
                    cmb = bass.AP(self.cm64, a * 64 * 64, [[64, 64], [0, 15], [1, 64]])
                    self.stt(TB[a * 64:(a + 1) * 64, a + 7:a + 22, :], TB[a * 64:(a + 1) * 64, a + 7:a + 22, :],
                             1.0 / SCALE, cmb, ALU.mult, ALU.add, [tkey, "cm64"], [tkey])

            for tt in range(2):
                sl = slice(tt * 512, (tt + 1) * 512)
                if tt == 0:
                    blks = [("own", kb, 2 * kb + 7) for kb in range(6)] + [("oth", kb, 2 * kb - 9) for kb in (6, 7)]
                else:
                    blks = [("own", kb, 2 * kb - 1) for kb in range(2, 8)] + [("oth", kb, 15 + 2 * kb) for kb in (0, 1)]
                nb = len(blks)
                po, pz = (4, 5) if gidx % 2 == 0 else (6, 7)
                tasks = []
                for i_b, (who, kb, d0) in enumerate(blks):
                    pidx = tt * 8 + i_b
                    if who == "own":
                        kap, kk, vap, vk = kto[:, kb * 128:(kb + 1) * 128], ktok, vo[:, kb, :], vok
                    else:
                        kap, kk, vap, vk = ktt[:, kb * 128:(kb + 1) * 128], kttk, vt[:, kb, :], vtk

                    def s1(i, kap=kap, kk=kk, pidx=pidx, d0=d0, qq=qq, qqk=qqk, sl=sl, TB=TB, tkey=tkey):
                        b = i % 4
                        self.mm(self.ps[b][:], kap, qq[:, sl], True, False, list(kk) + list(qqk), [f"ps{b}"])
                        self.mm(self.ps[b][:], self.rowind[:],
                                bass.AP(self.rowmask, pidx * 8, [[128, 2], [1, 8], [0, 64]]), False, True,
                                ["rowind", "rowmask"], [f"ps{b}"])
                        j0 = 21 - d0
                        self.tt("dve", self.tmp[:, b, :], self.ps[b][:],
                                TB[:, j0:j0 + 8, :].rearrange("p j n -> p (j n)"), ALU.add,
                                [f"ps{b}", tkey], [f"tmp{b}"])
                        self.act(self.pt[:, b, :], self.tmp[:, b, :], AF.Exp, [f"tmp{b}"], [f"pt{b}"], scale=SCALE)

                    def s2(i, vap=vap, vk=vk, i_b=i_b, nb=nb, po=po, pz=pz):
                        b = i % 4
                        self.mm(self.ps[po][:], vap, self.pt[:, b, :], i_b == 0, i_b == nb - 1, [vk, f"pt{b}"], [f"ps{po}"])
                        self.mm(self.ps[pz][:], self.ones[:], self.pt[:, b, :], i_b == 0, i_b == nb - 1,
                                ["ones", f"pt{b}"], [f"ps{pz}"])
                    tasks.append((s1, s2))

                def post_a(pz=pz):
                    self.recip_act(self.rz[:], self.ps[pz][:], [f"ps{pz}"], ["rz"])

                def post_b(h=h, sl=sl, tt=tt, gg=gg, ggk=ggk, po=po):
                    self.tt("dve", self.o_s[:, 0, :], self.ps[po][:], self.rz[:], ALU.mult, [f"ps{po}", "rz"], ["o_s"])
                    self.tt("pool", self.hT[:, h, sl], self.o_s[:, 0, :], gg[:, sl], ALU.mult,
                            ["o_s", ggk], [f"h{h}_{tt}"])
                groups.append(dict(loads=loads if tt == 0 else None, tasks=tasks, post=[(1, post_a), (3, post_b)]))
                gidx += 1
        self.run_pipeline(groups, 3)

    def attn_C(self, l):
        nc = self.nc
        lam_init = 0.8 - 0.6 * math.exp(-0.3 * l)
        slopes = alibi(8)
        self.dma("sp", self.lam_s[:], self.lam_d.ap(), (), ["lam_s"])
        self.dma("sp", self.subln_s[:], self.subln_d.ap(), (), ["subln_s"])
        self.dma("sp", self.tab[:, 0:1920], self.Town_d.ap(), (), ["tabL"])
        self.dma("sp", self.tab[:, 1920:3840], self.Toth_d.ap(), (), ["tabH"])
        lt = self.lam_t
        for i in range(2):
            self.tt("dve", self.zt[:, 0:128], self.lam_s[:, (2 * i) * 128:(2 * i + 1) * 128],
                    self.lam_s[:, (2 * i + 1) * 128:(2 * i + 2) * 128], ALU.mult, ["lam_s"], ["zt"])
            self.S.add("dve", (lambda i=i: nc.vector.reduce_sum(out=lt[:, i:i + 1], in_=self.zt[:, 0:128], axis=AX.X)),
                       ["zt"], ["lam_t"])
        self.act(lt[:, 2:4], lt[:, 0:2], AF.Exp, ["lam_t"], ["lam_t"])
        self.tt("dve", lt[:, 4:5], lt[:, 3:4], lt[:, 2:3], ALU.subtract, ["lam_t"], ["lam_t"])
        self.ts(lt[:, 5:6], lt[:, 4:5], -lam_init, None, ALU.add, None, ["lam_t"], ["neglam"])
        self.ts(self.subln_s[:], self.subln_s[:], 1.0 - lam_init, None, ALU.mult, None, ["subln_s"], ["subln_s"])
        neglam = lt[:, 5:6]
        Town = self.tab[:, 0:1920]
        Toth = self.tab[:, 1920:3840]
        ktown = [(self.kt_own, ["kt_own"]), (self.qb[1], ["qb1"])]
        ktoth = [(self.kt_oth, ["kt_oth"]), (self.stage[0], ["stage0_0", "stage0_1"])]
        for h in range(8):
            def late_loads(h=h):
                ap, rk = self.gk(l, 1, h * 256, 0, T)
                self.dma("sp", ktoth[0][0][:], ap, rk, ktoth[0][1])
                for c4 in range(2):
                    ap, rk = self.gv(l, 1, 4 * c4, 4, h * 256, 256)
                    self.dma("sp", self.v_oth_s[:, 4 * c4:4 * c4 + 4, :], ap, rk, ["v_oth0", "v_oth1"])
                f1 = h * 256 + 128
                ap, rk = self.gk(l, 0, f1, 0, T)
                self.dma("sp", ktown[1][0][:], ap, rk, ktown[1][1])
                ap, rk = self.gk(l, 1, f1, 0, T)
                self.dma("sp", ktoth[1][0][:], ap, rk, ktoth[1][1])
                self.dma("sp", self.gb[:], self.g_r[l][h * 256:(h + 1) * 256, :].rearrange("(c p) n -> p c n", p=128),
                         [f"g{l}"], ["gb0", "gb1"])
            ap, rk = self.gk(l, 0, h * 256, 0, T)
            self.dma("sp", ktown[0][0][:], ap, rk, ktown[0][1])
            for c4 in range(2):
                ap, rk = self.gv(l, 0, 4 * c4, 4, h * 256, 256)
                self.dma("sp", self.v_own_s[:, 4 * c4:4 * c4 + 4, :], ap, rk, ["v_own0", "v_own1"])
            groups = []
            gidx = 0
            for tt in range(2):
                sl = slice(tt * 512, (tt + 1) * 512)
                for m in range(2):
                    f0 = h * 256 + m * 128
                    gp = gidx % 2
                    q1 = self.qb[0][:, gp * 512:(gp + 1) * 512]
                    qk = f"qb0_{gp}"
                    po = 4

                    def loads(f0=f0, sl=sl, q1=q1, qk=qk, first=(gidx == 0)):
                        self.dma("sp", q1, self.q_r[l][f0:f0 + 128, sl], [f"q{l}"], [qk])
                        if first:
                            late_loads()

                    tasks = []
                    for i_b in range(16):
                        kb = i_b % 8
                        if i_b < 8:
                            kap, kk, vap, vk, Tt, tk = ktown[m][0][:, kb * 128:(kb + 1) * 128], ktown[m][1], self.v_own_s, ["v_own0", "v_own1"], Town, "tabL"
                        else:
                            kap, kk, vap, vk, Tt, tk = ktoth[m][0][:, kb * 128:(kb + 1) * 128], ktoth[m][1], self.v_oth_s, ["v_oth0", "v_oth1"], Toth, "tabH"
                        off = tt * 512 - kb * 128 + 896

                        def s1(i, kap=kap, kk=kk, Tt=Tt, tk=tk, off=off, q1=q1, qk=qk):
                            b = i % 4
                            self.mm(self.ps[b][:], kap, q1, True, True, list(kk) + [qk], [f"ps{b}"])
                            self.stt(self.tmp[:, b, :], Tt[:, off:off + 512], slopes[h], self.ps[b][:],
                                     ALU.mult, ALU.add, [f"ps{b}", tk], [f"tmp{b}"])
                            self.act(self.pt[:, b, :], self.tmp[:, b, :], AF.Exp, [f"tmp{b}"], [f"pt{b}"], scale=SCALE)

                        def s2(i, vap=vap, vk=vk, kb=kb, i_b=i_b, po=po):
                            b = i % 4
                            for hf in range(2):
                                self.mm(self.ps[po + hf][:], vap[:, kb, hf * 128:(hf + 1) * 128], self.pt[:, b, :],
                                        i_b == 0, i_b == 15, list(vk) + [f"pt{b}"], [f"ps{po + hf}"])
                            self.mm(self.ps[po + 2][:], self.ones[:], self.pt[:, b, :], i_b == 0, i_b == 15,
                                    ["ones", f"pt{b}"], [f"ps{po + 2}"])
                        tasks.append((s1, s2))

                    def post_a(po=po):
                        self.recip_act(self.rz[:], self.ps[po + 2][:], [f"ps{po + 2}"], ["rz"])

                    def post(m=m, po=po, sl=sl, tt=tt):
                        if m == 0:
                            for hf in range(2):
                                self.tt("dve", self.on0[:, hf, :], self.ps[po + hf][:], self.rz[:], ALU.mult,
                                        [f"ps{po + hf}", "rz"], ["on0"])
                            return
                        for hf in range(2):
                            self.tt("dve", self.o_s[:, hf, :], self.ps[po + hf][:], self.rz[:], ALU.mult,
                                    [f"ps{po + hf}", "rz"], ["o_s"])
                        self.stt(self.on0[:], self.o_s[:], neglam, self.on0[:], ALU.mult, ALU.add,
                                 ["o_s", "on0", "neglam"], ["on0"])
                        sqb = self.sq[0]
                        self.act(sqb[:], self.on0[:], AF.Square, ["on0"], ["sq0"])
                        for hf in range(2):
                            self.mm(self.ps[7][:], self.ones[:], sqb[:, hf, :], hf == 0, hf == 1, ["ones", "sq0"], ["ps7"])
                        self.recip_act(self.rz[:], self.ps[7][:], ["ps7"], ["rz"], in_scale=1.0 / 256, in_bias=EPS, power=-0.5)
                        for hf in range(2):
                            self.stt(self.on0[:, hf, :], self.on0[:, hf, :], self.subln_s[:, hf:hf + 1], self.rz[:],
                                     ALU.mult, ALU.mult, ["on0", "subln_s", "rz"], ["on0"])
                        self.tt("pool", self.hT[:, 2 * h:2 * h + 2, sl], self.on0[:], self.gb[:, :, sl], ALU.mult,
                                ["on0", "gb0", "gb1"], [f"h{2 * h}_{tt}", f"h{2 * h + 1}_{tt}"])
                    groups.append(dict(loads=loads, tasks=tasks, post=[(0, post_a), (0, post)]))
                    gidx += 1
            self.run_pipeline(groups, 3)

    def phase_E(self, l):
        nc = self.nc
        ev = 0
        pend = None
        for s in range(D // SLABW):
            slab, skey = self.next_slab()
            for mi in range(SLABW // 128):
                m = s * (SLABW // 128) + mi
                ys = self.ystage[ev % 2]
                ysk = f"ystage{ev % 2}"
                for tt in range(2):
                    sl = slice(tt * 512, (tt + 1) * 512)
                    pi = (ev * 2 + tt) % 4
                    ps = self.ps[pi]
                    for kc in range(KC):
                        self.mm(ps[:], slab[:, kc, mi * 128:(mi + 1) * 128], self.hT[:, kc, sl],
                                kc == 0, kc == KC - 1, [skey, f"h{kc}_{tt}"], [f"ps{pi}"])
                    self.S.add("dve", (lambda o=ys[:, sl], i=ps[:]: nc.vector.tensor_copy(out=o, in_=i)),
                               [f"ps{pi}"], [ysk + f"_{tt}"])
                    sqb = self.sq[tt]
                    self.act(sqb[:, 0, :], ys[:, sl], AF.Square, [ysk + f"_{tt}"], [f"sq{tt}"])
                    if pend is not None:
                        pend()
                    pend = (lambda tt=tt, m=m, sqb=sqb: self.mm(self.ps[6 + tt][:], self.ones[:], sqb[:, 0, :], m == 0,
                                                               m == KC - 1, ["ones", f"sq{tt}"], [f"ps{6 + tt}"]))
                self.dma("sp", self.yT[m * 128:(m + 1) * 128, :], ys[:], [ysk + "_0", ysk + "_1"], ["yT"])
                ev += 1
        pend()
        for tt in range(2):
            r = self.rstd[tt]
            self.recip_act(r[:], self.ps[6 + tt][:], [f"ps{6 + tt}"], [f"rstd{tt}"], in_scale=1.0 / D, in_bias=EPS, power=-0.5)
        for kc in range(KC):
            ys = self.ystage[kc % 2]
            ysk = f"ystage{kc % 2}"
            self.dma("sp", ys[:], self.yT[kc * 128:(kc + 1) * 128, :], ["yT"], [ysk + "_0", ysk + "_1"])
            for tt in range(2):
                sl = slice(tt * 512, (tt + 1) * 512)
                self.stt(ys[:, sl], ys[:, sl], self.gpost[:, l * KC + kc:l * KC + kc + 1], self.rstd[tt][:],
                         ALU.mult, ALU.mult, [ysk + f"_{tt}", "gpost", f"rstd{tt}"], [ysk + f"_{tt}"])
                self.tt("dve", self.xT[:, kc, sl], self.xT[:, kc, sl], ys[:, sl], ALU.add,
                        [f"x{kc}_{tt}", ysk + f"_{tt}"], [f"x{kc}_{tt}"])
                self.act(self.hT[:, kc, sl], self.xT[:, kc, sl], AF.Copy, [f"x{kc}_{tt}"], [f"h{kc}_{tt}"])
        self.dma("pool", self.pTb[:], self.pT_d[l].ap().rearrange("(c p) n -> p c n", p=128), (), ["pTb"])
        ev = 0
        for s in range(D // SLABW):
            slab, skey = self.next_slab()
            pes = self.peslab[s % 2]
            pek = f"peslab{s % 2}"
            if s == 0:
                self.dma("pool", pes[:], self.pe_proj_d[l][:, 0:SLABW].rearrange("(c p) n -> p c n", p=128), (), [pek])
            if s + 1 < D // SLABW:
                s1_ = s + 1
                self.dma("pool", self.peslab[s1_ % 2][:],
                         self.pe_proj_d[l][:, s1_ * SLABW:(s1_ + 1) * SLABW].rearrange("(c p) n -> p c n", p=128),
                         (), [f"peslab{s1_ % 2}"])
            for mi in range(SLABW // 128):
                m = s * (SLABW // 128) + mi
                for tt in range(2):
                    sl = slice(tt * 512, (tt + 1) * 512)
                    pi = (ev * 2 + tt) % 4
                    ps = self.ps[pi]
                    pp = self.ps[4 + (ev * 2 + tt) % 2]
                    ppk = f"ps{4 + (ev * 2 + tt) % 2}"
                    for kc in range(KC):
                        self.mm(ps[:], slab[:, kc, mi * 128:(mi + 1) * 128], self.hT[:, kc, sl],
                                kc == 0, kc == KC - 1, [skey, f"h{kc}_{tt}"], [f"ps{pi}"])
                    for c in range(2):
                        self.mm(pp[:], pes[:, c, mi * 128:(mi + 1) * 128], self.pTb[:, c, sl], c == 0, c == 1,
                                [pek, "pTb"], [ppk])
                    gs = self.gate_st[(ev * 2 + tt) % 2]
                    gk = f"gate{(ev * 2 + tt) % 2}"
                    self.act(gs[:], ps[:], AF.Sigmoid, [f"ps{pi}"], [gk])
                    self.tt("dve", gs[:], gs[:], pp[:], ALU.mult, [gk, ppk], [gk])
                    self.tt("dve", self.xT[:, m, sl], self.xT[:, m, sl], gs[:], ALU.add,
                            [f"x{m}_{tt}", gk], [f"x{m}_{tt}"])
                ev += 1


def _const_tables():
    sl = alibi(16)
    r = np.arange(128)[:, None]
    c = np.arange(128)[None, :]
    tabA = np.zeros((4, 128, 3, 4, 128), np.float32)
    for kvh in range(4):
        for j in range(3):
            rel = c - r + (1 - j) * 128
            ok = np.abs(rel) <= 128
            for g in range(4):
                h = kvh * 4 + g
                tabA[kvh, :, j, g, :] = np.where(ok, -sl[h] * np.abs(rel), -BIG) / SCALE
    tabA = tabA.reshape(4, 128, 1536)
    kx = np.arange(64)[:, None]
    qx = np.arange(64)[None, :]
    cs = np.clip(qx - 8, 0, 48)
    okc = (kx >= cs) & (kx < cs + 16)
    cm = np.where(okc, 0.0, -BIG / SCALE).astype(np.float32)
    colmask = np.ascontiguousarray(np.broadcast_to(cm[None, :, :], (2, 64, 64)).reshape(128, 64))
    rowind = np.zeros((2, 128), np.float32)
    rowind[0, :64] = 1
    rowind[1, 64:] = 1
    u = np.arange(1920)[None, :]
    rr = np.arange(128)[:, None]
    Town = (-np.abs(u - 896 - rr) / SCALE).astype(np.float32)
    Toth = [(-np.abs(dl + u - 896 - rr) / SCALE).astype(np.float32) for dl in (-1024, 1024)]
    return tabA, colmask, rowind.astype(ml_dtypes.bfloat16), Town, Toth


def _rowmask(half):
    rm = np.full((2, 16, 8), -BIG, np.float32)
    for tt in range(2):
        if tt == 0:
            blks = [("own", kb) for kb in range(6)] + [("oth", kb) for kb in (6, 7)]
        else:
            blks = [("own", kb) for kb in range(2, 8)] + [("oth", kb) for kb in (0, 1)]
        for i, (who, kb) in enumerate(blks):
            base = half * 16 if who == "own" else (1 - half) * 16
            for a in range(2):
                ky = base + 2 * kb + a
                for e in range(8):
                    qy = half * 16 + tt * 8 + e
                    rs = min(max(qy - 4, 0), 24)
                    if rs <= ky < rs + 8:
                        rm[a, tt * 8 + i, e] = 0.0
    return rm.reshape(2, 128).astype(ml_dtypes.bfloat16)


def _core_inputs(c, inputs, x_state):
    b, half = c // 2, c % 2
    tabA, colmask, rowind, Town, Toth = _CONST
    ts = slice(half * T, (half + 1) * T)
    rpb = inputs["b_rpb"][0]
    rpr = np.zeros((16, 15, 127), np.float32)
    rpr[:, :, 48:79] = rpb[:, ::-1, ::-1]
    rpr = np.ascontiguousarray(np.broadcast_to(rpr.reshape(16, 1, 1905), (16, 64, 1905)))
    edge = np.zeros((128, 2), np.float32)
    edge[:, half] = -BIG
    m = {
        "x_in": x_state[c],
        "gpre": np.ascontiguousarray(inputs["norm_pre"].reshape(DEPTH, KC, 128).transpose(2, 0, 1).reshape(128, DEPTH * KC)),
        "gpost": np.ascontiguousarray(inputs["norm_post"].reshape(DEPTH, KC, 128).transpose(2, 0, 1).reshape(128, DEPTH * KC)),
    }
    for l in range(DEPTH):
        kind, j = l % 3, l // 3
        m[f"w_in{l}"] = inputs[("a_w_in", "b_w_in", "c_w_in")[kind]][j]
        m[f"pT{l}"] = np.ascontiguousarray(inputs["p"][l, b, ts, :].T)
        m[f"w_out{l}"] = inputs["w_out"][l]
        m[f"pe_proj{l}"] = inputs["pe_proj"][l]
        m[f"pe_gate{l}"] = inputs["pe_gate"][l]
    m.update({
        "a_sink_bc": np.ascontiguousarray(np.broadcast_to(inputs["a_sink"].reshape(1, 32), (128, 32))),
        "tabA": tabA, "edgeA": edge, "rpr": rpr, "colmaskB": colmask,
        "rowmaskB": _rowmask(half), "rowindB": rowind,
        "c_lambda_bc": np.ascontiguousarray(np.broadcast_to(inputs["c_lambda"][0].reshape(1, 512), (128, 512))),
        "c_subln_t": np.ascontiguousarray(inputs["c_subln"][0].reshape(2, 128).T),
        "TR0C": Town if half == 0 else Toth[1], "TR1C": Toth[0] if half == 0 else Town,
    })
    return m


_CONST = _const_tables()
_PROG_CACHE = {}


def _get_prog(phases, fused):
    key = (tuple(phases), fused)
    if key not in _PROG_CACHE:
        p = Prog(list(phases), fused)
        p.build()
        _PROG_CACHE[key] = p
    return _PROG_CACHE[key]


SPLIT_LAUNCHES = [[("P", 0)], [("AE", 0), ("P", 1)], [("AE", 1), ("P", 2)], [("AE", 2), ("P", 3)], [("AE", 3)]]
FUSED_LAUNCH = [("P", 0), ("AE", 0), ("P", 1), ("AE", 1), ("P", 2), ("AE", 2), ("P", 3), ("AE", 3)]
MODE = "fused"


def _gathered(a0, a1):
    rows, cols = a0.shape
    rc = min(rows, (2 << 20) // (cols * 2))
    n = rows // rc
    return np.ascontiguousarray(np.stack([a0.reshape(n, rc, cols), a1.reshape(n, rc, cols)], axis=1).reshape(2 * rows, cols))


def run_launch(phases, fused, inputs, x_state, extra):
    prog = _get_prog(phases, fused)
    in_maps = []
    for c in range(NCORES):
        m = _core_inputs(c, inputs, x_state)
        m.update(extra[c])
        in_maps.append({k: v for k, v in m.items() if k in prog.ext_in})
    res = run_bass_kernel_spmd(prog.nc, in_maps, core_ids=list(range(NCORES)))
    return [{k: np.asarray(r[k]) for k in prog.ext_out} for r in res.results]


def kernel(**inputs):
    inputs = {k: np.asarray(v) for k, v in inputs.items()}
    x = inputs["x"]
    x_state = [np.ascontiguousarray(x[c // 2, (c % 2) * T:(c % 2 + 1) * T, :].T) for c in range(NCORES)]
    if MODE == "fused":
        outs = run_launch(FUSED_LAUNCH, True, inputs, x_state, [{} for _ in range(NCORES)])
        x_state = [o["x_out"] for o in outs]
    else:
        extra = [{} for _ in range(NCORES)]
        for phases in SPLIT_LAUNCHES:
            outs = run_launch(phases, False, inputs, x_state, extra)
            x_state = [o["x_out"] for o in outs]
            extra = [{} for _ in range(NCORES)]
            for ph, l in phases:
                if ph == "P":
                    for c in range(NCORES):
                        o, o0, o1 = outs[c], outs[c & ~1], outs[c | 1]
                        extra[c] = {f"q{l}_i": o[f"q{l}_o"], f"g{l}_i": o[f"g{l}_o"],
                                    f"k{l}_i": o[f"k{l}_o"], f"v{l}_i": o[f"v{l}_o"],
                                    f"kall{l}_i": _gathered(o0[f"k{l}_o"], o1[f"k{l}_o"]),
                                    f"vall{l}_i": _gathered(o0[f"v{l}_o"], o1[f"v{l}_o"])}
    out = np.empty((4, 2048, D), np.float32)
    for c in range(NCORES):
        out[c // 2, (c % 2) * T:(c % 2 + 1) * T, :] = x_state[c].T
    return out
```
